# Optimizing a Trainium2 kernel written in Bass

```python
import math
import jax, jax.numpy as jnp
from jax import lax
import numpy as np

D_MODEL = 1024
BATCH = 8
SEQ = 2048
DEPTH = 2
DEC_BATCH = 128
DEC_SEQ = 4
PAST_LEN = 16384
PAGE_SIZE = 128

HEAD_DIM = 64
A_WIDTH = D_MODEL // 2
N_HEADS = A_WIDTH // HEAD_DIM
DECAY_LORA = max(32, int(round(1.8 * D_MODEL ** 0.5 / 32)) * 32)
AAA_LORA = max(32, int(round(1.8 * D_MODEL ** 0.5 / 32)) * 32)
GATE_LORA = max(32, int(round(0.6 * D_MODEL ** 0.8 / 32)) * 32)
A_PROJ = 3 * A_WIDTH + DECAY_LORA + AAA_LORA + GATE_LORA
A_SPLITS = (A_WIDTH, 2 * A_WIDTH, 3 * A_WIDTH, 3 * A_WIDTH + DECAY_LORA, 3 * A_WIDTH + DECAY_LORA + AAA_LORA)
GN_EPS = 64e-5
POOL_WINDOWS = (2, 4, 8, 16)
N_POOL_GROUPS = len(POOL_WINDOWS)
B_WIDTH = D_MODEL // 2
POOL_GW = B_WIDTH // N_POOL_GROUPS
POOL_HIST = max(POOL_WINDOWS) - 1
P_TOTAL = A_PROJ + B_WIDTH
D_FF = 256 * ((8 * D_MODEL // 3 + 255) // 256)
N_SUB = 3
NORM_EPS = 1e-6

kernel_name = "rwkv7_pool_gated_hybrid_step"


def _rmsnorm(x, g):
    x32 = x.astype(jnp.float32)
    y = x32 * lax.rsqrt(jnp.mean(x32 * x32, axis=-1, keepdims=True) + NORM_EPS) * g.astype(jnp.float32)
    return y.astype(x.dtype)


def _modnorm(x, g, shift, scale):
    return _rmsnorm(x, g) * (1 + scale) + shift


def _swiglu(u, w_in, w_out):
    gate, up = jnp.split(u @ w_in, 2, axis=-1)
    return (jax.nn.silu(gate) * up) @ w_out


def _wkv_step(S, inp):
    r_t, w_t, k_t, v_t, a_t, b_t = inp
    sa = jnp.einsum('bhvk,bhk->bhv', S, a_t)
    S = S * w_t[:, :, None, :] + sa[..., None] * b_t[:, :, None, :] + v_t[..., None] * k_t[:, :, None, :]
    y = jnp.einsum('bhvk,bhk->bhv', S, r_t)
    return S, y


def _rwkv7(pa, shift0, wkv0, mu, w0, w2, a0, a2, g2, k_k, k_a, r_k, ln_w, ln_b):
    f32 = jnp.float32
    Bn, L, _ = pa.shape
    prev = jnp.concatenate([shift0.astype(pa.dtype), pa[:, :-1]], axis=1)
    xs = pa + (prev - pa) * mu
    r, k, v, wl, al, gl = jnp.split(xs, A_SPLITS, axis=-1)
    wlog = -jax.nn.softplus(-(w0 + jnp.tanh(wl) @ w2).astype(f32)) - 0.5
    decay = jnp.exp(-jnp.exp(wlog))
    a = jax.nn.sigmoid((a0 + al @ a2).astype(f32))
    g = jax.nn.sigmoid(gl) @ g2
    hs = lambda t: t.reshape(Bn, L, N_HEADS, HEAD_DIM)
    kk = hs((k * k_k).astype(f32))
    kk = kk / jnp.maximum(jnp.sqrt(jnp.sum(kk * kk, axis=-1, keepdims=True)), 1e-12)
    k = hs(k.astype(f32) * (1 + (a - 1) * k_a.astype(f32)))
    r = hs(r.astype(f32))
    v = hs(v.astype(f32))
    a = hs(a)
    decay = hs(decay)
    tm = lambda t: jnp.moveaxis(t, 1, 0)
    S_T, y = lax.scan(_wkv_step, wkv0.astype(f32),
                      (tm(r), tm(decay), tm(k), tm(v), tm(-kk), tm(kk * a)))
    y = jnp.moveaxis(y, 0, 1)
    mean = jnp.mean(y, axis=-1, keepdims=True)
    var = jnp.mean(jnp.square(y - mean), axis=-1, keepdims=True)
    y = ((y - mean) * lax.rsqrt(var + GN_EPS)).reshape(Bn, L, A_WIDTH)
    y = y * ln_w.astype(f32) + ln_b.astype(f32)
    bonus = jnp.sum(r * k * r_k.astype(f32), axis=-1, keepdims=True) * v
    y = (y + bonus.reshape(Bn, L, A_WIDTH)) * g.astype(f32)
    return y.astype(pa.dtype), S_T, pa[:, -1:]


def _pool(pb, hist, pos, w_pool, pool_scale):
    Bn, L, _ = pb.shape
    xp = jnp.concatenate([hist.astype(pb.dtype), pb], axis=1)
    cs = jnp.cumsum(xp.astype(jnp.float32), axis=1)
    cs = jnp.concatenate([jnp.zeros((Bn, 1, B_WIDTH), jnp.float32), cs], axis=1)
    outs = []
    for gi, w in enumerate(POOL_WINDOWS):
        lo, hi = gi * POOL_GW, (gi + 1) * POOL_GW
        s = cs[:, POOL_HIST + 1:POOL_HIST + 1 + L, lo:hi] - cs[:, POOL_HIST + 1 - w:POOL_HIST + 1 - w + L, lo:hi]
        cnt = jnp.minimum(w, pos + 1).astype(jnp.float32)
        outs.append(s / cnt[None, :, None])
    pooled = jnp.stack(outs, axis=2)
    d = pooled - pb.reshape(Bn, L, N_POOL_GROUPS, POOL_GW).astype(jnp.float32)
    y = jnp.einsum('blgc,gcd->blgd', d.astype(pb.dtype), w_pool).reshape(Bn, L, B_WIDTH)
    return y * pool_scale, xp[:, -POOL_HIST:]


def _trunk(x, c, pos, wkv_in, shift_in, pool_in, W):
    (norm_g, w_mod, b_mod, w_ffn_in, w_ffn_out, w_in, mu_shift, w0, w2, a0, a2, g2, k_k, k_a, r_k,
     ln_x_w, ln_x_b, w_pool, pool_scale, w_br_a, w_br_b, w_gate, b_gate, w_out, final_g) = W
    Bn = x.shape[0]
    new_wkv, new_shift, new_pool = [], [], []
    for l in range(DEPTH):
        mod = (jax.nn.silu(c) @ w_mod[l] + b_mod[l]).reshape(Bn, 3 * N_SUB, D_MODEL)[:, None]
        u = _modnorm(x, norm_g[l, 0], mod[:, :, 0], mod[:, :, 1])
        x = x + 0.5 * mod[:, :, 2] * _swiglu(u, w_ffn_in[l, 0], w_ffn_out[l, 0])
        u = _modnorm(x, norm_g[l, 1], mod[:, :, 3], mod[:, :, 4])
        p = u @ w_in[l]
        ya, S_T, sh = _rwkv7(p[..., :A_PROJ], shift_in[l], wkv_in[l], mu_shift[l], w0[l], w2[l], a0[l],
                             a2[l], g2[l], k_k[l], k_a[l], r_k[l], ln_x_w[l], ln_x_b[l])
        yb, ph = _pool(p[..., A_PROJ:], pool_in[l], pos, w_pool[l], pool_scale[l])
        gates = jax.nn.sigmoid((u @ w_gate[l] + b_gate[l]).astype(jnp.float32)).astype(x.dtype)
        merged = gates[..., :D_MODEL] * (ya @ w_br_a[l]) + gates[..., D_MODEL:] * (yb @ w_br_b[l])
        x = x + mod[:, :, 5] * (merged @ w_out[l])
        u = _modnorm(x, norm_g[l, 2], mod[:, :, 6], mod[:, :, 7])
        x = x + 0.5 * mod[:, :, 8] * _swiglu(u, w_ffn_in[l, 1], w_ffn_out[l, 1])
        new_wkv.append(S_T.astype(x.dtype))
        new_shift.append(sh)
        new_pool.append(ph)
    y = _rmsnorm(x, final_g)
    return y, jnp.stack(new_wkv), jnp.stack(new_shift), jnp.stack(new_pool)


def setup_inputs(seed: int = 0) -> dict:
    key = jax.random.key(seed)
    ks = iter(jax.random.split(key, 48))

    def nrm(shape, scale):
        return jax.random.normal(next(ks), shape, jnp.float32) * scale

    D = D_MODEL
    return {
        "x_prompt": nrm((BATCH, SEQ, D), 1.0),
        "x_sample": nrm((DEC_BATCH, DEC_SEQ, D), 1.0),
        "state_wkv": nrm((DEPTH, DEC_BATCH, N_HEADS, HEAD_DIM, HEAD_DIM), 0.3),
        "state_shift": nrm((DEPTH, DEC_BATCH, 1, A_PROJ), 1.0),
        "state_pool": nrm((DEPTH, DEC_BATCH, POOL_HIST, B_WIDTH), 1.0),
        "c_prompt": nrm((BATCH, D), 1.0),
        "c_sample": nrm((DEC_BATCH, D), 1.0),
        "norm_g": 1.0 + nrm((DEPTH, N_SUB, D), 0.05),
        "w_mod": nrm((DEPTH, D, 3 * N_SUB * D), 0.5 * D ** -0.5),
        "b_mod": nrm((DEPTH, 3 * N_SUB * D), 0.02),
        "w_ffn_in": nrm((DEPTH, 2, D, 2 * D_FF), D ** -0.5),
        "w_ffn_out": nrm((DEPTH, 2, D_FF, D), D_FF ** -0.5),
        "w_in": nrm((DEPTH, D, P_TOTAL), D ** -0.5),
        "mu_shift": jax.random.uniform(next(ks), (DEPTH, A_PROJ), jnp.float32),
        "w0": -1.0 + nrm((DEPTH, A_WIDTH), 1.0),
        "w2": nrm((DEPTH, DECAY_LORA, A_WIDTH), DECAY_LORA ** -0.5),
        "a0": nrm((DEPTH, A_WIDTH), 0.5),
        "a2": nrm((DEPTH, AAA_LORA, A_WIDTH), AAA_LORA ** -0.5),
        "g2": nrm((DEPTH, GATE_LORA, A_WIDTH), GATE_LORA ** -0.5),
        "k_k": 0.85 + nrm((DEPTH, A_WIDTH), 0.05),
        "k_a": 1.0 + nrm((DEPTH, A_WIDTH), 0.05),
        "r_k": nrm((DEPTH, N_HEADS, HEAD_DIM), 0.1),
        "ln_x_w": 1.0 + nrm((DEPTH, A_WIDTH), 0.05),
        "ln_x_b": nrm((DEPTH, A_WIDTH), 0.02),
        "w_pool": nrm((DEPTH, N_POOL_GROUPS, POOL_GW, POOL_GW), POOL_GW ** -0.5),
        "pool_scale": 1.0 + nrm((DEPTH, B_WIDTH), 0.1),
        "w_br_a": nrm((DEPTH, A_WIDTH, D), A_WIDTH ** -0.5),
        "w_br_b": nrm((DEPTH, B_WIDTH, D), B_WIDTH ** -0.5),
        "w_gate": nrm((DEPTH, D, 2 * D), D ** -0.5),
        "b_gate": nrm((DEPTH, 2 * D), 0.02),
        "w_out": nrm((DEPTH, D, D), D ** -0.5),
        "final_g": 1.0 + nrm((D,), 0.05),
    }


def reference(x_prompt, x_sample, state_wkv, state_shift, state_pool, c_prompt, c_sample,
              norm_g, w_mod, b_mod, w_ffn_in, w_ffn_out, w_in, mu_shift, w0, w2, a0, a2, g2,
              k_k, k_a, r_k, ln_x_w, ln_x_b, w_pool, pool_scale, w_br_a, w_br_b, w_gate, b_gate,
              w_out, final_g):
    W = (norm_g, w_mod, b_mod, w_ffn_in, w_ffn_out, w_in, mu_shift, w0, w2, a0, a2, g2, k_k, k_a, r_k,
         ln_x_w, ln_x_b, w_pool, pool_scale, w_br_a, w_br_b, w_gate, b_gate, w_out, final_g)
    bp = x_prompt.shape[0]
    wkv0 = jnp.zeros((DEPTH, bp, N_HEADS, HEAD_DIM, HEAD_DIM), jnp.float32)
    shift0 = jnp.zeros((DEPTH, bp, 1, A_PROJ), x_prompt.dtype)
    pool0 = jnp.zeros((DEPTH, bp, POOL_HIST, B_WIDTH), x_prompt.dtype)
    pos_p = jnp.arange(x_prompt.shape[1], dtype=jnp.int32)
    pos_s = PAST_LEN + jnp.arange(x_sample.shape[1], dtype=jnp.int32)
    y_prompt, wkv_p, shift_p, pool_p = _trunk(x_prompt, c_prompt, pos_p, wkv0, shift0, pool0, W)
    y_sample, wkv_s, shift_s, pool_s = _trunk(x_sample, c_sample, pos_s, state_wkv, state_shift, state_pool, W)
    return (y_prompt, y_sample, wkv_p, shift_p, pool_p, wkv_s, shift_s, pool_s)
```

```python
import numpy as np
from contextlib import ExitStack
import concourse.bass as bass
import concourse.mybir as mybir
from concourse.bass_utils import run_bass_kernel_spmd

F32 = mybir.dt.float32
BF16 = mybir.dt.bfloat16
ALU = mybir.AluOpType
AF = mybir.ActivationFunctionType
AX = mybir.AxisListType


class _Op:
    __slots__ = ("idx", "eng", "fn", "deps", "dma", "inc", "incval", "sem", "waits", "ring_prev", "gidx")

    def __init__(self, idx, eng, fn, deps, dma):
        self.idx, self.eng, self.fn, self.deps, self.dma = idx, eng, fn, deps, dma
        self.inc = False
        self.incval = 0
        self.sem = None
        self.waits = []
        self.ring_prev = None


def _region(ap):
    t = ap.tensor
    name = ap.name
    space = str(ap.space)
    pat = ap.ap
    off = int(ap.offset)
    es = mybir.dt.size(ap.dtype)
    if space == "DRAM":
        lo = off
        hi = off + 1
        for st, cnt in pat:
            hi += abs(int(st)) * (int(cnt) - 1)
        return (name, 0, 1, lo * es, hi * es)
    if "PSUM" in space.upper():
        return (name, 0, 128, 0, 1 << 30)
    shp = list(t.shape)
    pstep = 1
    for s in shp[1:]:
        pstep *= int(s)
    p0 = off // pstep
    f0 = off % pstep
    st0, cnt0 = pat[0]
    if int(st0) == pstep or int(cnt0) == 1:
        npart = int(cnt0)
        rest = pat[1:]
    else:
        npart = 1
        rest = pat
    hi = f0 + 1
    for st, cnt in rest:
        hi += abs(int(st)) * (int(cnt) - 1)
    return (name, p0, p0 + npart, f0 * es, hi * es)


class Sched:
    COMPUTE = ("pe", "act", "dve", "pool")
    RING = 8

    def __init__(self, nc):
        self.nc = nc
        self.ops = []
        self.rec = {}
        self.nd = {"sp": 0, "act": 0, "pool": 0}

    def op(self, eng, fn, reads=(), writes=(), dma=False):
        idx = len(self.ops)
        deps = set()
        rr = [_region(a) for a in reads]
        ww = [_region(a) for a in writes]
        for (name, p0, p1, f0, f1) in rr:
            for r in self.rec.get(name, ()):
                if r[5] and r[0] < p1 and p0 < r[1] and r[2] < f1 and f0 < r[3]:
                    deps.add((r[4], "raw"))
        for (name, p0, p1, f0, f1) in ww:
            for r in self.rec.get(name, ()):
                if r[0] < p1 and p0 < r[1] and r[2] < f1 and f0 < r[3]:
                    deps.add((r[4], "waw" if r[5] else "war"))
        o = _Op(idx, eng, fn, deps, dma)
        self.ops.append(o)
        for (name, p0, p1, f0, f1) in ww:
            lst = self.rec.setdefault(name, [])
            lst[:] = [r for r in lst if not (p0 <= r[0] and r[1] <= p1 and f0 <= r[2] and r[3] <= f1)]
            lst.append([p0, p1, f0, f1, idx, True])
        for (name, p0, p1, f0, f1) in rr:
            lst = self.rec.setdefault(name, [])
            lst[:] = [r for r in lst if not ((not r[5]) and self.ops[r[4]].eng == eng
                                             and (not self.ops[r[4]].dma) and (not dma)
                                             and p0 <= r[0] and r[1] <= p1 and f0 <= r[2] and r[3] <= f1)]
            lst.append([p0, p1, f0, f1, idx, False])
        return o

    NSEM = 12
    CH = 512

    def lower(self, stack):
        nc = self.nc
        ops = self.ops
        for o in ops:
            need = []
            best = {}
            for (d, kind) in o.deps:
                p = ops[d]
                if p.dma:
                    need.append(d)
                elif o.dma or p.eng != o.eng or o.eng != "pe":
                    if p.eng not in best or best[p.eng] < d:
                        best[p.eng] = d
            need.extend(best.values())
            o.deps = need
            for d in need:
                ops[d].inc = True
        self.csem = {e: [stack.enter_context(nc.semaphore("s_%s%d" % (e, i))) for i in range(self.NSEM)]
                     for e in self.COMPUTE}
        self.rings = {q: [stack.enter_context(nc.semaphore("r_%s%d" % (q, i))) for i in range(self.RING)]
                      for q in ("sp", "act", "pool")}
        cnt = {e: 0 for e in self.COMPUTE}
        dk = {"sp": 0, "act": 0, "pool": 0}
        dma_final = {}
        for o in ops:
            if o.dma:
                k = dk[o.eng]
                dk[o.eng] += 1
                o.sem = self.rings[o.eng][k % self.RING]
                o.incval = 16 * (k // self.RING + 1)
                o.ring_prev = (o.sem, 16 * (k // self.RING)) if k >= self.RING else None
                dma_final[(o.eng, k % self.RING)] = (o.sem, o.incval)
                o.gidx = None
            elif o.inc:
                g = cnt[o.eng]
                cnt[o.eng] += 1
                epoch = g // self.CH
                o.sem = self.csem[o.eng][epoch % self.NSEM]
                o.incval = (epoch // self.NSEM) * self.CH + (g % self.CH) + 1
                o.gidx = g
        waited_c = {e: {} for e in ("pe", "act", "dve", "pool", "sp")}
        waited_d = {e: {} for e in ("pe", "act", "dve", "pool", "sp")}
        for o in ops:
            wl = []
            wd = waited_d[o.eng]
            wc = waited_c[o.eng]
            if o.dma and o.ring_prev is not None:
                sem, val = o.ring_prev
                if wd.get(id(sem), 0) < val:
                    wd[id(sem)] = val
                    wl.append((sem, val))
            for d in o.deps:
                p = ops[d]
                if p.dma:
                    if wd.get(id(p.sem), 0) < p.incval:
                        wd[id(p.sem)] = p.incval
                        wl.append((p.sem, p.incval))
                else:
                    if wc.get(p.eng, -1) < p.gidx:
                        wc[p.eng] = p.gidx
                        wl.append((p.sem, p.incval))
            o.waits = wl
        self.final_waits = list(dma_final.values())
        per = {e: [] for e in ("pe", "act", "dve", "pool", "sp")}
        for o in ops:
            per[o.eng].append(o)

        def run(engobj, lst, final=False):
            for o in lst:
                for (sem, val) in o.waits:
                    engobj.wait_ge(sem, val)
                ins = o.fn(engobj)
                if o.dma:
                    ins.then_inc(o.sem, 16)
                elif o.inc:
                    ins.then_inc(o.sem, 1)
            if final:
                for (sem, val) in self.final_waits:
                    engobj.wait_ge(sem, val)

        with nc.Block() as block:
            @block.tensor
            def _(e):
                run(e, per["pe"])

            @block.scalar
            def _(e):
                run(e, per["act"])

            @block.vector
            def _(e):
                run(e, per["dve"])

            @block.gpsimd
            def _(e):
                run(e, per["pool"])

            @block.sync
            def _(e):
                run(e, per["sp"], final=True)


NCORES = 8
D = 1024
KC = 8
TP = 2048
NSEQ = 16
DEC = 4
TS = NSEQ * DEC
T = TP + TS
APJ = 1824
PT = 2336
DFF = 2816
NFC = DFF // 128
LN_EPS = 64e-5
NORM_EPS = 1e-6
DECAY_K = float(np.exp(-0.5))

WNAMES = ["norm_g", "w_mod", "b_mod", "w_ffn_in", "w_ffn_out", "w_in", "mu_shift", "w0", "w2", "a0", "a2", "g2",
          "k_k", "k_a", "r_k", "ln_x_w", "ln_x_b", "w_pool", "pool_scale", "w_br_a", "w_br_b", "w_gate", "b_gate",
          "w_out", "final_g"]
WSHAPES = {
    "norm_g": [2, 3, D], "w_mod": [2, D, 9 * D], "b_mod": [2, 9 * D], "w_ffn_in": [2, 2, D, 2 * DFF],
    "w_ffn_out": [2, 2, DFF, D], "w_in": [2, D, PT], "mu_shift": [2, APJ], "w0": [2, 512], "w2": [2, 64, 512],
    "a0": [2, 512], "a2": [2, 64, 512], "g2": [2, 160, 512], "k_k": [2, 512], "k_a": [2, 512], "r_k": [2, 8, 64],
    "ln_x_w": [2, 512], "ln_x_b": [2, 512], "w_pool": [2, 4, 128, 128], "pool_scale": [2, 512],
    "w_br_a": [2, 512, D], "w_br_b": [2, 512, D], "w_gate": [2, D, 2 * D], "b_gate": [2, 2 * D], "w_out": [2, D, D],
    "final_g": [D],
}


def _make_consts():
    cols = {}
    parts = []
    pos = [0]

    def add(name, arr):
        a = np.zeros((128, arr.shape[1]), np.float32)
        a[:arr.shape[0]] = arr
        cols[name] = (pos[0], arr.shape[1])
        pos[0] += arr.shape[1]
        parts.append(a)

    p = np.arange(128)
    add("ident", np.eye(128, dtype=np.float32))
    add("ones", np.ones((128, 128), np.float32))
    same = (p[:, None] // 64) == (p[None, :] // 64)
    add("bones", same.astype(np.float32))
    s = p[:, None] % 64
    t = p[None, :] % 64
    add("msu64", (same & (s < t)).astype(np.float32))
    add("msuT64", (same & (s > t)).astype(np.float32))
    add("mu64", (same & (s <= t)).astype(np.float32))
    add("istack64", (p[:, None] % 64 == np.arange(64)[None, :]).astype(np.float32))
    add("tokmask64", (p[:, None] // 64 == np.arange(2)[None, :]).astype(np.float32))
    q = np.arange(8)
    same4 = (q[:, None] // 4) == (q[None, :] // 4)
    s4 = q[:, None] % 4
    t4 = q[None, :] % 4
    add("msu4", (same4 & (s4 < t4)).astype(np.float32))
    add("msuT4", (same4 & (s4 > t4)).astype(np.float32))
    add("mu4", (same4 & (s4 <= t4)).astype(np.float32))
    add("istack4", (q[:, None] % 4 == np.arange(8)[None, :]).astype(np.float32))
    add("tokmask4", (q[:, None] // 4 == np.arange(2)[None, :]).astype(np.float32))
    tt = np.arange(128)
    add("start64", np.broadcast_to((tt % 64 == 0).astype(np.float32)[None, :], (128, 128)).copy())
    add("nstart64", np.broadcast_to((tt % 64 != 0).astype(np.float32)[None, :], (128, 128)).copy())
    add("start4", np.broadcast_to((tt[:64] % 4 == 0).astype(np.float32)[None, :], (128, 64)).copy())
    add("nstart4", np.broadcast_to((tt[:64] % 4 != 0).astype(np.float32)[None, :], (128, 64)).copy())
    ratio = np.zeros((4, 15), np.float32)
    for g, w in enumerate((2, 4, 8, 16)):
        for i in range(15):
            ratio[g, i] = w / min(w, i + 1)
    add("ratio", np.broadcast_to(ratio.reshape(1, 60), (128, 60)).copy())
    return np.concatenate(parts, axis=1), cols


DBG = {}
CONSTS_NP, CCOLS = _make_consts()
NCC = CONSTS_NP.shape[1]

VR = {}
_r = 0
for _n, _k in (("norm_g", 24), ("mu", 15), ("w0", 4), ("a0", 4), ("k_k", 4), ("k_a", 4), ("r_k", 4), ("ln_w", 4),
               ("ln_b", 4), ("pool_scale", 4), ("b_gate", 16), ("final_g", 8)):
    VR[_n] = (_r, _k)
    _r += _k
NVR = _r


def _prod(s):
    r = 1
    for v in s:
        r *= int(v)
    return r


def _view(ap2, shape):
    if len(shape) == 1:
        return ap2
    names = "abcdef"[:len(shape)]
    kw = {names[i]: int(shape[i]) for i in range(len(shape))}
    return ap2.rearrange("p (%s) -> p %s" % (" ".join(names), " ".join(names)), **kw)


class Arena:
    def __init__(self, base_bf16, nelem):
        self.base = base_bf16
        self.n = nelem
        self.off = 0
        self.peak = 0
        self.log = []

    def reset(self, off=0):
        self.off = off

    def alloc(self, shape, dtype):
        n = _prod(shape)
        nb = n * 2 if dtype == F32 else n
        off = (self.off + 15) // 16 * 16
        assert off + nb <= self.n, "arena overflow: need %d have %d" % (off + nb, self.n)
        v = self.base[:, off:off + nb]
        if dtype == F32:
            v = v.bitcast(F32)
        self.off = off + nb
        self.peak = max(self.peak, self.off)
        self.log.append((off, tuple(shape), "f32" if dtype == F32 else "bf16"))
        return _view(v, shape)


class KB:
    def __init__(self, nc, S, banks):
        self.nc, self.S, self.banks = nc, S, banks
        self.bi = 0
        self.flip = 0

    def bank(self):
        b = self.banks[self.bi % len(self.banks)]
        self.bi += 1
        return b

    def mm(self, out, lhsT, rhs, start=True, stop=True):
        self.S.op("pe", lambda e: e.matmul(out, lhsT=lhsT, rhs=rhs, start=start, stop=stop),
                  reads=[lhsT, rhs], writes=[out])

    def dma(self, q, out, in_):
        self.S.op(q, lambda e: e.dma_start(out=out, in_=in_), reads=[in_], writes=[out], dma=True)

    def tt(self, out, in0, in1, op, eng="dve"):
        self.S.op(eng, lambda e: e.tensor_tensor(out=out, in0=in0, in1=in1, op=op), reads=[in0, in1], writes=[out])

    def ts(self, out, in0, s1, s2, op0, op1=None, eng="dve"):
        rd = [in0] + [s for s in (s1, s2) if not isinstance(s, (int, float)) and s is not None]
        if op1 is None:
            self.S.op(eng, lambda e: e.tensor_scalar(out=out, in0=in0, scalar1=s1, scalar2=None, op0=op0),
                      reads=rd, writes=[out])
        else:
            self.S.op(eng, lambda e: e.tensor_scalar(out=out, in0=in0, scalar1=s1, scalar2=s2, op0=op0, op1=op1),
                      reads=rd, writes=[out])

    def stt(self, out, in0, scalar, in1, op0, op1, eng="dve"):
        rd = [in0, in1] + ([] if isinstance(scalar, (int, float)) else [scalar])
        self.S.op(eng, lambda e: e.scalar_tensor_tensor(out=out, in0=in0, scalar=scalar, in1=in1, op0=op0, op1=op1),
                  reads=rd, writes=[out])

    def act(self, out, in_, func, bias=None, scale=None):
        rd = [in_] + [s for s in (bias, scale) if s is not None and not isinstance(s, (int, float))]
        kw = {}
        if bias is not None:
            kw["bias"] = bias
        if scale is not None:
            kw["scale"] = scale
        self.S.op("act", lambda e: e.activation(out=out, in_=in_, func=func, **kw), reads=rd, writes=[out])

    def cp(self, out, in_, eng=None):
        if eng is None:
            self.flip ^= 1
            eng = "act" if self.flip else "dve"
        if eng == "act":
            self.S.op("act", lambda e: e.activation(out=out, in_=in_, func=AF.Copy), reads=[in_], writes=[out])
        else:
            self.S.op(eng, lambda e: e.tensor_copy(out=out, in_=in_), reads=[in_], writes=[out])

    def recip(self, out, in_):
        self.S.op("dve", lambda e: e.reciprocal(out=out, in_=in_), reads=[in_], writes=[out])

    def memset(self, out, val, eng="dve"):
        self.S.op(eng, lambda e: e.memset(out, val), writes=[out])

    def reduce_sum(self, out, in_):
        self.S.op("dve", lambda e: e.tensor_reduce(out=out, in_=in_, axis=AX.X, op=ALU.add), reads=[in_], writes=[out])

    def scan(self, out, d0, d1, init):
        self.S.op("dve", lambda e: e.tensor_tensor_scan(out=out, data0=d0, data1=d1, initial=init, op0=ALU.mult,
                                                        op1=ALU.add), reads=[d0, d1], writes=[out])


def build_program(stop=None, dbg=False, nlayers=2):
    nc = bass.Bass("TRN2", target_bir_lowering=False)
    I = {}

    def din(name, shape):
        I[name] = nc.dram_tensor(name, list(shape), F32, kind="ExternalInput").ap()

    din("xin", [T, D]); din("cin", [17, D]); din("swkv", [2, NSEQ, 8, 64, 64]); din("sshift", [2, NSEQ, APJ])
    din("spool", [2, NSEQ, 15, 512]); din("consts", [128, NCC])
    for n in WNAMES:
        din(n, WSHAPES[n])
    O = {}

    def dout(name, shape):
        O[name] = nc.dram_tensor(name, list(shape), F32, kind="ExternalOutput").ap()

    dout("y", [T, D]); dout("wkv_p", [2, 8, 64, 64]); dout("shift_p", [2, APJ]); dout("pool_p", [2, 15, 512])
    dout("wkv_s", [2, NSEQ, 8, 64, 64]); dout("shift_s", [2, NSEQ, APJ]); dout("pool_s", [2, NSEQ, 15, 512])
    if dbg:
        dout("dbgX", [128, KC, T]); dout("dbgA", [128, 8, T])
    YAB = nc.dram_tensor("yab_scratch", [128, 8, T], BF16, kind="Internal").ap()

    ARN = 59392
    with ExitStack() as st:
        def sb(name, shape, dt):
            return st.enter_context(nc.sbuf_tensor(name, list(shape), dt))

        X = sb("X", [128, KC, T], F32)
        CF = sb("CF", [128, NCC], F32)
        CB = sb("CB", [128, NCC], BF16)
        ARt = sb("AR", [128, ARN], BF16)
        MOD = sb("MOD", [128, 72, 17], F32)
        VEC = sb("VEC", [128, NVR], F32)
        BM = sb("BM", [128, 72], F32)
        SCT = sb("SCT", [128, 8, 17], F32)
        GSp = sb("GSp", [128, 3, 8], F32); SHp = sb("SHp", [128, 3, 8], F32); COp = sb("COp", [128, 3, 8], F32)
        GSs = sb("GSs", [128, 3, 8, 16], F32); SHs = sb("SHs", [128, 3, 8, 16], F32); COs = sb("COs", [128, 3, 8, 16], F32)
        OMKA = sb("OMKA", [128, 4], F32)
        TMS = sb("TMS", [128, 64], F32)
        banks = [st.enter_context(nc.psum_tensor("ps%d" % i, [128, 512], F32)) for i in range(8)]
        S = Sched(nc)
        kb = KB(nc, S, banks)
        ar = Arena(ARt[:], ARN)

        def cf(name):
            c0, n = CCOLS[name]
            return CF[:, c0:c0 + n]

        def cb(name):
            c0, n = CCOLS[name]
            return CB[:, c0:c0 + n]

        def vec(name):
            r0, k = VR[name]
            return VEC[:, r0:r0 + k]

        kb.dma("sp", CF[:], I["consts"])
        kb.cp(CB[:], CF[:], eng="dve")

        ar.reset()
        XT = [ar.alloc([D], F32) for _ in range(2)]
        CROW = ar.alloc([D], F32)
        for i in range(17):
            n = 128 if i < 16 else 64
            xt = XT[i % 2]
            kb.dma("sp", xt[0:n, :], I["xin"][i * 128:i * 128 + n, :])
            for half in range(2):
                bk = kb.bank()
                for cc in range(4):
                    c = half * 4 + cc
                    kb.mm(bk[:, cc * 128:cc * 128 + n], lhsT=xt[0:n, c * 128:(c + 1) * 128], rhs=cf("ident")[0:n, 0:n])
                kb.cp(X[:, half * 4:half * 4 + 4, i * 128:i * 128 + n], _view(bk[:, 0:512], [4, 128])[:, :, 0:n])
        kb.dma("sp", CROW[0:17, :], I["cin"])
        kb.act(CROW[0:17, :], CROW[0:17, :], AF.Silu)
        bk = kb.bank()
        for c in range(8):
            kb.mm(bk[:, c * 17:(c + 1) * 17], lhsT=CROW[0:17, c * 128:(c + 1) * 128], rhs=cf("ident")[0:17, 0:17])
        kb.cp(SCT[:], _view(bk[:, 0:136], [8, 17]), eng="dve")

        def load_layer_vectors(l):
            ar.reset()
            ROWS = ar.alloc([128], F32)
            BMR = ar.alloc([128], F32)
            WM = [ar.alloc([8, 512], F32) for _ in range(2)]
            kb.memset(ROWS[:], 0.0)

            def rows(name, src):
                r0, k = VR[name]
                kb.dma("sp", ROWS[r0:r0 + k, :], src)

            rows("norm_g", I["norm_g"][l].rearrange("j (c p) -> (j c) p", p=128))
            r0, _ = VR["mu"]
            kb.dma("sp", ROWS[r0:r0 + 14, :], I["mu_shift"][l, 0:1792].rearrange("(c p) -> c p", p=128))
            kb.dma("sp", ROWS[r0 + 14:r0 + 15, 0:32], I["mu_shift"][l:l + 1, 1792:1824])
            for nm, src in (("w0", "w0"), ("a0", "a0"), ("k_k", "k_k"), ("k_a", "k_a"), ("ln_w", "ln_x_w"),
                            ("ln_b", "ln_x_b"), ("pool_scale", "pool_scale")):
                rows(nm, I[src][l].rearrange("(c p) -> c p", p=128))
            rows("r_k", I["r_k"][l].rearrange("(c h) k -> c (h k)", h=2))
            rows("b_gate", I["b_gate"][l].rearrange("(c p) -> c p", p=128))
            rows("final_g", I["final_g"].rearrange("(c p) -> c p", p=128))
            bk = kb.bank()
            kb.mm(bk[:, 0:NVR], lhsT=ROWS[0:NVR, :], rhs=cf("ident")[0:NVR, 0:NVR])
            kb.cp(VEC[:], bk[:, 0:NVR], eng="dve")
            kb.dma("sp", BMR[0:72, :], I["b_mod"][l].rearrange("(c p) -> c p", p=128))
            bk = kb.bank()
            kb.mm(bk[:, 0:72], lhsT=BMR[0:72, :], rhs=cf("ident")[0:72, 0:72])
            kb.cp(BM[:], bk[:, 0:72], eng="dve")
            kb.ts(OMKA[:], vec("k_a"), -1.0, 1.0, ALU.mult, ALU.add)
            wm = I["w_mod"][l].rearrange("(k p) n -> p k n", p=128)
            kb.dma("sp", WM[0][:], wm[:, :, 0:512])
            bk = None
            for blk in range(18):
                if blk + 1 < 18:
                    kb.dma("sp", WM[(blk + 1) % 2][:], wm[:, :, (blk + 1) * 512:(blk + 2) * 512])
                w = WM[blk % 2]
                for oc in range(4):
                    mc = blk * 4 + oc
                    if mc % 24 == 0:
                        bk = kb.bank()
                    o = bk[:, (mc % 24) * 17:(mc % 24) * 17 + 17]
                    for k in range(8):
                        kb.mm(o, lhsT=w[:, k, oc * 128:(oc + 1) * 128], rhs=SCT[:, k, :], start=(k == 0), stop=(k == 7))
                    if mc % 24 == 23:
                        g = mc // 24
                        kb.tt(MOD[:, g * 24:(g + 1) * 24, :], _view(bk[:, 0:408], [24, 17]),
                              BM[:, g * 24:(g + 1) * 24, None].to_broadcast([128, 24, 17]), ALU.add)
            MODv = MOD[:].rearrange("p (j k c) s -> p j k c s", j=3, k=3)
            NG = _view(vec("norm_g"), [3, 8])
            kb.ts(GSp[:], MODv[:, :, 1, :, 0], 1.0, None, ALU.add)
            kb.tt(GSp[:], GSp[:], NG, ALU.mult)
            kb.cp(SHp[:], MODv[:, :, 0, :, 0], eng="dve")
            kb.cp(COp[:], MODv[:, :, 2, :, 0], eng="dve")
            kb.ts(COp[:, 0, :], COp[:, 0, :], 0.5, None, ALU.mult)
            kb.ts(COp[:, 2, :], COp[:, 2, :], 0.5, None, ALU.mult)
            for j in range(3):
                kb.ts(GSs[:, j], MODv[:, j, 1, :, 1:17], 1.0, None, ALU.add)
                kb.tt(GSs[:, j], GSs[:, j], NG[:, j, :, None].to_broadcast([128, 8, 16]), ALU.mult)
                kb.cp(SHs[:, j], MODv[:, j, 0, :, 1:17], eng="dve")
                kb.ts(COs[:, j], MODv[:, j, 2, :, 1:17], (1.0 if j == 1 else 0.5), None, ALU.mult)

        def modnorm(tok0, n, j, U, SQ, RS, TT):
            is_s = tok0 >= TP
            kb.act(SQ[:, :, 0:n], X[:, :, tok0:tok0 + n], AF.Square)
            bk = kb.bank()
            for c in range(8):
                kb.mm(bk[:, 0:n], lhsT=cb("ones"), rhs=SQ[:, c, 0:n], start=(c == 0), stop=(c == 7))
            kb.act(RS[:, 0:n], bk[:, 0:n], AF.Sqrt, bias=NORM_EPS, scale=1.0 / D)
            kb.recip(RS[:, 0:n], RS[:, 0:n])
            for c in range(8):
                t = TT[c % 2]
                if not is_s:
                    kb.stt(t[:, 0:n], X[:, c, tok0:tok0 + n], GSp[:, j, c:c + 1], RS[:, 0:n], ALU.mult, ALU.mult)
                    kb.act(U[:, c, 0:n], t[:, 0:n], AF.Identity, bias=SHp[:, j, c:c + 1], scale=1.0)
                else:
                    tv = _view(t[:, 0:n], [16, 4])
                    kb.tt(tv, _view(X[:, c, tok0:tok0 + n], [16, 4]), GSs[:, j, c, :, None].to_broadcast([128, 16, 4]), ALU.mult)
                    kb.tt(t[:, 0:n], t[:, 0:n], RS[:, 0:n], ALU.mult)
                    kb.tt(_view(U[:, c, 0:n], [16, 4]), tv, SHs[:, j, c, :, None].to_broadcast([128, 16, 4]), ALU.add)

        def resid(m, tok0, n, bo, j):
            if tok0 < TP:
                kb.stt(X[:, m, tok0:tok0 + n], bo[:, 0:n], COp[:, j, m:m + 1], X[:, m, tok0:tok0 + n], ALU.mult, ALU.add)
            else:
                kb.tt(_view(TMS[:, 0:n], [16, 4]), _view(bo[:, 0:n], [16, 4]),
                      COs[:, j, m, :, None].to_broadcast([128, 16, 4]), ALU.mult)
                kb.tt(X[:, m, tok0:tok0 + n], X[:, m, tok0:tok0 + n], TMS[:, 0:n], ALU.add)

        TILES = [(0, 512), (512, 512), (1024, 512), (1536, 512), (2048, 64)]

        def ffn(l, f):
            j = 0 if f == 0 else 2
            ar.reset()
            U = ar.alloc([8, T], BF16)
            WG = [ar.alloc([8, 512], BF16) for _ in range(2)]
            WU = [ar.alloc([8, 512], BF16) for _ in range(2)]
            WO = [ar.alloc([4, 1024], BF16) for _ in range(2)]
            H = [ar.alloc([4, 512], BF16) for _ in range(2)]
            SG = [ar.alloc([512], BF16) for _ in range(2)]
            SQ = ar.alloc([8, 512], BF16)
            RS = ar.alloc([512], F32)
            TT = [ar.alloc([512], F32) for _ in range(2)]
            win = I["w_ffn_in"][l, f].rearrange("(k p) n -> p k n", p=128)
            wout = I["w_ffn_out"][l, f].rearrange("(j p) n -> p j n", p=128)
            groups = [(0, 4), (4, 4), (8, 4), (12, 4), (16, 4), (20, 2)]

            def load(gi):
                c0, ng = groups[gi]
                b = gi % 2
                kb.dma("pool", WG[b][:, :, 0:ng * 128], win[:, :, c0 * 128:(c0 + ng) * 128])
                kb.dma("pool", WU[b][:, :, 0:ng * 128], win[:, :, DFF + c0 * 128:DFF + (c0 + ng) * 128])
                kb.dma("pool", WO[b][:, 0:ng, :], wout[:, c0:c0 + ng, :])

            load(0)
            for (t0, n) in TILES:
                modnorm(t0, n, j, U[:, :, t0:t0 + n], SQ, RS, TT)
            hb = 0
            for gi, (c0, ng) in enumerate(groups):
                if gi + 1 < len(groups):
                    load(gi + 1)
                b = gi % 2
                for (t0, n) in TILES:
                    h = H[hb % 2]
                    hb += 1
                    for jj in range(ng):
                        bg = kb.bank()
                        bu = kb.bank()
                        for k in range(8):
                            kb.mm(bg[:, 0:n], lhsT=WG[b][:, k, jj * 128:(jj + 1) * 128], rhs=U[:, k, t0:t0 + n],
                                  start=(k == 0), stop=(k == 7))
                        for k in range(8):
                            kb.mm(bu[:, 0:n], lhsT=WU[b][:, k, jj * 128:(jj + 1) * 128], rhs=U[:, k, t0:t0 + n],
                                  start=(k == 0), stop=(k == 7))
                        sg = SG[jj % 2]
                        kb.act(sg[:, 0:n], bg[:, 0:n], AF.Silu)
                        kb.tt(h[:, jj, 0:n], bu[:, 0:n], sg[:, 0:n], ALU.mult)
                    for m in range(8):
                        bo = kb.bank()
                        for jj in range(ng):
                            kb.mm(bo[:, 0:n], lhsT=WO[b][:, jj, m * 128:(m + 1) * 128], rhs=h[:, jj, 0:n],
                                  start=(jj == 0), stop=(jj == ng - 1))
                        resid(m, t0, n, bo, j)

        def final_out():
            ar.reset()
            YT = [ar.alloc([D], F32) for _ in range(2)]
            SQ = ar.alloc([8, 128], BF16)
            RS = ar.alloc([128], F32)
            YN = [ar.alloc([8, 128], F32) for _ in range(2)]
            FG = vec("final_g")
            for i in range(17):
                n = 128 if i < 16 else 64
                t0 = i * 128
                yn = YN[i % 2]
                kb.act(SQ[:, :, 0:n], X[:, :, t0:t0 + n], AF.Square)
                bk = kb.bank()
                for c in range(8):
                    kb.mm(bk[:, 0:n], lhsT=cb("ones"), rhs=SQ[:, c, 0:n], start=(c == 0), stop=(c == 7))
                kb.act(RS[:, 0:n], bk[:, 0:n], AF.Sqrt, bias=NORM_EPS, scale=1.0 / D)
                kb.recip(RS[:, 0:n], RS[:, 0:n])
                for c in range(8):
                    kb.stt(yn[:, c, 0:n], X[:, c, t0:t0 + n], FG[:, c:c + 1], RS[:, 0:n], ALU.mult, ALU.mult)
                yt = YT[i % 2]
                for half in range(2):
                    bk = kb.bank()
                    for cc in range(4):
                        c = half * 4 + cc
                        kb.mm(bk[0:n, cc * 128:(cc + 1) * 128], lhsT=yn[:, c, 0:n], rhs=cf("ident"))
                    kb.cp(yt[0:n, half * 512:(half + 1) * 512], bk[0:n, 0:512])
                kb.dma("sp", O["y"][t0:t0 + n, :], yt[0:n, :])

        def dbg_dump_x():
            if dbg:
                kb.dma("sp", O["dbgX"], X[:])

        mixer = _make_mixer(nc, kb, ar, I, O, X, YAB, cf, cb, vec, modnorm, resid, OMKA, TILES, dbg)

        for l in range(nlayers):
            load_layer_vectors(l)
            ffn(l, 0)
            if stop == "ffn0":
                break
            mixer(l, only_a=(stop == "passA"))
            if stop in ("mix0", "passA"):
                break
            ffn(l, 1)
        dbg_dump_x()
        final_out()
        S.lower(st)
        print("ops", len(S.ops), "arena peak KiB", ar.peak * 2 / 1024.0)
    return nc


def _make_mixer(nc, kb, ar, I, O, X, YAB, cf, cb, vec, modnorm, resid, OMKA, TILES, dbg):
    NB = 64

    def bc(ap, shape):
        return ap.to_broadcast(list(shape))

    def pass_a(l):
        ar.reset()
        WIN = ar.alloc([8, APJ], BF16)
        W2T = ar.alloc([512], BF16)
        A2T = ar.alloc([512], BF16)
        G2T = ar.alloc([2, 512], BF16)
        kb.dma("pool", WIN, I["w_in"][l].rearrange("(k p) n -> p k n", p=128)[:, :, 0:APJ])
        kb.memset(W2T[:], 0.0)
        kb.memset(A2T[:], 0.0)
        kb.dma("pool", W2T[0:64, :], I["w2"][l])
        kb.dma("pool", A2T[64:128, :], I["a2"][l])
        kb.dma("pool", G2T[:, 0, :], I["g2"][l, 0:128, :])
        kb.dma("pool", G2T[0:32, 1, :], I["g2"][l, 128:160, :])
        UB = ar.alloc([8, NB], BF16)
        SQn = ar.alloc([8, NB], BF16)
        RSn = ar.alloc([NB], F32)
        TTn = [ar.alloc([NB], F32) for _ in range(2)]
        PA = ar.alloc([15, 80], F32)
        XS = ar.alloc([15, NB], F32)
        LAST = ar.alloc([15], F32)
        SHS = ar.alloc([15, 16], F32)
        SHO = ar.alloc([15, 16], F32)
        LIN = ar.alloc([3, NB], BF16)
        f4 = lambda: ar.alloc([4, NB], F32)
        t_wd, t_p, t_pex, t_ip, t_aa, t_gg, t_kk, t_t1, t_t2, t_k2, t_bon = [f4() for _ in range(11)]
        SQK = ar.alloc([4, NB], BF16)
        RKR = ar.alloc([4, NB], BF16)
        PC = ar.alloc([64], F32)
        blk = lambda: ar.alloc([512], BF16)
        AT, RT, BT, KT, BH, KH, VB = [blk() for _ in range(7)]
        NM, NTM, QB, QTB, AAK, ARB, ARK, MM = [blk() for _ in range(8)]
        BHT = ar.alloc([4, 128], BF16)
        KHT = ar.alloc([4, 128], BF16)
        VT = ar.alloc([4, 64], BF16)
        ZT = ar.alloc([4, 64], BF16)
        UT = ar.alloc([4, 64], BF16)
        SQY = ar.alloc([4, 64], F32)
        YC = ar.alloc([4, 64], F32)
        YNB = ar.alloc([4, 128], BF16)
        ST1 = ar.alloc([4], F32); ST2 = ar.alloc([4], F32); STM = ar.alloc([4], F32); STV = ar.alloc([4], F32)
        YF = ar.alloc([4, NB], F32)
        YAb = ar.alloc([4, NB], BF16)
        S32 = ar.alloc([4, 64], F32)
        SBF = [ar.alloc([4, 64], BF16) for _ in range(2)]
        SI = ar.alloc([8, 64], F32)
        SO = ar.alloc([4, 128], F32)
        SROW = ar.alloc([APJ], F32)
        SHT = ar.alloc([15, 128], F32)

        for t in (AT, RT, BT, KT, BH, KH, VB):
            kb.memset(t[:], 0.0)
        kb.memset(PA[:], 0.0)
        kb.memset(LAST[:], 0.0)
        kb.memset(XS[:], 0.0)
        kb.memset(S32[:], 0.0)
        kb.memset(SBF[0][:], 0.0)
        kb.memset(LIN[:], 0.0)
        kb.dma("sp", SROW[0:16, :], I["sshift"][l])
        kb.memset(SHS[:], 0.0)
        for half in range(2):
            bk = kb.bank()
            ms = range(0, 8) if half == 0 else range(8, 15)
            for m in ms:
                Mm = 128 if m < 14 else 32
                kb.mm(bk[0:Mm, (m % 8) * 16:(m % 8) * 16 + 16], lhsT=SROW[0:16, m * 128:m * 128 + Mm], rhs=cf("ident")[0:16, 0:16])
            if half == 0:
                kb.cp(SHS[:, 0:8, :], _view(bk[:, 0:128], [8, 16]), eng="dve")
            else:
                kb.cp(SHS[:, 8:14, :], _view(bk[:, 0:96], [6, 16]), eng="dve")
                kb.cp(SHS[0:32, 14, :], bk[0:32, 96:112], eng="dve")

        MU = vec("mu")
        sbi = [0]

        def block(tok0, is_s):
            n = NB
            C = 4 if is_s else 64
            R = 2 * C
            NQ = 512 // R
            nch = NQ // 4
            sfx = "4" if is_s else "64"
            msu, msuT, mu_, istk, tokm = cf("msu" + sfx), cf("msuT" + sfx), cf("mu" + sfx), cb("istack" + sfx), cf("tokmask" + sfx)
            modnorm(tok0, n, 1, UB, SQn, RSn, TTn)
            for half in range(2):
                bk = kb.bank()
                ms = range(0, 8) if half == 0 else range(8, 15)
                for m in ms:
                    Mm = 128 if m < 14 else 32
                    for k in range(8):
                        kb.mm(bk[0:Mm, (m % 8) * 64:(m % 8) * 64 + 64], lhsT=WIN[:, k, m * 128:m * 128 + Mm], rhs=UB[:, k, :],
                              start=(k == 0), stop=(k == 7))
                if not is_s:
                    if half == 0:
                        kb.cp(PA[:, 0:8, 1:65], _view(bk[:, 0:512], [8, 64]), eng="act")
                    else:
                        kb.cp(PA[:, 8:14, 1:65], _view(bk[:, 0:384], [6, 64]), eng="act")
                        kb.cp(PA[0:32, 14, 1:65], bk[0:32, 384:448], eng="act")
                else:
                    PAs = PA[:].rearrange("p m (s t) -> p m s t", t=5)
                    if half == 0:
                        kb.cp(PAs[:, 0:8, :, 1:5], bk[:, 0:512].rearrange("p (m s t) -> p m s t", m=8, t=4), eng="act")
                    else:
                        kb.cp(PAs[:, 8:14, :, 1:5], bk[:, 0:384].rearrange("p (m s t) -> p m s t", m=6, t=4), eng="act")
                        kb.cp(PAs[0:32, 14, :, 1:5], bk[0:32, 384:448].rearrange("p (s t) -> p s t", t=4), eng="act")
            if not is_s:
                kb.cp(PA[:, :, 0], LAST[:], eng="dve")
                kb.tt(XS[:], PA[:, :, 0:64], PA[:, :, 1:65], ALU.subtract)
                kb.tt(XS[:], XS[:], bc(MU[:, :, None], [128, 15, 64]), ALU.mult)
                kb.tt(XS[:], XS[:], PA[:, :, 1:65], ALU.add)
                kb.cp(LAST[:], PA[:, :, 64], eng="dve")
            else:
                PAs = PA[:].rearrange("p m (s t) -> p m s t", t=5)
                XSs = XS[:].rearrange("p m (s t) -> p m s t", t=4)
                kb.cp(PAs[:, :, :, 0], SHS[:], eng="dve")
                for m0, m1 in ((0, 8), (8, 15)):
                    kb.tt(XSs[:, m0:m1], PAs[:, m0:m1, :, 0:4], PAs[:, m0:m1, :, 1:5], ALU.subtract)
                kb.tt(XS[:], XS[:], bc(MU[:, :, None], [128, 15, 64]), ALU.mult)
                for m0, m1 in ((0, 8), (8, 15)):
                    kb.tt(XSs[:, m0:m1], XSs[:, m0:m1], PAs[:, m0:m1, :, 1:5], ALU.add)
                kb.cp(SHO[:], PAs[:, :, :, 4], eng="dve")
            xr, xk, xv = XS[:, 0:4, :], XS[:, 4:8, :], XS[:, 8:12, :]
            if (DBG.get("cut_s", 99) if is_s else DBG.get("cut", 99)) <= 1:
                return
            kb.act(LIN[0:64, 0, :], XS[0:64, 12, :], AF.Tanh)
            kb.cp(LIN[64:128, 0, :], XS[64:128, 12, :], eng="dve")
            kb.act(LIN[:, 1, :], XS[:, 13, :], AF.Sigmoid)
            kb.act(LIN[0:32, 2, :], XS[0:32, 14, :], AF.Sigmoid)
            bw = kb.bank()
            for j in range(4):
                kb.mm(bw[:, j * 64:(j + 1) * 64], lhsT=W2T[:, j * 128:(j + 1) * 128], rhs=LIN[:, 0, :])
                kb.mm(bw[:, 256 + j * 64:256 + (j + 1) * 64], lhsT=A2T[:, j * 128:(j + 1) * 128], rhs=LIN[:, 0, :])
            bg = kb.bank()
            for j in range(4):
                kb.mm(bg[:, j * 64:(j + 1) * 64], lhsT=G2T[:, 0, j * 128:(j + 1) * 128], rhs=LIN[:, 1, :], start=True, stop=False)
                kb.mm(bg[:, j * 64:(j + 1) * 64], lhsT=G2T[0:32, 1, j * 128:(j + 1) * 128], rhs=LIN[0:32, 2, :], start=False, stop=True)
            kb.tt(t_t1[:], _view(bw[:, 0:256], [4, 64]), bc(vec("w0")[:, :, None], [128, 4, 64]), ALU.add)
            kb.act(t_t1[:], t_t1[:], AF.Sigmoid)
            kb.act(t_wd[:], t_t1[:], AF.Exp, scale=-DECAY_K)
            kb.tt(t_aa[:], _view(bw[:, 256:512], [4, 64]), bc(vec("a0")[:, :, None], [128, 4, 64]), ALU.add)
            kb.act(t_aa[:], t_aa[:], AF.Sigmoid)
            kb.cp(t_gg[:], _view(bg[:, 0:256], [4, 64]), eng="act")
            if (DBG.get("cut_s", 99) if is_s else DBG.get("cut", 99)) <= 2:
                return
            kb.tt(t_t1[:], t_wd[:], bc(cf("nstart" + sfx)[:, None, 0:64], [128, 4, 64]), ALU.mult)
            kb.tt(t_t2[:], t_wd[:], bc(cf("start" + sfx)[:, None, 0:64], [128, 4, 64]), ALU.mult)
            for j in range(4):
                kb.scan(t_p[:, j, :], t_t1[:, j, :], t_t2[:, j, :], 1.0)
            kb.recip(t_ip[:], t_p[:])
            kb.recip(t_t1[:], t_wd[:])
            kb.tt(t_pex[:], t_p[:], t_t1[:], ALU.mult)
            kb.cp(PC[:, 0:NQ].rearrange("p (s j) -> p j s", j=4),
                  t_p[:].rearrange("p j (s c) -> p j s c", c=C)[:, :, :, C - 1], eng="dve")
            kb.tt(t_kk[:], xk, bc(vec("k_k")[:, :, None], [128, 4, 64]), ALU.mult)
            kb.act(SQK[:], t_kk[:], AF.Square)
            bs = kb.bank()
            for j in range(4):
                kb.mm(bs[:, j * 64:(j + 1) * 64], lhsT=cb("bones"), rhs=SQK[:, j, :])
            kb.act(t_t1[:], _view(bs[:, 0:256], [4, 64]), AF.Sqrt)
            kb.ts(t_t1[:], t_t1[:], 1e-12, None, ALU.max)
            kb.recip(t_t1[:], t_t1[:])
            kb.tt(t_kk[:], t_kk[:], t_t1[:], ALU.mult)
            kb.tt(t_t2[:], t_kk[:], t_aa[:], ALU.mult)
            kb.tt(t_t2[:], t_t2[:], t_ip[:], ALU.mult)
            kb.tt(t_t1[:], t_aa[:], bc(vec("k_a")[:, :, None], [128, 4, 64]), ALU.mult)
            kb.tt(t_t1[:], t_t1[:], bc(OMKA[:, :, None], [128, 4, 64]), ALU.add)
            kb.tt(t_k2[:], xk, t_t1[:], ALU.mult)
            kb.tt(t_t1[:], xr, t_k2[:], ALU.mult)
            kb.tt(RKR[:], t_t1[:], bc(vec("r_k")[:, :, None], [128, 4, 64]), ALU.mult)
            kb.tt(t_t1[:], t_k2[:], t_ip[:], ALU.mult)
            brk = kb.bank()
            for j in range(4):
                kb.mm(brk[:, j * 64:(j + 1) * 64], lhsT=cb("bones"), rhs=RKR[:, j, :])
            kb.tt(t_bon[:], _view(brk[:, 0:256], [4, 64]), xv, ALU.mult)
            if (DBG.get("cut_s", 99) if is_s else DBG.get("cut", 99)) <= 3:
                return
            PCq = PC[:, 0:NQ].rearrange("p (s j) -> p j s", j=4)
            kb.stt(t_wd[:], t_kk[:], -1.0, t_pex[:], ALU.mult, ALU.mult)
            for h in range(2):
                ps_ = slice(64 * h, 64 * h + 64)

                def dst(tile):
                    return tile[ps_, :].rearrange("p (s j r) -> p j s r", j=4, r=R)[:, :, :, h * C:(h + 1) * C]

                def src(t3):
                    return t3[ps_].rearrange("p j (s c) -> p j s c", c=C)

                kb.cp(dst(AT), src(t_wd), eng="dve")
                kb.tt(dst(RT), src(xr), src(t_p), ALU.mult)
                kb.cp(dst(BT), src(t_t2), eng="dve")
                kb.cp(dst(KT), src(t_t1), eng="act")
                kb.tt(dst(BH), src(t_t2), bc(PCq[ps_, :, :, None], [64, 4, nch, C]), ALU.mult)
                kb.tt(dst(KH), src(t_t1), bc(PCq[ps_, :, :, None], [64, 4, nch, C]), ALU.mult)
                kb.cp(dst(VB), src(xv), eng="act")
            if (DBG.get("cut_s", 99) if is_s else DBG.get("cut", 99)) <= 4:
                return
            Mq = lambda tile: tile[0:R, :].rearrange("p (q r) -> p q r", r=R)
            KR = 128
            MqK = lambda tile: tile[0:KR, :].rearrange("p (q r) -> p q r", r=R)
            Fq = lambda tile: tile[:, :].rearrange("p (q r) -> p q r", r=R)
            def prod(lt, rt, mask, out_t, eng="dve"):
                bk_ = kb.bank()
                for q in range(NQ):
                    kb.mm(bk_[0:R, q * R:(q + 1) * R], lhsT=Fq(lt)[:, q, :], rhs=Fq(rt)[:, q, :])
                kb.tt(Mq(out_t), bk_[0:R, :].rearrange("p (q r) -> p q r", r=R), bc(mask[0:R, None, 0:R], [R, NQ, R]), ALU.mult)

            prod(BT, AT, msu, NM)
            prod(AT, BT, msuT, NTM)
            prod(KT, AT, msu, AAK)
            prod(BT, RT, mu_, ARB)
            prod(KT, RT, mu_, ARK)
            if (DBG.get("cut_s", 99) if is_s else DBG.get("cut", 99)) <= 5:
                return
            kb.tt(Mq(MM), Mq(NM), bc(cf("ident")[0:R, None, 0:R], [R, NQ, R]), ALU.add)
            nlev = 5 if not is_s else 1
            Q, QT = NM, NTM
            Qn, QTn = QB, QTB
            for lev in range(nlev):
                last = (lev == nlev - 1)
                b2 = kb.bank()
                for q in range(NQ):
                    kb.mm(b2[0:R, q * R:(q + 1) * R], lhsT=MqK(Q)[:, q, :], rhs=MqK(QT)[:, q, :])
                kb.cp(QTn[0:R, :], b2[0:R, :], eng="act")
                if not last:
                    b1 = kb.bank()
                    for q in range(NQ):
                        kb.mm(b1[0:R, q * R:(q + 1) * R], lhsT=MqK(QT)[:, q, :], rhs=MqK(Q)[:, q, :])
                    kb.cp(Qn[0:R, :], b1[0:R, :], eng="act")
                b3 = kb.bank()
                for q in range(NQ):
                    kb.mm(b3[0:R, q * R:(q + 1) * R], lhsT=MqK(QTn)[:, q, :], rhs=MqK(MM)[:, q, :])
                kb.tt(MM[0:R, :], b3[0:R, :], MM[0:R, :], ALU.add)
                Q, QT, Qn, QTn = Qn, QTn, Q, QT
            if (DBG.get("cut_s", 99) if is_s else DBG.get("cut", 99)) <= 6:
                return
            for gi in range(nch):
                q0 = gi * 4
                if (DBG.get("cut_s", 99) if is_s else DBG.get("cut", 99)) <= 7:
                    continue
                if is_s:
                    seq = gi
                    kb.dma("sp", SI[0:64, :, :], I["swkv"][l, seq].rearrange("h v k -> v h k"))
                    sb_in = SBF[sbi[0] % 2]
                    if DBG.get("sl_mode", 0) != 1:
                        bk_ = kb.bank()
                        for j in range(4):
                            kb.mm(bk_[:, j * 64:(j + 1) * 64], lhsT=SI[0:64, 2 * j:2 * j + 2, :].rearrange("p h k -> p (h k)"),
                                  rhs=cf("ident")[0:64, 0:64])
                        if DBG.get("sl_mode", 0) != 2:
                            if DBG.get("sl_mode", 0) != 4:
                                kb.cp(S32[:], _view(bk_[:, 0:256], [4, 64]), eng="dve")
                            if DBG.get("sl_mode", 0) != 3:
                                kb.cp(sb_in[:], S32[:], eng="act")
                else:
                    sb_in = SBF[sbi[0] % 2]
                sb_out = SBF[(sbi[0] + 1) % 2]
                sbi[0] += 1
                if (DBG.get("cut_s", 99) if is_s else DBG.get("cut", 99)) <= 8:
                    continue
                b_ = kb.bank()
                for j in range(4):
                    kb.mm(b_[0:R, j * 128:(j + 1) * 128], lhsT=Fq(BH)[:, q0 + j, :], rhs=cb("ident"))
                kb.cp(BHT[0:R], _view(b_[0:R, 0:512], [4, 128]), eng="act")
                b_ = kb.bank()
                for j in range(4):
                    kb.mm(b_[0:R, j * 128:(j + 1) * 128], lhsT=Fq(KH)[:, q0 + j, :], rhs=cb("ident"))
                kb.cp(KHT[0:R], _view(b_[0:R, 0:512], [4, 128]), eng="dve")
                b_ = kb.bank()
                for j in range(4):
                    kb.mm(b_[0:R, j * 64:(j + 1) * 64], lhsT=Fq(VB)[:, q0 + j, :], rhs=cb("istack64"))
                kb.cp(VT[0:R], _view(b_[0:R, 0:256], [4, 64]), eng="act")
                if (DBG.get("cut_s", 99) if is_s else DBG.get("cut", 99)) <= 9:
                    continue
                bz = kb.bank()
                for j in range(4):
                    kb.mm(bz[0:R, j * 64:(j + 1) * 64], lhsT=MqK(AAK)[:, q0 + j, :], rhs=VT[0:KR, j, :], start=True, stop=False)
                    kb.mm(bz[0:R, j * 64:(j + 1) * 64], lhsT=Fq(AT)[:, q0 + j, :], rhs=sb_in[:, j, :], start=False, stop=True)
                kb.cp(ZT[0:R], _view(bz[0:R, 0:256], [4, 64]), eng="act")
                bu = kb.bank()
                for j in range(4):
                    kb.mm(bu[0:R, j * 64:(j + 1) * 64], lhsT=MqK(MM)[:, q0 + j, :], rhs=ZT[0:KR, j, :])
                kb.cp(UT[0:R], _view(bu[0:R, 0:256], [4, 64]), eng="dve")
                by = kb.bank()
                for j in range(4):
                    o = by[0:R, j * 64:(j + 1) * 64]
                    kb.mm(o, lhsT=Fq(RT)[:, q0 + j, :], rhs=sb_in[:, j, :], start=True, stop=False)
                    kb.mm(o, lhsT=MqK(ARB)[:, q0 + j, :], rhs=UT[0:KR, j, :], start=False, stop=False)
                    kb.mm(o, lhsT=MqK(ARK)[:, q0 + j, :], rhs=VT[0:KR, j, :], start=False, stop=True)
                bs_ = kb.bank()
                for j in range(4):
                    o = bs_[:, j * 64:(j + 1) * 64]
                    kb.mm(o, lhsT=BHT[0:KR, j, :], rhs=UT[0:KR, j, :], start=True, stop=False)
                    kb.mm(o, lhsT=KHT[0:KR, j, :], rhs=VT[0:KR, j, :], start=False, stop=True)
                kb.tt(S32[:], S32[:], bc(PC[:, q0:q0 + 4, None], [128, 4, 64]), ALU.mult)
                kb.tt(S32[:], S32[:], _view(bs_[:, 0:256], [4, 64]), ALU.add)
                kb.cp(sb_out[:], S32[:], eng="act")
                if (DBG.get("cut_s", 99) if is_s else DBG.get("cut", 99)) <= 10:
                    continue
                kb.cp(YC[0:R], _view(by[0:R, 0:256], [4, 64]), eng="act")
                Yv = YC[0:R]
                kb.reduce_sum(ST1[0:R], Yv)
                kb.act(SQY[0:R], Yv, AF.Square)
                kb.reduce_sum(ST2[0:R], SQY[0:R])
                kb.ts(STM[0:R], ST1[0:R], 1.0 / 64, None, ALU.mult)
                kb.tt(STV[0:R], STM[0:R], STM[0:R], ALU.mult)
                kb.stt(STV[0:R], ST2[0:R], 1.0 / 64, STV[0:R], ALU.mult, ALU.subtract)
                kb.act(STV[0:R], STV[0:R], AF.Sqrt, bias=LN_EPS)
                kb.recip(STV[0:R], STV[0:R])
                kb.tt(YC[0:R], Yv, bc(STM[0:R, :, None], [R, 4, 64]), ALU.subtract)
                kb.tt(YC[0:R], YC[0:R], bc(STV[0:R, :, None], [R, 4, 64]), ALU.mult)
                for h in range(2):
                    kb.ts(YNB[0:R, :, h * 64:(h + 1) * 64], YC[0:R], tokm[0:R, h:h + 1], None, ALU.mult)
                if (DBG.get("cut_s", 99) if is_s else DBG.get("cut", 99)) <= 11:
                    continue
                bf_ = kb.bank()
                CW = max(C, 8)
                for j in range(4):
                    kb.mm(bf_[:, j * CW:(j + 1) * CW], lhsT=YNB[0:KR, j, :], rhs=istk[0:KR, 0:CW])
                kb.cp(YF[:, :, gi * C:(gi + 1) * C], _view(bf_[:, 0:4 * CW], [4, CW])[:, :, 0:C], eng="act")
                if (DBG.get("cut_s", 99) if is_s else DBG.get("cut", 99)) <= 12:
                    continue
                if is_s or tok0 + n == TP:
                    bo_ = kb.bank()
                    for j in range(4):
                        kb.mm(bo_[0:64, j * 128:(j + 1) * 128], lhsT=S32[:, j, :], rhs=cf("ident"))
                    kb.cp(SO[0:64], _view(bo_[0:64, 0:512], [4, 128]), eng="dve")
                    dst_ = O["wkv_s"][l, gi] if is_s else O["wkv_p"][l]
                    kb.dma("sp", dst_.rearrange("h v k -> v h k"), SO[0:64].rearrange("p j (h k) -> p (j h) k", h=2))
            kb.tt(t_t1[:], YF[:], bc(vec("ln_w")[:, :, None], [128, 4, 64]), ALU.mult)
            kb.tt(t_t1[:], t_t1[:], bc(vec("ln_b")[:, :, None], [128, 4, 64]), ALU.add)
            kb.tt(t_t1[:], t_t1[:], t_bon[:], ALU.add)
            kb.tt(YAb[:], t_t1[:], t_gg[:], ALU.mult)
            kb.dma("sp", YAB[:, 0:4, tok0:tok0 + n], YAb[:])

        for b in range(min(TP // NB, DBG.get("nblk", 999))):
            block(b * NB, False)
        if DBG.get("cut", 99) < 99:
            return
        if DBG.get("nosamp", False):
            bk = kb.bank()
            kb.mm(bk[0:15, 0:128], lhsT=LAST[:, 0:15], rhs=cf("ident"))
            kb.cp(SHT[0:15, 0, :], bk[0:15, 0:128], eng="dve")
            kb.dma("sp", O["shift_p"][l, 0:1792].rearrange("(c p) -> c p", p=128), SHT[0:14, 0, :])
            kb.dma("sp", O["shift_p"][l:l + 1, 1792:1824], SHT[14:15, 0, 0:32])
            return
        bk = kb.bank()
        kb.mm(bk[0:15, 0:128], lhsT=LAST[:, 0:15], rhs=cf("ident"))
        kb.cp(SHT[0:15, 0, :], bk[0:15, 0:128], eng="dve")
        kb.dma("sp", O["shift_p"][l, 0:1792].rearrange("(c p) -> c p", p=128), SHT[0:14, 0, :])
        kb.dma("sp", O["shift_p"][l:l + 1, 1792:1824], SHT[14:15, 0, 0:32])
        for t in (AT, RT, BT, KT, BH, KH, VB, NM, NTM, QB, QTB, AAK, ARB, ARK, MM, BHT, KHT, VT, ZT, UT, YNB):
            kb.memset(t[:], 0.0)
        block(TP, True)
        for g4 in range(4):
            bk = kb.bank()
            ms = range(g4 * 4, min(g4 * 4 + 4, 15))
            for m in ms:
                Mm = 128 if m < 14 else 32
                kb.mm(bk[0:16, (m % 4) * 128:(m % 4) * 128 + Mm], lhsT=SHO[0:Mm, m, :], rhs=cf("ident")[0:Mm, 0:Mm])
            nm = len(ms)
            if g4 < 3:
                kb.cp(SHT[0:16, g4 * 4:g4 * 4 + 4, :], _view(bk[0:16, 0:512], [4, 128]), eng="dve")
            else:
                kb.cp(SHT[0:16, 12:14, :], _view(bk[0:16, 0:256], [2, 128]), eng="dve")
                kb.cp(SHT[0:16, 14, 0:32], bk[0:16, 256:288], eng="dve")
        kb.dma("sp", O["shift_s"][l, :, 0:1792], SHT[0:16, 0:14, :].rearrange("p m c -> p (m c)"))
        kb.dma("sp", O["shift_s"][l, :, 1792:1824], SHT[0:16, 14, 0:32])

    def pass_pool(l):
        ar.reset()
        WPB = ar.alloc([8, 512], BF16)
        WPL = ar.alloc([4, 128], BF16)
        kb.dma("pool", WPB, I["w_in"][l].rearrange("(k p) n -> p k n", p=128)[:, :, APJ:PT])
        kb.dma("pool", WPL, I["w_pool"][l].rearrange("g c d -> c g d"))
        UB = ar.alloc([8, 512], BF16)
        SQ = ar.alloc([8, 512], BF16)
        RS = ar.alloc([512], F32)
        TT = [ar.alloc([512], F32) for _ in range(2)]
        PBH = ar.alloc([4, 527], F32)
        SA = ar.alloc([4, 527], F32)
        SB = ar.alloc([4, 527], F32)
        DP = ar.alloc([4, 512], BF16)
        YBb = ar.alloc([4, 512], BF16)
        PROW = ar.alloc([512], F32)
        POUT = ar.alloc([512], F32)
        TMPH = ar.alloc([4, 120], F32)
        PS = vec("pool_scale")
        WIN_ = (2, 4, 8, 16)
        kb.memset(PBH[:], 0.0)

        def wsum(x, sa, sb_, L, nd):
            def sl(v, g0, g1, a, b_):
                return v[:, g0:g1, a:b_] if nd == 3 else v[:, g0:g1, :, a:b_]
            kb.tt(sl(sa, 0, 4, 1, L), sl(x, 0, 4, 1, L), sl(x, 0, 4, 0, L - 1), ALU.add)
            kb.tt(sl(sb_, 1, 4, 3, L), sl(sa, 1, 4, 3, L), sl(sa, 1, 4, 1, L - 2), ALU.add)
            kb.tt(sl(sa, 2, 4, 7, L), sl(sb_, 2, 4, 7, L), sl(sb_, 2, 4, 3, L - 4), ALU.add)
            kb.tt(sl(sb_, 3, 4, 15, L), sl(sa, 3, 4, 15, L), sl(sa, 3, 4, 7, L - 8), ALU.add)
            return [sa, sb_, sa, sb_]

        for (t0, n) in TILES[:4]:
            modnorm(t0, n, 1, UB, SQ, RS, TT)
            for g in range(4):
                bk = kb.bank()
                for k in range(8):
                    kb.mm(bk[:, 0:n], lhsT=WPB[:, k, g * 128:(g + 1) * 128], rhs=UB[:, k, 0:n], start=(k == 0), stop=(k == 7))
                kb.cp(PBH[:, g, 15:15 + n], bk[:, 0:n], eng="act")
            L = 15 + n
            fin = wsum(PBH, SA, SB, L, 3)
            for g in range(4):
                if t0 == 0:
                    kb.tt(fin[g][:, g, 15:30], fin[g][:, g, 15:30], cf("ratio")[:, g * 15:(g + 1) * 15], ALU.mult)
                kb.stt(DP[:, g, 0:n], fin[g][:, g, 15:L], 1.0 / WIN_[g], PBH[:, g, 15:L], ALU.mult, ALU.subtract)
            for g in range(4):
                bk = kb.bank()
                kb.mm(bk[:, 0:n], lhsT=WPL[:, g, :], rhs=DP[:, g, 0:n])
                kb.ts(YBb[:, g, 0:n], bk[:, 0:n], PS[:, g:g + 1], None, ALU.mult)
            kb.dma("sp", YAB[:, 4:8, t0:t0 + n], YBb[:, :, 0:n])
            kb.cp(TMPH[:, :, 0:15], PBH[:, :, n:n + 15], eng="dve")
            kb.cp(PBH[:, :, 0:15], TMPH[:, :, 0:15], eng="dve")
        bk = kb.bank()
        for g in range(4):
            kb.mm(bk[0:15, g * 128:(g + 1) * 128], lhsT=TMPH[:, g, 0:15], rhs=cf("ident"))
        kb.cp(POUT[0:15, :], bk[0:15, 0:512], eng="dve")
        kb.dma("sp", O["pool_p"][l], POUT[0:15, :])
        PBs = PBH[:, :, 0:304].rearrange("p g (s t) -> p g s t", t=19)
        SAs = SA[:, :, 0:304].rearrange("p g (s t) -> p g s t", t=19)
        SBs = SB[:, :, 0:304].rearrange("p g (s t) -> p g s t", t=19)
        sp_rows = I["spool"][l].rearrange("s i c -> (s i) c")
        for hh in range(2):
            kb.dma("sp", PROW[0:120, :], sp_rows[hh * 120:(hh + 1) * 120, :])
            bk = kb.bank()
            for g in range(4):
                kb.mm(bk[:, g * 120:(g + 1) * 120], lhsT=PROW[0:120, g * 128:(g + 1) * 128], rhs=cf("ident")[0:120, 0:120])
            kb.cp(PBs[:, :, hh * 8:(hh + 1) * 8, 0:15], bk[:, 0:480].rearrange("p (g s t) -> p g s t", g=4, t=15), eng="dve")
        modnorm(TP, 64, 1, UB, SQ, RS, TT)
        bk = kb.bank()
        for g in range(4):
            for k in range(8):
                kb.mm(bk[:, g * 64:(g + 1) * 64], lhsT=WPB[:, k, g * 128:(g + 1) * 128], rhs=UB[:, k, 0:64], start=(k == 0), stop=(k == 7))
        kb.cp(PBs[:, :, :, 15:19], bk[:, 0:256].rearrange("p (g s t) -> p g s t", g=4, t=4), eng="act")
        fin = wsum(PBs, SAs, SBs, 19, 4)
        fv = [SAs, SBs, SAs, SBs]
        for g in range(4):
            kb.stt(_view(DP[:, g, 0:64], [16, 4]), fv[g][:, g, :, 15:19], 1.0 / WIN_[g], PBs[:, g, :, 15:19], ALU.mult, ALU.subtract)
        bk = kb.bank()
        for g in range(4):
            kb.mm(bk[:, g * 64:(g + 1) * 64], lhsT=WPL[:, g, :], rhs=DP[:, g, 0:64])
        for g in range(4):
            kb.ts(YBb[:, g, 0:64], bk[:, g * 64:(g + 1) * 64], PS[:, g:g + 1], None, ALU.mult)
        kb.dma("sp", YAB[:, 4:8, TP:T], YBb[:, :, 0:64])
        po_rows = O["pool_s"][l].rearrange("s i c -> (s i) c")
        for hh in range(2):
            kb.cp(TMPH[:].rearrange("p g (s t) -> p g s t", t=15), PBs[:, :, hh * 8:(hh + 1) * 8, 4:19], eng="dve")
            bk = kb.bank()
            for g in range(4):
                kb.mm(bk[0:120, g * 128:(g + 1) * 128], lhsT=TMPH[:, g, :], rhs=cf("ident"))
            kb.cp(POUT[0:120, :], bk[0:120, 0:512], eng="dve")
            kb.dma("sp", po_rows[hh * 120:(hh + 1) * 120, :], POUT[0:120, :])

    def pass_b(l):
        ar.reset()
        WGT = ar.alloc([8, 2048], BF16)
        WBA = ar.alloc([4, 1024], BF16)
        WBB = ar.alloc([4, 1024], BF16)
        WOT = ar.alloc([8, 1024], BF16)
        kb.dma("pool", WGT, I["w_gate"][l].rearrange("(k p) n -> p k n", p=128))
        kb.dma("pool", WBA, I["w_br_a"][l].rearrange("(j p) n -> p j n", p=128))
        kb.dma("pool", WBB, I["w_br_b"][l].rearrange("(j p) n -> p j n", p=128))
        kb.dma("pool", WOT, I["w_out"][l].rearrange("(k p) n -> p k n", p=128))
        YT = ar.alloc([8, 512], BF16)
        UB = ar.alloc([8, 512], BF16)
        SQ = ar.alloc([8, 512], BF16)
        RS = ar.alloc([512], F32)
        TT = [ar.alloc([512], F32) for _ in range(2)]
        GA = ar.alloc([512], F32); GB = ar.alloc([512], F32); T1 = ar.alloc([512], F32); T2 = ar.alloc([512], F32)
        MG = ar.alloc([8, 512], BF16)
        BGv = vec("b_gate")
        for (t0, n) in TILES:
            kb.dma("sp", YT[:, :, 0:n], YAB[:, :, t0:t0 + n])
            modnorm(t0, n, 1, UB, SQ, RS, TT)
            for m in range(8):
                ba = kb.bank(); bb = kb.bank(); bc_ = kb.bank(); bd = kb.bank()
                for k in range(8):
                    kb.mm(ba[:, 0:n], lhsT=WGT[:, k, m * 128:(m + 1) * 128], rhs=UB[:, k, 0:n], start=(k == 0), stop=(k == 7))
                for k in range(8):
                    kb.mm(bb[:, 0:n], lhsT=WGT[:, k, 1024 + m * 128:1024 + (m + 1) * 128], rhs=UB[:, k, 0:n], start=(k == 0), stop=(k == 7))
                for j in range(4):
                    kb.mm(bc_[:, 0:n], lhsT=WBA[:, j, m * 128:(m + 1) * 128], rhs=YT[:, j, 0:n], start=(j == 0), stop=(j == 3))
                for j in range(4):
                    kb.mm(bd[:, 0:n], lhsT=WBB[:, j, m * 128:(m + 1) * 128], rhs=YT[:, 4 + j, 0:n], start=(j == 0), stop=(j == 3))
                kb.act(GA[:, 0:n], ba[:, 0:n], AF.Sigmoid, bias=BGv[:, m:m + 1], scale=1.0)
                kb.act(GB[:, 0:n], bb[:, 0:n], AF.Sigmoid, bias=BGv[:, 8 + m:9 + m], scale=1.0)
                kb.tt(T1[:, 0:n], bc_[:, 0:n], GA[:, 0:n], ALU.mult)
                kb.tt(T2[:, 0:n], bd[:, 0:n], GB[:, 0:n], ALU.mult)
                kb.tt(MG[:, m, 0:n], T1[:, 0:n], T2[:, 0:n], ALU.add)
            for m2 in range(8):
                bo = kb.bank()
                for m in range(8):
                    kb.mm(bo[:, 0:n], lhsT=WOT[:, m, m2 * 128:(m2 + 1) * 128], rhs=MG[:, m, 0:n], start=(m == 0), stop=(m == 7))
                resid(m2, t0, n, bo, 1)

    def mixer(l, only_a=False):
        ar.log = []
        pass_a(l)
        if only_a:
            DBG["passA_log"] = list(ar.log)
            return
        pass_pool(l)
        pass_b(l)

    return mixer


def _shard_inputs(inputs):
    maps = []
    shared = {n: np.ascontiguousarray(np.asarray(inputs[n], dtype=np.float32)) for n in WNAMES}
    xp = np.asarray(inputs["x_prompt"], dtype=np.float32)
    xs = np.asarray(inputs["x_sample"], dtype=np.float32)
    cp_ = np.asarray(inputs["c_prompt"], dtype=np.float32)
    cs = np.asarray(inputs["c_sample"], dtype=np.float32)
    swkv = np.asarray(inputs["state_wkv"], dtype=np.float32)
    ssh = np.asarray(inputs["state_shift"], dtype=np.float32)
    spl = np.asarray(inputs["state_pool"], dtype=np.float32)
    for i in range(NCORES):
        sl = slice(NSEQ * i, NSEQ * (i + 1))
        m = dict(shared)
        m["xin"] = np.ascontiguousarray(np.concatenate([xp[i], xs[sl].reshape(TS, D)], axis=0))
        m["cin"] = np.ascontiguousarray(np.concatenate([cp_[i:i + 1], cs[sl]], axis=0))
        m["swkv"] = np.ascontiguousarray(swkv[:, sl])
        m["sshift"] = np.ascontiguousarray(ssh[:, sl, 0, :])
        m["spool"] = np.ascontiguousarray(spl[:, sl])
        m["consts"] = CONSTS_NP
        maps.append(m)
    return maps


_NC_CACHE = {}
DBG = {}


def kernel(**inputs):
    if "nc" not in _NC_CACHE:
        _NC_CACHE["nc"] = build_program()
    nc = _NC_CACHE["nc"]
    maps = _shard_inputs(inputs)
    res = run_bass_kernel_spmd(nc, maps, core_ids=list(range(NCORES)))
    R = res.results
    y_p = np.stack([R[i]["y"][:TP] for i in range(NCORES)], axis=0)
    y_s = np.concatenate([R[i]["y"][TP:].reshape(NSEQ, DEC, D) for i in range(NCORES)], axis=0)
    wkv_p = np.stack([R[i]["wkv_p"] for i in range(NCORES)], axis=1)
    shift_p = np.stack([R[i]["shift_p"] for i in range(NCORES)], axis=1)[:, :, None, :]
    pool_p = np.stack([R[i]["pool_p"] for i in range(NCORES)], axis=1)
    wkv_s = np.concatenate([R[i]["wkv_s"] for i in range(NCORES)], axis=1)
    shift_s = np.concatenate([R[i]["shift_s"] for i in range(NCORES)], axis=1)[:, :, None, :]
    pool_s = np.concatenate([R[i]["pool_s"] for i in range(NCORES)], axis=1)
    f = lambda a: np.ascontiguousarray(a, dtype=np.float32)
    return (f(y_p), f(y_s), f(wkv_p), f(shift_p), f(pool_p), f(wkv_s), f(shift_s), f(pool_s))
```

```python
import numpy as np
from contextlib import ExitStack
import concourse.bass as bass
import concourse.mybir as mybir
from concourse.bass_utils import run_bass_kernel_spmd

F32 = mybir.dt.float32
BF16 = mybir.dt.bfloat16
ALU = mybir.AluOpType
AF = mybir.ActivationFunctionType
AX = mybir.AxisListType


class _Op:
    __slots__ = ("idx", "eng", "fn", "deps", "dma", "inc", "incval", "sem", "waits", "ring_prev", "gidx")

    def __init__(self, idx, eng, fn, deps, dma):
        self.idx, self.eng, self.fn, self.deps, self.dma = idx, eng, fn, deps, dma
        self.inc = False
        self.incval = 0
        self.sem = None
        self.waits = []
        self.ring_prev = None


def _region(ap):
    t = ap.tensor
    name = ap.name
    space = str(ap.space)
    pat = ap.ap
    off = int(ap.offset)
    es = mybir.dt.size(ap.dtype)
    if space == "DRAM":
        lo = off
        hi = off + 1
        for st, cnt in pat:
            hi += abs(int(st)) * (int(cnt) - 1)
        return (name, 0, 1, lo * es, hi * es)
    if "PSUM" in space.upper():
        return (name, 0, 128, 0, 1 << 30)
    shp = list(t.shape)
    pstep = 1
    for s in shp[1:]:
        pstep *= int(s)
    p0 = off // pstep
    f0 = off % pstep
    st0, cnt0 = pat[0]
    if int(st0) == pstep or int(cnt0) == 1:
        npart = int(cnt0)
        rest = pat[1:]
    else:
        npart = 1
        rest = pat
    hi = f0 + 1
    for st, cnt in rest:
        hi += abs(int(st)) * (int(cnt) - 1)
    return (name, p0, p0 + npart, f0 * es, hi * es)


class Sched:
    COMPUTE = ("pe", "act", "dve", "pool")
    RING = 8

    def __init__(self, nc):
        self.nc = nc
        self.ops = []
        self.rec = {}
        self.nd = {"sp": 0, "act": 0, "pool": 0}

    def op(self, eng, fn, reads=(), writes=(), dma=False):
        idx = len(self.ops)
        deps = set()
        rr = [_region(a) for a in reads]
        ww = [_region(a) for a in writes]
        for (name, p0, p1, f0, f1) in rr:
            for r in self.rec.get(name, ()):
                if r[5] and r[0] < p1 and p0 < r[1] and r[2] < f1 and f0 < r[3]:
                    deps.add((r[4], "raw"))
        for (name, p0, p1, f0, f1) in ww:
            for r in self.rec.get(name, ()):
                if r[0] < p1 and p0 < r[1] and r[2] < f1 and f0 < r[3]:
                    deps.add((r[4], "waw" if r[5] else "war"))
        o = _Op(idx, eng, fn, deps, dma)
        self.ops.append(o)
        for (name, p0, p1, f0, f1) in ww:
            lst = self.rec.setdefault(name, [])
            lst[:] = [r for r in lst if not (p0 <= r[0] and r[1] <= p1 and f0 <= r[2] and r[3] <= f1)]
            lst.append([p0, p1, f0, f1, idx, True])
        for (name, p0, p1, f0, f1) in rr:
            lst = self.rec.setdefault(name, [])
            lst[:] = [r for r in lst if not ((not r[5]) and self.ops[r[4]].eng == eng
                                             and (not self.ops[r[4]].dma) and (not dma)
                                             and p0 <= r[0] and r[1] <= p1 and f0 <= r[2] and r[3] <= f1)]
            lst.append([p0, p1, f0, f1, idx, False])
        return o

    NSEM = 12
    CH = 512

    def lower(self, stack):
        nc = self.nc
        ops = self.ops
        for o in ops:
            need = []
            best = {}
            for (d, kind) in o.deps:
                p = ops[d]
                if p.dma:
                    need.append(d)
                elif o.dma or p.eng != o.eng or o.eng != "pe":
                    if p.eng not in best or best[p.eng] < d:
                        best[p.eng] = d
            need.extend(best.values())
            o.deps = need
            for d in need:
                ops[d].inc = True
        self.csem = {e: [stack.enter_context(nc.semaphore("s_%s%d" % (e, i))) for i in range(self.NSEM)]
                     for e in self.COMPUTE}
        self.rings = {q: [stack.enter_context(nc.semaphore("r_%s%d" % (q, i))) for i in range(self.RING)]
                      for q in ("sp", "act", "pool")}
        cnt = {e: 0 for e in self.COMPUTE}
        dk = {"sp": 0, "act": 0, "pool": 0}
        dma_final = {}
        for o in ops:
            if o.dma:
                k = dk[o.eng]
                dk[o.eng] += 1
                o.sem = self.rings[o.eng][k % self.RING]
                o.incval = 16 * (k // self.RING + 1)
                o.ring_prev = (o.sem, 16 * (k // self.RING)) if k >= self.RING else None
                dma_final[(o.eng, k % self.RING)] = (o.sem, o.incval)
                o.gidx = None
            elif o.inc:
                g = cnt[o.eng]
                cnt[o.eng] += 1
                epoch = g // self.CH
                o.sem = self.csem[o.eng][epoch % self.NSEM]
                o.incval = (epoch // self.NSEM) * self.CH + (g % self.CH) + 1
                o.gidx = g
        waited_c = {e: {} for e in ("pe", "act", "dve", "pool", "sp")}
        waited_d = {e: {} for e in ("pe", "act", "dve", "pool", "sp")}
        for o in ops:
            wl = []
            wd = waited_d[o.eng]
            wc = waited_c[o.eng]
            if o.dma and o.ring_prev is not None:
                sem, val = o.ring_prev
                if wd.get(id(sem), 0) < val:
                    wd[id(sem)] = val
                    wl.append((sem, val))
            for d in o.deps:
                p = ops[d]
                if p.dma:
                    if wd.get(id(p.sem), 0) < p.incval:
                        wd[id(p.sem)] = p.incval
                        wl.append((p.sem, p.incval))
                else:
                    if wc.get(p.eng, -1) < p.gidx:
                        wc[p.eng] = p.gidx
                        wl.append((p.sem, p.incval))
            o.waits = wl
        self.final_waits = list(dma_final.values())
        per = {e: [] for e in ("pe", "act", "dve", "pool", "sp")}
        for o in ops:
            per[o.eng].append(o)

        def run(engobj, lst, final=False):
            for o in lst:
                for (sem, val) in o.waits:
                    engobj.wait_ge(sem, val)
                ins = o.fn(engobj)
                if o.dma:
                    ins.then_inc(o.sem, 16)
                elif o.inc:
                    ins.then_inc(o.sem, 1)
            if final:
                for (sem, val) in self.final_waits:
                    engobj.wait_ge(sem, val)

        with nc.Block() as block:
            @block.tensor
            def _(e):
                run(e, per["pe"])

            @block.scalar
            def _(e):
                run(e, per["act"])

            @block.vector
            def _(e):
                run(e, per["dve"])

            @block.gpsimd
            def _(e):
                run(e, per["pool"])

            @block.sync
            def _(e):
                run(e, per["sp"], final=True)


NCORES = 8
D = 1024
KC = 8
TP = 2048
NSEQ = 16
DEC = 4
TS = NSEQ * DEC
T = TP + TS
APJ = 1824
PT = 2336
DFF = 2816
NFC = DFF // 128
LN_EPS = 64e-5
NORM_EPS = 1e-6
DECAY_K = float(np.exp(-0.5))

WNAMES = ["norm_g", "w_mod", "b_mod", "w_ffn_in", "w_ffn_out", "w_in", "mu_shift", "w0", "w2", "a0", "a2", "g2",
          "k_k", "k_a", "r_k", "ln_x_w", "ln_x_b", "w_pool", "pool_scale", "w_br_a", "w_br_b", "w_gate", "b_gate",
          "w_out", "final_g"]
WSHAPES = {
    "norm_g": [2, 3, D], "w_mod": [2, D, 9 * D], "b_mod": [2, 9 * D], "w_ffn_in": [2, 2, D, 2 * DFF],
    "w_ffn_out": [2, 2, DFF, D], "w_in": [2, D, PT], "mu_shift": [2, APJ], "w0": [2, 512], "w2": [2, 64, 512],
    "a0": [2, 512], "a2": [2, 64, 512], "g2": [2, 160, 512], "k_k": [2, 512], "k_a": [2, 512], "r_k": [2, 8, 64],
    "ln_x_w": [2, 512], "ln_x_b": [2, 512], "w_pool": [2, 4, 128, 128], "pool_scale": [2, 512],
    "w_br_a": [2, 512, D], "w_br_b": [2, 512, D], "w_gate": [2, D, 2 * D], "b_gate": [2, 2 * D], "w_out": [2, D, D],
    "final_g": [D],
}


def _make_consts():
    cols = {}
    parts = []
    pos = [0]

    def add(name, arr):
        a = np.zeros((128, arr.shape[1]), np.float32)
        a[:arr.shape[0]] = arr
        cols[name] = (pos[0], arr.shape[1])
        pos[0] += arr.shape[1]
        parts.append(a)

    p = np.arange(128)
    add("ident", np.eye(128, dtype=np.float32))
    add("ones", np.ones((128, 128), np.float32))
    same = (p[:, None] // 64) == (p[None, :] // 64)
    add("bones", same.astype(np.float32))
    s = p[:, None] % 64
    t = p[None, :] % 64
    add("msu64", (same & (s < t)).astype(np.float32))
    add("msuT64", (same & (s > t)).astype(np.float32))
    add("mu64", (same & (s <= t)).astype(np.float32))
    add("istack64", (p[:, None] % 64 == np.arange(64)[None, :]).astype(np.float32))
    add("tokmask64", (p[:, None] // 64 == np.arange(2)[None, :]).astype(np.float32))
    q = np.arange(8)
    same4 = (q[:, None] // 4) == (q[None, :] // 4)
    s4 = q[:, None] % 4
    t4 = q[None, :] % 4
    add("msu4", (same4 & (s4 < t4)).astype(np.float32))
    add("msuT4", (same4 & (s4 > t4)).astype(np.float32))
    add("mu4", (same4 & (s4 <= t4)).astype(np.float32))
    add("istack4", (q[:, None] % 4 == np.arange(8)[None, :]).astype(np.float32))
    add("tokmask4", (q[:, None] // 4 == np.arange(2)[None, :]).astype(np.float32))
    tt = np.arange(128)
    add("start64", np.broadcast_to((tt % 64 == 0).astype(np.float32)[None, :], (128, 128)).copy())
    add("nstart64", np.broadcast_to((tt % 64 != 0).astype(np.float32)[None, :], (128, 128)).copy())
    add("start4", np.broadcast_to((tt[:64] % 4 == 0).astype(np.float32)[None, :], (128, 64)).copy())
    add("nstart4", np.broadcast_to((tt[:64] % 4 != 0).astype(np.float32)[None, :], (128, 64)).copy())
    ratio = np.zeros((4, 15), np.float32)
    for g, w in enumerate((2, 4, 8, 16)):
        for i in range(15):
            ratio[g, i] = w / min(w, i + 1)
    add("ratio", np.broadcast_to(ratio.reshape(1, 60), (128, 60)).copy())
    return np.concatenate(parts, axis=1), cols


DBG = {}
CONSTS_NP, CCOLS = _make_consts()
NCC = CONSTS_NP.shape[1]

VR = {}
_r = 0
for _n, _k in (("norm_g", 24), ("mu", 15), ("w0", 4), ("a0", 4), ("k_k", 4), ("k_a", 4), ("r_k", 4), ("ln_w", 4),
               ("ln_b", 4), ("pool_scale", 4), ("b_gate", 16), ("final_g", 8)):
    VR[_n] = (_r, _k)
    _r += _k
NVR = _r


def _prod(s):
    r = 1
    for v in s:
        r *= int(v)
    return r


def _view(ap2, shape):
    if len(shape) == 1:
        return ap2
    names = "abcdef"[:len(shape)]
    kw = {names[i]: int(shape[i]) for i in range(len(shape))}
    return ap2.rearrange("p (%s) -> p %s" % (" ".join(names), " ".join(names)), **kw)


class Arena:
    def __init__(self, base_bf16, nelem):
        self.base = base_bf16
        self.n = nelem
        self.off = 0
        self.peak = 0
        self.log = []

    def reset(self, off=0):
        self.off = off

    def alloc(self, shape, dtype):
        n = _prod(shape)
        nb = n * 2 if dtype == F32 else n
        off = (self.off + 15) // 16 * 16
        assert off + nb <= self.n, "arena overflow: need %d have %d" % (off + nb, self.n)
        v = self.base[:, off:off + nb]
        if dtype == F32:
            v = v.bitcast(F32)
        self.off = off + nb
        self.peak = max(self.peak, self.off)
        self.log.append((off, tuple(shape), "f32" if dtype == F32 else "bf16"))
        return _view(v, shape)


class KB:
    def __init__(self, nc, S, banks):
        self.nc, self.S, self.banks = nc, S, banks
        self.bi = 0
        self.flip = 0
        self.sub = {}

    def bank(self):
        b = self.banks[self.bi % len(self.banks)]
        self.bi += 1
        return b

    def bank_of(self, ids):
        c = self.sub.get(ids, 0)
        self.sub[ids] = c + 1
        return self.banks[ids[c % len(ids)]]

    def mm(self, out, lhsT, rhs, start=True, stop=True):
        self.S.op("pe", lambda e: e.matmul(out, lhsT=lhsT, rhs=rhs, start=start, stop=stop),
                  reads=[lhsT, rhs], writes=[out])

    def dma(self, q, out, in_):
        self.S.op(q, lambda e: e.dma_start(out=out, in_=in_), reads=[in_], writes=[out], dma=True)

    def tt(self, out, in0, in1, op, eng="dve"):
        self.S.op(eng, lambda e: e.tensor_tensor(out=out, in0=in0, in1=in1, op=op), reads=[in0, in1], writes=[out])

    def ts(self, out, in0, s1, s2, op0, op1=None, eng="dve"):
        rd = [in0] + [s for s in (s1, s2) if not isinstance(s, (int, float)) and s is not None]
        if op1 is None:
            self.S.op(eng, lambda e: e.tensor_scalar(out=out, in0=in0, scalar1=s1, scalar2=None, op0=op0),
                      reads=rd, writes=[out])
        else:
            self.S.op(eng, lambda e: e.tensor_scalar(out=out, in0=in0, scalar1=s1, scalar2=s2, op0=op0, op1=op1),
                      reads=rd, writes=[out])

    def stt(self, out, in0, scalar, in1, op0, op1, eng="dve"):
        rd = [in0, in1] + ([] if isinstance(scalar, (int, float)) else [scalar])
        self.S.op(eng, lambda e: e.scalar_tensor_tensor(out=out, in0=in0, scalar=scalar, in1=in1, op0=op0, op1=op1),
                  reads=rd, writes=[out])

    def act(self, out, in_, func, bias=None, scale=None):
        rd = [in_] + [s for s in (bias, scale) if s is not None and not isinstance(s, (int, float))]
        kw = {}
        if bias is not None:
            kw["bias"] = bias
        if scale is not None:
            kw["scale"] = scale
        self.S.op("act", lambda e: e.activation(out=out, in_=in_, func=func, **kw), reads=rd, writes=[out])

    def cp(self, out, in_, eng=None):
        if eng is None:
            self.flip ^= 1
            eng = "act" if self.flip else "dve"
        if eng == "act":
            self.S.op("act", lambda e: e.activation(out=out, in_=in_, func=AF.Copy), reads=[in_], writes=[out])
        else:
            self.S.op(eng, lambda e: e.tensor_copy(out=out, in_=in_), reads=[in_], writes=[out])

    def recip(self, out, in_):
        self.S.op("dve", lambda e: e.reciprocal(out=out, in_=in_), reads=[in_], writes=[out])

    def memset(self, out, val, eng="dve"):
        self.S.op(eng, lambda e: e.memset(out, val), writes=[out])

    def reduce_sum(self, out, in_):
        self.S.op("dve", lambda e: e.tensor_reduce(out=out, in_=in_, axis=AX.X, op=ALU.add), reads=[in_], writes=[out])

    def scan(self, out, d0, d1, init):
        self.S.op("dve", lambda e: e.tensor_tensor_scan(out=out, data0=d0, data1=d1, initial=init, op0=ALU.mult,
                                                        op1=ALU.add), reads=[d0, d1], writes=[out])


def build_program(stop=None, dbg=False, nlayers=2):
    nc = bass.Bass("TRN2", target_bir_lowering=False)
    I = {}

    def din(name, shape):
        I[name] = nc.dram_tensor(name, list(shape), F32, kind="ExternalInput").ap()

    din("xin", [T, D]); din("cin", [17, D]); din("swkv", [2, NSEQ, 8, 64, 64]); din("sshift", [2, NSEQ, APJ])
    din("spool", [2, NSEQ, 15, 512]); din("consts", [128, NCC])
    for n in WNAMES:
        din(n, WSHAPES[n])
    O = {}

    def dout(name, shape):
        O[name] = nc.dram_tensor(name, list(shape), F32, kind="ExternalOutput").ap()

    dout("y", [T, D]); dout("wkv_p", [2, 8, 64, 64]); dout("shift_p", [2, APJ]); dout("pool_p", [2, 15, 512])
    dout("wkv_s", [2, NSEQ, 8, 64, 64]); dout("shift_s", [2, NSEQ, APJ]); dout("pool_s", [2, NSEQ, 15, 512])
    if dbg:
        dout("dbgX", [128, KC, T]); dout("dbgA", [128, 8, T])
    YAB = nc.dram_tensor("yab_scratch", [128, 8, T], BF16, kind="Internal").ap()

    ARN = 60416
    with ExitStack() as st:
        def sb(name, shape, dt):
            return st.enter_context(nc.sbuf_tensor(name, list(shape), dt))

        X = sb("X", [128, KC, T], F32)
        CF = sb("CF", [128, NCC], F32)
        CB = sb("CB", [128, NCC], BF16)
        ARt = sb("AR", [128, ARN], BF16)
        MOD = sb("MOD", [128, 72, 17], F32)
        VEC = sb("VEC", [128, NVR], F32)
        BM = sb("BM", [128, 72], F32)
        SCT = sb("SCT", [128, 8, 17], F32)
        GSp = sb("GSp", [128, 3, 8], F32); SHp = sb("SHp", [128, 3, 8], F32); COp = sb("COp", [128, 3, 8], F32)
        GSs = sb("GSs", [128, 3, 8, 16], F32); SHs = sb("SHs", [128, 3, 8, 16], F32); COs = sb("COs", [128, 3, 8, 16], F32)
        OMKA = sb("OMKA", [128, 4], F32)
        TMS = sb("TMS", [128, 64], F32)
        banks = [st.enter_context(nc.psum_tensor("ps%d" % i, [128, 512], F32)) for i in range(8)]
        S = Sched(nc)
        kb = KB(nc, S, banks)
        ar = Arena(ARt[:], ARN)

        def cf(name):
            c0, n = CCOLS[name]
            return CF[:, c0:c0 + n]

        def cb(name):
            c0, n = CCOLS[name]
            return CB[:, c0:c0 + n]

        def vec(name):
            r0, k = VR[name]
            return VEC[:, r0:r0 + k]

        kb.dma("sp", CF[:], I["consts"])
        kb.cp(CB[:], CF[:], eng="dve")

        ar.reset()
        XT = [ar.alloc([D], F32) for _ in range(2)]
        CROW = ar.alloc([D], F32)
        for i in range(17):
            n = 128 if i < 16 else 64
            xt = XT[i % 2]
            kb.dma("sp", xt[0:n, :], I["xin"][i * 128:i * 128 + n, :])
            for half in range(2):
                bk = kb.bank()
                for cc in range(4):
                    c = half * 4 + cc
                    kb.mm(bk[:, cc * 128:cc * 128 + n], lhsT=xt[0:n, c * 128:(c + 1) * 128], rhs=cf("ident")[0:n, 0:n])
                kb.cp(X[:, half * 4:half * 4 + 4, i * 128:i * 128 + n], _view(bk[:, 0:512], [4, 128])[:, :, 0:n])
        kb.dma("sp", CROW[0:17, :], I["cin"])
        kb.act(CROW[0:17, :], CROW[0:17, :], AF.Silu)
        bk = kb.bank()
        for c in range(8):
            kb.mm(bk[:, c * 17:(c + 1) * 17], lhsT=CROW[0:17, c * 128:(c + 1) * 128], rhs=cf("ident")[0:17, 0:17])
        kb.cp(SCT[:], _view(bk[:, 0:136], [8, 17]), eng="dve")

        def load_layer_vectors(l):
            ar.reset()
            ROWS = ar.alloc([128], F32)
            BMR = ar.alloc([128], F32)
            WM = [ar.alloc([8, 512], F32) for _ in range(2)]
            kb.memset(ROWS[:], 0.0)

            def rows(name, src):
                r0, k = VR[name]
                kb.dma("sp", ROWS[r0:r0 + k, :], src)

            rows("norm_g", I["norm_g"][l].rearrange("j (c p) -> (j c) p", p=128))
            r0, _ = VR["mu"]
            kb.dma("sp", ROWS[r0:r0 + 14, :], I["mu_shift"][l, 0:1792].rearrange("(c p) -> c p", p=128))
            kb.dma("sp", ROWS[r0 + 14:r0 + 15, 0:32], I["mu_shift"][l:l + 1, 1792:1824])
            for nm, src in (("w0", "w0"), ("a0", "a0"), ("k_k", "k_k"), ("k_a", "k_a"), ("ln_w", "ln_x_w"),
                            ("ln_b", "ln_x_b"), ("pool_scale", "pool_scale")):
                rows(nm, I[src][l].rearrange("(c p) -> c p", p=128))
            rows("r_k", I["r_k"][l].rearrange("(c h) k -> c (h k)", h=2))
            rows("b_gate", I["b_gate"][l].rearrange("(c p) -> c p", p=128))
            rows("final_g", I["final_g"].rearrange("(c p) -> c p", p=128))
            bk = kb.bank()
            kb.mm(bk[:, 0:NVR], lhsT=ROWS[0:NVR, :], rhs=cf("ident")[0:NVR, 0:NVR])
            kb.cp(VEC[:], bk[:, 0:NVR], eng="dve")
            kb.dma("sp", BMR[0:72, :], I["b_mod"][l].rearrange("(c p) -> c p", p=128))
            bk = kb.bank()
            kb.mm(bk[:, 0:72], lhsT=BMR[0:72, :], rhs=cf("ident")[0:72, 0:72])
            kb.cp(BM[:], bk[:, 0:72], eng="dve")
            kb.ts(OMKA[:], vec("k_a"), -1.0, 1.0, ALU.mult, ALU.add)
            wm = I["w_mod"][l].rearrange("(k p) n -> p k n", p=128)
            kb.dma("sp", WM[0][:], wm[:, :, 0:512])
            bk = None
            for blk in range(18):
                if blk + 1 < 18:
                    kb.dma("sp", WM[(blk + 1) % 2][:], wm[:, :, (blk + 1) * 512:(blk + 2) * 512])
                w = WM[blk % 2]
                for oc in range(4):
                    mc = blk * 4 + oc
                    if mc % 24 == 0:
                        bk = kb.bank()
                    o = bk[:, (mc % 24) * 17:(mc % 24) * 17 + 17]
                    for k in range(8):
                        kb.mm(o, lhsT=w[:, k, oc * 128:(oc + 1) * 128], rhs=SCT[:, k, :], start=(k == 0), stop=(k == 7))
                    if mc % 24 == 23:
                        g = mc // 24
                        kb.tt(MOD[:, g * 24:(g + 1) * 24, :], _view(bk[:, 0:408], [24, 17]),
                              BM[:, g * 24:(g + 1) * 24, None].to_broadcast([128, 24, 17]), ALU.add)
            MODv = MOD[:].rearrange("p (j k c) s -> p j k c s", j=3, k=3)
            NG = _view(vec("norm_g"), [3, 8])
            kb.ts(GSp[:], MODv[:, :, 1, :, 0], 1.0, None, ALU.add)
            kb.tt(GSp[:], GSp[:], NG, ALU.mult)
            kb.cp(SHp[:], MODv[:, :, 0, :, 0], eng="dve")
            kb.cp(COp[:], MODv[:, :, 2, :, 0], eng="dve")
            kb.ts(COp[:, 0, :], COp[:, 0, :], 0.5, None, ALU.mult)
            kb.ts(COp[:, 2, :], COp[:, 2, :], 0.5, None, ALU.mult)
            for j in range(3):
                kb.ts(GSs[:, j], MODv[:, j, 1, :, 1:17], 1.0, None, ALU.add)
                kb.tt(GSs[:, j], GSs[:, j], NG[:, j, :, None].to_broadcast([128, 8, 16]), ALU.mult)
                kb.cp(SHs[:, j], MODv[:, j, 0, :, 1:17], eng="dve")
                kb.ts(COs[:, j], MODv[:, j, 2, :, 1:17], (1.0 if j == 1 else 0.5), None, ALU.mult)

        def modnorm(tok0, n, j, U, SQ, RS, TT, bank=None):
            is_s = tok0 >= TP
            kb.act(SQ[:, :, 0:n], X[:, :, tok0:tok0 + n], AF.Square)
            bk = kb.bank() if bank is None else bank()
            for c in range(8):
                kb.mm(bk[:, 0:n], lhsT=cb("ones"), rhs=SQ[:, c, 0:n], start=(c == 0), stop=(c == 7))
            kb.act(RS[:, 0:n], bk[:, 0:n], AF.Sqrt, bias=NORM_EPS, scale=1.0 / D)
            kb.recip(RS[:, 0:n], RS[:, 0:n])
            for c in range(8):
                t = TT[c % 2]
                if not is_s:
                    kb.stt(t[:, 0:n], X[:, c, tok0:tok0 + n], GSp[:, j, c:c + 1], RS[:, 0:n], ALU.mult, ALU.mult)
                    kb.act(U[:, c, 0:n], t[:, 0:n], AF.Identity, bias=SHp[:, j, c:c + 1], scale=1.0)
                else:
                    tv = _view(t[:, 0:n], [16, 4])
                    kb.tt(tv, _view(X[:, c, tok0:tok0 + n], [16, 4]), GSs[:, j, c, :, None].to_broadcast([128, 16, 4]), ALU.mult)
                    kb.tt(t[:, 0:n], t[:, 0:n], RS[:, 0:n], ALU.mult)
                    kb.tt(_view(U[:, c, 0:n], [16, 4]), tv, SHs[:, j, c, :, None].to_broadcast([128, 16, 4]), ALU.add)

        def resid(m, tok0, n, bo, j):
            if tok0 < TP:
                kb.stt(X[:, m, tok0:tok0 + n], bo[:, 0:n], COp[:, j, m:m + 1], X[:, m, tok0:tok0 + n], ALU.mult, ALU.add)
            else:
                kb.tt(_view(TMS[:, 0:n], [16, 4]), _view(bo[:, 0:n], [16, 4]),
                      COs[:, j, m, :, None].to_broadcast([128, 16, 4]), ALU.mult)
                kb.tt(X[:, m, tok0:tok0 + n], X[:, m, tok0:tok0 + n], TMS[:, 0:n], ALU.add)

        TILES = [(0, 512), (512, 512), (1024, 512), (1536, 512), (2048, 64)]

        def ffn(l, f):
            j = 0 if f == 0 else 2
            ar.reset()
            U = ar.alloc([8, T], BF16)
            WG = [ar.alloc([8, 512], BF16) for _ in range(2)]
            WU = [ar.alloc([8, 512], BF16) for _ in range(2)]
            WO = [ar.alloc([4, 1024], BF16) for _ in range(2)]
            H = [ar.alloc([4, 512], BF16) for _ in range(2)]
            SG = [ar.alloc([512], BF16) for _ in range(2)]
            SQ = ar.alloc([8, 512], BF16)
            RS = ar.alloc([512], F32)
            TT = [ar.alloc([512], F32) for _ in range(2)]
            win = I["w_ffn_in"][l, f].rearrange("(k p) n -> p k n", p=128)
            wout = I["w_ffn_out"][l, f].rearrange("(j p) n -> p j n", p=128)
            groups = [(0, 4), (4, 4), (8, 4), (12, 4), (16, 4), (20, 2)]

            def load(gi):
                c0, ng = groups[gi]
                b = gi % 2
                kb.dma("pool", WG[b][:, :, 0:ng * 128], win[:, :, c0 * 128:(c0 + ng) * 128])
                kb.dma("pool", WU[b][:, :, 0:ng * 128], win[:, :, DFF + c0 * 128:DFF + (c0 + ng) * 128])
                kb.dma("pool", WO[b][:, 0:ng, :], wout[:, c0:c0 + ng, :])

            load(0)
            for (t0, n) in TILES:
                modnorm(t0, n, j, U[:, :, t0:t0 + n], SQ, RS, TT)
            hb = 0
            for gi, (c0, ng) in enumerate(groups):
                if gi + 1 < len(groups):
                    load(gi + 1)
                b = gi % 2
                for (t0, n) in TILES:
                    h = H[hb % 2]
                    hb += 1
                    for jj in range(ng):
                        bg = kb.bank()
                        bu = kb.bank()
                        for k in range(8):
                            kb.mm(bg[:, 0:n], lhsT=WG[b][:, k, jj * 128:(jj + 1) * 128], rhs=U[:, k, t0:t0 + n],
                                  start=(k == 0), stop=(k == 7))
                        for k in range(8):
                            kb.mm(bu[:, 0:n], lhsT=WU[b][:, k, jj * 128:(jj + 1) * 128], rhs=U[:, k, t0:t0 + n],
                                  start=(k == 0), stop=(k == 7))
                        sg = SG[jj % 2]
                        kb.act(sg[:, 0:n], bg[:, 0:n], AF.Silu)
                        kb.tt(h[:, jj, 0:n], bu[:, 0:n], sg[:, 0:n], ALU.mult)
                    for m in range(8):
                        bo = kb.bank()
                        for jj in range(ng):
                            kb.mm(bo[:, 0:n], lhsT=WO[b][:, jj, m * 128:(m + 1) * 128], rhs=h[:, jj, 0:n],
                                  start=(jj == 0), stop=(jj == ng - 1))
                        resid(m, t0, n, bo, j)

        def final_out():
            ar.reset()
            YT = [ar.alloc([D], F32) for _ in range(2)]
            SQ = ar.alloc([8, 128], BF16)
            RS = ar.alloc([128], F32)
            YN = [ar.alloc([8, 128], F32) for _ in range(2)]
            FG = vec("final_g")
            for i in range(17):
                n = 128 if i < 16 else 64
                t0 = i * 128
                yn = YN[i % 2]
                kb.act(SQ[:, :, 0:n], X[:, :, t0:t0 + n], AF.Square)
                bk = kb.bank()
                for c in range(8):
                    kb.mm(bk[:, 0:n], lhsT=cb("ones"), rhs=SQ[:, c, 0:n], start=(c == 0), stop=(c == 7))
                kb.act(RS[:, 0:n], bk[:, 0:n], AF.Sqrt, bias=NORM_EPS, scale=1.0 / D)
                kb.recip(RS[:, 0:n], RS[:, 0:n])
                for c in range(8):
                    kb.stt(yn[:, c, 0:n], X[:, c, t0:t0 + n], FG[:, c:c + 1], RS[:, 0:n], ALU.mult, ALU.mult)
                yt = YT[i % 2]
                for half in range(2):
                    bk = kb.bank()
                    for cc in range(4):
                        c = half * 4 + cc
                        kb.mm(bk[0:n, cc * 128:(cc + 1) * 128], lhsT=yn[:, c, 0:n], rhs=cf("ident"))
                    kb.cp(yt[0:n, half * 512:(half + 1) * 512], bk[0:n, 0:512])
                kb.dma("sp", O["y"][t0:t0 + n, :], yt[0:n, :])

        def dbg_dump_x():
            if dbg:
                kb.dma("sp", O["dbgX"], X[:])

        mixer = _make_mixer(nc, kb, ar, I, O, X, YAB, cf, cb, vec, modnorm, resid, OMKA, TILES, dbg)

        def mark(name):
            DBG.setdefault("marks", []).append((name, sum(1 for o in S.ops if o.eng == "pe"), len(S.ops)))

        DBG["marks"] = []
        for l in range(nlayers):
            mark("vec%d" % l)
            load_layer_vectors(l)
            mark("ffn%d0" % l)
            ffn(l, 0)
            if stop == "ffn0":
                break
            mark("mixer%d" % l)
            mixer(l, only_a=(stop == "passA"))
            if stop in ("mix0", "passA"):
                break
            mark("ffn%d1" % l)
            ffn(l, 1)
        mark("final")
        dbg_dump_x()
        final_out()
        S.lower(st)
        print("ops", len(S.ops), "arena peak KiB", ar.peak * 2 / 1024.0)
    return nc


def _make_mixer(nc, kb, ar, I, O, X, YAB, cf, cb, vec, modnorm, resid, OMKA, TILES, dbg):
    NB = 64

    def bc(ap, shape):
        return ap.to_broadcast(list(shape))

    def pass_a(l):
        ar.reset()
        WIN = ar.alloc([8, APJ], BF16)
        W2T = ar.alloc([512], BF16)
        A2T = ar.alloc([512], BF16)
        G2T = ar.alloc([2, 512], BF16)
        kb.memset(W2T[:], 0.0)
        kb.memset(A2T[:], 0.0)
        kb.dma("pool", WIN, I["w_in"][l].rearrange("(k p) n -> p k n", p=128)[:, :, 0:APJ])
        kb.dma("pool", W2T[0:64, :], I["w2"][l])
        kb.dma("pool", A2T[64:128, :], I["a2"][l])
        kb.dma("pool", G2T[:, 0, :], I["g2"][l, 0:128, :])
        kb.dma("pool", G2T[0:32, 1, :], I["g2"][l, 128:160, :])
        UB = ar.alloc([8, NB], BF16)
        SQn = ar.alloc([8, NB], BF16)
        RSn = ar.alloc([NB], F32)
        TTn = [ar.alloc([NB], F32) for _ in range(2)]
        PA = ar.alloc([15, 80], F32)
        XS = ar.alloc([15, NB], F32)
        LAST = ar.alloc([15], F32)
        SHS = ar.alloc([15, 16], F32)
        SHO = ar.alloc([15, 16], F32)
        LIN = ar.alloc([3, NB], BF16)
        f4 = lambda: ar.alloc([4, NB], F32)
        t_wd, t_p, t_pex, t_ip, t_aa, t_kk, t_t1, t_t2, t_k2 = [f4() for _ in range(9)]
        SQK = ar.alloc([4, NB], BF16)
        RKR = ar.alloc([4, NB], BF16)
        blk = lambda: ar.alloc([512], BF16)
        AT3 = [blk() for _ in range(3)]
        RT3 = [blk() for _ in range(3)]
        PC3 = [ar.alloc([64], F32) for _ in range(3)]
        GG3 = [f4() for _ in range(3)]
        BON3 = [f4() for _ in range(3)]
        BT2 = [blk() for _ in range(2)]; KT2 = [blk() for _ in range(2)]; BH2 = [blk() for _ in range(3)]
        KH2 = [blk() for _ in range(3)]; VB2 = [blk() for _ in range(3)]
        NM, NTM, QB, QTB = [blk() for _ in range(4)]
        AAK2 = [blk() for _ in range(2)]; ARB2 = [blk() for _ in range(2)]; ARK2 = [blk() for _ in range(2)]
        MM2 = [blk() for _ in range(2)]
        BHT = ar.alloc([4, 128], BF16)
        KHT = ar.alloc([4, 128], BF16)
        VT = ar.alloc([4, 64], BF16)
        ZT = ar.alloc([4, 64], BF16)
        UT = ar.alloc([4, 64], BF16)
        SQY = ar.alloc([4, 64], F32)
        YC = ar.alloc([4, 64], F32)
        YNB = ar.alloc([4, 128], BF16)
        ST1 = ar.alloc([4], F32); ST2 = ar.alloc([4], F32); STM = ar.alloc([4], F32); STV = ar.alloc([4], F32)
        YF = ar.alloc([4, NB], F32)
        p2a = f4()
        YAb = ar.alloc([4, NB], BF16)
        S32 = ar.alloc([4, 64], F32)
        SBF = [ar.alloc([4, 64], BF16) for _ in range(2)]
        SI = ar.alloc([8, 64], F32)
        SO = ar.alloc([4, 128], F32)
        SHT = ar.alloc([15, 128], F32)
        SROW = SHT[:].rearrange("p m c -> p (m c)")[:, 0:APJ]
        DBG["passA_kib"] = ar.off * 2 / 1024.0

        for t in AT3 + RT3 + BT2 + KT2 + BH2 + KH2 + VB2:
            kb.memset(t[:], 0.0)
        kb.memset(PA[:], 0.0)
        kb.memset(LAST[:], 0.0)
        kb.memset(XS[:], 0.0)
        kb.memset(S32[:], 0.0)
        kb.memset(SBF[0][:], 0.0)
        kb.memset(LIN[:], 0.0)
        kb.dma("sp", SROW[0:16, :], I["sshift"][l])
        kb.memset(SHS[:], 0.0)
        for half in range(2):
            bk = kb.bank()
            ms = range(0, 8) if half == 0 else range(8, 15)
            for m in ms:
                Mm = 128 if m < 14 else 32
                kb.mm(bk[0:Mm, (m % 8) * 16:(m % 8) * 16 + 16], lhsT=SROW[0:16, m * 128:m * 128 + Mm], rhs=cf("ident")[0:16, 0:16])
            if half == 0:
                kb.cp(SHS[:, 0:8, :], _view(bk[:, 0:128], [8, 16]), eng="dve")
            else:
                kb.cp(SHS[:, 8:14, :], _view(bk[:, 0:96], [6, 16]), eng="dve")
                kb.cp(SHS[0:32, 14, :], bk[0:32, 96:112], eng="dve")

        MU = vec("mu")
        sbi = [0]
        NBLK = TP // NB
        bankA = lambda: kb.bank_of((0, 1, 2))
        bankB = lambda: kb.bank_of((3, 4, 5))
        bankC = lambda: kb.bank_of((6, 7))

        def geom(b):
            is_s = (b == NBLK)
            C = 4 if is_s else 64
            R = 2 * C
            NQ = 512 // R
            return is_s, C, R, NQ, NQ // 4, ("4" if is_s else "64"), b * NB

        def stage1a(b):
            is_s, C, R, NQ, nch, sfx, tok0 = geom(b)
            n = NB
            AT, RT, PC, t_gg, t_bon = AT3[b % 3], RT3[b % 3], PC3[b % 3], GG3[b % 3], BON3[b % 3]
            BT, KT, BH, KH, VB = BT2[b % 2], KT2[b % 2], BH2[b % 3], KH2[b % 3], VB2[b % 3]
            if is_s:
                for t in (AT, RT, BT, KT, BH, KH, VB):
                    kb.memset(t[:], 0.0)
                yield
            modnorm(tok0, n, 1, UB, SQn, RSn, TTn, bank=bankA)
            yield
            for half in range(2):
                bk = bankA()
                ms = range(0, 8) if half == 0 else range(8, 15)
                for m in ms:
                    Mm = 128 if m < 14 else 32
                    for k in range(8):
                        kb.mm(bk[0:Mm, (m % 8) * 64:(m % 8) * 64 + 64], lhsT=WIN[:, k, m * 128:m * 128 + Mm], rhs=UB[:, k, :],
                              start=(k == 0), stop=(k == 7))
                    if m % 2 == 1:
                        yield
                if not is_s:
                    if half == 0:
                        kb.cp(PA[:, 0:8, 1:65], _view(bk[:, 0:512], [8, 64]), eng="act")
                    else:
                        kb.cp(PA[:, 8:14, 1:65], _view(bk[:, 0:384], [6, 64]), eng="act")
                        kb.cp(PA[0:32, 14, 1:65], bk[0:32, 384:448], eng="act")
                else:
                    PAs = PA[:].rearrange("p m (s t) -> p m s t", t=5)
                    if half == 0:
                        kb.cp(PAs[:, 0:8, :, 1:5], bk[:, 0:512].rearrange("p (m s t) -> p m s t", m=8, t=4), eng="act")
                    else:
                        kb.cp(PAs[:, 8:14, :, 1:5], bk[:, 0:384].rearrange("p (m s t) -> p m s t", m=6, t=4), eng="act")
                        kb.cp(PAs[0:32, 14, :, 1:5], bk[0:32, 384:448].rearrange("p (s t) -> p s t", t=4), eng="act")
                yield
            if not is_s:
                kb.cp(PA[:, :, 0], LAST[:], eng="dve")
                kb.tt(XS[:], PA[:, :, 0:64], PA[:, :, 1:65], ALU.subtract)
                yield
                kb.tt(XS[:], XS[:], bc(MU[:, :, None], [128, 15, 64]), ALU.mult)
                yield
                kb.tt(XS[:], XS[:], PA[:, :, 1:65], ALU.add)
                kb.cp(LAST[:], PA[:, :, 64], eng="dve")
                yield
            else:
                PAs = PA[:].rearrange("p m (s t) -> p m s t", t=5)
                XSs = XS[:].rearrange("p m (s t) -> p m s t", t=4)
                kb.cp(PAs[:, :, :, 0], SHS[:], eng="dve")
                for m0, m1 in ((0, 8), (8, 15)):
                    kb.tt(XSs[:, m0:m1], PAs[:, m0:m1, :, 0:4], PAs[:, m0:m1, :, 1:5], ALU.subtract)
                yield
                kb.tt(XS[:], XS[:], bc(MU[:, :, None], [128, 15, 64]), ALU.mult)
                yield
                for m0, m1 in ((0, 8), (8, 15)):
                    kb.tt(XSs[:, m0:m1], XSs[:, m0:m1], PAs[:, m0:m1, :, 1:5], ALU.add)
                kb.cp(SHO[:], PAs[:, :, :, 4], eng="dve")
                yield
            xr, xk, xv = XS[:, 0:4, :], XS[:, 4:8, :], XS[:, 8:12, :]
            kb.act(LIN[0:64, 0, :], XS[0:64, 12, :], AF.Tanh)
            kb.cp(LIN[64:128, 0, :], XS[64:128, 12, :], eng="dve")
            kb.act(LIN[:, 1, :], XS[:, 13, :], AF.Sigmoid)
            kb.act(LIN[0:32, 2, :], XS[0:32, 14, :], AF.Sigmoid)
            yield
            bw = bankA()
            for j in range(4):
                kb.mm(bw[:, j * 64:(j + 1) * 64], lhsT=W2T[:, j * 128:(j + 1) * 128], rhs=LIN[:, 0, :])
                kb.mm(bw[:, 256 + j * 64:256 + (j + 1) * 64], lhsT=A2T[:, j * 128:(j + 1) * 128], rhs=LIN[:, 0, :])
            bg = bankA()
            for j in range(4):
                kb.mm(bg[:, j * 64:(j + 1) * 64], lhsT=G2T[:, 0, j * 128:(j + 1) * 128], rhs=LIN[:, 1, :], start=True, stop=False)
                kb.mm(bg[:, j * 64:(j + 1) * 64], lhsT=G2T[0:32, 1, j * 128:(j + 1) * 128], rhs=LIN[0:32, 2, :], start=False, stop=True)
            yield
            kb.tt(t_t1[:], _view(bw[:, 0:256], [4, 64]), bc(vec("w0")[:, :, None], [128, 4, 64]), ALU.add)
            kb.tt(t_aa[:], _view(bw[:, 256:512], [4, 64]), bc(vec("a0")[:, :, None], [128, 4, 64]), ALU.add)
            kb.cp(t_gg[:], _view(bg[:, 0:256], [4, 64]), eng="act")
            yield
            kb.act(t_t1[:], t_t1[:], AF.Sigmoid)
            kb.act(t_aa[:], t_aa[:], AF.Sigmoid)
            yield
            kb.act(t_wd[:], t_t1[:], AF.Exp, scale=-DECAY_K)
            kb.tt(t_kk[:], xk, bc(vec("k_k")[:, :, None], [128, 4, 64]), ALU.mult)
            yield
            kb.act(SQK[:], t_kk[:], AF.Square)
            kb.tt(t_t1[:], t_wd[:], bc(cf("nstart" + sfx)[:, None, 0:64], [128, 4, 64]), ALU.mult)
            kb.tt(t_t2[:], t_wd[:], bc(cf("start" + sfx)[:, None, 0:64], [128, 4, 64]), ALU.mult)
            yield
            bs = bankA()
            for j in range(4):
                kb.mm(bs[:, j * 64:(j + 1) * 64], lhsT=cb("bones"), rhs=SQK[:, j, :])
            for j in range(4):
                kb.scan(t_p[:, j, :], t_t1[:, j, :], t_t2[:, j, :], 1.0)
            yield
            kb.act(t_t1[:], _view(bs[:, 0:256], [4, 64]), AF.Sqrt)
            kb.recip(t_ip[:], t_p[:])
            kb.recip(t_t2[:], t_wd[:])
            yield
            kb.tt(t_pex[:], t_p[:], t_t2[:], ALU.mult)
            kb.cp(PC[:, 0:NQ].rearrange("p (s j) -> p j s", j=4),
                  t_p[:].rearrange("p j (s c) -> p j s c", c=C)[:, :, :, C - 1], eng="dve")
            kb.ts(t_t1[:], t_t1[:], 1e-12, None, ALU.max)
            yield
            kb.recip(t_t1[:], t_t1[:])
            yield
            kb.tt(t_kk[:], t_kk[:], t_t1[:], ALU.mult)
            yield
            kb.tt(t_t2[:], t_kk[:], t_aa[:], ALU.mult)
            kb.tt(t_t1[:], t_aa[:], bc(vec("k_a")[:, :, None], [128, 4, 64]), ALU.mult)
            kb.stt(t_wd[:], t_kk[:], -1.0, t_pex[:], ALU.mult, ALU.mult)
            yield
            kb.tt(t_t2[:], t_t2[:], t_ip[:], ALU.mult)
            kb.tt(t_t1[:], t_t1[:], bc(OMKA[:, :, None], [128, 4, 64]), ALU.add)
            yield
            kb.tt(t_k2[:], xk, t_t1[:], ALU.mult)
            yield
            kb.tt(t_t1[:], xr, t_k2[:], ALU.mult)
            kb.tt(t_aa[:], t_k2[:], t_ip[:], ALU.mult)
            yield
            kb.tt(RKR[:], t_t1[:], bc(vec("r_k")[:, :, None], [128, 4, 64]), ALU.mult)
            yield
            brk = bankA()
            for j in range(4):
                kb.mm(brk[:, j * 64:(j + 1) * 64], lhsT=cb("bones"), rhs=RKR[:, j, :])
            PCq = PC[:, 0:NQ].rearrange("p (s j) -> p j s", j=4)
            for h in range(2):
                ps_ = slice(64 * h, 64 * h + 64)

                def dst(tile):
                    return tile[ps_, :].rearrange("p (s j r) -> p j s r", j=4, r=R)[:, :, :, h * C:(h + 1) * C]

                def src(t3):
                    return t3[ps_].rearrange("p j (s c) -> p j s c", c=C)

                kb.cp(dst(AT), src(t_wd), eng="dve")
                kb.tt(dst(RT), src(xr), src(t_p), ALU.mult)
                yield
                kb.cp(dst(BT), src(t_t2), eng="act")
                kb.cp(dst(KT), src(t_aa), eng="act")
                kb.tt(dst(BH), src(t_t2), bc(PCq[ps_, :, :, None], [64, 4, nch, C]), ALU.mult)
                yield
                kb.tt(dst(KH), src(t_aa), bc(PCq[ps_, :, :, None], [64, 4, nch, C]), ALU.mult)
                kb.cp(dst(VB), src(xv), eng="act")
                yield
            kb.tt(t_bon[:], _view(brk[:, 0:256], [4, 64]), xv, ALU.mult)
            yield
            if b == NBLK - 1:
                bk = bankA()
                kb.mm(bk[0:15, 0:128], lhsT=LAST[:, 0:15], rhs=cf("ident"))
                kb.cp(SHT[0:15, 0, :], bk[0:15, 0:128], eng="dve")
                kb.dma("sp", O["shift_p"][l, 0:1792].rearrange("(c p) -> c p", p=128), SHT[0:14, 0, :])
                kb.dma("sp", O["shift_p"][l:l + 1, 1792:1824], SHT[14:15, 0, 0:32])
                yield
            if is_s:
                for g4 in range(4):
                    bk = bankA()
                    ms = range(g4 * 4, min(g4 * 4 + 4, 15))
                    for m in ms:
                        Mm = 128 if m < 14 else 32
                        kb.mm(bk[0:16, (m % 4) * 128:(m % 4) * 128 + Mm], lhsT=SHO[0:Mm, m, :], rhs=cf("ident")[0:Mm, 0:Mm])
                    if g4 < 3:
                        kb.cp(SHT[0:16, g4 * 4:g4 * 4 + 4, :], _view(bk[0:16, 0:512], [4, 128]), eng="dve")
                    else:
                        kb.cp(SHT[0:16, 12:14, :], _view(bk[0:16, 0:256], [2, 128]), eng="dve")
                        kb.cp(SHT[0:16, 14, 0:32], bk[0:16, 256:288], eng="dve")
                    yield
                kb.dma("sp", O["shift_s"][l, :, 0:1792], SHT[0:16, 0:14, :].rearrange("p m c -> p (m c)"))
                kb.dma("sp", O["shift_s"][l, :, 1792:1824], SHT[0:16, 14, 0:32])
                yield

        def stage1b(b):
            is_s, C, R, NQ, nch, sfx, tok0 = geom(b)
            AT, RT = AT3[b % 3], RT3[b % 3]
            BT, KT = BT2[b % 2], KT2[b % 2]
            AAK, ARB, ARK, MM = AAK2[b % 2], ARB2[b % 2], ARK2[b % 2], MM2[b % 2]
            msu, msuT, mu_ = cf("msu" + sfx), cf("msuT" + sfx), cf("mu" + sfx)
            if is_s:
                for t in (NM, NTM, QB, QTB, AAK, ARB, ARK, MM):
                    kb.memset(t[:], 0.0)
                yield
            Mq = lambda tile: tile[0:R, :].rearrange("p (q r) -> p q r", r=R)
            MqK = lambda tile: tile[:, :].rearrange("p (q r) -> p q r", r=R)
            Fq = MqK

            def prod(lt, rt, mask, out_t):
                bk_ = bankB()
                for q in range(NQ):
                    kb.mm(bk_[0:R, q * R:(q + 1) * R], lhsT=Fq(lt)[:, q, :], rhs=Fq(rt)[:, q, :])
                    if q % 8 == 7:
                        yield
                kb.tt(Mq(out_t), bk_[0:R, :].rearrange("p (q r) -> p q r", r=R), bc(mask[0:R, None, 0:R], [R, NQ, R]), ALU.mult)
                yield

            yield from prod(BT, AT, msu, NM)
            yield from prod(AT, BT, msuT, NTM)
            kb.tt(Mq(MM), Mq(NM), bc(cf("ident")[0:R, None, 0:R], [R, NQ, R]), ALU.add)
            yield
            nlev = 5 if not is_s else 1
            Q, QT = NM, NTM
            Qn, QTn = QB, QTB
            extra = [(KT, AT, msu, AAK), (BT, RT, mu_, ARB), (KT, RT, mu_, ARK)]
            for lev in range(nlev):
                last = (lev == nlev - 1)
                b2 = bankB()
                for q in range(NQ):
                    kb.mm(b2[0:R, q * R:(q + 1) * R], lhsT=MqK(Q)[:, q, :], rhs=MqK(QT)[:, q, :])
                    if q % 8 == 7:
                        yield
                kb.cp(QTn[0:R, :], b2[0:R, :], eng="act")
                yield
                if not last:
                    b1 = bankB()
                    for q in range(NQ):
                        kb.mm(b1[0:R, q * R:(q + 1) * R], lhsT=MqK(QT)[:, q, :], rhs=MqK(Q)[:, q, :])
                        if q % 8 == 7:
                            yield
                    kb.cp(Qn[0:R, :], b1[0:R, :], eng="act")
                    yield
                b3 = bankB()
                for q in range(NQ):
                    kb.mm(b3[0:R, q * R:(q + 1) * R], lhsT=MqK(QTn)[:, q, :], rhs=MqK(MM)[:, q, :])
                    if q % 8 == 7:
                        yield
                kb.tt(MM[0:R, :], b3[0:R, :], MM[0:R, :], ALU.add)
                yield
                Q, QT, Qn, QTn = Qn, QTn, Q, QT
                if extra:
                    yield from prod(*extra.pop(0))
            while extra:
                yield from prod(*extra.pop(0))

        def stage2(b):
            is_s, C, R, NQ, nch, sfx, tok0 = geom(b)
            n = NB
            AT, RT, PC, t_gg, t_bon = AT3[b % 3], RT3[b % 3], PC3[b % 3], GG3[b % 3], BON3[b % 3]
            BH, KH, VB = BH2[b % 3], KH2[b % 3], VB2[b % 3]
            AAK, ARB, ARK, MM = AAK2[b % 2], ARB2[b % 2], ARK2[b % 2], MM2[b % 2]
            istk, tokm = cb("istack" + sfx), cf("tokmask" + sfx)
            if is_s:
                for t in (BHT, KHT, VT, ZT, UT, YNB):
                    kb.memset(t[:], 0.0)
                yield
            KR = 128
            MqK = lambda tile: tile[:, :].rearrange("p (q r) -> p q r", r=R)
            Fq = MqK
            for gi in range(nch):
                q0 = gi * 4
                if is_s:
                    seq = gi
                    kb.dma("sp", SI[0:64, :, :], I["swkv"][l, seq].rearrange("h v k -> v h k"))
                    sb_in = SBF[sbi[0] % 2]
                    bk_ = bankC()
                    for j in range(4):
                        kb.mm(bk_[:, j * 64:(j + 1) * 64], lhsT=SI[0:64, 2 * j:2 * j + 2, :].rearrange("p h k -> p (h k)"),
                              rhs=cf("ident")[0:64, 0:64])
                    kb.cp(S32[:], _view(bk_[:, 0:256], [4, 64]), eng="dve")
                    kb.cp(sb_in[:], S32[:], eng="act")
                    yield
                else:
                    sb_in = SBF[sbi[0] % 2]
                sb_out = SBF[(sbi[0] + 1) % 2]
                sbi[0] += 1
                b_ = bankC()
                for j in range(4):
                    kb.mm(b_[0:R, j * 128:(j + 1) * 128], lhsT=Fq(BH)[:, q0 + j, :], rhs=cb("ident"))
                kb.cp(BHT[0:R], _view(b_[0:R, 0:512], [4, 128]), eng="act")
                yield
                b_ = bankC()
                for j in range(4):
                    kb.mm(b_[0:R, j * 128:(j + 1) * 128], lhsT=Fq(KH)[:, q0 + j, :], rhs=cb("ident"))
                kb.cp(KHT[0:R], _view(b_[0:R, 0:512], [4, 128]), eng="dve")
                yield
                b_ = bankC()
                for j in range(4):
                    kb.mm(b_[0:R, j * 64:(j + 1) * 64], lhsT=Fq(VB)[:, q0 + j, :], rhs=cb("istack64"))
                kb.cp(VT[0:R], _view(b_[0:R, 0:256], [4, 64]), eng="act")
                yield
                bz = bankC()
                for j in range(4):
                    kb.mm(bz[0:R, j * 64:(j + 1) * 64], lhsT=MqK(AAK)[:, q0 + j, :], rhs=VT[0:KR, j, :], start=True, stop=False)
                    kb.mm(bz[0:R, j * 64:(j + 1) * 64], lhsT=Fq(AT)[:, q0 + j, :], rhs=sb_in[:, j, :], start=False, stop=True)
                kb.cp(ZT[0:R], _view(bz[0:R, 0:256], [4, 64]), eng="act")
                yield
                bu = bankC()
                for j in range(4):
                    kb.mm(bu[0:R, j * 64:(j + 1) * 64], lhsT=MqK(MM)[:, q0 + j, :], rhs=ZT[0:KR, j, :])
                kb.cp(UT[0:R], _view(bu[0:R, 0:256], [4, 64]), eng="dve")
                yield
                bs_ = bankC()
                for j in range(4):
                    o = bs_[:, j * 64:(j + 1) * 64]
                    kb.mm(o, lhsT=BHT[0:KR, j, :], rhs=UT[0:KR, j, :], start=True, stop=False)
                    kb.mm(o, lhsT=KHT[0:KR, j, :], rhs=VT[0:KR, j, :], start=False, stop=True)
                by = bankC()
                for j in range(4):
                    o = by[0:R, j * 64:(j + 1) * 64]
                    kb.mm(o, lhsT=Fq(RT)[:, q0 + j, :], rhs=sb_in[:, j, :], start=True, stop=False)
                    kb.mm(o, lhsT=MqK(ARB)[:, q0 + j, :], rhs=UT[0:KR, j, :], start=False, stop=False)
                    kb.mm(o, lhsT=MqK(ARK)[:, q0 + j, :], rhs=VT[0:KR, j, :], start=False, stop=True)
                kb.tt(S32[:], S32[:], bc(PC[:, q0:q0 + 4, None], [128, 4, 64]), ALU.mult)
                yield
                kb.tt(S32[:], S32[:], _view(bs_[:, 0:256], [4, 64]), ALU.add)
                kb.cp(YC[0:R], _view(by[0:R, 0:256], [4, 64]), eng="act")
                yield
                kb.cp(sb_out[:], S32[:], eng="act")
                Yv = YC[0:R]
                kb.reduce_sum(ST1[0:R], Yv)
                yield
                kb.act(SQY[0:R], Yv, AF.Square)
                kb.ts(STM[0:R], ST1[0:R], 1.0 / 64, None, ALU.mult)
                yield
                kb.reduce_sum(ST2[0:R], SQY[0:R])
                kb.tt(STV[0:R], STM[0:R], STM[0:R], ALU.mult)
                yield
                kb.stt(STV[0:R], ST2[0:R], 1.0 / 64, STV[0:R], ALU.mult, ALU.subtract)
                kb.tt(YC[0:R], Yv, bc(STM[0:R, :, None], [R, 4, 64]), ALU.subtract)
                yield
                kb.act(STV[0:R], STV[0:R], AF.Sqrt, bias=LN_EPS)
                yield
                kb.recip(STV[0:R], STV[0:R])
                yield
                kb.tt(YC[0:R], YC[0:R], bc(STV[0:R, :, None], [R, 4, 64]), ALU.mult)
                yield
                for h in range(2):
                    kb.ts(YNB[0:R, :, h * 64:(h + 1) * 64], YC[0:R], tokm[0:R, h:h + 1], None, ALU.mult)
                yield
                bf_ = bankC()
                CW = max(C, 8)
                for j in range(4):
                    kb.mm(bf_[:, j * CW:(j + 1) * CW], lhsT=YNB[0:KR, j, :], rhs=istk[0:KR, 0:CW])
                kb.cp(YF[:, :, gi * C:(gi + 1) * C], _view(bf_[:, 0:4 * CW], [4, CW])[:, :, 0:C], eng="act")
                yield
                if is_s or b == NBLK - 1:
                    bo_ = bankC()
                    for j in range(4):
                        kb.mm(bo_[0:64, j * 128:(j + 1) * 128], lhsT=S32[:, j, :], rhs=cf("ident"))
                    kb.cp(SO[0:64], _view(bo_[0:64, 0:512], [4, 128]), eng="dve")
                    dst_ = O["wkv_s"][l, gi] if is_s else O["wkv_p"][l]
                    kb.dma("sp", dst_.rearrange("h v k -> v h k"), SO[0:64].rearrange("p j (h k) -> p (j h) k", h=2))
                    yield
            kb.tt(p2a[:], YF[:], bc(vec("ln_w")[:, :, None], [128, 4, 64]), ALU.mult)
            yield
            kb.tt(p2a[:], p2a[:], bc(vec("ln_b")[:, :, None], [128, 4, 64]), ALU.add)
            yield
            kb.tt(p2a[:], p2a[:], t_bon[:], ALU.add)
            yield
            kb.tt(YAb[:], p2a[:], t_gg[:], ALU.mult)
            kb.dma("sp", YAB[:, 0:4, tok0:tok0 + n], YAb[:])
            yield

        nblocks = NBLK + 1
        if DBG.get("nblk") is not None:
            nblocks = DBG["nblk"]
        for step in range(nblocks + 2):
            gens = []
            if step < nblocks:
                gens.append(stage1a(step))
            if 0 <= step - 1 < nblocks:
                gens.append(stage1b(step - 1))
            if 0 <= step - 2 < nblocks:
                gens.append(stage2(step - 2))
            if DBG.get("no_interleave", False):
                for g in gens[::-1]:
                    for _ in g:
                        pass
                continue
            alive = list(gens)
            while alive:
                for g in list(alive):
                    try:
                        next(g)
                    except StopIteration:
                        alive.remove(g)

    def pass_pool(l):
        ar.reset()
        WPB = ar.alloc([8, 512], BF16)
        WPL = ar.alloc([4, 128], BF16)
        kb.dma("pool", WPB, I["w_in"][l].rearrange("(k p) n -> p k n", p=128)[:, :, APJ:PT])
        kb.dma("pool", WPL, I["w_pool"][l].rearrange("g c d -> c g d"))
        UB = ar.alloc([8, 512], BF16)
        SQ = ar.alloc([8, 512], BF16)
        RS = ar.alloc([512], F32)
        TT = [ar.alloc([512], F32) for _ in range(2)]
        PBH = ar.alloc([4, 527], F32)
        SA = ar.alloc([4, 527], F32)
        SB = ar.alloc([4, 527], F32)
        DP = ar.alloc([4, 512], BF16)
        YBb = ar.alloc([4, 512], BF16)
        PROW = ar.alloc([512], F32)
        POUT = ar.alloc([512], F32)
        TMPH = ar.alloc([4, 120], F32)
        PS = vec("pool_scale")
        WIN_ = (2, 4, 8, 16)
        kb.memset(PBH[:], 0.0)

        def wsum(x, sa, sb_, L, nd):
            def sl(v, g0, g1, a, b_):
                return v[:, g0:g1, a:b_] if nd == 3 else v[:, g0:g1, :, a:b_]
            kb.tt(sl(sa, 0, 4, 1, L), sl(x, 0, 4, 1, L), sl(x, 0, 4, 0, L - 1), ALU.add)
            kb.tt(sl(sb_, 1, 4, 3, L), sl(sa, 1, 4, 3, L), sl(sa, 1, 4, 1, L - 2), ALU.add)
            kb.tt(sl(sa, 2, 4, 7, L), sl(sb_, 2, 4, 7, L), sl(sb_, 2, 4, 3, L - 4), ALU.add)
            kb.tt(sl(sb_, 3, 4, 15, L), sl(sa, 3, 4, 15, L), sl(sa, 3, 4, 7, L - 8), ALU.add)
            return [sa, sb_, sa, sb_]

        for (t0, n) in TILES[:4]:
            modnorm(t0, n, 1, UB, SQ, RS, TT)
            for g in range(4):
                bk = kb.bank()
                for k in range(8):
                    kb.mm(bk[:, 0:n], lhsT=WPB[:, k, g * 128:(g + 1) * 128], rhs=UB[:, k, 0:n], start=(k == 0), stop=(k == 7))
                kb.cp(PBH[:, g, 15:15 + n], bk[:, 0:n], eng="act")
            L = 15 + n
            fin = wsum(PBH, SA, SB, L, 3)
            for g in range(4):
                if t0 == 0:
                    kb.tt(fin[g][:, g, 15:30], fin[g][:, g, 15:30], cf("ratio")[:, g * 15:(g + 1) * 15], ALU.mult)
                kb.stt(DP[:, g, 0:n], fin[g][:, g, 15:L], 1.0 / WIN_[g], PBH[:, g, 15:L], ALU.mult, ALU.subtract)
            for g in range(4):
                bk = kb.bank()
                kb.mm(bk[:, 0:n], lhsT=WPL[:, g, :], rhs=DP[:, g, 0:n])
                kb.ts(YBb[:, g, 0:n], bk[:, 0:n], PS[:, g:g + 1], None, ALU.mult)
            kb.dma("sp", YAB[:, 4:8, t0:t0 + n], YBb[:, :, 0:n])
            kb.cp(TMPH[:, :, 0:15], PBH[:, :, n:n + 15], eng="dve")
            kb.cp(PBH[:, :, 0:15], TMPH[:, :, 0:15], eng="dve")
        bk = kb.bank()
        for g in range(4):
            kb.mm(bk[0:15, g * 128:(g + 1) * 128], lhsT=TMPH[:, g, 0:15], rhs=cf("ident"))
        kb.cp(POUT[0:15, :], bk[0:15, 0:512], eng="dve")
        kb.dma("sp", O["pool_p"][l], POUT[0:15, :])
        PBs = PBH[:, :, 0:304].rearrange("p g (s t) -> p g s t", t=19)
        SAs = SA[:, :, 0:304].rearrange("p g (s t) -> p g s t", t=19)
        SBs = SB[:, :, 0:304].rearrange("p g (s t) -> p g s t", t=19)
        sp_rows = I["spool"][l].rearrange("s i c -> (s i) c")
        for hh in range(2):
            kb.dma("sp", PROW[0:120, :], sp_rows[hh * 120:(hh + 1) * 120, :])
            bk = kb.bank()
            for g in range(4):
                kb.mm(bk[:, g * 120:(g + 1) * 120], lhsT=PROW[0:120, g * 128:(g + 1) * 128], rhs=cf("ident")[0:120, 0:120])
            kb.cp(PBs[:, :, hh * 8:(hh + 1) * 8, 0:15], bk[:, 0:480].rearrange("p (g s t) -> p g s t", g=4, t=15), eng="dve")
        modnorm(TP, 64, 1, UB, SQ, RS, TT)
        bk = kb.bank()
        for g in range(4):
            for k in range(8):
                kb.mm(bk[:, g * 64:(g + 1) * 64], lhsT=WPB[:, k, g * 128:(g + 1) * 128], rhs=UB[:, k, 0:64], start=(k == 0), stop=(k == 7))
        kb.cp(PBs[:, :, :, 15:19], bk[:, 0:256].rearrange("p (g s t) -> p g s t", g=4, t=4), eng="act")
        fin = wsum(PBs, SAs, SBs, 19, 4)
        fv = [SAs, SBs, SAs, SBs]
        for g in range(4):
            kb.stt(_view(DP[:, g, 0:64], [16, 4]), fv[g][:, g, :, 15:19], 1.0 / WIN_[g], PBs[:, g, :, 15:19], ALU.mult, ALU.subtract)
        bk = kb.bank()
        for g in range(4):
            kb.mm(bk[:, g * 64:(g + 1) * 64], lhsT=WPL[:, g, :], rhs=DP[:, g, 0:64])
        for g in range(4):
            kb.ts(YBb[:, g, 0:64], bk[:, g * 64:(g + 1) * 64], PS[:, g:g + 1], None, ALU.mult)
        kb.dma("sp", YAB[:, 4:8, TP:T], YBb[:, :, 0:64])
        po_rows = O["pool_s"][l].rearrange("s i c -> (s i) c")
        for hh in range(2):
            kb.cp(TMPH[:].rearrange("p g (s t) -> p g s t", t=15), PBs[:, :, hh * 8:(hh + 1) * 8, 4:19], eng="dve")
            bk = kb.bank()
            for g in range(4):
                kb.mm(bk[0:120, g * 128:(g + 1) * 128], lhsT=TMPH[:, g, :], rhs=cf("ident"))
            kb.cp(POUT[0:120, :], bk[0:120, 0:512], eng="dve")
            kb.dma("sp", po_rows[hh * 120:(hh + 1) * 120, :], POUT[0:120, :])

    def pass_b(l):
        ar.reset()
        WGT = ar.alloc([8, 2048], BF16)
        WBA = ar.alloc([4, 1024], BF16)
        WBB = ar.alloc([4, 1024], BF16)
        WOT = ar.alloc([8, 1024], BF16)
        kb.dma("pool", WGT, I["w_gate"][l].rearrange("(k p) n -> p k n", p=128))
        kb.dma("pool", WBA, I["w_br_a"][l].rearrange("(j p) n -> p j n", p=128))
        kb.dma("pool", WBB, I["w_br_b"][l].rearrange("(j p) n -> p j n", p=128))
        kb.dma("pool", WOT, I["w_out"][l].rearrange("(k p) n -> p k n", p=128))
        YT = ar.alloc([8, 512], BF16)
        UB = ar.alloc([8, 512], BF16)
        SQ = ar.alloc([8, 512], BF16)
        RS = ar.alloc([512], F32)
        TT = [ar.alloc([512], F32) for _ in range(2)]
        GA = ar.alloc([512], F32); GB = ar.alloc([512], F32); T1 = ar.alloc([512], F32); T2 = ar.alloc([512], F32)
        MG = ar.alloc([8, 512], BF16)
        BGv = vec("b_gate")
        for (t0, n) in TILES:
            kb.dma("sp", YT[:, :, 0:n], YAB[:, :, t0:t0 + n])
            modnorm(t0, n, 1, UB, SQ, RS, TT)
            for m in range(8):
                ba = kb.bank(); bb = kb.bank(); bc_ = kb.bank(); bd = kb.bank()
                for k in range(8):
                    kb.mm(ba[:, 0:n], lhsT=WGT[:, k, m * 128:(m + 1) * 128], rhs=UB[:, k, 0:n], start=(k == 0), stop=(k == 7))
                for k in range(8):
                    kb.mm(bb[:, 0:n], lhsT=WGT[:, k, 1024 + m * 128:1024 + (m + 1) * 128], rhs=UB[:, k, 0:n], start=(k == 0), stop=(k == 7))
                for j in range(4):
                    kb.mm(bc_[:, 0:n], lhsT=WBA[:, j, m * 128:(m + 1) * 128], rhs=YT[:, j, 0:n], start=(j == 0), stop=(j == 3))
                for j in range(4):
                    kb.mm(bd[:, 0:n], lhsT=WBB[:, j, m * 128:(m + 1) * 128], rhs=YT[:, 4 + j, 0:n], start=(j == 0), stop=(j == 3))
                kb.act(GA[:, 0:n], ba[:, 0:n], AF.Sigmoid, bias=BGv[:, m:m + 1], scale=1.0)
                kb.act(GB[:, 0:n], bb[:, 0:n], AF.Sigmoid, bias=BGv[:, 8 + m:9 + m], scale=1.0)
                kb.tt(T1[:, 0:n], bc_[:, 0:n], GA[:, 0:n], ALU.mult)
                kb.tt(T2[:, 0:n], bd[:, 0:n], GB[:, 0:n], ALU.mult)
                kb.tt(MG[:, m, 0:n], T1[:, 0:n], T2[:, 0:n], ALU.add)
            for m2 in range(8):
                bo = kb.bank()
                for m in range(8):
                    kb.mm(bo[:, 0:n], lhsT=WOT[:, m, m2 * 128:(m2 + 1) * 128], rhs=MG[:, m, 0:n], start=(m == 0), stop=(m == 7))
                resid(m2, t0, n, bo, 1)

    def mixer(l, only_a=False):
        ar.log = []
        pass_a(l)
        if only_a:
            DBG["passA_log"] = list(ar.log)
            return
        DBG["marks"].append(("pool%d" % l, sum(1 for o in kb.S.ops if o.eng == "pe"), len(kb.S.ops)))
        pass_pool(l)
        DBG["marks"].append(("passB%d" % l, sum(1 for o in kb.S.ops if o.eng == "pe"), len(kb.S.ops)))
        pass_b(l)

    return mixer


def _shard_inputs(inputs):
    maps = []
    shared = {n: np.ascontiguousarray(np.asarray(inputs[n], dtype=np.float32)) for n in WNAMES}
    xp = np.asarray(inputs["x_prompt"], dtype=np.float32)
    xs = np.asarray(inputs["x_sample"], dtype=np.float32)
    cp_ = np.asarray(inputs["c_prompt"], dtype=np.float32)
    cs = np.asarray(inputs["c_sample"], dtype=np.float32)
    swkv = np.asarray(inputs["state_wkv"], dtype=np.float32)
    ssh = np.asarray(inputs["state_shift"], dtype=np.float32)
    spl = np.asarray(inputs["state_pool"], dtype=np.float32)
    for i in range(NCORES):
        sl = slice(NSEQ * i, NSEQ * (i + 1))
        m = dict(shared)
        m["xin"] = np.ascontiguousarray(np.concatenate([xp[i], xs[sl].reshape(TS, D)], axis=0))
        m["cin"] = np.ascontiguousarray(np.concatenate([cp_[i:i + 1], cs[sl]], axis=0))
        m["swkv"] = np.ascontiguousarray(swkv[:, sl])
        m["sshift"] = np.ascontiguousarray(ssh[:, sl, 0, :])
        m["spool"] = np.ascontiguousarray(spl[:, sl])
        m["consts"] = CONSTS_NP
        maps.append(m)
    return maps


_NC_CACHE = {}
DBG = {}


def kernel(**inputs):
    if "nc" not in _NC_CACHE:
        _NC_CACHE["nc"] = build_program()
    nc = _NC_CACHE["nc"]
    maps = _shard_inputs(inputs)
    res = run_bass_kernel_spmd(nc, maps, core_ids=list(range(NCORES)))
    R = res.results
    y_p = np.stack([R[i]["y"][:TP] for i in range(NCORES)], axis=0)
    y_s = np.concatenate([R[i]["y"][TP:].reshape(NSEQ, DEC, D) for i in range(NCORES)], axis=0)
    wkv_p = np.stack([R[i]["wkv_p"] for i in range(NCORES)], axis=1)
    shift_p = np.stack([R[i]["shift_p"] for i in range(NCORES)], axis=1)[:, :, None, :]
    pool_p = np.stack([R[i]["pool_p"] for i in range(NCORES)], axis=1)
    wkv_s = np.concatenate([R[i]["wkv_s"] for i in range(NCORES)], axis=1)
    shift_s = np.concatenate([R[i]["shift_s"] for i in range(NCORES)], axis=1)[:, :, None, :]
    pool_s = np.concatenate([R[i]["pool_s"] for i in range(NCORES)], axis=1)
    f = lambda a: np.ascontiguousarray(a, dtype=np.float32)
    return (f(y_p), f(y_s), f(wkv_p), f(shift_p), f(pool_p), f(wkv_s), f(shift_s), f(pool_s))
```

```python
import numpy as np
from contextlib import ExitStack
import concourse.bass as bass
import concourse.mybir as mybir
from concourse.bass_utils import run_bass_kernel_spmd

F32 = mybir.dt.float32
BF16 = mybir.dt.bfloat16
ALU = mybir.AluOpType
AF = mybir.ActivationFunctionType
AX = mybir.AxisListType


class _Op:
    __slots__ = ("idx", "eng", "fn", "deps", "dma", "inc", "incval", "sem", "waits", "ring_prev", "gidx")

    def __init__(self, idx, eng, fn, deps, dma):
        self.idx, self.eng, self.fn, self.deps, self.dma = idx, eng, fn, deps, dma
        self.inc = False
        self.incval = 0
        self.sem = None
        self.waits = []
        self.ring_prev = None


def _region(ap):
    t = ap.tensor
    name = ap.name
    space = str(ap.space)
    pat = ap.ap
    off = int(ap.offset)
    es = mybir.dt.size(ap.dtype)
    if space == "DRAM":
        lo = off
        hi = off + 1
        for st, cnt in pat:
            hi += abs(int(st)) * (int(cnt) - 1)
        return (name, 0, 1, lo * es, hi * es)
    if "PSUM" in space.upper():
        return (name, 0, 128, 0, 1 << 30)
    shp = list(t.shape)
    pstep = 1
    for s in shp[1:]:
        pstep *= int(s)
    p0 = off // pstep
    f0 = off % pstep
    st0, cnt0 = pat[0]
    if int(st0) == pstep or int(cnt0) == 1:
        npart = int(cnt0)
        rest = pat[1:]
    else:
        npart = 1
        rest = pat
    hi = f0 + 1
    for st, cnt in rest:
        hi += abs(int(st)) * (int(cnt) - 1)
    return (name, p0, p0 + npart, f0 * es, hi * es)


class Sched:
    COMPUTE = ("pe", "act", "dve", "pool")
    RING = 8

    def __init__(self, nc):
        self.nc = nc
        self.ops = []
        self.rec = {}
        self.nd = {"sp": 0, "act": 0, "pool": 0}

    def op(self, eng, fn, reads=(), writes=(), dma=False):
        idx = len(self.ops)
        deps = set()
        rr = [_region(a) for a in reads]
        ww = [_region(a) for a in writes]
        for (name, p0, p1, f0, f1) in rr:
            for r in self.rec.get(name, ()):
                if r[5] and r[0] < p1 and p0 < r[1] and r[2] < f1 and f0 < r[3]:
                    deps.add((r[4], "raw"))
        for (name, p0, p1, f0, f1) in ww:
            for r in self.rec.get(name, ()):
                if r[0] < p1 and p0 < r[1] and r[2] < f1 and f0 < r[3]:
                    deps.add((r[4], "waw" if r[5] else "war"))
        o = _Op(idx, eng, fn, deps, dma)
        self.ops.append(o)
        for (name, p0, p1, f0, f1) in ww:
            lst = self.rec.setdefault(name, [])
            lst[:] = [r for r in lst if not (p0 <= r[0] and r[1] <= p1 and f0 <= r[2] and r[3] <= f1)]
            lst.append([p0, p1, f0, f1, idx, True])
        for (name, p0, p1, f0, f1) in rr:
            lst = self.rec.setdefault(name, [])
            lst[:] = [r for r in lst if not ((not r[5]) and self.ops[r[4]].eng == eng
                                             and (not self.ops[r[4]].dma) and (not dma)
                                             and p0 <= r[0] and r[1] <= p1 and f0 <= r[2] and r[3] <= f1)]
            lst.append([p0, p1, f0, f1, idx, False])
        return o

    NSEM = 12
    CH = 512

    def lower(self, stack):
        nc = self.nc
        ops = self.ops
        for o in ops:
            need = []
            best = {}
            for (d, kind) in o.deps:
                p = ops[d]
                if p.dma:
                    need.append(d)
                elif o.dma or p.eng != o.eng or o.eng != "pe":
                    if p.eng not in best or best[p.eng] < d:
                        best[p.eng] = d
            need.extend(best.values())
            o.deps = need
            for d in need:
                ops[d].inc = True
        self.csem = {e: [stack.enter_context(nc.semaphore("s_%s%d" % (e, i))) for i in range(self.NSEM)]
                     for e in self.COMPUTE}
        self.rings = {q: [stack.enter_context(nc.semaphore("r_%s%d" % (q, i))) for i in range(self.RING)]
                      for q in ("sp", "act", "pool")}
        cnt = {e: 0 for e in self.COMPUTE}
        dk = {"sp": 0, "act": 0, "pool": 0}
        dma_final = {}
        for o in ops:
            if o.dma:
                k = dk[o.eng]
                dk[o.eng] += 1
                o.sem = self.rings[o.eng][k % self.RING]
                o.incval = 16 * (k // self.RING + 1)
                o.ring_prev = (o.sem, 16 * (k // self.RING)) if k >= self.RING else None
                dma_final[(o.eng, k % self.RING)] = (o.sem, o.incval)
                o.gidx = None
            elif o.inc:
                g = cnt[o.eng]
                cnt[o.eng] += 1
                epoch = g // self.CH
                o.sem = self.csem[o.eng][epoch % self.NSEM]
                o.incval = (epoch // self.NSEM) * self.CH + (g % self.CH) + 1
                o.gidx = g
        waited_c = {e: {} for e in ("pe", "act", "dve", "pool", "sp")}
        waited_d = {e: {} for e in ("pe", "act", "dve", "pool", "sp")}
        for o in ops:
            wl = []
            wd = waited_d[o.eng]
            wc = waited_c[o.eng]
            if o.dma and o.ring_prev is not None:
                sem, val = o.ring_prev
                if wd.get(id(sem), 0) < val:
                    wd[id(sem)] = val
                    wl.append((sem, val))
            for d in o.deps:
                p = ops[d]
                if p.dma:
                    if wd.get(id(p.sem), 0) < p.incval:
                        wd[id(p.sem)] = p.incval
                        wl.append((p.sem, p.incval))
                else:
                    if wc.get(p.eng, -1) < p.gidx:
                        wc[p.eng] = p.gidx
                        wl.append((p.sem, p.incval))
            o.waits = wl
        self.final_waits = list(dma_final.values())
        per = {e: [] for e in ("pe", "act", "dve", "pool", "sp")}
        for o in ops:
            per[o.eng].append(o)

        def run(engobj, lst, final=False):
            for o in lst:
                for (sem, val) in o.waits:
                    engobj.wait_ge(sem, val)
                ins = o.fn(engobj)
                if o.dma:
                    ins.then_inc(o.sem, 16)
                elif o.inc:
                    ins.then_inc(o.sem, 1)
            if final:
                for (sem, val) in self.final_waits:
                    engobj.wait_ge(sem, val)

        with nc.Block() as block:
            @block.tensor
            def _(e):
                run(e, per["pe"])

            @block.scalar
            def _(e):
                run(e, per["act"])

            @block.vector
            def _(e):
                run(e, per["dve"])

            @block.gpsimd
            def _(e):
                run(e, per["pool"])

            @block.sync
            def _(e):
                run(e, per["sp"], final=True)


NCORES = 8
D = 1024
KC = 8
TP = 2048
NSEQ = 16
DEC = 4
TS = NSEQ * DEC
T = TP + TS
APJ = 1824
PT = 2336
DFF = 2816
NFC = DFF // 128
LN_EPS = 64e-5
NORM_EPS = 1e-6
DECAY_K = float(np.exp(-0.5))

WNAMES = ["norm_g", "w_mod", "b_mod", "w_ffn_in", "w_ffn_out", "w_in", "mu_shift", "w0", "w2", "a0", "a2", "g2",
          "k_k", "k_a", "r_k", "ln_x_w", "ln_x_b", "w_pool", "pool_scale", "w_br_a", "w_br_b", "w_gate", "b_gate",
          "w_out", "final_g"]
WSHAPES = {
    "norm_g": [2, 3, D], "w_mod": [2, D, 9 * D], "b_mod": [2, 9 * D], "w_ffn_in": [2, 2, D, 2 * DFF],
    "w_ffn_out": [2, 2, DFF, D], "w_in": [2, D, PT], "mu_shift": [2, APJ], "w0": [2, 512], "w2": [2, 64, 512],
    "a0": [2, 512], "a2": [2, 64, 512], "g2": [2, 160, 512], "k_k": [2, 512], "k_a": [2, 512], "r_k": [2, 8, 64],
    "ln_x_w": [2, 512], "ln_x_b": [2, 512], "w_pool": [2, 4, 128, 128], "pool_scale": [2, 512],
    "w_br_a": [2, 512, D], "w_br_b": [2, 512, D], "w_gate": [2, D, 2 * D], "b_gate": [2, 2 * D], "w_out": [2, D, D],
    "final_g": [D],
}


def _make_consts():
    cols = {}
    parts = []
    pos = [0]

    def add(name, arr):
        a = np.zeros((128, arr.shape[1]), np.float32)
        a[:arr.shape[0]] = arr
        cols[name] = (pos[0], arr.shape[1])
        pos[0] += arr.shape[1]
        parts.append(a)

    p = np.arange(128)
    add("ident", np.eye(128, dtype=np.float32))
    add("ones", np.ones((128, 128), np.float32))
    same = (p[:, None] // 64) == (p[None, :] // 64)
    add("bones", same.astype(np.float32))
    s = p[:, None] % 64
    t = p[None, :] % 64
    add("msu64", (same & (s < t)).astype(np.float32))
    add("msuT64", (same & (s > t)).astype(np.float32))
    add("mu64", (same & (s <= t)).astype(np.float32))
    add("istack64", (p[:, None] % 64 == np.arange(64)[None, :]).astype(np.float32))
    add("tokmask64", (p[:, None] // 64 == np.arange(2)[None, :]).astype(np.float32))
    q = np.arange(8)
    same4 = (q[:, None] // 4) == (q[None, :] // 4)
    s4 = q[:, None] % 4
    t4 = q[None, :] % 4
    add("msu4", (same4 & (s4 < t4)).astype(np.float32))
    add("msuT4", (same4 & (s4 > t4)).astype(np.float32))
    add("mu4", (same4 & (s4 <= t4)).astype(np.float32))
    add("istack4", (q[:, None] % 4 == np.arange(8)[None, :]).astype(np.float32))
    add("tokmask4", (q[:, None] // 4 == np.arange(2)[None, :]).astype(np.float32))
    tt = np.arange(128)
    add("start64", np.broadcast_to((tt % 64 == 0).astype(np.float32)[None, :], (128, 128)).copy())
    add("nstart64", np.broadcast_to((tt % 64 != 0).astype(np.float32)[None, :], (128, 128)).copy())
    add("start4", np.broadcast_to((tt[:64] % 4 == 0).astype(np.float32)[None, :], (128, 64)).copy())
    add("nstart4", np.broadcast_to((tt[:64] % 4 != 0).astype(np.float32)[None, :], (128, 64)).copy())
    ratio = np.zeros((4, 15), np.float32)
    for g, w in enumerate((2, 4, 8, 16)):
        for i in range(15):
            ratio[g, i] = w / min(w, i + 1)
    add("ratio", np.broadcast_to(ratio.reshape(1, 60), (128, 60)).copy())
    return np.concatenate(parts, axis=1), cols


DBG = {}
CONSTS_NP, CCOLS = _make_consts()
NCC = CONSTS_NP.shape[1]

VR = {}
_r = 0
for _n, _k in (("norm_g", 24), ("mu", 15), ("w0", 4), ("a0", 4), ("k_k", 4), ("k_a", 4), ("r_k", 4), ("ln_w", 4),
               ("ln_b", 4), ("pool_scale", 4), ("b_gate", 16), ("final_g", 8)):
    VR[_n] = (_r, _k)
    _r += _k
NVR = _r


def _prod(s):
    r = 1
    for v in s:
        r *= int(v)
    return r


def _view(ap2, shape):
    if len(shape) == 1:
        return ap2
    names = "abcdef"[:len(shape)]
    kw = {names[i]: int(shape[i]) for i in range(len(shape))}
    return ap2.rearrange("p (%s) -> p %s" % (" ".join(names), " ".join(names)), **kw)


class Arena:
    def __init__(self, base_bf16, nelem):
        self.base = base_bf16
        self.n = nelem
        self.off = 0
        self.peak = 0
        self.log = []

    def reset(self, off=0):
        self.off = off

    def alloc(self, shape, dtype):
        n = _prod(shape)
        nb = n * 2 if dtype == F32 else n
        off = (self.off + 15) // 16 * 16
        assert off + nb <= self.n, "arena overflow: need %d have %d" % (off + nb, self.n)
        v = self.base[:, off:off + nb]
        if dtype == F32:
            v = v.bitcast(F32)
        self.off = off + nb
        self.peak = max(self.peak, self.off)
        self.log.append((off, tuple(shape), "f32" if dtype == F32 else "bf16"))
        return _view(v, shape)


class KB:
    def __init__(self, nc, S, banks):
        self.nc, self.S, self.banks = nc, S, banks
        self.bi = 0
        self.flip = 0
        self.sub = {}

    def bank(self):
        b = self.banks[self.bi % len(self.banks)]
        self.bi += 1
        return b

    def bank_of(self, ids):
        c = self.sub.get(ids, 0)
        self.sub[ids] = c + 1
        return self.banks[ids[c % len(ids)]]

    def mm(self, out, lhsT, rhs, start=True, stop=True):
        self.S.op("pe", lambda e: e.matmul(out, lhsT=lhsT, rhs=rhs, start=start, stop=stop),
                  reads=[lhsT, rhs], writes=[out])

    def tr(self, out, in_, ident):
        self.S.op("pe", lambda e: e.transpose(out, in_, ident), reads=[in_, ident], writes=[out])

    def dma(self, q, out, in_):
        self.S.op(q, lambda e: e.dma_start(out=out, in_=in_), reads=[in_], writes=[out], dma=True)

    def tt(self, out, in0, in1, op, eng="dve"):
        self.S.op(eng, lambda e: e.tensor_tensor(out=out, in0=in0, in1=in1, op=op), reads=[in0, in1], writes=[out])

    def ts(self, out, in0, s1, s2, op0, op1=None, eng="dve"):
        rd = [in0] + [s for s in (s1, s2) if not isinstance(s, (int, float)) and s is not None]
        if op1 is None:
            self.S.op(eng, lambda e: e.tensor_scalar(out=out, in0=in0, scalar1=s1, scalar2=None, op0=op0),
                      reads=rd, writes=[out])
        else:
            self.S.op(eng, lambda e: e.tensor_scalar(out=out, in0=in0, scalar1=s1, scalar2=s2, op0=op0, op1=op1),
                      reads=rd, writes=[out])

    def stt(self, out, in0, scalar, in1, op0, op1, eng="dve"):
        rd = [in0, in1] + ([] if isinstance(scalar, (int, float)) else [scalar])
        self.S.op(eng, lambda e: e.scalar_tensor_tensor(out=out, in0=in0, scalar=scalar, in1=in1, op0=op0, op1=op1),
                  reads=rd, writes=[out])

    def act(self, out, in_, func, bias=None, scale=None):
        rd = [in_] + [s for s in (bias, scale) if s is not None and not isinstance(s, (int, float))]
        kw = {}
        if bias is not None:
            kw["bias"] = bias
        if scale is not None:
            kw["scale"] = scale
        self.S.op("act", lambda e: e.activation(out=out, in_=in_, func=func, **kw), reads=rd, writes=[out])

    def cp(self, out, in_, eng=None):
        if eng is None:
            self.flip ^= 1
            eng = "act" if self.flip else "dve"
        if eng == "act":
            self.S.op("act", lambda e: e.activation(out=out, in_=in_, func=AF.Copy), reads=[in_], writes=[out])
        else:
            self.S.op(eng, lambda e: e.tensor_copy(out=out, in_=in_), reads=[in_], writes=[out])

    def recip(self, out, in_):
        self.S.op("dve", lambda e: e.reciprocal(out=out, in_=in_), reads=[in_], writes=[out])

    def memset(self, out, val, eng="dve"):
        self.S.op(eng, lambda e: e.memset(out, val), writes=[out])

    def reduce_sum(self, out, in_):
        self.S.op("dve", lambda e: e.tensor_reduce(out=out, in_=in_, axis=AX.X, op=ALU.add), reads=[in_], writes=[out])

    def scan(self, out, d0, d1, init):
        self.S.op("dve", lambda e: e.tensor_tensor_scan(out=out, data0=d0, data1=d1, initial=init, op0=ALU.mult,
                                                        op1=ALU.add), reads=[d0, d1], writes=[out])


def build_program(stop=None, dbg=False, nlayers=2):
    nc = bass.Bass("TRN2", target_bir_lowering=False)
    I = {}

    def din(name, shape):
        I[name] = nc.dram_tensor(name, list(shape), F32, kind="ExternalInput").ap()

    din("xin", [T, D]); din("cin", [17, D]); din("swkv", [2, NSEQ, 8, 64, 64]); din("sshift", [2, NSEQ, APJ])
    din("spool", [2, NSEQ, 15, 512]); din("consts", [128, NCC])
    for n in WNAMES:
        din(n, WSHAPES[n])
    O = {}

    def dout(name, shape):
        O[name] = nc.dram_tensor(name, list(shape), F32, kind="ExternalOutput").ap()

    dout("y", [T, D]); dout("wkv_p", [2, 8, 64, 64]); dout("shift_p", [2, APJ]); dout("pool_p", [2, 15, 512])
    dout("wkv_s", [2, NSEQ, 8, 64, 64]); dout("shift_s", [2, NSEQ, APJ]); dout("pool_s", [2, NSEQ, 15, 512])
    if dbg:
        dout("dbgX", [128, KC, T]); dout("dbgA", [128, 8, T])
    YAB = nc.dram_tensor("yab_scratch", [128, 8, T], BF16, kind="Internal").ap()

    ARN = 60416
    with ExitStack() as st:
        def sb(name, shape, dt):
            return st.enter_context(nc.sbuf_tensor(name, list(shape), dt))

        X = sb("X", [128, KC, T], F32)
        CF = sb("CF", [128, NCC], F32)
        CB = sb("CB", [128, NCC], BF16)
        ARt = sb("AR", [128, ARN], BF16)
        MOD = sb("MOD", [128, 72, 17], F32)
        VEC = sb("VEC", [128, NVR], F32)
        BM = sb("BM", [128, 72], F32)
        SCT = sb("SCT", [128, 8, 17], F32)
        GSp = sb("GSp", [128, 3, 8], F32); SHp = sb("SHp", [128, 3, 8], F32); COp = sb("COp", [128, 3, 8], F32)
        GSs = sb("GSs", [128, 3, 8, 16], F32); SHs = sb("SHs", [128, 3, 8, 16], F32); COs = sb("COs", [128, 3, 8, 16], F32)
        OMKA = sb("OMKA", [128, 4], F32)
        TMS = sb("TMS", [128, 64], F32)
        banks = [st.enter_context(nc.psum_tensor("ps%d" % i, [128, 512], F32)) for i in range(8)]
        S = Sched(nc)
        kb = KB(nc, S, banks)
        ar = Arena(ARt[:], ARN)

        def cf(name):
            c0, n = CCOLS[name]
            return CF[:, c0:c0 + n]

        def cb(name):
            c0, n = CCOLS[name]
            return CB[:, c0:c0 + n]

        def vec(name):
            r0, k = VR[name]
            return VEC[:, r0:r0 + k]

        kb.dma("sp", CF[:], I["consts"])
        kb.cp(CB[:], CF[:], eng="dve")

        ar.reset()
        XT = [ar.alloc([D], F32) for _ in range(2)]
        CROW = ar.alloc([D], F32)
        for i in range(17):
            n = 128 if i < 16 else 64
            xt = XT[i % 2]
            kb.dma("sp", xt[0:n, :], I["xin"][i * 128:i * 128 + n, :])
            for half in range(2):
                bk = kb.bank()
                for cc in range(4):
                    c = half * 4 + cc
                    kb.tr(bk[:, cc * 128:cc * 128 + n], xt[0:n, c * 128:(c + 1) * 128], cf("ident")[0:n, 0:n])
                kb.cp(X[:, half * 4:half * 4 + 4, i * 128:i * 128 + n], _view(bk[:, 0:512], [4, 128])[:, :, 0:n])
        kb.dma("sp", CROW[0:17, :], I["cin"])
        kb.act(CROW[0:17, :], CROW[0:17, :], AF.Silu)
        bk = kb.bank()
        for c in range(8):
            kb.mm(bk[:, c * 17:(c + 1) * 17], lhsT=CROW[0:17, c * 128:(c + 1) * 128], rhs=cf("ident")[0:17, 0:17])
        kb.cp(SCT[:], _view(bk[:, 0:136], [8, 17]), eng="dve")

        def load_layer_vectors(l):
            ar.reset()
            ROWS = ar.alloc([128], F32)
            BMR = ar.alloc([128], F32)
            WM = [ar.alloc([8, 512], F32) for _ in range(2)]
            kb.memset(ROWS[:], 0.0)

            def rows(name, src):
                r0, k = VR[name]
                kb.dma("sp", ROWS[r0:r0 + k, :], src)

            rows("norm_g", I["norm_g"][l].rearrange("j (c p) -> (j c) p", p=128))
            r0, _ = VR["mu"]
            kb.dma("sp", ROWS[r0:r0 + 14, :], I["mu_shift"][l, 0:1792].rearrange("(c p) -> c p", p=128))
            kb.dma("sp", ROWS[r0 + 14:r0 + 15, 0:32], I["mu_shift"][l:l + 1, 1792:1824])
            for nm, src in (("w0", "w0"), ("a0", "a0"), ("k_k", "k_k"), ("k_a", "k_a"), ("ln_w", "ln_x_w"),
                            ("ln_b", "ln_x_b"), ("pool_scale", "pool_scale")):
                rows(nm, I[src][l].rearrange("(c p) -> c p", p=128))
            rows("r_k", I["r_k"][l].rearrange("(c h) k -> c (h k)", h=2))
            rows("b_gate", I["b_gate"][l].rearrange("(c p) -> c p", p=128))
            rows("final_g", I["final_g"].rearrange("(c p) -> c p", p=128))
            bk = kb.bank()
            kb.mm(bk[:, 0:NVR], lhsT=ROWS[0:NVR, :], rhs=cf("ident")[0:NVR, 0:NVR])
            kb.cp(VEC[:], bk[:, 0:NVR], eng="dve")
            kb.dma("sp", BMR[0:72, :], I["b_mod"][l].rearrange("(c p) -> c p", p=128))
            bk = kb.bank()
            kb.mm(bk[:, 0:72], lhsT=BMR[0:72, :], rhs=cf("ident")[0:72, 0:72])
            kb.cp(BM[:], bk[:, 0:72], eng="dve")
            kb.ts(OMKA[:], vec("k_a"), -1.0, 1.0, ALU.mult, ALU.add)
            wm = I["w_mod"][l].rearrange("(k p) n -> p k n", p=128)
            kb.dma("sp", WM[0][:], wm[:, :, 0:512])
            bk = None
            for blk in range(18):
                if blk + 1 < 18:
                    kb.dma("sp", WM[(blk + 1) % 2][:], wm[:, :, (blk + 1) * 512:(blk + 2) * 512])
                w = WM[blk % 2]
                for oc in range(4):
                    mc = blk * 4 + oc
                    if mc % 24 == 0:
                        bk = kb.bank()
                    o = bk[:, (mc % 24) * 17:(mc % 24) * 17 + 17]
                    for k in range(8):
                        kb.mm(o, lhsT=w[:, k, oc * 128:(oc + 1) * 128], rhs=SCT[:, k, :], start=(k == 0), stop=(k == 7))
                    if mc % 24 == 23:
                        g = mc // 24
                        kb.tt(MOD[:, g * 24:(g + 1) * 24, :], _view(bk[:, 0:408], [24, 17]),
                              BM[:, g * 24:(g + 1) * 24, None].to_broadcast([128, 24, 17]), ALU.add)
            MODv = MOD[:].rearrange("p (j k c) s -> p j k c s", j=3, k=3)
            NG = _view(vec("norm_g"), [3, 8])
            kb.ts(GSp[:], MODv[:, :, 1, :, 0], 1.0, None, ALU.add)
            kb.tt(GSp[:], GSp[:], NG, ALU.mult)
            kb.cp(SHp[:], MODv[:, :, 0, :, 0], eng="dve")
            kb.cp(COp[:], MODv[:, :, 2, :, 0], eng="dve")
            kb.ts(COp[:, 0, :], COp[:, 0, :], 0.5, None, ALU.mult)
            kb.ts(COp[:, 2, :], COp[:, 2, :], 0.5, None, ALU.mult)
            for j in range(3):
                kb.ts(GSs[:, j], MODv[:, j, 1, :, 1:17], 1.0, None, ALU.add)
                kb.tt(GSs[:, j], GSs[:, j], NG[:, j, :, None].to_broadcast([128, 8, 16]), ALU.mult)
                kb.cp(SHs[:, j], MODv[:, j, 0, :, 1:17], eng="dve")
                kb.ts(COs[:, j], MODv[:, j, 2, :, 1:17], (1.0 if j == 1 else 0.5), None, ALU.mult)

        def modnorm(tok0, n, j, U, SQ, RS, TT, bank=None):
            is_s = tok0 >= TP
            kb.act(SQ[:, :, 0:n], X[:, :, tok0:tok0 + n], AF.Square)
            bk = kb.bank() if bank is None else bank()
            for c in range(8):
                kb.mm(bk[:, 0:n], lhsT=cb("ones"), rhs=SQ[:, c, 0:n], start=(c == 0), stop=(c == 7))
            kb.act(RS[:, 0:n], bk[:, 0:n], AF.Sqrt, bias=NORM_EPS, scale=1.0 / D)
            kb.recip(RS[:, 0:n], RS[:, 0:n])
            for c in range(8):
                t = TT[c % 2]
                if not is_s:
                    kb.stt(t[:, 0:n], X[:, c, tok0:tok0 + n], GSp[:, j, c:c + 1], RS[:, 0:n], ALU.mult, ALU.mult)
                    kb.act(U[:, c, 0:n], t[:, 0:n], AF.Identity, bias=SHp[:, j, c:c + 1], scale=1.0)
                else:
                    tv = _view(t[:, 0:n], [16, 4])
                    kb.tt(tv, _view(X[:, c, tok0:tok0 + n], [16, 4]), GSs[:, j, c, :, None].to_broadcast([128, 16, 4]), ALU.mult)
                    kb.tt(t[:, 0:n], t[:, 0:n], RS[:, 0:n], ALU.mult)
                    kb.tt(_view(U[:, c, 0:n], [16, 4]), tv, SHs[:, j, c, :, None].to_broadcast([128, 16, 4]), ALU.add)

        def resid(m, tok0, n, bo, j):
            if tok0 < TP:
                kb.stt(X[:, m, tok0:tok0 + n], bo[:, 0:n], COp[:, j, m:m + 1], X[:, m, tok0:tok0 + n], ALU.mult, ALU.add)
            else:
                kb.tt(_view(TMS[:, 0:n], [16, 4]), _view(bo[:, 0:n], [16, 4]),
                      COs[:, j, m, :, None].to_broadcast([128, 16, 4]), ALU.mult)
                kb.tt(X[:, m, tok0:tok0 + n], X[:, m, tok0:tok0 + n], TMS[:, 0:n], ALU.add)

        TILES = [(0, 512), (512, 512), (1024, 512), (1536, 512), (2048, 64)]

        def ffn(l, f):
            j = 0 if f == 0 else 2
            ar.reset()
            U = ar.alloc([8, T], BF16)
            WG = [ar.alloc([8, 512], BF16) for _ in range(2)]
            WU = [ar.alloc([8, 512], BF16) for _ in range(2)]
            WO = [ar.alloc([4, 1024], BF16) for _ in range(2)]
            H = [ar.alloc([4, 512], BF16) for _ in range(2)]
            SG = [ar.alloc([512], BF16) for _ in range(2)]
            SQ = ar.alloc([8, 512], BF16)
            RS = ar.alloc([512], F32)
            TT = [ar.alloc([512], F32) for _ in range(2)]
            win = I["w_ffn_in"][l, f].rearrange("(k p) n -> p k n", p=128)
            wout = I["w_ffn_out"][l, f].rearrange("(j p) n -> p j n", p=128)
            groups = [(0, 4), (4, 4), (8, 4), (12, 4), (16, 4), (20, 2)]

            def load(gi):
                c0, ng = groups[gi]
                b = gi % 2
                kb.dma("pool", WG[b][:, :, 0:ng * 128], win[:, :, c0 * 128:(c0 + ng) * 128])
                kb.dma("pool", WU[b][:, :, 0:ng * 128], win[:, :, DFF + c0 * 128:DFF + (c0 + ng) * 128])
                kb.dma("pool", WO[b][:, 0:ng, :], wout[:, c0:c0 + ng, :])

            load(0)
            hb = 0
            for gi, (c0, ng) in enumerate(groups):
                if gi + 1 < len(groups):
                    load(gi + 1)
                b = gi % 2
                for ti, (t0, n) in enumerate(TILES):
                    if gi == 0:
                        if ti == 0:
                            modnorm(t0, n, j, U[:, :, t0:t0 + n], SQ, RS, TT)
                        if ti + 1 < len(TILES):
                            t1, n1 = TILES[ti + 1]
                            modnorm(t1, n1, j, U[:, :, t1:t1 + n1], SQ, RS, TT)
                    h = H[hb % 2]
                    hb += 1
                    for jj in range(ng):
                        bg = kb.bank()
                        bu = kb.bank()
                        for k in range(8):
                            kb.mm(bg[:, 0:n], lhsT=WG[b][:, k, jj * 128:(jj + 1) * 128], rhs=U[:, k, t0:t0 + n],
                                  start=(k == 0), stop=(k == 7))
                        for k in range(8):
                            kb.mm(bu[:, 0:n], lhsT=WU[b][:, k, jj * 128:(jj + 1) * 128], rhs=U[:, k, t0:t0 + n],
                                  start=(k == 0), stop=(k == 7))
                        sg = SG[jj % 2]
                        kb.act(sg[:, 0:n], bg[:, 0:n], AF.Silu)
                        kb.tt(h[:, jj, 0:n], bu[:, 0:n], sg[:, 0:n], ALU.mult)
                    for m in range(8):
                        bo = kb.bank()
                        for jj in range(ng):
                            kb.mm(bo[:, 0:n], lhsT=WO[b][:, jj, m * 128:(m + 1) * 128], rhs=h[:, jj, 0:n],
                                  start=(jj == 0), stop=(jj == ng - 1))
                        resid(m, t0, n, bo, j)

        def final_out():
            ar.reset()
            YT = [ar.alloc([D], F32) for _ in range(2)]
            SQ = ar.alloc([8, 128], BF16)
            RS = ar.alloc([128], F32)
            YN = [ar.alloc([8, 128], F32) for _ in range(2)]
            FG = vec("final_g")
            for i in range(17):
                n = 128 if i < 16 else 64
                t0 = i * 128
                yn = YN[i % 2]
                kb.act(SQ[:, :, 0:n], X[:, :, t0:t0 + n], AF.Square)
                bk = kb.bank()
                for c in range(8):
                    kb.mm(bk[:, 0:n], lhsT=cb("ones"), rhs=SQ[:, c, 0:n], start=(c == 0), stop=(c == 7))
                kb.act(RS[:, 0:n], bk[:, 0:n], AF.Sqrt, bias=NORM_EPS, scale=1.0 / D)
                kb.recip(RS[:, 0:n], RS[:, 0:n])
                for c in range(8):
                    kb.stt(yn[:, c, 0:n], X[:, c, t0:t0 + n], FG[:, c:c + 1], RS[:, 0:n], ALU.mult, ALU.mult)
                yt = YT[i % 2]
                for half in range(2):
                    bk = kb.bank()
                    for cc in range(4):
                        c = half * 4 + cc
                        kb.tr(bk[0:n, cc * 128:(cc + 1) * 128], yn[:, c, 0:n], cf("ident"))
                    kb.cp(yt[0:n, half * 512:(half + 1) * 512], bk[0:n, 0:512])
                kb.dma("sp", O["y"][t0:t0 + n, :], yt[0:n, :])

        def dbg_dump_x():
            if dbg:
                kb.dma("sp", O["dbgX"], X[:])

        mixer = _make_mixer(nc, kb, ar, I, O, X, YAB, cf, cb, vec, modnorm, resid, OMKA, TILES, dbg)

        def mark(name):
            DBG.setdefault("marks", []).append((name, sum(1 for o in S.ops if o.eng == "pe"), len(S.ops)))

        DBG["marks"] = []
        for l in range(nlayers):
            mark("vec%d" % l)
            load_layer_vectors(l)
            mark("ffn%d0" % l)
            ffn(l, 0)
            if stop == "ffn0":
                break
            mark("mixer%d" % l)
            mixer(l, only_a=(stop == "passA"))
            if stop in ("mix0", "passA"):
                break
            mark("ffn%d1" % l)
            ffn(l, 1)
        mark("final")
        dbg_dump_x()
        final_out()
        S.lower(st)
        print("ops", len(S.ops), "arena peak KiB", ar.peak * 2 / 1024.0)
    return nc


def _make_mixer(nc, kb, ar, I, O, X, YAB, cf, cb, vec, modnorm, resid, OMKA, TILES, dbg):
    NB = 64

    def bc(ap, shape):
        return ap.to_broadcast(list(shape))

    def pass_a(l):
        ar.reset()
        WIN = ar.alloc([8, APJ], BF16)
        W2T = ar.alloc([512], BF16)
        A2T = ar.alloc([512], BF16)
        G2T = ar.alloc([2, 512], BF16)
        kb.memset(W2T[:], 0.0)
        kb.memset(A2T[:], 0.0)
        kb.dma("pool", WIN, I["w_in"][l].rearrange("(k p) n -> p k n", p=128)[:, :, 0:APJ])
        kb.dma("pool", W2T[0:64, :], I["w2"][l])
        kb.dma("pool", A2T[64:128, :], I["a2"][l])
        kb.dma("pool", G2T[:, 0, :], I["g2"][l, 0:128, :])
        kb.dma("pool", G2T[0:32, 1, :], I["g2"][l, 128:160, :])
        UB = ar.alloc([8, NB], BF16)
        SQn = ar.alloc([8, NB], BF16)
        RSn = ar.alloc([NB], F32)
        TTn = [ar.alloc([NB], F32) for _ in range(2)]
        PA = ar.alloc([15, 80], F32)
        XS = ar.alloc([15, NB], F32)
        LAST = ar.alloc([15], F32)
        SHS = ar.alloc([15, 16], F32)
        SHO = ar.alloc([15, 16], F32)
        LIN = ar.alloc([3, NB], BF16)
        f4 = lambda: ar.alloc([4, NB], F32)
        t_wd, t_p, t_pex, t_ip, t_aa, t_kk, t_t1, t_t2, t_k2 = [f4() for _ in range(9)]
        SQK = ar.alloc([4, NB], BF16)
        RKR = ar.alloc([4, NB], BF16)
        blk = lambda: ar.alloc([512], BF16)
        AT3 = [blk() for _ in range(3)]
        RT3 = [blk() for _ in range(3)]
        PC3 = [ar.alloc([64], F32) for _ in range(3)]
        GG3 = [f4() for _ in range(3)]
        BON3 = [f4() for _ in range(3)]
        BT2 = [blk() for _ in range(2)]; KT2 = [blk() for _ in range(2)]; BH2 = [blk() for _ in range(3)]
        KH2 = [blk() for _ in range(3)]; VB2 = [blk() for _ in range(3)]
        NM, NTM, QB, QTB = [blk() for _ in range(4)]
        AAK2 = [blk() for _ in range(2)]; ARB2 = [blk() for _ in range(2)]; ARK2 = [blk() for _ in range(2)]
        MM2 = [blk() for _ in range(2)]
        BHT = ar.alloc([4, 128], BF16)
        KHT = ar.alloc([4, 128], BF16)
        VT = ar.alloc([4, 64], BF16)
        ZT = ar.alloc([4, 64], BF16)
        UT = ar.alloc([4, 64], BF16)
        SQY = ar.alloc([4, 64], F32)
        YC = ar.alloc([4, 64], F32)
        YNB = ar.alloc([4, 128], BF16)
        ST1 = ar.alloc([4], F32); ST2 = ar.alloc([4], F32); STM = ar.alloc([4], F32); STV = ar.alloc([4], F32)
        YF = ar.alloc([4, NB], F32)
        p2a = f4()
        YAb = ar.alloc([4, NB], BF16)
        S32 = ar.alloc([4, 64], F32)
        SBF = [ar.alloc([4, 64], BF16) for _ in range(2)]
        SI = ar.alloc([8, 64], F32)
        SO = ar.alloc([4, 128], F32)
        SHT = ar.alloc([15, 128], F32)
        SROW = SHT[:].rearrange("p m c -> p (m c)")[:, 0:APJ]
        DBG["passA_kib"] = ar.off * 2 / 1024.0

        for t in AT3 + RT3 + BT2 + KT2 + BH2 + KH2 + VB2:
            kb.memset(t[:], 0.0)
        kb.memset(PA[:], 0.0)
        kb.memset(LAST[:], 0.0)
        kb.memset(XS[:], 0.0)
        kb.memset(S32[:], 0.0)
        kb.memset(SBF[0][:], 0.0)
        kb.memset(LIN[:], 0.0)
        kb.dma("sp", SROW[0:16, :], I["sshift"][l])
        kb.memset(SHS[:], 0.0)
        for half in range(2):
            bk = kb.bank()
            ms = range(0, 8) if half == 0 else range(8, 15)
            for m in ms:
                Mm = 128 if m < 14 else 32
                kb.mm(bk[0:Mm, (m % 8) * 16:(m % 8) * 16 + 16], lhsT=SROW[0:16, m * 128:m * 128 + Mm], rhs=cf("ident")[0:16, 0:16])
            if half == 0:
                kb.cp(SHS[:, 0:8, :], _view(bk[:, 0:128], [8, 16]), eng="dve")
            else:
                kb.cp(SHS[:, 8:14, :], _view(bk[:, 0:96], [6, 16]), eng="dve")
                kb.cp(SHS[0:32, 14, :], bk[0:32, 96:112], eng="dve")

        MU = vec("mu")
        sbi = [0]
        NBLK = TP // NB
        bankA = lambda: kb.bank_of((0, 1, 2))
        bankB = lambda: kb.bank_of((3, 4, 5))
        bankC = lambda: kb.bank_of((6, 7))

        def geom(b):
            is_s = (b == NBLK)
            C = 4 if is_s else 64
            R = 2 * C
            NQ = 512 // R
            return is_s, C, R, NQ, NQ // 4, ("4" if is_s else "64"), b * NB

        def stage1a(b):
            is_s, C, R, NQ, nch, sfx, tok0 = geom(b)
            n = NB
            AT, RT, PC, t_gg, t_bon = AT3[b % 3], RT3[b % 3], PC3[b % 3], GG3[b % 3], BON3[b % 3]
            BT, KT, BH, KH, VB = BT2[b % 2], KT2[b % 2], BH2[b % 3], KH2[b % 3], VB2[b % 3]
            if is_s:
                for t in (AT, RT, BT, KT, BH, KH, VB):
                    kb.memset(t[:], 0.0)
                yield
            modnorm(tok0, n, 1, UB, SQn, RSn, TTn, bank=bankA)
            yield
            for half in range(2):
                bk = bankA()
                ms = range(0, 8) if half == 0 else range(8, 15)
                for m in ms:
                    Mm = 128 if m < 14 else 32
                    for k in range(8):
                        kb.mm(bk[0:Mm, (m % 8) * 64:(m % 8) * 64 + 64], lhsT=WIN[:, k, m * 128:m * 128 + Mm], rhs=UB[:, k, :],
                              start=(k == 0), stop=(k == 7))
                    if m % 2 == 1:
                        yield
                if not is_s:
                    if half == 0:
                        kb.cp(PA[:, 0:8, 1:65], _view(bk[:, 0:512], [8, 64]), eng="act")
                    else:
                        kb.cp(PA[:, 8:14, 1:65], _view(bk[:, 0:384], [6, 64]), eng="act")
                        kb.cp(PA[0:32, 14, 1:65], bk[0:32, 384:448], eng="act")
                else:
                    PAs = PA[:].rearrange("p m (s t) -> p m s t", t=5)
                    if half == 0:
                        kb.cp(PAs[:, 0:8, :, 1:5], bk[:, 0:512].rearrange("p (m s t) -> p m s t", m=8, t=4), eng="act")
                    else:
                        kb.cp(PAs[:, 8:14, :, 1:5], bk[:, 0:384].rearrange("p (m s t) -> p m s t", m=6, t=4), eng="act")
                        kb.cp(PAs[0:32, 14, :, 1:5], bk[0:32, 384:448].rearrange("p (s t) -> p s t", t=4), eng="act")
                yield
            if not is_s:
                kb.cp(PA[:, :, 0], LAST[:], eng="act")
                kb.tt(XS[:], PA[:, :, 0:64], PA[:, :, 1:65], ALU.subtract)
                yield
                kb.tt(XS[:], XS[:], bc(MU[:, :, None], [128, 15, 64]), ALU.mult)
                yield
                kb.tt(XS[:], XS[:], PA[:, :, 1:65], ALU.add)
                kb.cp(LAST[:], PA[:, :, 64], eng="act")
                yield
            else:
                PAs = PA[:].rearrange("p m (s t) -> p m s t", t=5)
                XSs = XS[:].rearrange("p m (s t) -> p m s t", t=4)
                kb.cp(PAs[:, :, :, 0], SHS[:], eng="dve")
                for m0, m1 in ((0, 8), (8, 15)):
                    kb.tt(XSs[:, m0:m1], PAs[:, m0:m1, :, 0:4], PAs[:, m0:m1, :, 1:5], ALU.subtract)
                yield
                kb.tt(XS[:], XS[:], bc(MU[:, :, None], [128, 15, 64]), ALU.mult)
                yield
                for m0, m1 in ((0, 8), (8, 15)):
                    kb.tt(XSs[:, m0:m1], XSs[:, m0:m1], PAs[:, m0:m1, :, 1:5], ALU.add)
                kb.cp(SHO[:], PAs[:, :, :, 4], eng="dve")
                yield
            xr, xk, xv = XS[:, 0:4, :], XS[:, 4:8, :], XS[:, 8:12, :]
            kb.act(LIN[0:64, 0, :], XS[0:64, 12, :], AF.Tanh)
            kb.cp(LIN[64:128, 0, :], XS[64:128, 12, :], eng="act")
            kb.act(LIN[:, 1, :], XS[:, 13, :], AF.Sigmoid)
            kb.act(LIN[0:32, 2, :], XS[0:32, 14, :], AF.Sigmoid)
            yield
            bw = bankA()
            for j in range(4):
                kb.mm(bw[:, j * 64:(j + 1) * 64], lhsT=W2T[:, j * 128:(j + 1) * 128], rhs=LIN[:, 0, :])
                kb.mm(bw[:, 256 + j * 64:256 + (j + 1) * 64], lhsT=A2T[:, j * 128:(j + 1) * 128], rhs=LIN[:, 0, :])
            bg = bankA()
            for j in range(4):
                kb.mm(bg[:, j * 64:(j + 1) * 64], lhsT=G2T[:, 0, j * 128:(j + 1) * 128], rhs=LIN[:, 1, :], start=True, stop=False)
                kb.mm(bg[:, j * 64:(j + 1) * 64], lhsT=G2T[0:32, 1, j * 128:(j + 1) * 128], rhs=LIN[0:32, 2, :], start=False, stop=True)
            yield
            kb.tt(t_t1[:], _view(bw[:, 0:256], [4, 64]), bc(vec("w0")[:, :, None], [128, 4, 64]), ALU.add)
            kb.tt(t_aa[:], _view(bw[:, 256:512], [4, 64]), bc(vec("a0")[:, :, None], [128, 4, 64]), ALU.add)
            kb.cp(t_gg[:], _view(bg[:, 0:256], [4, 64]), eng="act")
            yield
            kb.act(t_t1[:], t_t1[:], AF.Sigmoid)
            kb.act(t_aa[:], t_aa[:], AF.Sigmoid)
            yield
            kb.act(t_wd[:], t_t1[:], AF.Exp, scale=-DECAY_K)
            kb.tt(t_kk[:], xk, bc(vec("k_k")[:, :, None], [128, 4, 64]), ALU.mult)
            yield
            kb.act(SQK[:], t_kk[:], AF.Square)
            kb.tt(t_t1[:], t_wd[:], bc(cf("nstart" + sfx)[:, None, 0:64], [128, 4, 64]), ALU.mult)
            kb.tt(t_t2[:], t_wd[:], bc(cf("start" + sfx)[:, None, 0:64], [128, 4, 64]), ALU.mult)
            yield
            bs = bankA()
            for j in range(4):
                kb.mm(bs[:, j * 64:(j + 1) * 64], lhsT=cb("bones"), rhs=SQK[:, j, :])
            for j in range(4):
                kb.scan(t_p[:, j, :], t_t1[:, j, :], t_t2[:, j, :], 1.0)
            yield
            kb.act(t_t1[:], _view(bs[:, 0:256], [4, 64]), AF.Sqrt)
            kb.recip(t_ip[:], t_p[:])
            kb.recip(t_t2[:], t_wd[:])
            yield
            kb.tt(t_pex[:], t_p[:], t_t2[:], ALU.mult)
            kb.cp(PC[:, 0:NQ].rearrange("p (s j) -> p j s", j=4),
                  t_p[:].rearrange("p j (s c) -> p j s c", c=C)[:, :, :, C - 1], eng="dve")
            kb.ts(t_t1[:], t_t1[:], 1e-12, None, ALU.max)
            yield
            kb.recip(t_t1[:], t_t1[:])
            yield
            kb.tt(t_kk[:], t_kk[:], t_t1[:], ALU.mult)
            yield
            kb.tt(t_t2[:], t_kk[:], t_aa[:], ALU.mult)
            kb.tt(t_t1[:], t_aa[:], bc(vec("k_a")[:, :, None], [128, 4, 64]), ALU.mult)
            kb.stt(t_wd[:], t_kk[:], -1.0, t_pex[:], ALU.mult, ALU.mult)
            yield
            kb.tt(t_t2[:], t_t2[:], t_ip[:], ALU.mult)
            kb.tt(t_t1[:], t_t1[:], bc(OMKA[:, :, None], [128, 4, 64]), ALU.add)
            yield
            kb.tt(t_k2[:], xk, t_t1[:], ALU.mult)
            yield
            kb.tt(t_t1[:], xr, t_k2[:], ALU.mult)
            kb.tt(t_aa[:], t_k2[:], t_ip[:], ALU.mult)
            yield
            kb.tt(RKR[:], t_t1[:], bc(vec("r_k")[:, :, None], [128, 4, 64]), ALU.mult)
            yield
            brk = bankA()
            for j in range(4):
                kb.mm(brk[:, j * 64:(j + 1) * 64], lhsT=cb("bones"), rhs=RKR[:, j, :])
            PCq = PC[:, 0:NQ].rearrange("p (s j) -> p j s", j=4)
            for h in range(2):
                ps_ = slice(64 * h, 64 * h + 64)

                def dst(tile):
                    return tile[ps_, :].rearrange("p (s j r) -> p j s r", j=4, r=R)[:, :, :, h * C:(h + 1) * C]

                def src(t3):
                    return t3[ps_].rearrange("p j (s c) -> p j s c", c=C)

                kb.cp(dst(AT), src(t_wd), eng="act")
                kb.tt(dst(RT), src(xr), src(t_p), ALU.mult)
                yield
                kb.cp(dst(BT), src(t_t2), eng="act")
                kb.cp(dst(KT), src(t_aa), eng="act")
                kb.tt(dst(BH), src(t_t2), bc(PCq[ps_, :, :, None], [64, 4, nch, C]), ALU.mult)
                yield
                kb.tt(dst(KH), src(t_aa), bc(PCq[ps_, :, :, None], [64, 4, nch, C]), ALU.mult)
                kb.cp(dst(VB), src(xv), eng="act")
                yield
            kb.tt(t_bon[:], _view(brk[:, 0:256], [4, 64]), xv, ALU.mult)
            yield
            if b == NBLK - 1:
                bk = bankA()
                kb.mm(bk[0:15, 0:128], lhsT=LAST[:, 0:15], rhs=cf("ident"))
                kb.cp(SHT[0:15, 0, :], bk[0:15, 0:128], eng="dve")
                kb.dma("sp", O["shift_p"][l, 0:1792].rearrange("(c p) -> c p", p=128), SHT[0:14, 0, :])
                kb.dma("sp", O["shift_p"][l:l + 1, 1792:1824], SHT[14:15, 0, 0:32])
                yield
            if is_s:
                for g4 in range(4):
                    bk = bankA()
                    ms = range(g4 * 4, min(g4 * 4 + 4, 15))
                    for m in ms:
                        Mm = 128 if m < 14 else 32
                        kb.mm(bk[0:16, (m % 4) * 128:(m % 4) * 128 + Mm], lhsT=SHO[0:Mm, m, :], rhs=cf("ident")[0:Mm, 0:Mm])
                    if g4 < 3:
                        kb.cp(SHT[0:16, g4 * 4:g4 * 4 + 4, :], _view(bk[0:16, 0:512], [4, 128]), eng="dve")
                    else:
                        kb.cp(SHT[0:16, 12:14, :], _view(bk[0:16, 0:256], [2, 128]), eng="dve")
                        kb.cp(SHT[0:16, 14, 0:32], bk[0:16, 256:288], eng="dve")
                    yield
                kb.dma("sp", O["shift_s"][l, :, 0:1792], SHT[0:16, 0:14, :].rearrange("p m c -> p (m c)"))
                kb.dma("sp", O["shift_s"][l, :, 1792:1824], SHT[0:16, 14, 0:32])
                yield

        def stage1b(b):
            is_s, C, R, NQ, nch, sfx, tok0 = geom(b)
            AT, RT = AT3[b % 3], RT3[b % 3]
            BT, KT = BT2[b % 2], KT2[b % 2]
            AAK, ARB, ARK, MM = AAK2[b % 2], ARB2[b % 2], ARK2[b % 2], MM2[b % 2]
            msu, msuT, mu_ = cf("msu" + sfx), cf("msuT" + sfx), cf("mu" + sfx)
            if is_s:
                for t in (NM, NTM, QB, QTB, AAK, ARB, ARK, MM):
                    kb.memset(t[:], 0.0)
                yield
            Mq = lambda tile: tile[0:R, :].rearrange("p (q r) -> p q r", r=R)
            MqK = lambda tile: tile[:, :].rearrange("p (q r) -> p q r", r=R)
            Fq = MqK

            def prod(lt, rt, mask, out_t):
                bk_ = bankB()
                for q in range(NQ):
                    kb.mm(bk_[0:R, q * R:(q + 1) * R], lhsT=Fq(lt)[:, q, :], rhs=Fq(rt)[:, q, :])
                    if q % 8 == 7:
                        yield
                kb.tt(Mq(out_t), bk_[0:R, :].rearrange("p (q r) -> p q r", r=R), bc(mask[0:R, None, 0:R], [R, NQ, R]), ALU.mult)
                yield

            yield from prod(BT, AT, msu, NM)
            yield from prod(AT, BT, msuT, NTM)
            kb.tt(Mq(MM), Mq(NM), bc(cf("ident")[0:R, None, 0:R], [R, NQ, R]), ALU.add)
            yield
            nlev = 5 if not is_s else 1
            Q, QT = NM, NTM
            Qn, QTn = QB, QTB
            extra = [(KT, AT, msu, AAK), (BT, RT, mu_, ARB), (KT, RT, mu_, ARK)]
            for lev in range(nlev):
                last = (lev == nlev - 1)
                b2 = bankB()
                for q in range(NQ):
                    kb.mm(b2[0:R, q * R:(q + 1) * R], lhsT=MqK(Q)[:, q, :], rhs=MqK(QT)[:, q, :])
                    if q % 8 == 7:
                        yield
                kb.cp(QTn[0:R, :], b2[0:R, :], eng="act")
                yield
                if not last:
                    b1 = bankB()
                    for q in range(NQ):
                        kb.mm(b1[0:R, q * R:(q + 1) * R], lhsT=MqK(QT)[:, q, :], rhs=MqK(Q)[:, q, :])
                        if q % 8 == 7:
                            yield
                    kb.cp(Qn[0:R, :], b1[0:R, :], eng="act")
                    yield
                b3 = bankB()
                for q in range(NQ):
                    o3 = b3[0:R, q * R:(q + 1) * R]
                    kb.mm(o3, lhsT=MqK(QTn)[:, q, :], rhs=MqK(MM)[:, q, :], start=True, stop=False)
                    kb.mm(o3, lhsT=cb("ident")[:, 0:R], rhs=MqK(MM)[:, q, :], start=False, stop=True)
                    if q % 8 == 7:
                        yield
                kb.cp(MM[0:R, :], b3[0:R, :], eng="act")
                yield
                Q, QT, Qn, QTn = Qn, QTn, Q, QT
                if extra:
                    yield from prod(*extra.pop(0))
            while extra:
                yield from prod(*extra.pop(0))

        def stage2(b):
            is_s, C, R, NQ, nch, sfx, tok0 = geom(b)
            n = NB
            AT, RT, PC, t_gg, t_bon = AT3[b % 3], RT3[b % 3], PC3[b % 3], GG3[b % 3], BON3[b % 3]
            BH, KH, VB = BH2[b % 3], KH2[b % 3], VB2[b % 3]
            AAK, ARB, ARK, MM = AAK2[b % 2], ARB2[b % 2], ARK2[b % 2], MM2[b % 2]
            istk, tokm = cb("istack" + sfx), cf("tokmask" + sfx)
            if is_s:
                for t in (BHT, KHT, VT, ZT, UT, YNB):
                    kb.memset(t[:], 0.0)
                yield
            KR = 128
            MqK = lambda tile: tile[:, :].rearrange("p (q r) -> p q r", r=R)
            Fq = MqK
            for gi in range(nch):
                q0 = gi * 4
                if is_s:
                    seq = gi
                    kb.dma("sp", SI[0:64, :, :], I["swkv"][l, seq].rearrange("h v k -> v h k"))
                    sb_in = SBF[sbi[0] % 2]
                    bk_ = bankC()
                    for j in range(4):
                        kb.mm(bk_[:, j * 64:(j + 1) * 64], lhsT=SI[0:64, 2 * j:2 * j + 2, :].rearrange("p h k -> p (h k)"),
                              rhs=cf("ident")[0:64, 0:64])
                    kb.cp(S32[:], _view(bk_[:, 0:256], [4, 64]), eng="dve")
                    kb.cp(sb_in[:], S32[:], eng="act")
                    yield
                else:
                    sb_in = SBF[sbi[0] % 2]
                sb_out = SBF[(sbi[0] + 1) % 2]
                sbi[0] += 1
                b_ = bankC()
                for j in range(4):
                    kb.mm(b_[0:R, j * 128:(j + 1) * 128], lhsT=Fq(BH)[:, q0 + j, :], rhs=cb("ident"))
                kb.cp(BHT[0:R], _view(b_[0:R, 0:512], [4, 128]), eng="act")
                yield
                b_ = bankC()
                for j in range(4):
                    kb.mm(b_[0:R, j * 128:(j + 1) * 128], lhsT=Fq(KH)[:, q0 + j, :], rhs=cb("ident"))
                kb.cp(KHT[0:R], _view(b_[0:R, 0:512], [4, 128]), eng="act")
                yield
                b_ = bankC()
                for j in range(4):
                    kb.mm(b_[0:R, j * 64:(j + 1) * 64], lhsT=Fq(VB)[:, q0 + j, :], rhs=cb("istack64"))
                kb.cp(VT[0:R], _view(b_[0:R, 0:256], [4, 64]), eng="act")
                yield
                bz = bankC()
                for j in range(4):
                    kb.mm(bz[0:R, j * 64:(j + 1) * 64], lhsT=MqK(AAK)[:, q0 + j, :], rhs=VT[0:KR, j, :], start=True, stop=False)
                    kb.mm(bz[0:R, j * 64:(j + 1) * 64], lhsT=Fq(AT)[:, q0 + j, :], rhs=sb_in[:, j, :], start=False, stop=True)
                kb.cp(ZT[0:R], _view(bz[0:R, 0:256], [4, 64]), eng="act")
                yield
                bu = bankC()
                for j in range(4):
                    kb.mm(bu[0:R, j * 64:(j + 1) * 64], lhsT=MqK(MM)[:, q0 + j, :], rhs=ZT[0:KR, j, :])
                kb.cp(UT[0:R], _view(bu[0:R, 0:256], [4, 64]), eng="dve")
                yield
                bs_ = bankC()
                for j in range(4):
                    o = bs_[:, j * 64:(j + 1) * 64]
                    kb.mm(o, lhsT=BHT[0:KR, j, :], rhs=UT[0:KR, j, :], start=True, stop=False)
                    kb.mm(o, lhsT=KHT[0:KR, j, :], rhs=VT[0:KR, j, :], start=False, stop=True)
                by = bankC()
                for j in range(4):
                    o = by[0:R, j * 64:(j + 1) * 64]
                    kb.mm(o, lhsT=Fq(RT)[:, q0 + j, :], rhs=sb_in[:, j, :], start=True, stop=False)
                    kb.mm(o, lhsT=MqK(ARB)[:, q0 + j, :], rhs=UT[0:KR, j, :], start=False, stop=False)
                    kb.mm(o, lhsT=MqK(ARK)[:, q0 + j, :], rhs=VT[0:KR, j, :], start=False, stop=True)
                kb.tt(S32[:], S32[:], bc(PC[:, q0:q0 + 4, None], [128, 4, 64]), ALU.mult)
                yield
                kb.tt(S32[:], S32[:], _view(bs_[:, 0:256], [4, 64]), ALU.add)
                kb.cp(YC[0:R], _view(by[0:R, 0:256], [4, 64]), eng="act")
                yield
                kb.cp(sb_out[:], S32[:], eng="act")
                Yv = YC[0:R]
                kb.reduce_sum(ST1[0:R], Yv)
                yield
                kb.act(SQY[0:R], Yv, AF.Square)
                kb.ts(STM[0:R], ST1[0:R], 1.0 / 64, None, ALU.mult)
                yield
                kb.reduce_sum(ST2[0:R], SQY[0:R])
                kb.tt(STV[0:R], STM[0:R], STM[0:R], ALU.mult)
                yield
                kb.stt(STV[0:R], ST2[0:R], 1.0 / 64, STV[0:R], ALU.mult, ALU.subtract)
                kb.tt(YC[0:R], Yv, bc(STM[0:R, :, None], [R, 4, 64]), ALU.subtract)
                yield
                kb.act(STV[0:R], STV[0:R], AF.Sqrt, bias=LN_EPS)
                yield
                kb.recip(STV[0:R], STV[0:R])
                yield
                kb.tt(YC[0:R], YC[0:R], bc(STV[0:R, :, None], [R, 4, 64]), ALU.mult)
                yield
                for h in range(2):
                    kb.ts(YNB[0:R, :, h * 64:(h + 1) * 64], YC[0:R], tokm[0:R, h:h + 1], None, ALU.mult)
                yield
                bf_ = bankC()
                CW = max(C, 8)
                for j in range(4):
                    kb.mm(bf_[:, j * CW:(j + 1) * CW], lhsT=YNB[0:KR, j, :], rhs=istk[0:KR, 0:CW])
                kb.cp(YF[:, :, gi * C:(gi + 1) * C], _view(bf_[:, 0:4 * CW], [4, CW])[:, :, 0:C], eng="act")
                yield
                if is_s or b == NBLK - 1:
                    bo_ = bankC()
                    for j in range(4):
                        kb.mm(bo_[0:64, j * 128:(j + 1) * 128], lhsT=S32[:, j, :], rhs=cf("ident"))
                    kb.cp(SO[0:64], _view(bo_[0:64, 0:512], [4, 128]), eng="dve")
                    dst_ = O["wkv_s"][l, gi] if is_s else O["wkv_p"][l]
                    kb.dma("sp", dst_.rearrange("h v k -> v h k"), SO[0:64].rearrange("p j (h k) -> p (j h) k", h=2))
                    yield
            kb.tt(p2a[:], YF[:], bc(vec("ln_w")[:, :, None], [128, 4, 64]), ALU.mult)
            yield
            kb.tt(p2a[:], p2a[:], bc(vec("ln_b")[:, :, None], [128, 4, 64]), ALU.add)
            yield
            kb.tt(p2a[:], p2a[:], t_bon[:], ALU.add)
            yield
            kb.tt(YAb[:], p2a[:], t_gg[:], ALU.mult)
            kb.dma("sp", YAB[:, 0:4, tok0:tok0 + n], YAb[:])
            yield

        nblocks = NBLK + 1
        if DBG.get("nblk") is not None:
            nblocks = DBG["nblk"]
        for step in range(nblocks + 2):
            gens = []
            if step < nblocks:
                gens.append(stage1a(step))
            if 0 <= step - 1 < nblocks:
                gens.append(stage1b(step - 1))
            if 0 <= step - 2 < nblocks:
                gens.append(stage2(step - 2))
            if DBG.get("no_interleave", False):
                for g in gens[::-1]:
                    for _ in g:
                        pass
                continue
            alive = list(gens)
            while alive:
                for g in list(alive):
                    try:
                        next(g)
                    except StopIteration:
                        alive.remove(g)

    def pass_pool(l):
        ar.reset()
        WPB = ar.alloc([8, 512], BF16)
        WPL = ar.alloc([4, 128], BF16)
        kb.dma("pool", WPB, I["w_in"][l].rearrange("(k p) n -> p k n", p=128)[:, :, APJ:PT])
        kb.dma("pool", WPL, I["w_pool"][l].rearrange("g c d -> c g d"))
        UB = ar.alloc([8, 512], BF16)
        SQ = ar.alloc([8, 512], BF16)
        RS = ar.alloc([512], F32)
        TT = [ar.alloc([512], F32) for _ in range(2)]
        PBH = ar.alloc([4, 527], F32)
        SA = ar.alloc([4, 527], F32)
        SB = ar.alloc([4, 527], F32)
        DP = ar.alloc([4, 512], BF16)
        YBb = ar.alloc([4, 512], BF16)
        PROW = ar.alloc([512], F32)
        POUT = ar.alloc([512], F32)
        TMPH = ar.alloc([4, 120], F32)
        PS = vec("pool_scale")
        WIN_ = (2, 4, 8, 16)
        kb.memset(PBH[:], 0.0)

        def wsum(x, sa, sb_, L, nd):
            def sl(v, g0, g1, a, b_):
                return v[:, g0:g1, a:b_] if nd == 3 else v[:, g0:g1, :, a:b_]
            kb.tt(sl(sa, 0, 4, 1, L), sl(x, 0, 4, 1, L), sl(x, 0, 4, 0, L - 1), ALU.add)
            kb.tt(sl(sb_, 1, 4, 3, L), sl(sa, 1, 4, 3, L), sl(sa, 1, 4, 1, L - 2), ALU.add)
            kb.tt(sl(sa, 2, 4, 7, L), sl(sb_, 2, 4, 7, L), sl(sb_, 2, 4, 3, L - 4), ALU.add)
            kb.tt(sl(sb_, 3, 4, 15, L), sl(sa, 3, 4, 15, L), sl(sa, 3, 4, 7, L - 8), ALU.add)
            return [sa, sb_, sa, sb_]

        for (t0, n) in TILES[:4]:
            modnorm(t0, n, 1, UB, SQ, RS, TT)
            for g in range(4):
                bk = kb.bank()
                for k in range(8):
                    kb.mm(bk[:, 0:n], lhsT=WPB[:, k, g * 128:(g + 1) * 128], rhs=UB[:, k, 0:n], start=(k == 0), stop=(k == 7))
                kb.cp(PBH[:, g, 15:15 + n], bk[:, 0:n], eng="act")
            L = 15 + n
            fin = wsum(PBH, SA, SB, L, 3)
            for g in range(4):
                if t0 == 0:
                    kb.tt(fin[g][:, g, 15:30], fin[g][:, g, 15:30], cf("ratio")[:, g * 15:(g + 1) * 15], ALU.mult)
                kb.stt(DP[:, g, 0:n], fin[g][:, g, 15:L], 1.0 / WIN_[g], PBH[:, g, 15:L], ALU.mult, ALU.subtract)
            for g in range(4):
                bk = kb.bank()
                kb.mm(bk[:, 0:n], lhsT=WPL[:, g, :], rhs=DP[:, g, 0:n])
                kb.ts(YBb[:, g, 0:n], bk[:, 0:n], PS[:, g:g + 1], None, ALU.mult)
            kb.dma("sp", YAB[:, 4:8, t0:t0 + n], YBb[:, :, 0:n])
            kb.cp(TMPH[:, :, 0:15], PBH[:, :, n:n + 15], eng="dve")
            kb.cp(PBH[:, :, 0:15], TMPH[:, :, 0:15], eng="dve")
        bk = kb.bank()
        for g in range(4):
            kb.mm(bk[0:15, g * 128:(g + 1) * 128], lhsT=TMPH[:, g, 0:15], rhs=cf("ident"))
        kb.cp(POUT[0:15, :], bk[0:15, 0:512], eng="dve")
        kb.dma("sp", O["pool_p"][l], POUT[0:15, :])
        PBs = PBH[:, :, 0:304].rearrange("p g (s t) -> p g s t", t=19)
        SAs = SA[:, :, 0:304].rearrange("p g (s t) -> p g s t", t=19)
        SBs = SB[:, :, 0:304].rearrange("p g (s t) -> p g s t", t=19)
        sp_rows = I["spool"][l].rearrange("s i c -> (s i) c")
        for hh in range(2):
            kb.dma("sp", PROW[0:120, :], sp_rows[hh * 120:(hh + 1) * 120, :])
            bk = kb.bank()
            for g in range(4):
                kb.mm(bk[:, g * 120:(g + 1) * 120], lhsT=PROW[0:120, g * 128:(g + 1) * 128], rhs=cf("ident")[0:120, 0:120])
            kb.cp(PBs[:, :, hh * 8:(hh + 1) * 8, 0:15], bk[:, 0:480].rearrange("p (g s t) -> p g s t", g=4, t=15), eng="dve")
        modnorm(TP, 64, 1, UB, SQ, RS, TT)
        bk = kb.bank()
        for g in range(4):
            for k in range(8):
                kb.mm(bk[:, g * 64:(g + 1) * 64], lhsT=WPB[:, k, g * 128:(g + 1) * 128], rhs=UB[:, k, 0:64], start=(k == 0), stop=(k == 7))
        kb.cp(PBs[:, :, :, 15:19], bk[:, 0:256].rearrange("p (g s t) -> p g s t", g=4, t=4), eng="act")
        fin = wsum(PBs, SAs, SBs, 19, 4)
        fv = [SAs, SBs, SAs, SBs]
        for g in range(4):
            kb.stt(_view(DP[:, g, 0:64], [16, 4]), fv[g][:, g, :, 15:19], 1.0 / WIN_[g], PBs[:, g, :, 15:19], ALU.mult, ALU.subtract)
        bk = kb.bank()
        for g in range(4):
            kb.mm(bk[:, g * 64:(g + 1) * 64], lhsT=WPL[:, g, :], rhs=DP[:, g, 0:64])
        for g in range(4):
            kb.ts(YBb[:, g, 0:64], bk[:, g * 64:(g + 1) * 64], PS[:, g:g + 1], None, ALU.mult)
        kb.dma("sp", YAB[:, 4:8, TP:T], YBb[:, :, 0:64])
        po_rows = O["pool_s"][l].rearrange("s i c -> (s i) c")
        for hh in range(2):
            kb.cp(TMPH[:].rearrange("p g (s t) -> p g s t", t=15), PBs[:, :, hh * 8:(hh + 1) * 8, 4:19], eng="dve")
            bk = kb.bank()
            for g in range(4):
                kb.mm(bk[0:120, g * 128:(g + 1) * 128], lhsT=TMPH[:, g, :], rhs=cf("ident"))
            kb.cp(POUT[0:120, :], bk[0:120, 0:512], eng="dve")
            kb.dma("sp", po_rows[hh * 120:(hh + 1) * 120, :], POUT[0:120, :])

    def pass_b(l):
        ar.reset()
        WGT = ar.alloc([8, 2048], BF16)
        WBA = ar.alloc([4, 1024], BF16)
        WBB = ar.alloc([4, 1024], BF16)
        WOT = ar.alloc([8, 1024], BF16)
        kb.dma("pool", WGT, I["w_gate"][l].rearrange("(k p) n -> p k n", p=128))
        kb.dma("pool", WBA, I["w_br_a"][l].rearrange("(j p) n -> p j n", p=128))
        kb.dma("pool", WBB, I["w_br_b"][l].rearrange("(j p) n -> p j n", p=128))
        kb.dma("pool", WOT, I["w_out"][l].rearrange("(k p) n -> p k n", p=128))
        YT = ar.alloc([8, 512], BF16)
        UB = ar.alloc([8, 512], BF16)
        SQ = ar.alloc([8, 512], BF16)
        RS = ar.alloc([512], F32)
        TT = [ar.alloc([512], F32) for _ in range(2)]
        GA = ar.alloc([512], F32); GB = ar.alloc([512], F32); T1 = ar.alloc([512], F32); T2 = ar.alloc([512], F32)
        MG = ar.alloc([8, 512], BF16)
        BGv = vec("b_gate")
        for (t0, n) in TILES:
            kb.dma("sp", YT[:, :, 0:n], YAB[:, :, t0:t0 + n])
            modnorm(t0, n, 1, UB, SQ, RS, TT)
            for m in range(8):
                ba = kb.bank(); bb = kb.bank(); bc_ = kb.bank(); bd = kb.bank()
                for k in range(8):
                    kb.mm(ba[:, 0:n], lhsT=WGT[:, k, m * 128:(m + 1) * 128], rhs=UB[:, k, 0:n], start=(k == 0), stop=(k == 7))
                for k in range(8):
                    kb.mm(bb[:, 0:n], lhsT=WGT[:, k, 1024 + m * 128:1024 + (m + 1) * 128], rhs=UB[:, k, 0:n], start=(k == 0), stop=(k == 7))
                for j in range(4):
                    kb.mm(bc_[:, 0:n], lhsT=WBA[:, j, m * 128:(m + 1) * 128], rhs=YT[:, j, 0:n], start=(j == 0), stop=(j == 3))
                for j in range(4):
                    kb.mm(bd[:, 0:n], lhsT=WBB[:, j, m * 128:(m + 1) * 128], rhs=YT[:, 4 + j, 0:n], start=(j == 0), stop=(j == 3))
                kb.act(GA[:, 0:n], ba[:, 0:n], AF.Sigmoid, bias=BGv[:, m:m + 1], scale=1.0)
                kb.act(GB[:, 0:n], bb[:, 0:n], AF.Sigmoid, bias=BGv[:, 8 + m:9 + m], scale=1.0)
                kb.tt(T1[:, 0:n], bc_[:, 0:n], GA[:, 0:n], ALU.mult)
                kb.tt(T2[:, 0:n], bd[:, 0:n], GB[:, 0:n], ALU.mult)
                kb.tt(MG[:, m, 0:n], T1[:, 0:n], T2[:, 0:n], ALU.add)
            for m2 in range(8):
                bo = kb.bank()
                for m in range(8):
                    kb.mm(bo[:, 0:n], lhsT=WOT[:, m, m2 * 128:(m2 + 1) * 128], rhs=MG[:, m, 0:n], start=(m == 0), stop=(m == 7))
                resid(m2, t0, n, bo, 1)

    def mixer(l, only_a=False):
        ar.log = []
        pass_a(l)
        if only_a:
            DBG["passA_log"] = list(ar.log)
            return
        DBG["marks"].append(("pool%d" % l, sum(1 for o in kb.S.ops if o.eng == "pe"), len(kb.S.ops)))
        pass_pool(l)
        DBG["marks"].append(("passB%d" % l, sum(1 for o in kb.S.ops if o.eng == "pe"), len(kb.S.ops)))
        pass_b(l)

    return mixer


def _shard_inputs(inputs):
    maps = []
    shared = {n: np.ascontiguousarray(np.asarray(inputs[n], dtype=np.float32)) for n in WNAMES}
    xp = np.asarray(inputs["x_prompt"], dtype=np.float32)
    xs = np.asarray(inputs["x_sample"], dtype=np.float32)
    cp_ = np.asarray(inputs["c_prompt"], dtype=np.float32)
    cs = np.asarray(inputs["c_sample"], dtype=np.float32)
    swkv = np.asarray(inputs["state_wkv"], dtype=np.float32)
    ssh = np.asarray(inputs["state_shift"], dtype=np.float32)
    spl = np.asarray(inputs["state_pool"], dtype=np.float32)
    for i in range(NCORES):
        sl = slice(NSEQ * i, NSEQ * (i + 1))
        m = dict(shared)
        m["xin"] = np.ascontiguousarray(np.concatenate([xp[i], xs[sl].reshape(TS, D)], axis=0))
        m["cin"] = np.ascontiguousarray(np.concatenate([cp_[i:i + 1], cs[sl]], axis=0))
        m["swkv"] = np.ascontiguousarray(swkv[:, sl])
        m["sshift"] = np.ascontiguousarray(ssh[:, sl, 0, :])
        m["spool"] = np.ascontiguousarray(spl[:, sl])
        m["consts"] = CONSTS_NP
        maps.append(m)
    return maps


_NC_CACHE = {}
DBG = {}


def kernel(**inputs):
    if "nc" not in _NC_CACHE:
        _NC_CACHE["nc"] = build_program()
    nc = _NC_CACHE["nc"]
    maps = _shard_inputs(inputs)
    res = run_bass_kernel_spmd(nc, maps, core_ids=list(range(NCORES)))
    R = res.results
    y_p = np.stack([R[i]["y"][:TP] for i in range(NCORES)], axis=0)
    y_s = np.concatenate([R[i]["y"][TP:].reshape(NSEQ, DEC, D) for i in range(NCORES)], axis=0)
    wkv_p = np.stack([R[i]["wkv_p"] for i in range(NCORES)], axis=1)
    shift_p = np.stack([R[i]["shift_p"] for i in range(NCORES)], axis=1)[:, :, None, :]
    pool_p = np.stack([R[i]["pool_p"] for i in range(NCORES)], axis=1)
    wkv_s = np.concatenate([R[i]["wkv_s"] for i in range(NCORES)], axis=1)
    shift_s = np.concatenate([R[i]["shift_s"] for i in range(NCORES)], axis=1)[:, :, None, :]
    pool_s = np.concatenate([R[i]["pool_s"] for i in range(NCORES)], axis=1)
    f = lambda a: np.ascontiguousarray(a, dtype=np.float32)
    return (f(y_p), f(y_s), f(wkv_p), f(shift_p), f(pool_p), f(wkv_s), f(shift_s), f(pool_s))
```

```python
import numpy as np
from contextlib import ExitStack
import concourse.bass as bass
import concourse.mybir as mybir
from concourse.bass_utils import run_bass_kernel_spmd

F32 = mybir.dt.float32
BF16 = mybir.dt.bfloat16
ALU = mybir.AluOpType
AF = mybir.ActivationFunctionType
AX = mybir.AxisListType


class _Op:
    __slots__ = ("idx", "eng", "fn", "deps", "dma", "inc", "incval", "sem", "waits", "ring_prev", "gidx")

    def __init__(self, idx, eng, fn, deps, dma):
        self.idx, self.eng, self.fn, self.deps, self.dma = idx, eng, fn, deps, dma
        self.inc = False
        self.incval = 0
        self.sem = None
        self.waits = []
        self.ring_prev = None


def _region(ap):
    t = ap.tensor
    name = ap.name
    space = str(ap.space)
    pat = ap.ap
    off = int(ap.offset)
    es = mybir.dt.size(ap.dtype)
    if space == "DRAM":
        lo = off
        hi = off + 1
        for st, cnt in pat:
            hi += abs(int(st)) * (int(cnt) - 1)
        return (name, 0, 1, lo * es, hi * es)
    if "PSUM" in space.upper():
        return (name, 0, 128, 0, 1 << 30)
    shp = list(t.shape)
    pstep = 1
    for s in shp[1:]:
        pstep *= int(s)
    p0 = off // pstep
    f0 = off % pstep
    st0, cnt0 = pat[0]
    if int(st0) == pstep or int(cnt0) == 1:
        npart = int(cnt0)
        rest = pat[1:]
    else:
        npart = 1
        rest = pat
    hi = f0 + 1
    for st, cnt in rest:
        hi += abs(int(st)) * (int(cnt) - 1)
    return (name, p0, p0 + npart, f0 * es, hi * es)


class Sched:
    COMPUTE = ("pe", "act", "dve", "pool")
    RING = 8

    def __init__(self, nc):
        self.nc = nc
        self.ops = []
        self.rec = {}
        self.nd = {"sp": 0, "act": 0, "pool": 0}

    def op(self, eng, fn, reads=(), writes=(), dma=False):
        idx = len(self.ops)
        deps = set()
        rr = [_region(a) for a in reads]
        ww = [_region(a) for a in writes]
        for (name, p0, p1, f0, f1) in rr:
            for r in self.rec.get(name, ()):
                if r[5] and r[0] < p1 and p0 < r[1] and r[2] < f1 and f0 < r[3]:
                    deps.add((r[4], "raw"))
        for (name, p0, p1, f0, f1) in ww:
            for r in self.rec.get(name, ()):
                if r[0] < p1 and p0 < r[1] and r[2] < f1 and f0 < r[3]:
                    deps.add((r[4], "waw" if r[5] else "war"))
        o = _Op(idx, eng, fn, deps, dma)
        self.ops.append(o)
        for (name, p0, p1, f0, f1) in ww:
            lst = self.rec.setdefault(name, [])
            lst[:] = [r for r in lst if not (p0 <= r[0] and r[1] <= p1 and f0 <= r[2] and r[3] <= f1)]
            lst.append([p0, p1, f0, f1, idx, True])
        for (name, p0, p1, f0, f1) in rr:
            lst = self.rec.setdefault(name, [])
            lst[:] = [r for r in lst if not ((not r[5]) and self.ops[r[4]].eng == eng
                                             and (not self.ops[r[4]].dma) and (not dma)
                                             and p0 <= r[0] and r[1] <= p1 and f0 <= r[2] and r[3] <= f1)]
            lst.append([p0, p1, f0, f1, idx, False])
        return o

    NSEM = 12
    CH = 512

    def lower(self, stack):
        nc = self.nc
        ops = self.ops
        for o in ops:
            need = []
            best = {}
            for (d, kind) in o.deps:
                p = ops[d]
                if p.dma:
                    need.append(d)
                elif o.dma or p.eng != o.eng or o.eng != "pe":
                    if p.eng not in best or best[p.eng] < d:
                        best[p.eng] = d
            need.extend(best.values())
            o.deps = need
            for d in need:
                ops[d].inc = True
        self.csem = {e: [stack.enter_context(nc.semaphore("s_%s%d" % (e, i))) for i in range(self.NSEM)]
                     for e in self.COMPUTE}
        self.rings = {q: [stack.enter_context(nc.semaphore("r_%s%d" % (q, i))) for i in range(self.RING)]
                      for q in ("sp", "act", "pool")}
        cnt = {e: 0 for e in self.COMPUTE}
        dk = {"sp": 0, "act": 0, "pool": 0}
        dma_final = {}
        for o in ops:
            if o.dma:
                k = dk[o.eng]
                dk[o.eng] += 1
                o.sem = self.rings[o.eng][k % self.RING]
                o.incval = 16 * (k // self.RING + 1)
                o.ring_prev = (o.sem, 16 * (k // self.RING)) if k >= self.RING else None
                dma_final[(o.eng, k % self.RING)] = (o.sem, o.incval)
                o.gidx = None
            elif o.inc:
                g = cnt[o.eng]
                cnt[o.eng] += 1
                epoch = g // self.CH
                o.sem = self.csem[o.eng][epoch % self.NSEM]
                o.incval = (epoch // self.NSEM) * self.CH + (g % self.CH) + 1
                o.gidx = g
        waited_c = {e: {} for e in ("pe", "act", "dve", "pool", "sp")}
        waited_d = {e: {} for e in ("pe", "act", "dve", "pool", "sp")}
        for o in ops:
            wl = []
            wd = waited_d[o.eng]
            wc = waited_c[o.eng]
            if o.dma and o.ring_prev is not None:
                sem, val = o.ring_prev
                if wd.get(id(sem), 0) < val:
                    wd[id(sem)] = val
                    wl.append((sem, val))
            for d in o.deps:
                p = ops[d]
                if p.dma:
                    if wd.get(id(p.sem), 0) < p.incval:
                        wd[id(p.sem)] = p.incval
                        wl.append((p.sem, p.incval))
                else:
                    if wc.get(p.eng, -1) < p.gidx:
                        wc[p.eng] = p.gidx
                        wl.append((p.sem, p.incval))
            o.waits = wl
        self.final_waits = list(dma_final.values())
        per = {e: [] for e in ("pe", "act", "dve", "pool", "sp")}
        for o in ops:
            per[o.eng].append(o)

        def run(engobj, lst, final=False):
            for o in lst:
                for (sem, val) in o.waits:
                    engobj.wait_ge(sem, val)
                ins = o.fn(engobj)
                if o.dma:
                    ins.then_inc(o.sem, 16)
                elif o.inc:
                    ins.then_inc(o.sem, 1)
            if final:
                for (sem, val) in self.final_waits:
                    engobj.wait_ge(sem, val)

        with nc.Block() as block:
            @block.tensor
            def _(e):
                run(e, per["pe"])

            @block.scalar
            def _(e):
                run(e, per["act"])

            @block.vector
            def _(e):
                run(e, per["dve"])

            @block.gpsimd
            def _(e):
                run(e, per["pool"])

            @block.sync
            def _(e):
                run(e, per["sp"], final=True)


NCORES = 8
D = 1024
KC = 8
TP = 2048
NSEQ = 16
DEC = 4
TS = NSEQ * DEC
T = TP + TS
APJ = 1824
PT = 2336
DFF = 2816
NFC = DFF // 128
LN_EPS = 64e-5
NORM_EPS = 1e-6
DECAY_K = float(np.exp(-0.5))

WNAMES = ["norm_g", "w_mod", "b_mod", "w_ffn_in", "w_ffn_out", "w_in", "mu_shift", "w0", "w2", "a0", "a2", "g2",
          "k_k", "k_a", "r_k", "ln_x_w", "ln_x_b", "w_pool", "pool_scale", "w_br_a", "w_br_b", "w_gate", "b_gate",
          "w_out", "final_g"]
WSHAPES = {
    "norm_g": [2, 3, D], "w_mod": [2, D, 9 * D], "b_mod": [2, 9 * D], "w_ffn_in": [2, 2, D, 2 * DFF],
    "w_ffn_out": [2, 2, DFF, D], "w_in": [2, D, PT], "mu_shift": [2, APJ], "w0": [2, 512], "w2": [2, 64, 512],
    "a0": [2, 512], "a2": [2, 64, 512], "g2": [2, 160, 512], "k_k": [2, 512], "k_a": [2, 512], "r_k": [2, 8, 64],
    "ln_x_w": [2, 512], "ln_x_b": [2, 512], "w_pool": [2, 4, 128, 128], "pool_scale": [2, 512],
    "w_br_a": [2, 512, D], "w_br_b": [2, 512, D], "w_gate": [2, D, 2 * D], "b_gate": [2, 2 * D], "w_out": [2, D, D],
    "final_g": [D],
}


def _make_consts():
    cols = {}
    parts = []
    pos = [0]

    def add(name, arr):
        a = np.zeros((128, arr.shape[1]), np.float32)
        a[:arr.shape[0]] = arr
        cols[name] = (pos[0], arr.shape[1])
        pos[0] += arr.shape[1]
        parts.append(a)

    p = np.arange(128)
    add("ident", np.eye(128, dtype=np.float32))
    add("ones", np.ones((128, 128), np.float32))
    same = (p[:, None] // 64) == (p[None, :] // 64)
    add("bones", same.astype(np.float32))
    s = p[:, None] % 64
    t = p[None, :] % 64
    add("msu64", (same & (s < t)).astype(np.float32))
    add("msuT64", (same & (s > t)).astype(np.float32))
    add("mu64", (same & (s <= t)).astype(np.float32))
    add("istack64", (p[:, None] % 64 == np.arange(64)[None, :]).astype(np.float32))
    add("tokmask64", (p[:, None] // 64 == np.arange(2)[None, :]).astype(np.float32))
    q = np.arange(8)
    same4 = (q[:, None] // 4) == (q[None, :] // 4)
    s4 = q[:, None] % 4
    t4 = q[None, :] % 4
    add("msu4", (same4 & (s4 < t4)).astype(np.float32))
    add("msuT4", (same4 & (s4 > t4)).astype(np.float32))
    add("mu4", (same4 & (s4 <= t4)).astype(np.float32))
    add("istack4", (q[:, None] % 4 == np.arange(8)[None, :]).astype(np.float32))
    add("tokmask4", (q[:, None] // 4 == np.arange(2)[None, :]).astype(np.float32))
    tt = np.arange(128)
    add("start64", np.broadcast_to((tt % 64 == 0).astype(np.float32)[None, :], (128, 128)).copy())
    add("nstart64", np.broadcast_to((tt % 64 != 0).astype(np.float32)[None, :], (128, 128)).copy())
    add("start4", np.broadcast_to((tt[:64] % 4 == 0).astype(np.float32)[None, :], (128, 64)).copy())
    add("nstart4", np.broadcast_to((tt[:64] % 4 != 0).astype(np.float32)[None, :], (128, 64)).copy())
    ratio = np.zeros((4, 15), np.float32)
    for g, w in enumerate((2, 4, 8, 16)):
        for i in range(15):
            ratio[g, i] = w / min(w, i + 1)
    add("ratio", np.broadcast_to(ratio.reshape(1, 60), (128, 60)).copy())
    return np.concatenate(parts, axis=1), cols


DBG = {}
CONSTS_NP, CCOLS = _make_consts()
NCC = CONSTS_NP.shape[1]

VR = {}
_r = 0
for _n, _k in (("norm_g", 24), ("mu", 15), ("w0", 4), ("a0", 4), ("k_k", 4), ("k_a", 4), ("r_k", 4), ("ln_w", 4),
               ("ln_b", 4), ("pool_scale", 4), ("b_gate", 16), ("final_g", 8)):
    VR[_n] = (_r, _k)
    _r += _k
NVR = _r


def _prod(s):
    r = 1
    for v in s:
        r *= int(v)
    return r


def _view(ap2, shape):
    if len(shape) == 1:
        return ap2
    names = "abcdef"[:len(shape)]
    kw = {names[i]: int(shape[i]) for i in range(len(shape))}
    return ap2.rearrange("p (%s) -> p %s" % (" ".join(names), " ".join(names)), **kw)


class Arena:
    def __init__(self, base_bf16, nelem):
        self.base = base_bf16
        self.n = nelem
        self.off = 0
        self.peak = 0
        self.log = []

    def reset(self, off=0):
        self.off = off

    def alloc(self, shape, dtype):
        n = _prod(shape)
        nb = n * 2 if dtype == F32 else n
        off = (self.off + 15) // 16 * 16
        assert off + nb <= self.n, "arena overflow: need %d have %d" % (off + nb, self.n)
        v = self.base[:, off:off + nb]
        if dtype == F32:
            v = v.bitcast(F32)
        self.off = off + nb
        self.peak = max(self.peak, self.off)
        self.log.append((off, tuple(shape), "f32" if dtype == F32 else "bf16"))
        return _view(v, shape)


class KB:
    def __init__(self, nc, S, banks):
        self.nc, self.S, self.banks = nc, S, banks
        self.bi = 0
        self.flip = 0
        self.sub = {}

    def bank(self):
        b = self.banks[self.bi % len(self.banks)]
        self.bi += 1
        return b

    def bank_of(self, ids):
        c = self.sub.get(ids, 0)
        self.sub[ids] = c + 1
        return self.banks[ids[c % len(ids)]]

    def mm(self, out, lhsT, rhs, start=True, stop=True):
        self.S.op("pe", lambda e: e.matmul(out, lhsT=lhsT, rhs=rhs, start=start, stop=stop),
                  reads=[lhsT, rhs], writes=[out])

    def tr(self, out, in_, ident):
        self.S.op("pe", lambda e: e.transpose(out, in_, ident), reads=[in_, ident], writes=[out])

    def dma(self, q, out, in_):
        self.S.op(q, lambda e: e.dma_start(out=out, in_=in_), reads=[in_], writes=[out], dma=True)

    def tt(self, out, in0, in1, op, eng="dve"):
        self.S.op(eng, lambda e: e.tensor_tensor(out=out, in0=in0, in1=in1, op=op), reads=[in0, in1], writes=[out])

    def ts(self, out, in0, s1, s2, op0, op1=None, eng="dve"):
        rd = [in0] + [s for s in (s1, s2) if not isinstance(s, (int, float)) and s is not None]
        if op1 is None:
            self.S.op(eng, lambda e: e.tensor_scalar(out=out, in0=in0, scalar1=s1, scalar2=None, op0=op0),
                      reads=rd, writes=[out])
        else:
            self.S.op(eng, lambda e: e.tensor_scalar(out=out, in0=in0, scalar1=s1, scalar2=s2, op0=op0, op1=op1),
                      reads=rd, writes=[out])

    def stt(self, out, in0, scalar, in1, op0, op1, eng="dve"):
        rd = [in0, in1] + ([] if isinstance(scalar, (int, float)) else [scalar])
        self.S.op(eng, lambda e: e.scalar_tensor_tensor(out=out, in0=in0, scalar=scalar, in1=in1, op0=op0, op1=op1),
                  reads=rd, writes=[out])

    def act(self, out, in_, func, bias=None, scale=None):
        rd = [in_] + [s for s in (bias, scale) if s is not None and not isinstance(s, (int, float))]
        kw = {}
        if bias is not None:
            kw["bias"] = bias
        if scale is not None:
            kw["scale"] = scale
        self.S.op("act", lambda e: e.activation(out=out, in_=in_, func=func, **kw), reads=rd, writes=[out])

    def cp(self, out, in_, eng=None):
        if eng is None:
            self.flip ^= 1
            eng = "act" if self.flip else "dve"
        if eng == "act":
            self.S.op("act", lambda e: e.activation(out=out, in_=in_, func=AF.Copy), reads=[in_], writes=[out])
        else:
            self.S.op(eng, lambda e: e.tensor_copy(out=out, in_=in_), reads=[in_], writes=[out])

    def recip(self, out, in_):
        self.S.op("dve", lambda e: e.reciprocal(out=out, in_=in_), reads=[in_], writes=[out])

    def memset(self, out, val, eng="dve"):
        self.S.op(eng, lambda e: e.memset(out, val), writes=[out])

    def reduce_sum(self, out, in_):
        self.S.op("dve", lambda e: e.tensor_reduce(out=out, in_=in_, axis=AX.X, op=ALU.add), reads=[in_], writes=[out])

    def scan(self, out, d0, d1, init):
        self.S.op("dve", lambda e: e.tensor_tensor_scan(out=out, data0=d0, data1=d1, initial=init, op0=ALU.mult,
                                                        op1=ALU.add), reads=[d0, d1], writes=[out])


def build_program(stop=None, dbg=False, nlayers=2):
    nc = bass.Bass("TRN2", target_bir_lowering=False)
    I = {}

    def din(name, shape):
        I[name] = nc.dram_tensor(name, list(shape), F32, kind="ExternalInput").ap()

    din("xin", [T, D]); din("cin", [17, D]); din("swkv", [2, NSEQ, 8, 64, 64]); din("sshift", [2, NSEQ, APJ])
    din("spool", [2, NSEQ, 15, 512]); din("consts", [128, NCC])
    for n in WNAMES:
        din(n, WSHAPES[n])
    O = {}

    def dout(name, shape):
        O[name] = nc.dram_tensor(name, list(shape), F32, kind="ExternalOutput").ap()

    dout("y", [T, D]); dout("wkv_p", [2, 8, 64, 64]); dout("shift_p", [2, APJ]); dout("pool_p", [2, 15, 512])
    dout("wkv_s", [2, NSEQ, 8, 64, 64]); dout("shift_s", [2, NSEQ, APJ]); dout("pool_s", [2, NSEQ, 15, 512])
    if dbg:
        dout("dbgX", [128, KC, T]); dout("dbgA", [128, 8, T])
    YAB = nc.dram_tensor("yab_scratch", [128, 8, T], BF16, kind="Internal").ap()

    ARN = 61696
    with ExitStack() as st:
        def sb(name, shape, dt):
            return st.enter_context(nc.sbuf_tensor(name, list(shape), dt))

        X = sb("X", [128, KC, T], F32)
        CF = sb("CF", [128, NCC], F32)
        CB = sb("CB", [128, NCC], BF16)
        ARt = sb("AR", [128, ARN], BF16)
        MOD = sb("MOD", [128, 72, 17], F32)
        VEC = sb("VEC", [128, NVR], F32)
        BM = sb("BM", [128, 72], F32)
        SCT = sb("SCT", [128, 8, 17], F32)
        GSp = sb("GSp", [128, 3, 8], F32); SHp = sb("SHp", [128, 3, 8], F32); COp = sb("COp", [128, 3, 8], F32)
        GSs = sb("GSs", [128, 3, 8, 16], F32); SHs = sb("SHs", [128, 3, 8, 16], F32); COs = sb("COs", [128, 3, 8, 16], F32)
        OMKA = sb("OMKA", [128, 4], F32)
        TMS = sb("TMS", [128, 64], F32)
        banks = [st.enter_context(nc.psum_tensor("ps%d" % i, [128, 512], F32)) for i in range(8)]
        S = Sched(nc)
        kb = KB(nc, S, banks)
        ar = Arena(ARt[:], ARN)

        def cf(name):
            c0, n = CCOLS[name]
            return CF[:, c0:c0 + n]

        def cb(name):
            c0, n = CCOLS[name]
            return CB[:, c0:c0 + n]

        def vec(name):
            r0, k = VR[name]
            return VEC[:, r0:r0 + k]

        kb.dma("sp", CF[:], I["consts"])
        kb.cp(CB[:], CF[:], eng="dve")

        ar.reset()
        XT = [ar.alloc([D], F32) for _ in range(2)]
        CROW = ar.alloc([D], F32)
        for i in range(17):
            n = 128 if i < 16 else 64
            xt = XT[i % 2]
            kb.dma("sp", xt[0:n, :], I["xin"][i * 128:i * 128 + n, :])
            for half in range(2):
                bk = kb.bank()
                for cc in range(4):
                    c = half * 4 + cc
                    kb.tr(bk[:, cc * 128:cc * 128 + n], xt[0:n, c * 128:(c + 1) * 128], cf("ident")[0:n, 0:n])
                kb.cp(X[:, half * 4:half * 4 + 4, i * 128:i * 128 + n], _view(bk[:, 0:512], [4, 128])[:, :, 0:n])
        kb.dma("sp", CROW[0:17, :], I["cin"])
        kb.act(CROW[0:17, :], CROW[0:17, :], AF.Silu)
        bk = kb.bank()
        for c in range(8):
            kb.mm(bk[:, c * 17:(c + 1) * 17], lhsT=CROW[0:17, c * 128:(c + 1) * 128], rhs=cf("ident")[0:17, 0:17])
        kb.cp(SCT[:], _view(bk[:, 0:136], [8, 17]), eng="dve")

        def load_layer_vectors(l):
            ar.reset()
            ROWS = ar.alloc([128], F32)
            BMR = ar.alloc([128], F32)
            WM = [ar.alloc([8, 512], F32) for _ in range(2)]
            kb.memset(ROWS[:], 0.0)

            def rows(name, src):
                r0, k = VR[name]
                kb.dma("sp", ROWS[r0:r0 + k, :], src)

            rows("norm_g", I["norm_g"][l].rearrange("j (c p) -> (j c) p", p=128))
            r0, _ = VR["mu"]
            kb.dma("sp", ROWS[r0:r0 + 14, :], I["mu_shift"][l, 0:1792].rearrange("(c p) -> c p", p=128))
            kb.dma("sp", ROWS[r0 + 14:r0 + 15, 0:32], I["mu_shift"][l:l + 1, 1792:1824])
            for nm, src in (("w0", "w0"), ("a0", "a0"), ("k_k", "k_k"), ("k_a", "k_a"), ("ln_w", "ln_x_w"),
                            ("ln_b", "ln_x_b"), ("pool_scale", "pool_scale")):
                rows(nm, I[src][l].rearrange("(c p) -> c p", p=128))
            rows("r_k", I["r_k"][l].rearrange("(c h) k -> c (h k)", h=2))
            rows("b_gate", I["b_gate"][l].rearrange("(c p) -> c p", p=128))
            rows("final_g", I["final_g"].rearrange("(c p) -> c p", p=128))
            bk = kb.bank()
            kb.mm(bk[:, 0:NVR], lhsT=ROWS[0:NVR, :], rhs=cf("ident")[0:NVR, 0:NVR])
            kb.cp(VEC[:], bk[:, 0:NVR], eng="dve")
            kb.dma("sp", BMR[0:72, :], I["b_mod"][l].rearrange("(c p) -> c p", p=128))
            bk = kb.bank()
            kb.mm(bk[:, 0:72], lhsT=BMR[0:72, :], rhs=cf("ident")[0:72, 0:72])
            kb.cp(BM[:], bk[:, 0:72], eng="dve")
            kb.ts(OMKA[:], vec("k_a"), -1.0, 1.0, ALU.mult, ALU.add)
            wm = I["w_mod"][l].rearrange("(k p) n -> p k n", p=128)
            kb.dma("sp", WM[0][:], wm[:, :, 0:512])
            bk = None
            for blk in range(18):
                if blk + 1 < 18:
                    kb.dma("sp", WM[(blk + 1) % 2][:], wm[:, :, (blk + 1) * 512:(blk + 2) * 512])
                w = WM[blk % 2]
                for oc in range(4):
                    mc = blk * 4 + oc
                    if mc % 24 == 0:
                        bk = kb.bank()
                    o = bk[:, (mc % 24) * 17:(mc % 24) * 17 + 17]
                    for k in range(8):
                        kb.mm(o, lhsT=w[:, k, oc * 128:(oc + 1) * 128], rhs=SCT[:, k, :], start=(k == 0), stop=(k == 7))
                    if mc % 24 == 23:
                        g = mc // 24
                        kb.tt(MOD[:, g * 24:(g + 1) * 24, :], _view(bk[:, 0:408], [24, 17]),
                              BM[:, g * 24:(g + 1) * 24, None].to_broadcast([128, 24, 17]), ALU.add)
            MODv = MOD[:].rearrange("p (j k c) s -> p j k c s", j=3, k=3)
            NG = _view(vec("norm_g"), [3, 8])
            kb.ts(GSp[:], MODv[:, :, 1, :, 0], 1.0, None, ALU.add)
            kb.tt(GSp[:], GSp[:], NG, ALU.mult)
            kb.cp(SHp[:], MODv[:, :, 0, :, 0], eng="dve")
            kb.cp(COp[:], MODv[:, :, 2, :, 0], eng="dve")
            kb.ts(COp[:, 0, :], COp[:, 0, :], 0.5, None, ALU.mult)
            kb.ts(COp[:, 2, :], COp[:, 2, :], 0.5, None, ALU.mult)
            for j in range(3):
                kb.ts(GSs[:, j], MODv[:, j, 1, :, 1:17], 1.0, None, ALU.add)
                kb.tt(GSs[:, j], GSs[:, j], NG[:, j, :, None].to_broadcast([128, 8, 16]), ALU.mult)
                kb.cp(SHs[:, j], MODv[:, j, 0, :, 1:17], eng="dve")
                kb.ts(COs[:, j], MODv[:, j, 2, :, 1:17], (1.0 if j == 1 else 0.5), None, ALU.mult)

        def modnorm(tok0, n, j, U, SQ, RS, TT, bank=None):
            is_s = tok0 >= TP
            kb.act(SQ[:, :, 0:n], X[:, :, tok0:tok0 + n], AF.Square)
            bk = kb.bank() if bank is None else bank()
            for c in range(8):
                kb.mm(bk[:, 0:n], lhsT=cb("ones"), rhs=SQ[:, c, 0:n], start=(c == 0), stop=(c == 7))
            kb.act(RS[:, 0:n], bk[:, 0:n], AF.Sqrt, bias=NORM_EPS, scale=1.0 / D)
            kb.recip(RS[:, 0:n], RS[:, 0:n])
            for c in range(8):
                t = TT[c % 2]
                if not is_s:
                    kb.stt(t[:, 0:n], X[:, c, tok0:tok0 + n], GSp[:, j, c:c + 1], RS[:, 0:n], ALU.mult, ALU.mult)
                    kb.act(U[:, c, 0:n], t[:, 0:n], AF.Identity, bias=SHp[:, j, c:c + 1], scale=1.0)
                else:
                    tv = _view(t[:, 0:n], [16, 4])
                    kb.tt(tv, _view(X[:, c, tok0:tok0 + n], [16, 4]), GSs[:, j, c, :, None].to_broadcast([128, 16, 4]), ALU.mult)
                    kb.tt(t[:, 0:n], t[:, 0:n], RS[:, 0:n], ALU.mult)
                    kb.tt(_view(U[:, c, 0:n], [16, 4]), tv, SHs[:, j, c, :, None].to_broadcast([128, 16, 4]), ALU.add)

        def resid(m, tok0, n, bo, j):
            if tok0 < TP:
                kb.stt(X[:, m, tok0:tok0 + n], bo[:, 0:n], COp[:, j, m:m + 1], X[:, m, tok0:tok0 + n], ALU.mult, ALU.add)
            else:
                kb.tt(_view(TMS[:, 0:n], [16, 4]), _view(bo[:, 0:n], [16, 4]),
                      COs[:, j, m, :, None].to_broadcast([128, 16, 4]), ALU.mult)
                kb.tt(X[:, m, tok0:tok0 + n], X[:, m, tok0:tok0 + n], TMS[:, 0:n], ALU.add)

        TILES = [(0, 512), (512, 512), (1024, 512), (1536, 512), (2048, 64)]

        def ffn(l, f):
            j = 0 if f == 0 else 2
            ar.reset()
            U = ar.alloc([8, T], BF16)
            WG = [ar.alloc([8, 512], BF16) for _ in range(2)]
            WU = [ar.alloc([8, 512], BF16) for _ in range(2)]
            WO = [ar.alloc([4, 1024], BF16) for _ in range(2)]
            H = [ar.alloc([4, 512], BF16) for _ in range(2)]
            SG = [ar.alloc([512], BF16) for _ in range(2)]
            SQ = ar.alloc([8, 512], BF16)
            RS = ar.alloc([512], F32)
            TT = [ar.alloc([512], F32) for _ in range(2)]
            win = I["w_ffn_in"][l, f].rearrange("(k p) n -> p k n", p=128)
            wout = I["w_ffn_out"][l, f].rearrange("(j p) n -> p j n", p=128)
            groups = [(0, 4), (4, 4), (8, 4), (12, 4), (16, 4), (20, 2)]

            def load(gi):
                c0, ng = groups[gi]
                b = gi % 2
                kb.dma("pool", WG[b][:, :, 0:ng * 128], win[:, :, c0 * 128:(c0 + ng) * 128])
                kb.dma("pool", WU[b][:, :, 0:ng * 128], win[:, :, DFF + c0 * 128:DFF + (c0 + ng) * 128])
                kb.dma("pool", WO[b][:, 0:ng, :], wout[:, c0:c0 + ng, :])

            load(0)
            hb = 0
            for gi, (c0, ng) in enumerate(groups):
                if gi + 1 < len(groups):
                    load(gi + 1)
                b = gi % 2
                for ti, (t0, n) in enumerate(TILES):
                    if gi == 0:
                        if ti == 0:
                            modnorm(t0, n, j, U[:, :, t0:t0 + n], SQ, RS, TT)
                        if ti + 1 < len(TILES):
                            t1, n1 = TILES[ti + 1]
                            modnorm(t1, n1, j, U[:, :, t1:t1 + n1], SQ, RS, TT)
                    h = H[hb % 2]
                    hb += 1
                    for jj in range(ng):
                        bg = kb.bank()
                        bu = kb.bank()
                        for k in range(8):
                            kb.mm(bg[:, 0:n], lhsT=WG[b][:, k, jj * 128:(jj + 1) * 128], rhs=U[:, k, t0:t0 + n],
                                  start=(k == 0), stop=(k == 7))
                        for k in range(8):
                            kb.mm(bu[:, 0:n], lhsT=WU[b][:, k, jj * 128:(jj + 1) * 128], rhs=U[:, k, t0:t0 + n],
                                  start=(k == 0), stop=(k == 7))
                        sg = SG[jj % 2]
                        kb.act(sg[:, 0:n], bg[:, 0:n], AF.Silu)
                        kb.tt(h[:, jj, 0:n], bu[:, 0:n], sg[:, 0:n], ALU.mult)
                    for m in range(8):
                        bo = kb.bank()
                        for jj in range(ng):
                            kb.mm(bo[:, 0:n], lhsT=WO[b][:, jj, m * 128:(m + 1) * 128], rhs=h[:, jj, 0:n],
                                  start=(jj == 0), stop=(jj == ng - 1))
                        resid(m, t0, n, bo, j)

        def final_out():
            ar.reset()
            YT = [ar.alloc([D], F32) for _ in range(2)]
            SQ = ar.alloc([8, 128], BF16)
            RS = ar.alloc([128], F32)
            YN = [ar.alloc([8, 128], F32) for _ in range(2)]
            FG = vec("final_g")
            for i in range(17):
                n = 128 if i < 16 else 64
                t0 = i * 128
                yn = YN[i % 2]
                kb.act(SQ[:, :, 0:n], X[:, :, t0:t0 + n], AF.Square)
                bk = kb.bank()
                for c in range(8):
                    kb.mm(bk[:, 0:n], lhsT=cb("ones"), rhs=SQ[:, c, 0:n], start=(c == 0), stop=(c == 7))
                kb.act(RS[:, 0:n], bk[:, 0:n], AF.Sqrt, bias=NORM_EPS, scale=1.0 / D)
                kb.recip(RS[:, 0:n], RS[:, 0:n])
                for c in range(8):
                    kb.stt(yn[:, c, 0:n], X[:, c, t0:t0 + n], FG[:, c:c + 1], RS[:, 0:n], ALU.mult, ALU.mult)
                yt = YT[i % 2]
                for half in range(2):
                    bk = kb.bank()
                    for cc in range(4):
                        c = half * 4 + cc
                        kb.tr(bk[0:n, cc * 128:(cc + 1) * 128], yn[:, c, 0:n], cf("ident"))
                    kb.cp(yt[0:n, half * 512:(half + 1) * 512], bk[0:n, 0:512])
                kb.dma("sp", O["y"][t0:t0 + n, :], yt[0:n, :])

        def dbg_dump_x():
            if dbg:
                kb.dma("sp", O["dbgX"], X[:])

        mixer = _make_mixer(nc, kb, ar, I, O, X, YAB, cf, cb, vec, modnorm, resid, OMKA, TILES, dbg)

        def mark(name):
            DBG.setdefault("marks", []).append((name, sum(1 for o in S.ops if o.eng == "pe"), len(S.ops)))

        DBG["marks"] = []
        for l in range(nlayers):
            mark("vec%d" % l)
            load_layer_vectors(l)
            mark("ffn%d0" % l)
            ffn(l, 0)
            if stop == "ffn0":
                break
            mark("mixer%d" % l)
            mixer(l, only_a=(stop == "passA"))
            if stop in ("mix0", "passA"):
                break
            mark("ffn%d1" % l)
            ffn(l, 1)
        mark("final")
        dbg_dump_x()
        final_out()
        S.lower(st)
        print("ops", len(S.ops), "arena peak KiB", ar.peak * 2 / 1024.0)
    return nc


def _make_mixer(nc, kb, ar, I, O, X, YAB, cf, cb, vec, modnorm, resid, OMKA, TILES, dbg):
    NB = 64

    def bc(ap, shape):
        return ap.to_broadcast(list(shape))

    def pass_a(l):
        ar.reset()
        WIN = ar.alloc([8, APJ], BF16)
        W2T = ar.alloc([512], BF16)
        A2T = ar.alloc([512], BF16)
        G2T = ar.alloc([2, 512], BF16)
        kb.memset(W2T[:], 0.0)
        kb.memset(A2T[:], 0.0)
        kb.dma("pool", WIN, I["w_in"][l].rearrange("(k p) n -> p k n", p=128)[:, :, 0:APJ])
        kb.dma("pool", W2T[0:64, :], I["w2"][l])
        kb.dma("pool", A2T[64:128, :], I["a2"][l])
        kb.dma("pool", G2T[:, 0, :], I["g2"][l, 0:128, :])
        kb.dma("pool", G2T[0:32, 1, :], I["g2"][l, 128:160, :])
        UB = ar.alloc([8, NB], BF16)
        SQn = ar.alloc([8, NB], BF16)
        RSn = ar.alloc([NB], F32)
        TTn = [ar.alloc([NB], F32) for _ in range(2)]
        PA = ar.alloc([15, 80], F32)
        XS2 = [ar.alloc([15, NB], F32) for _ in range(2)]
        LAST = ar.alloc([15], F32)
        SHS = ar.alloc([15, 16], F32)
        SHO = ar.alloc([15, 16], F32)
        LIN = ar.alloc([3, NB], BF16)
        f4 = lambda: ar.alloc([4, NB], F32)
        t_wd, t_p, t_pex, t_ip, t_aa, t_kk, t_t1, t_t2, t_k2 = [f4() for _ in range(9)]
        SQK = ar.alloc([4, NB], BF16)
        RKR = ar.alloc([4, NB], BF16)
        blk = lambda: ar.alloc([512], BF16)
        AT3 = [blk() for _ in range(3)]
        RT3 = [blk() for _ in range(3)]
        PC3 = [ar.alloc([64], F32) for _ in range(3)]
        GG3 = [f4() for _ in range(3)]
        BON3 = [f4() for _ in range(3)]
        BT2 = [blk() for _ in range(2)]; KT2 = [blk() for _ in range(2)]; BH2 = [blk() for _ in range(3)]
        KH2 = [blk() for _ in range(3)]; VB2 = [blk() for _ in range(3)]
        NM, NTM, QB, QTB = [blk() for _ in range(4)]
        AAK2 = [blk() for _ in range(2)]; ARB2 = [blk() for _ in range(2)]; ARK2 = [blk() for _ in range(2)]
        MM2 = [blk() for _ in range(2)]
        BHT = ar.alloc([4, 128], BF16)
        KHT = ar.alloc([4, 128], BF16)
        VT = ar.alloc([4, 64], BF16)
        ZT = ar.alloc([4, 64], BF16)
        UT = ar.alloc([4, 64], BF16)
        SQY = ar.alloc([4, 64], F32)
        YC = ar.alloc([4, 64], F32)
        YNB = ar.alloc([4, 128], BF16)
        ST1 = ar.alloc([4], F32); ST2 = ar.alloc([4], F32); STM = ar.alloc([4], F32); STV = ar.alloc([4], F32)
        YF = ar.alloc([4, NB], F32)
        p2a = f4()
        YAb = ar.alloc([4, NB], BF16)
        S32 = ar.alloc([4, 64], F32)
        SBF = [ar.alloc([4, 64], BF16) for _ in range(2)]
        SI = ar.alloc([8, 64], F32)
        SO = ar.alloc([4, 128], F32)
        SHT = ar.alloc([15, 128], F32)
        SROW = SHT[:].rearrange("p m c -> p (m c)")[:, 0:APJ]
        DBG["passA_kib"] = ar.off * 2 / 1024.0

        for t in AT3 + RT3 + BT2 + KT2 + BH2 + KH2 + VB2:
            kb.memset(t[:], 0.0)
        kb.memset(PA[:], 0.0)
        kb.memset(LAST[:], 0.0)
        kb.memset(XS2[0][:], 0.0)
        kb.memset(XS2[1][:], 0.0)
        kb.memset(S32[:], 0.0)
        kb.memset(SBF[0][:], 0.0)
        kb.memset(LIN[:], 0.0)
        kb.dma("sp", SROW[0:16, :], I["sshift"][l])
        kb.memset(SHS[:], 0.0)
        for half in range(2):
            bk = kb.bank()
            ms = range(0, 8) if half == 0 else range(8, 15)
            for m in ms:
                Mm = 128 if m < 14 else 32
                kb.mm(bk[0:Mm, (m % 8) * 16:(m % 8) * 16 + 16], lhsT=SROW[0:16, m * 128:m * 128 + Mm], rhs=cf("ident")[0:16, 0:16])
            if half == 0:
                kb.cp(SHS[:, 0:8, :], _view(bk[:, 0:128], [8, 16]), eng="dve")
            else:
                kb.cp(SHS[:, 8:14, :], _view(bk[:, 0:96], [6, 16]), eng="dve")
                kb.cp(SHS[0:32, 14, :], bk[0:32, 96:112], eng="dve")

        MU = vec("mu")
        sbi = [0]
        NBLK = TP // NB
        bankA0 = lambda: kb.bank_of((0, 1))
        bankA = lambda: kb.bank_of((2, 3))
        bankB = lambda: kb.bank_of((4, 5))
        bankC = lambda: kb.bank_of((6, 7))

        def geom(b):
            is_s = (b == NBLK)
            C = 4 if is_s else 64
            R = 2 * C
            NQ = 512 // R
            return is_s, C, R, NQ, NQ // 4, ("4" if is_s else "64"), b * NB

        def stage1a(b):
            is_s, C, R, NQ, nch, sfx, tok0 = geom(b)
            n = NB
            XS = XS2[b % 2]
            modnorm(tok0, n, 1, UB, SQn, RSn, TTn, bank=bankA0)
            yield
            for half in range(2):
                bk = bankA0()
                ms = range(0, 8) if half == 0 else range(8, 15)
                for m in ms:
                    Mm = 128 if m < 14 else 32
                    for k in range(8):
                        kb.mm(bk[0:Mm, (m % 8) * 64:(m % 8) * 64 + 64], lhsT=WIN[:, k, m * 128:m * 128 + Mm], rhs=UB[:, k, :],
                              start=(k == 0), stop=(k == 7))
                    if m % 2 == 1:
                        yield
                if not is_s:
                    if half == 0:
                        kb.cp(PA[:, 0:8, 1:65], _view(bk[:, 0:512], [8, 64]), eng="act")
                    else:
                        kb.cp(PA[:, 8:14, 1:65], _view(bk[:, 0:384], [6, 64]), eng="act")
                        kb.cp(PA[0:32, 14, 1:65], bk[0:32, 384:448], eng="act")
                else:
                    PAs = PA[:].rearrange("p m (s t) -> p m s t", t=5)
                    if half == 0:
                        kb.cp(PAs[:, 0:8, :, 1:5], bk[:, 0:512].rearrange("p (m s t) -> p m s t", m=8, t=4), eng="act")
                    else:
                        kb.cp(PAs[:, 8:14, :, 1:5], bk[:, 0:384].rearrange("p (m s t) -> p m s t", m=6, t=4), eng="act")
                        kb.cp(PAs[0:32, 14, :, 1:5], bk[0:32, 384:448].rearrange("p (s t) -> p s t", t=4), eng="act")
                yield
            if not is_s:
                kb.cp(PA[:, :, 0], LAST[:], eng="act")
                kb.tt(XS[:], PA[:, :, 0:64], PA[:, :, 1:65], ALU.subtract)
                yield
                kb.tt(XS[:], XS[:], bc(MU[:, :, None], [128, 15, 64]), ALU.mult)
                yield
                kb.tt(XS[:], XS[:], PA[:, :, 1:65], ALU.add)
                kb.cp(LAST[:], PA[:, :, 64], eng="act")
                yield
            else:
                PAs = PA[:].rearrange("p m (s t) -> p m s t", t=5)
                XSs = XS[:].rearrange("p m (s t) -> p m s t", t=4)
                kb.cp(PAs[:, :, :, 0], SHS[:], eng="dve")
                for m0, m1 in ((0, 8), (8, 15)):
                    kb.tt(XSs[:, m0:m1], PAs[:, m0:m1, :, 0:4], PAs[:, m0:m1, :, 1:5], ALU.subtract)
                yield
                kb.tt(XS[:], XS[:], bc(MU[:, :, None], [128, 15, 64]), ALU.mult)
                yield
                for m0, m1 in ((0, 8), (8, 15)):
                    kb.tt(XSs[:, m0:m1], XSs[:, m0:m1], PAs[:, m0:m1, :, 1:5], ALU.add)
                kb.cp(SHO[:], PAs[:, :, :, 4], eng="dve")
                yield
            if b == NBLK - 1:
                bk = bankA0()
                kb.mm(bk[0:15, 0:128], lhsT=LAST[:, 0:15], rhs=cf("ident"))
                kb.cp(SHT[0:15, 0, :], bk[0:15, 0:128], eng="dve")
                kb.dma("sp", O["shift_p"][l, 0:1792].rearrange("(c p) -> c p", p=128), SHT[0:14, 0, :])
                kb.dma("sp", O["shift_p"][l:l + 1, 1792:1824], SHT[14:15, 0, 0:32])
                yield
            if is_s:
                for g4 in range(4):
                    bk = bankA0()
                    ms = range(g4 * 4, min(g4 * 4 + 4, 15))
                    for m in ms:
                        Mm = 128 if m < 14 else 32
                        kb.mm(bk[0:16, (m % 4) * 128:(m % 4) * 128 + Mm], lhsT=SHO[0:Mm, m, :], rhs=cf("ident")[0:Mm, 0:Mm])
                    if g4 < 3:
                        kb.cp(SHT[0:16, g4 * 4:g4 * 4 + 4, :], _view(bk[0:16, 0:512], [4, 128]), eng="dve")
                    else:
                        kb.cp(SHT[0:16, 12:14, :], _view(bk[0:16, 0:256], [2, 128]), eng="dve")
                        kb.cp(SHT[0:16, 14, 0:32], bk[0:16, 256:288], eng="dve")
                    yield
                kb.dma("sp", O["shift_s"][l, :, 0:1792], SHT[0:16, 0:14, :].rearrange("p m c -> p (m c)"))
                kb.dma("sp", O["shift_s"][l, :, 1792:1824], SHT[0:16, 14, 0:32])
                yield

        def stage1a2(b):
            is_s, C, R, NQ, nch, sfx, tok0 = geom(b)
            n = NB
            XS = XS2[b % 2]
            AT, RT, PC, t_gg, t_bon = AT3[b % 3], RT3[b % 3], PC3[b % 3], GG3[b % 3], BON3[b % 3]
            BT, KT, BH, KH, VB = BT2[b % 2], KT2[b % 2], BH2[b % 3], KH2[b % 3], VB2[b % 3]
            if is_s:
                for t in (AT, RT, BT, KT, BH, KH, VB):
                    kb.memset(t[:], 0.0)
                yield
            xr, xk, xv = XS[:, 0:4, :], XS[:, 4:8, :], XS[:, 8:12, :]
            kb.act(LIN[0:64, 0, :], XS[0:64, 12, :], AF.Tanh)
            kb.cp(LIN[64:128, 0, :], XS[64:128, 12, :], eng="act")
            kb.act(LIN[:, 1, :], XS[:, 13, :], AF.Sigmoid)
            kb.act(LIN[0:32, 2, :], XS[0:32, 14, :], AF.Sigmoid)
            yield
            bw = bankA()
            for j in range(4):
                kb.mm(bw[:, j * 64:(j + 1) * 64], lhsT=W2T[:, j * 128:(j + 1) * 128], rhs=LIN[:, 0, :])
                kb.mm(bw[:, 256 + j * 64:256 + (j + 1) * 64], lhsT=A2T[:, j * 128:(j + 1) * 128], rhs=LIN[:, 0, :])
            bg = bankA()
            for j in range(4):
                kb.mm(bg[:, j * 64:(j + 1) * 64], lhsT=G2T[:, 0, j * 128:(j + 1) * 128], rhs=LIN[:, 1, :], start=True, stop=False)
                kb.mm(bg[:, j * 64:(j + 1) * 64], lhsT=G2T[0:32, 1, j * 128:(j + 1) * 128], rhs=LIN[0:32, 2, :], start=False, stop=True)
            yield
            kb.tt(t_t1[:], _view(bw[:, 0:256], [4, 64]), bc(vec("w0")[:, :, None], [128, 4, 64]), ALU.add)
            kb.tt(t_aa[:], _view(bw[:, 256:512], [4, 64]), bc(vec("a0")[:, :, None], [128, 4, 64]), ALU.add)
            kb.cp(t_gg[:], _view(bg[:, 0:256], [4, 64]), eng="act")
            yield
            kb.act(t_t1[:], t_t1[:], AF.Sigmoid)
            kb.act(t_aa[:], t_aa[:], AF.Sigmoid)
            yield
            kb.act(t_wd[:], t_t1[:], AF.Exp, scale=-DECAY_K)
            kb.tt(t_kk[:], xk, bc(vec("k_k")[:, :, None], [128, 4, 64]), ALU.mult)
            yield
            kb.act(SQK[:], t_kk[:], AF.Square)
            kb.tt(t_t1[:], t_wd[:], bc(cf("nstart" + sfx)[:, None, 0:64], [128, 4, 64]), ALU.mult)
            kb.tt(t_t2[:], t_wd[:], bc(cf("start" + sfx)[:, None, 0:64], [128, 4, 64]), ALU.mult)
            yield
            bs = bankA()
            for j in range(4):
                kb.mm(bs[:, j * 64:(j + 1) * 64], lhsT=cb("bones"), rhs=SQK[:, j, :])
            for j in range(4):
                kb.scan(t_p[:, j, :], t_t1[:, j, :], t_t2[:, j, :], 1.0)
            yield
            kb.act(t_t1[:], _view(bs[:, 0:256], [4, 64]), AF.Sqrt)
            kb.recip(t_ip[:], t_p[:])
            kb.recip(t_t2[:], t_wd[:])
            yield
            kb.tt(t_pex[:], t_p[:], t_t2[:], ALU.mult)
            kb.cp(PC[:, 0:NQ].rearrange("p (s j) -> p j s", j=4),
                  t_p[:].rearrange("p j (s c) -> p j s c", c=C)[:, :, :, C - 1], eng="dve")
            kb.ts(t_t1[:], t_t1[:], 1e-12, None, ALU.max)
            yield
            kb.recip(t_t1[:], t_t1[:])
            yield
            kb.tt(t_kk[:], t_kk[:], t_t1[:], ALU.mult)
            yield
            kb.tt(t_t2[:], t_kk[:], t_aa[:], ALU.mult)
            kb.tt(t_t1[:], t_aa[:], bc(vec("k_a")[:, :, None], [128, 4, 64]), ALU.mult)
            kb.stt(t_wd[:], t_kk[:], -1.0, t_pex[:], ALU.mult, ALU.mult)
            yield
            kb.tt(t_t2[:], t_t2[:], t_ip[:], ALU.mult)
            kb.tt(t_t1[:], t_t1[:], bc(OMKA[:, :, None], [128, 4, 64]), ALU.add)
            yield
            kb.tt(t_k2[:], xk, t_t1[:], ALU.mult)
            yield
            kb.tt(t_t1[:], xr, t_k2[:], ALU.mult)
            kb.tt(t_aa[:], t_k2[:], t_ip[:], ALU.mult)
            yield
            kb.tt(RKR[:], t_t1[:], bc(vec("r_k")[:, :, None], [128, 4, 64]), ALU.mult)
            yield
            brk = bankA()
            for j in range(4):
                kb.mm(brk[:, j * 64:(j + 1) * 64], lhsT=cb("bones"), rhs=RKR[:, j, :])
            PCq = PC[:, 0:NQ].rearrange("p (s j) -> p j s", j=4)
            for h in range(2):
                ps_ = slice(64 * h, 64 * h + 64)

                def dst(tile):
                    return tile[ps_, :].rearrange("p (s j r) -> p j s r", j=4, r=R)[:, :, :, h * C:(h + 1) * C]

                def src(t3):
                    return t3[ps_].rearrange("p j (s c) -> p j s c", c=C)

                kb.cp(dst(AT), src(t_wd), eng="act")
                kb.tt(dst(RT), src(xr), src(t_p), ALU.mult)
                yield
                kb.cp(dst(BT), src(t_t2), eng="act")
                kb.cp(dst(KT), src(t_aa), eng="act")
                kb.tt(dst(BH), src(t_t2), bc(PCq[ps_, :, :, None], [64, 4, nch, C]), ALU.mult)
                yield
                kb.tt(dst(KH), src(t_aa), bc(PCq[ps_, :, :, None], [64, 4, nch, C]), ALU.mult)
                kb.cp(dst(VB), src(xv), eng="act")
                yield
            kb.tt(t_bon[:], _view(brk[:, 0:256], [4, 64]), xv, ALU.mult)
            yield

        def stage1b(b):
            is_s, C, R, NQ, nch, sfx, tok0 = geom(b)
            AT, RT = AT3[b % 3], RT3[b % 3]
            BT, KT = BT2[b % 2], KT2[b % 2]
            AAK, ARB, ARK, MM = AAK2[b % 2], ARB2[b % 2], ARK2[b % 2], MM2[b % 2]
            msu, msuT, mu_ = cf("msu" + sfx), cf("msuT" + sfx), cf("mu" + sfx)
            if is_s:
                for t in (NM, NTM, QB, QTB, AAK, ARB, ARK, MM):
                    kb.memset(t[:], 0.0)
                yield
            Mq = lambda tile: tile[0:R, :].rearrange("p (q r) -> p q r", r=R)
            MqK = lambda tile: tile[:, :].rearrange("p (q r) -> p q r", r=R)
            Fq = MqK

            def prod(lt, rt, mask, out_t):
                bk_ = bankB()
                for q in range(NQ):
                    kb.mm(bk_[0:R, q * R:(q + 1) * R], lhsT=Fq(lt)[:, q, :], rhs=Fq(rt)[:, q, :])
                    if q % 8 == 7:
                        yield
                kb.tt(Mq(out_t), bk_[0:R, :].rearrange("p (q r) -> p q r", r=R), bc(mask[0:R, None, 0:R], [R, NQ, R]), ALU.mult)
                yield

            yield from prod(BT, AT, msu, NM)
            yield from prod(AT, BT, msuT, NTM)
            kb.tt(Mq(MM), Mq(NM), bc(cf("ident")[0:R, None, 0:R], [R, NQ, R]), ALU.add)
            yield
            nlev = 5 if not is_s else 1
            Q, QT = NM, NTM
            Qn, QTn = QB, QTB
            extra = [(KT, AT, msu, AAK), (BT, RT, mu_, ARB), (KT, RT, mu_, ARK)]
            for lev in range(nlev):
                last = (lev == nlev - 1)
                b2 = bankB()
                for q in range(NQ):
                    kb.mm(b2[0:R, q * R:(q + 1) * R], lhsT=MqK(Q)[:, q, :], rhs=MqK(QT)[:, q, :])
                    if q % 8 == 7:
                        yield
                kb.cp(QTn[0:R, :], b2[0:R, :], eng="act")
                yield
                if not last:
                    b1 = bankB()
                    for q in range(NQ):
                        kb.mm(b1[0:R, q * R:(q + 1) * R], lhsT=MqK(QT)[:, q, :], rhs=MqK(Q)[:, q, :])
                        if q % 8 == 7:
                            yield
                    kb.cp(Qn[0:R, :], b1[0:R, :], eng="act")
                    yield
                b3 = bankB()
                for q in range(NQ):
                    o3 = b3[0:R, q * R:(q + 1) * R]
                    kb.mm(o3, lhsT=MqK(QTn)[:, q, :], rhs=MqK(MM)[:, q, :], start=True, stop=False)
                    kb.mm(o3, lhsT=cb("ident")[:, 0:R], rhs=MqK(MM)[:, q, :], start=False, stop=True)
                    if q % 8 == 7:
                        yield
                kb.cp(MM[0:R, :], b3[0:R, :], eng="act")
                yield
                Q, QT, Qn, QTn = Qn, QTn, Q, QT
                if extra:
                    yield from prod(*extra.pop(0))
            while extra:
                yield from prod(*extra.pop(0))

        def stage2(b):
            is_s, C, R, NQ, nch, sfx, tok0 = geom(b)
            n = NB
            AT, RT, PC, t_gg, t_bon = AT3[b % 3], RT3[b % 3], PC3[b % 3], GG3[b % 3], BON3[b % 3]
            BH, KH, VB = BH2[b % 3], KH2[b % 3], VB2[b % 3]
            AAK, ARB, ARK, MM = AAK2[b % 2], ARB2[b % 2], ARK2[b % 2], MM2[b % 2]
            istk, tokm = cb("istack" + sfx), cf("tokmask" + sfx)
            if is_s:
                for t in (BHT, KHT, VT, ZT, UT, YNB):
                    kb.memset(t[:], 0.0)
                yield
            KR = 128
            MqK = lambda tile: tile[:, :].rearrange("p (q r) -> p q r", r=R)
            Fq = MqK
            for gi in range(nch):
                q0 = gi * 4
                if is_s:
                    seq = gi
                    kb.dma("sp", SI[0:64, :, :], I["swkv"][l, seq].rearrange("h v k -> v h k"))
                    sb_in = SBF[sbi[0] % 2]
                    bk_ = bankC()
                    for j in range(4):
                        kb.mm(bk_[:, j * 64:(j + 1) * 64], lhsT=SI[0:64, 2 * j:2 * j + 2, :].rearrange("p h k -> p (h k)"),
                              rhs=cf("ident")[0:64, 0:64])
                    kb.cp(S32[:], _view(bk_[:, 0:256], [4, 64]), eng="dve")
                    kb.cp(sb_in[:], S32[:], eng="act")
                    yield
                else:
                    sb_in = SBF[sbi[0] % 2]
                sb_out = SBF[(sbi[0] + 1) % 2]
                sbi[0] += 1
                b_ = bankC()
                for j in range(4):
                    kb.mm(b_[0:R, j * 128:(j + 1) * 128], lhsT=Fq(BH)[:, q0 + j, :], rhs=cb("ident"))
                kb.cp(BHT[0:R], _view(b_[0:R, 0:512], [4, 128]), eng="act")
                yield
                b_ = bankC()
                for j in range(4):
                    kb.mm(b_[0:R, j * 128:(j + 1) * 128], lhsT=Fq(KH)[:, q0 + j, :], rhs=cb("ident"))
                kb.cp(KHT[0:R], _view(b_[0:R, 0:512], [4, 128]), eng="act")
                yield
                b_ = bankC()
                for j in range(4):
                    kb.mm(b_[0:R, j * 64:(j + 1) * 64], lhsT=Fq(VB)[:, q0 + j, :], rhs=cb("istack64"))
                kb.cp(VT[0:R], _view(b_[0:R, 0:256], [4, 64]), eng="act")
                yield
                bz = bankC()
                for j in range(4):
                    kb.mm(bz[0:R, j * 64:(j + 1) * 64], lhsT=MqK(AAK)[:, q0 + j, :], rhs=VT[0:KR, j, :], start=True, stop=False)
                    kb.mm(bz[0:R, j * 64:(j + 1) * 64], lhsT=Fq(AT)[:, q0 + j, :], rhs=sb_in[:, j, :], start=False, stop=True)
                kb.cp(ZT[0:R], _view(bz[0:R, 0:256], [4, 64]), eng="act")
                yield
                bu = bankC()
                for j in range(4):
                    kb.mm(bu[0:R, j * 64:(j + 1) * 64], lhsT=MqK(MM)[:, q0 + j, :], rhs=ZT[0:KR, j, :])
                kb.cp(UT[0:R], _view(bu[0:R, 0:256], [4, 64]), eng="dve")
                yield
                bs_ = bankC()
                for j in range(4):
                    o = bs_[:, j * 64:(j + 1) * 64]
                    kb.mm(o, lhsT=BHT[0:KR, j, :], rhs=UT[0:KR, j, :], start=True, stop=False)
                    kb.mm(o, lhsT=KHT[0:KR, j, :], rhs=VT[0:KR, j, :], start=False, stop=True)
                by = bankC()
                for j in range(4):
                    o = by[0:R, j * 64:(j + 1) * 64]
                    kb.mm(o, lhsT=Fq(RT)[:, q0 + j, :], rhs=sb_in[:, j, :], start=True, stop=False)
                    kb.mm(o, lhsT=MqK(ARB)[:, q0 + j, :], rhs=UT[0:KR, j, :], start=False, stop=False)
                    kb.mm(o, lhsT=MqK(ARK)[:, q0 + j, :], rhs=VT[0:KR, j, :], start=False, stop=True)
                kb.tt(S32[:], S32[:], bc(PC[:, q0:q0 + 4, None], [128, 4, 64]), ALU.mult)
                yield
                kb.tt(S32[:], S32[:], _view(bs_[:, 0:256], [4, 64]), ALU.add)
                kb.cp(YC[0:R], _view(by[0:R, 0:256], [4, 64]), eng="act")
                yield
                kb.cp(sb_out[:], S32[:], eng="act")
                Yv = YC[0:R]
                kb.reduce_sum(ST1[0:R], Yv)
                yield
                kb.act(SQY[0:R], Yv, AF.Square)
                kb.ts(STM[0:R], ST1[0:R], 1.0 / 64, None, ALU.mult)
                yield
                kb.reduce_sum(ST2[0:R], SQY[0:R])
                kb.tt(STV[0:R], STM[0:R], STM[0:R], ALU.mult)
                yield
                kb.stt(STV[0:R], ST2[0:R], 1.0 / 64, STV[0:R], ALU.mult, ALU.subtract)
                kb.tt(YC[0:R], Yv, bc(STM[0:R, :, None], [R, 4, 64]), ALU.subtract)
                yield
                kb.act(STV[0:R], STV[0:R], AF.Sqrt, bias=LN_EPS)
                yield
                kb.recip(STV[0:R], STV[0:R])
                yield
                kb.tt(YC[0:R], YC[0:R], bc(STV[0:R, :, None], [R, 4, 64]), ALU.mult)
                yield
                for h in range(2):
                    kb.ts(YNB[0:R, :, h * 64:(h + 1) * 64], YC[0:R], tokm[0:R, h:h + 1], None, ALU.mult)
                yield
                bf_ = bankC()
                CW = max(C, 8)
                for j in range(4):
                    kb.mm(bf_[:, j * CW:(j + 1) * CW], lhsT=YNB[0:KR, j, :], rhs=istk[0:KR, 0:CW])
                kb.cp(YF[:, :, gi * C:(gi + 1) * C], _view(bf_[:, 0:4 * CW], [4, CW])[:, :, 0:C], eng="act")
                yield
                if is_s or b == NBLK - 1:
                    bo_ = bankC()
                    for j in range(4):
                        kb.mm(bo_[0:64, j * 128:(j + 1) * 128], lhsT=S32[:, j, :], rhs=cf("ident"))
                    kb.cp(SO[0:64], _view(bo_[0:64, 0:512], [4, 128]), eng="dve")
                    dst_ = O["wkv_s"][l, gi] if is_s else O["wkv_p"][l]
                    kb.dma("sp", dst_.rearrange("h v k -> v h k"), SO[0:64].rearrange("p j (h k) -> p (j h) k", h=2))
                    yield
            kb.tt(p2a[:], YF[:], bc(vec("ln_w")[:, :, None], [128, 4, 64]), ALU.mult)
            yield
            kb.tt(p2a[:], p2a[:], bc(vec("ln_b")[:, :, None], [128, 4, 64]), ALU.add)
            yield
            kb.tt(p2a[:], p2a[:], t_bon[:], ALU.add)
            yield
            kb.tt(YAb[:], p2a[:], t_gg[:], ALU.mult)
            kb.dma("sp", YAB[:, 0:4, tok0:tok0 + n], YAb[:])
            yield

        nblocks = NBLK + 1
        if DBG.get("nblk") is not None:
            nblocks = DBG["nblk"]
        for step in range(nblocks + 3):
            gens = []
            if step < nblocks:
                gens.append(stage1a(step))
            if 0 <= step - 1 < nblocks:
                gens.append(stage1a2(step - 1))
            if 0 <= step - 2 < nblocks:
                gens.append(stage1b(step - 2))
            if 0 <= step - 3 < nblocks:
                gens.append(stage2(step - 3))
            if DBG.get("no_interleave", False):
                for g in gens[::-1]:
                    for _ in g:
                        pass
                continue
            alive = list(gens)
            while alive:
                for g in list(alive):
                    try:
                        next(g)
                    except StopIteration:
                        alive.remove(g)

    def pass_pool(l):
        ar.reset()
        WPB = ar.alloc([8, 512], BF16)
        WPL = ar.alloc([4, 128], BF16)
        kb.dma("pool", WPB, I["w_in"][l].rearrange("(k p) n -> p k n", p=128)[:, :, APJ:PT])
        kb.dma("pool", WPL, I["w_pool"][l].rearrange("g c d -> c g d"))
        UB = ar.alloc([8, 512], BF16)
        SQ = ar.alloc([8, 512], BF16)
        RS = ar.alloc([512], F32)
        TT = [ar.alloc([512], F32) for _ in range(2)]
        PBH = ar.alloc([4, 527], F32)
        SA = ar.alloc([4, 527], F32)
        SB = ar.alloc([4, 527], F32)
        DP = ar.alloc([4, 512], BF16)
        YBb = ar.alloc([4, 512], BF16)
        PROW = ar.alloc([512], F32)
        POUT = ar.alloc([512], F32)
        TMPH = ar.alloc([4, 120], F32)
        PS = vec("pool_scale")
        WIN_ = (2, 4, 8, 16)
        kb.memset(PBH[:], 0.0)

        def wsum(x, sa, sb_, L, nd):
            def sl(v, g0, g1, a, b_):
                return v[:, g0:g1, a:b_] if nd == 3 else v[:, g0:g1, :, a:b_]
            kb.tt(sl(sa, 0, 4, 1, L), sl(x, 0, 4, 1, L), sl(x, 0, 4, 0, L - 1), ALU.add)
            kb.tt(sl(sb_, 1, 4, 3, L), sl(sa, 1, 4, 3, L), sl(sa, 1, 4, 1, L - 2), ALU.add)
            kb.tt(sl(sa, 2, 4, 7, L), sl(sb_, 2, 4, 7, L), sl(sb_, 2, 4, 3, L - 4), ALU.add)
            kb.tt(sl(sb_, 3, 4, 15, L), sl(sa, 3, 4, 15, L), sl(sa, 3, 4, 7, L - 8), ALU.add)
            return [sa, sb_, sa, sb_]

        for (t0, n) in TILES[:4]:
            modnorm(t0, n, 1, UB, SQ, RS, TT)
            for g in range(4):
                bk = kb.bank()
                for k in range(8):
                    kb.mm(bk[:, 0:n], lhsT=WPB[:, k, g * 128:(g + 1) * 128], rhs=UB[:, k, 0:n], start=(k == 0), stop=(k == 7))
                kb.cp(PBH[:, g, 15:15 + n], bk[:, 0:n], eng="act")
            L = 15 + n
            fin = wsum(PBH, SA, SB, L, 3)
            for g in range(4):
                if t0 == 0:
                    kb.tt(fin[g][:, g, 15:30], fin[g][:, g, 15:30], cf("ratio")[:, g * 15:(g + 1) * 15], ALU.mult)
                kb.stt(DP[:, g, 0:n], fin[g][:, g, 15:L], 1.0 / WIN_[g], PBH[:, g, 15:L], ALU.mult, ALU.subtract)
            for g in range(4):
                bk = kb.bank()
                kb.mm(bk[:, 0:n], lhsT=WPL[:, g, :], rhs=DP[:, g, 0:n])
                kb.ts(YBb[:, g, 0:n], bk[:, 0:n], PS[:, g:g + 1], None, ALU.mult)
            kb.dma("sp", YAB[:, 4:8, t0:t0 + n], YBb[:, :, 0:n])
            kb.cp(TMPH[:, :, 0:15], PBH[:, :, n:n + 15], eng="dve")
            kb.cp(PBH[:, :, 0:15], TMPH[:, :, 0:15], eng="dve")
        bk = kb.bank()
        for g in range(4):
            kb.mm(bk[0:15, g * 128:(g + 1) * 128], lhsT=TMPH[:, g, 0:15], rhs=cf("ident"))
        kb.cp(POUT[0:15, :], bk[0:15, 0:512], eng="dve")
        kb.dma("sp", O["pool_p"][l], POUT[0:15, :])
        PBs = PBH[:, :, 0:304].rearrange("p g (s t) -> p g s t", t=19)
        SAs = SA[:, :, 0:304].rearrange("p g (s t) -> p g s t", t=19)
        SBs = SB[:, :, 0:304].rearrange("p g (s t) -> p g s t", t=19)
        sp_rows = I["spool"][l].rearrange("s i c -> (s i) c")
        for hh in range(2):
            kb.dma("sp", PROW[0:120, :], sp_rows[hh * 120:(hh + 1) * 120, :])
            bk = kb.bank()
            for g in range(4):
                kb.mm(bk[:, g * 120:(g + 1) * 120], lhsT=PROW[0:120, g * 128:(g + 1) * 128], rhs=cf("ident")[0:120, 0:120])
            kb.cp(PBs[:, :, hh * 8:(hh + 1) * 8, 0:15], bk[:, 0:480].rearrange("p (g s t) -> p g s t", g=4, t=15), eng="dve")
        modnorm(TP, 64, 1, UB, SQ, RS, TT)
        bk = kb.bank()
        for g in range(4):
            for k in range(8):
                kb.mm(bk[:, g * 64:(g + 1) * 64], lhsT=WPB[:, k, g * 128:(g + 1) * 128], rhs=UB[:, k, 0:64], start=(k == 0), stop=(k == 7))
        kb.cp(PBs[:, :, :, 15:19], bk[:, 0:256].rearrange("p (g s t) -> p g s t", g=4, t=4), eng="act")
        fin = wsum(PBs, SAs, SBs, 19, 4)
        fv = [SAs, SBs, SAs, SBs]
        for g in range(4):
            kb.stt(_view(DP[:, g, 0:64], [16, 4]), fv[g][:, g, :, 15:19], 1.0 / WIN_[g], PBs[:, g, :, 15:19], ALU.mult, ALU.subtract)
        bk = kb.bank()
        for g in range(4):
            kb.mm(bk[:, g * 64:(g + 1) * 64], lhsT=WPL[:, g, :], rhs=DP[:, g, 0:64])
        for g in range(4):
            kb.ts(YBb[:, g, 0:64], bk[:, g * 64:(g + 1) * 64], PS[:, g:g + 1], None, ALU.mult)
        kb.dma("sp", YAB[:, 4:8, TP:T], YBb[:, :, 0:64])
        po_rows = O["pool_s"][l].rearrange("s i c -> (s i) c")
        for hh in range(2):
            kb.cp(TMPH[:].rearrange("p g (s t) -> p g s t", t=15), PBs[:, :, hh * 8:(hh + 1) * 8, 4:19], eng="dve")
            bk = kb.bank()
            for g in range(4):
                kb.mm(bk[0:120, g * 128:(g + 1) * 128], lhsT=TMPH[:, g, :], rhs=cf("ident"))
            kb.cp(POUT[0:120, :], bk[0:120, 0:512], eng="dve")
            kb.dma("sp", po_rows[hh * 120:(hh + 1) * 120, :], POUT[0:120, :])

    def pass_b(l):
        ar.reset()
        WGT = ar.alloc([8, 2048], BF16)
        WBA = ar.alloc([4, 1024], BF16)
        WBB = ar.alloc([4, 1024], BF16)
        WOT = ar.alloc([8, 1024], BF16)
        kb.dma("pool", WGT, I["w_gate"][l].rearrange("(k p) n -> p k n", p=128))
        kb.dma("pool", WBA, I["w_br_a"][l].rearrange("(j p) n -> p j n", p=128))
        kb.dma("pool", WBB, I["w_br_b"][l].rearrange("(j p) n -> p j n", p=128))
        kb.dma("pool", WOT, I["w_out"][l].rearrange("(k p) n -> p k n", p=128))
        YT = ar.alloc([8, 512], BF16)
        UB = ar.alloc([8, 512], BF16)
        SQ = ar.alloc([8, 512], BF16)
        RS = ar.alloc([512], F32)
        TT = [ar.alloc([512], F32) for _ in range(2)]
        GA = ar.alloc([512], F32); GB = ar.alloc([512], F32); T1 = ar.alloc([512], F32); T2 = ar.alloc([512], F32)
        MG = ar.alloc([8, 512], BF16)
        BGv = vec("b_gate")
        for (t0, n) in TILES:
            kb.dma("sp", YT[:, :, 0:n], YAB[:, :, t0:t0 + n])
            modnorm(t0, n, 1, UB, SQ, RS, TT)
            for m in range(8):
                ba = kb.bank(); bb = kb.bank(); bc_ = kb.bank(); bd = kb.bank()
                for k in range(8):
                    kb.mm(ba[:, 0:n], lhsT=WGT[:, k, m * 128:(m + 1) * 128], rhs=UB[:, k, 0:n], start=(k == 0), stop=(k == 7))
                for k in range(8):
                    kb.mm(bb[:, 0:n], lhsT=WGT[:, k, 1024 + m * 128:1024 + (m + 1) * 128], rhs=UB[:, k, 0:n], start=(k == 0), stop=(k == 7))
                for j in range(4):
                    kb.mm(bc_[:, 0:n], lhsT=WBA[:, j, m * 128:(m + 1) * 128], rhs=YT[:, j, 0:n], start=(j == 0), stop=(j == 3))
                for j in range(4):
                    kb.mm(bd[:, 0:n], lhsT=WBB[:, j, m * 128:(m + 1) * 128], rhs=YT[:, 4 + j, 0:n], start=(j == 0), stop=(j == 3))
                kb.act(GA[:, 0:n], ba[:, 0:n], AF.Sigmoid, bias=BGv[:, m:m + 1], scale=1.0)
                kb.act(GB[:, 0:n], bb[:, 0:n], AF.Sigmoid, bias=BGv[:, 8 + m:9 + m], scale=1.0)
                kb.tt(T1[:, 0:n], bc_[:, 0:n], GA[:, 0:n], ALU.mult)
                kb.tt(T2[:, 0:n], bd[:, 0:n], GB[:, 0:n], ALU.mult)
                kb.tt(MG[:, m, 0:n], T1[:, 0:n], T2[:, 0:n], ALU.add)
            for m2 in range(8):
                bo = kb.bank()
                for m in range(8):
                    kb.mm(bo[:, 0:n], lhsT=WOT[:, m, m2 * 128:(m2 + 1) * 128], rhs=MG[:, m, 0:n], start=(m == 0), stop=(m == 7))
                resid(m2, t0, n, bo, 1)

    def mixer(l, only_a=False):
        ar.log = []
        pass_a(l)
        if only_a:
            DBG["passA_log"] = list(ar.log)
            return
        DBG["marks"].append(("pool%d" % l, sum(1 for o in kb.S.ops if o.eng == "pe"), len(kb.S.ops)))
        pass_pool(l)
        DBG["marks"].append(("passB%d" % l, sum(1 for o in kb.S.ops if o.eng == "pe"), len(kb.S.ops)))
        pass_b(l)

    return mixer


def _shard_inputs(inputs):
    maps = []
    shared = {n: np.ascontiguousarray(np.asarray(inputs[n], dtype=np.float32)) for n in WNAMES}
    xp = np.asarray(inputs["x_prompt"], dtype=np.float32)
    xs = np.asarray(inputs["x_sample"], dtype=np.float32)
    cp_ = np.asarray(inputs["c_prompt"], dtype=np.float32)
    cs = np.asarray(inputs["c_sample"], dtype=np.float32)
    swkv = np.asarray(inputs["state_wkv"], dtype=np.float32)
    ssh = np.asarray(inputs["state_shift"], dtype=np.float32)
    spl = np.asarray(inputs["state_pool"], dtype=np.float32)
    for i in range(NCORES):
        sl = slice(NSEQ * i, NSEQ * (i + 1))
        m = dict(shared)
        m["xin"] = np.ascontiguousarray(np.concatenate([xp[i], xs[sl].reshape(TS, D)], axis=0))
        m["cin"] = np.ascontiguousarray(np.concatenate([cp_[i:i + 1], cs[sl]], axis=0))
        m["swkv"] = np.ascontiguousarray(swkv[:, sl])
        m["sshift"] = np.ascontiguousarray(ssh[:, sl, 0, :])
        m["spool"] = np.ascontiguousarray(spl[:, sl])
        m["consts"] = CONSTS_NP
        maps.append(m)
    return maps


_NC_CACHE = {}
DBG = {}


def kernel(**inputs):
    if "nc" not in _NC_CACHE:
        _NC_CACHE["nc"] = build_program()
    nc = _NC_CACHE["nc"]
    maps = _shard_inputs(inputs)
    res = run_bass_kernel_spmd(nc, maps, core_ids=list(range(NCORES)))
    R = res.results
    y_p = np.stack([R[i]["y"][:TP] for i in range(NCORES)], axis=0)
    y_s = np.concatenate([R[i]["y"][TP:].reshape(NSEQ, DEC, D) for i in range(NCORES)], axis=0)
    wkv_p = np.stack([R[i]["wkv_p"] for i in range(NCORES)], axis=1)
    shift_p = np.stack([R[i]["shift_p"] for i in range(NCORES)], axis=1)[:, :, None, :]
    pool_p = np.stack([R[i]["pool_p"] for i in range(NCORES)], axis=1)
    wkv_s = np.concatenate([R[i]["wkv_s"] for i in range(NCORES)], axis=1)
    shift_s = np.concatenate([R[i]["shift_s"] for i in range(NCORES)], axis=1)[:, :, None, :]
    pool_s = np.concatenate([R[i]["pool_s"] for i in range(NCORES)], axis=1)
    f = lambda a: np.ascontiguousarray(a, dtype=np.float32)
    return (f(y_p), f(y_s), f(wkv_p), f(shift_p), f(pool_p), f(wkv_s), f(shift_s), f(pool_s))
```

```python
import numpy as np
from contextlib import ExitStack
import concourse.bass as bass
import concourse.mybir as mybir
from concourse.bass_utils import run_bass_kernel_spmd

F32 = mybir.dt.float32
BF16 = mybir.dt.bfloat16
ALU = mybir.AluOpType
AF = mybir.ActivationFunctionType
AX = mybir.AxisListType


class _Op:
    __slots__ = ("idx", "eng", "fn", "deps", "dma", "inc", "incval", "sem", "waits", "ring_prev", "gidx")

    def __init__(self, idx, eng, fn, deps, dma):
        self.idx, self.eng, self.fn, self.deps, self.dma = idx, eng, fn, deps, dma
        self.inc = False
        self.incval = 0
        self.sem = None
        self.waits = []
        self.ring_prev = None


def _region(ap):
    t = ap.tensor
    name = ap.name
    space = str(ap.space)
    pat = ap.ap
    off = int(ap.offset)
    es = mybir.dt.size(ap.dtype)
    if space == "DRAM":
        lo = off
        hi = off + 1
        for st, cnt in pat:
            hi += abs(int(st)) * (int(cnt) - 1)
        return (name, 0, 1, lo * es, hi * es)
    if "PSUM" in space.upper():
        return (name, 0, 128, 0, 1 << 30)
    shp = list(t.shape)
    pstep = 1
    for s in shp[1:]:
        pstep *= int(s)
    p0 = off // pstep
    f0 = off % pstep
    st0, cnt0 = pat[0]
    if int(st0) == pstep or int(cnt0) == 1:
        npart = int(cnt0)
        rest = pat[1:]
    else:
        npart = 1
        rest = pat
    hi = f0 + 1
    for st, cnt in rest:
        hi += abs(int(st)) * (int(cnt) - 1)
    return (name, p0, p0 + npart, f0 * es, hi * es)


class Sched:
    COMPUTE = ("pe", "act", "dve", "pool")
    RING = 8

    def __init__(self, nc):
        self.nc = nc
        self.ops = []
        self.rec = {}
        self.nd = {"sp": 0, "act": 0, "pool": 0}

    def op(self, eng, fn, reads=(), writes=(), dma=False):
        idx = len(self.ops)
        deps = set()
        rr = [_region(a) for a in reads]
        ww = [_region(a) for a in writes]
        for (name, p0, p1, f0, f1) in rr:
            for r in self.rec.get(name, ()):
                if r[5] and r[0] < p1 and p0 < r[1] and r[2] < f1 and f0 < r[3]:
                    deps.add((r[4], "raw"))
        for (name, p0, p1, f0, f1) in ww:
            for r in self.rec.get(name, ()):
                if r[0] < p1 and p0 < r[1] and r[2] < f1 and f0 < r[3]:
                    deps.add((r[4], "waw" if r[5] else "war"))
        o = _Op(idx, eng, fn, deps, dma)
        self.ops.append(o)
        for (name, p0, p1, f0, f1) in ww:
            lst = self.rec.setdefault(name, [])
            lst[:] = [r for r in lst if not (p0 <= r[0] and r[1] <= p1 and f0 <= r[2] and r[3] <= f1)]
            lst.append([p0, p1, f0, f1, idx, True])
        for (name, p0, p1, f0, f1) in rr:
            lst = self.rec.setdefault(name, [])
            lst[:] = [r for r in lst if not ((not r[5]) and self.ops[r[4]].eng == eng
                                             and (not self.ops[r[4]].dma) and (not dma)
                                             and p0 <= r[0] and r[1] <= p1 and f0 <= r[2] and r[3] <= f1)]
            lst.append([p0, p1, f0, f1, idx, False])
        return o

    NSEM = 12
    CH = 512

    def lower(self, stack):
        nc = self.nc
        ops = self.ops
        for o in ops:
            need = []
            best = {}
            for (d, kind) in o.deps:
                p = ops[d]
                if p.dma:
                    need.append(d)
                elif o.dma or p.eng != o.eng or o.eng != "pe":
                    if p.eng not in best or best[p.eng] < d:
                        best[p.eng] = d
            need.extend(best.values())
            o.deps = need
            for d in need:
                ops[d].inc = True
        self.csem = {e: [stack.enter_context(nc.semaphore("s_%s%d" % (e, i))) for i in range(self.NSEM)]
                     for e in self.COMPUTE}
        self.rings = {q: [stack.enter_context(nc.semaphore("r_%s%d" % (q, i))) for i in range(self.RING)]
                      for q in ("sp", "act", "pool")}
        cnt = {e: 0 for e in self.COMPUTE}
        dk = {"sp": 0, "act": 0, "pool": 0}
        dma_final = {}
        for o in ops:
            if o.dma:
                k = dk[o.eng]
                dk[o.eng] += 1
                o.sem = self.rings[o.eng][k % self.RING]
                o.incval = 16 * (k // self.RING + 1)
                o.ring_prev = (o.sem, 16 * (k // self.RING)) if k >= self.RING else None
                dma_final[(o.eng, k % self.RING)] = (o.sem, o.incval)
                o.gidx = None
            elif o.inc:
                g = cnt[o.eng]
                cnt[o.eng] += 1
                epoch = g // self.CH
                o.sem = self.csem[o.eng][epoch % self.NSEM]
                o.incval = (epoch // self.NSEM) * self.CH + (g % self.CH) + 1
                o.gidx = g
        waited_c = {e: {} for e in ("pe", "act", "dve", "pool", "sp")}
        waited_d = {e: {} for e in ("pe", "act", "dve", "pool", "sp")}
        for o in ops:
            wl = []
            wd = waited_d[o.eng]
            wc = waited_c[o.eng]
            if o.dma and o.ring_prev is not None:
                sem, val = o.ring_prev
                if wd.get(id(sem), 0) < val:
                    wd[id(sem)] = val
                    wl.append((sem, val))
            for d in o.deps:
                p = ops[d]
                if p.dma:
                    if wd.get(id(p.sem), 0) < p.incval:
                        wd[id(p.sem)] = p.incval
                        wl.append((p.sem, p.incval))
                else:
                    if wc.get(p.eng, -1) < p.gidx:
                        wc[p.eng] = p.gidx
                        wl.append((p.sem, p.incval))
            o.waits = wl
        self.final_waits = list(dma_final.values())
        per = {e: [] for e in ("pe", "act", "dve", "pool", "sp")}
        for o in ops:
            per[o.eng].append(o)

        def run(engobj, lst, final=False):
            for o in lst:
                for (sem, val) in o.waits:
                    engobj.wait_ge(sem, val)
                ins = o.fn(engobj)
                if o.dma:
                    ins.then_inc(o.sem, 16)
                elif o.inc:
                    ins.then_inc(o.sem, 1)
            if final:
                for (sem, val) in self.final_waits:
                    engobj.wait_ge(sem, val)

        with nc.Block() as block:
            @block.tensor
            def _(e):
                run(e, per["pe"])

            @block.scalar
            def _(e):
                run(e, per["act"])

            @block.vector
            def _(e):
                run(e, per["dve"])

            @block.gpsimd
            def _(e):
                run(e, per["pool"])

            @block.sync
            def _(e):
                run(e, per["sp"], final=True)


NCORES = 8
D = 1024
KC = 8
TP = 2048
NSEQ = 16
DEC = 4
TS = NSEQ * DEC
T = TP + TS
APJ = 1824
PT = 2336
DFF = 2816
NFC = DFF // 128
LN_EPS = 64e-5
NORM_EPS = 1e-6
DECAY_K = float(np.exp(-0.5))

WNAMES = ["norm_g", "w_mod", "b_mod", "w_ffn_in", "w_ffn_out", "w_in", "mu_shift", "w0", "w2", "a0", "a2", "g2",
          "k_k", "k_a", "r_k", "ln_x_w", "ln_x_b", "w_pool", "pool_scale", "w_br_a", "w_br_b", "w_gate", "b_gate",
          "w_out", "final_g"]
WSHAPES = {
    "norm_g": [2, 3, D], "w_mod": [2, D, 9 * D], "b_mod": [2, 9 * D], "w_ffn_in": [2, 2, D, 2 * DFF],
    "w_ffn_out": [2, 2, DFF, D], "w_in": [2, D, PT], "mu_shift": [2, APJ], "w0": [2, 512], "w2": [2, 64, 512],
    "a0": [2, 512], "a2": [2, 64, 512], "g2": [2, 160, 512], "k_k": [2, 512], "k_a": [2, 512], "r_k": [2, 8, 64],
    "ln_x_w": [2, 512], "ln_x_b": [2, 512], "w_pool": [2, 4, 128, 128], "pool_scale": [2, 512],
    "w_br_a": [2, 512, D], "w_br_b": [2, 512, D], "w_gate": [2, D, 2 * D], "b_gate": [2, 2 * D], "w_out": [2, D, D],
    "final_g": [D],
}


def _make_consts():
    cols = {}
    parts = []
    pos = [0]

    def add(name, arr):
        a = np.zeros((128, arr.shape[1]), np.float32)
        a[:arr.shape[0]] = arr
        cols[name] = (pos[0], arr.shape[1])
        pos[0] += arr.shape[1]
        parts.append(a)

    p = np.arange(128)
    add("ident", np.eye(128, dtype=np.float32))
    add("ones", np.ones((128, 128), np.float32))
    same = (p[:, None] // 64) == (p[None, :] // 64)
    add("bones", same.astype(np.float32))
    s = p[:, None] % 64
    t = p[None, :] % 64
    add("msu64", (same & (s < t)).astype(np.float32))
    add("msuT64", (same & (s > t)).astype(np.float32))
    add("mu64", (same & (s <= t)).astype(np.float32))
    add("istack64", (p[:, None] % 64 == np.arange(64)[None, :]).astype(np.float32))
    add("tokmask64", (p[:, None] // 64 == np.arange(2)[None, :]).astype(np.float32))
    q = np.arange(8)
    same4 = (q[:, None] // 4) == (q[None, :] // 4)
    s4 = q[:, None] % 4
    t4 = q[None, :] % 4
    add("msu4", (same4 & (s4 < t4)).astype(np.float32))
    add("msuT4", (same4 & (s4 > t4)).astype(np.float32))
    add("mu4", (same4 & (s4 <= t4)).astype(np.float32))
    add("istack4", (q[:, None] % 4 == np.arange(8)[None, :]).astype(np.float32))
    add("tokmask4", (q[:, None] // 4 == np.arange(2)[None, :]).astype(np.float32))
    tt = np.arange(128)
    add("start64", np.broadcast_to((tt % 64 == 0).astype(np.float32)[None, :], (128, 128)).copy())
    add("nstart64", np.broadcast_to((tt % 64 != 0).astype(np.float32)[None, :], (128, 128)).copy())
    add("start4", np.broadcast_to((tt[:64] % 4 == 0).astype(np.float32)[None, :], (128, 64)).copy())
    add("nstart4", np.broadcast_to((tt[:64] % 4 != 0).astype(np.float32)[None, :], (128, 64)).copy())
    ratio = np.zeros((4, 15), np.float32)
    for g, w in enumerate((2, 4, 8, 16)):
        for i in range(15):
            ratio[g, i] = w / min(w, i + 1)
    add("ratio", np.broadcast_to(ratio.reshape(1, 60), (128, 60)).copy())
    return np.concatenate(parts, axis=1), cols


DBG = {}
CONSTS_NP, CCOLS = _make_consts()
NCC = CONSTS_NP.shape[1]

VR = {}
_r = 0
for _n, _k in (("norm_g", 24), ("mu", 15), ("w0", 4), ("a0", 4), ("k_k", 4), ("k_a", 4), ("r_k", 4), ("ln_w", 4),
               ("ln_b", 4), ("pool_scale", 4), ("b_gate", 16), ("final_g", 8)):
    VR[_n] = (_r, _k)
    _r += _k
NVR = _r


def _prod(s):
    r = 1
    for v in s:
        r *= int(v)
    return r


def _view(ap2, shape):
    if len(shape) == 1:
        return ap2
    names = "abcdef"[:len(shape)]
    kw = {names[i]: int(shape[i]) for i in range(len(shape))}
    return ap2.rearrange("p (%s) -> p %s" % (" ".join(names), " ".join(names)), **kw)


class Arena:
    def __init__(self, base_bf16, nelem):
        self.base = base_bf16
        self.n = nelem
        self.off = 0
        self.peak = 0
        self.log = []

    def reset(self, off=0):
        self.off = off

    def alloc(self, shape, dtype):
        n = _prod(shape)
        nb = n * 2 if dtype == F32 else n
        off = (self.off + 15) // 16 * 16
        assert off + nb <= self.n, "arena overflow: need %d have %d" % (off + nb, self.n)
        v = self.base[:, off:off + nb]
        if dtype == F32:
            v = v.bitcast(F32)
        self.off = off + nb
        self.peak = max(self.peak, self.off)
        self.log.append((off, tuple(shape), "f32" if dtype == F32 else "bf16"))
        return _view(v, shape)


class KB:
    def __init__(self, nc, S, banks):
        self.nc, self.S, self.banks = nc, S, banks
        self.bi = 0
        self.flip = 0
        self.sub = {}

    def bank(self):
        b = self.banks[self.bi % len(self.banks)]
        self.bi += 1
        return b

    def bank_of(self, ids):
        c = self.sub.get(ids, 0)
        self.sub[ids] = c + 1
        return self.banks[ids[c % len(ids)]]

    def mm(self, out, lhsT, rhs, start=True, stop=True):
        self.S.op("pe", lambda e: e.matmul(out, lhsT=lhsT, rhs=rhs, start=start, stop=stop),
                  reads=[lhsT, rhs], writes=[out])

    def tr(self, out, in_, ident):
        self.S.op("pe", lambda e: e.transpose(out, in_, ident), reads=[in_, ident], writes=[out])

    def dma(self, q, out, in_):
        self.S.op(q, lambda e: e.dma_start(out=out, in_=in_), reads=[in_], writes=[out], dma=True)

    def tt(self, out, in0, in1, op, eng="dve"):
        self.S.op(eng, lambda e: e.tensor_tensor(out=out, in0=in0, in1=in1, op=op), reads=[in0, in1], writes=[out])

    def ts(self, out, in0, s1, s2, op0, op1=None, eng="dve"):
        rd = [in0] + [s for s in (s1, s2) if not isinstance(s, (int, float)) and s is not None]
        if op1 is None:
            self.S.op(eng, lambda e: e.tensor_scalar(out=out, in0=in0, scalar1=s1, scalar2=None, op0=op0),
                      reads=rd, writes=[out])
        else:
            self.S.op(eng, lambda e: e.tensor_scalar(out=out, in0=in0, scalar1=s1, scalar2=s2, op0=op0, op1=op1),
                      reads=rd, writes=[out])

    def stt(self, out, in0, scalar, in1, op0, op1, eng="dve"):
        rd = [in0, in1] + ([] if isinstance(scalar, (int, float)) else [scalar])
        self.S.op(eng, lambda e: e.scalar_tensor_tensor(out=out, in0=in0, scalar=scalar, in1=in1, op0=op0, op1=op1),
                  reads=rd, writes=[out])

    def act(self, out, in_, func, bias=None, scale=None):
        rd = [in_] + [s for s in (bias, scale) if s is not None and not isinstance(s, (int, float))]
        kw = {}
        if bias is not None:
            kw["bias"] = bias
        if scale is not None:
            kw["scale"] = scale
        self.S.op("act", lambda e: e.activation(out=out, in_=in_, func=func, **kw), reads=rd, writes=[out])

    def cp(self, out, in_, eng=None):
        if eng is None:
            self.flip ^= 1
            eng = "act" if self.flip else "dve"
        if eng == "act":
            self.S.op("act", lambda e: e.activation(out=out, in_=in_, func=AF.Copy), reads=[in_], writes=[out])
        else:
            self.S.op(eng, lambda e: e.tensor_copy(out=out, in_=in_), reads=[in_], writes=[out])

    def recip(self, out, in_):
        self.S.op("dve", lambda e: e.reciprocal(out=out, in_=in_), reads=[in_], writes=[out])

    def memset(self, out, val, eng="dve"):
        self.S.op(eng, lambda e: e.memset(out, val), writes=[out])

    def reduce_sum(self, out, in_):
        self.S.op("dve", lambda e: e.tensor_reduce(out=out, in_=in_, axis=AX.X, op=ALU.add), reads=[in_], writes=[out])

    def scan(self, out, d0, d1, init):
        self.S.op("dve", lambda e: e.tensor_tensor_scan(out=out, data0=d0, data1=d1, initial=init, op0=ALU.mult,
                                                        op1=ALU.add), reads=[d0, d1], writes=[out])


def build_program(stop=None, dbg=False, nlayers=2):
    nc = bass.Bass("TRN2", target_bir_lowering=False)
    I = {}

    def din(name, shape):
        I[name] = nc.dram_tensor(name, list(shape), F32, kind="ExternalInput").ap()

    din("xin", [T, D]); din("cin", [17, D]); din("swkv", [2, NSEQ, 8, 64, 64]); din("sshift", [2, NSEQ, APJ])
    din("spool", [2, NSEQ, 15, 512]); din("consts", [128, NCC])
    for n in WNAMES:
        din(n, WSHAPES[n])
    O = {}

    def dout(name, shape):
        O[name] = nc.dram_tensor(name, list(shape), F32, kind="ExternalOutput").ap()

    dout("y", [T, D]); dout("wkv_p", [2, 8, 64, 64]); dout("shift_p", [2, APJ]); dout("pool_p", [2, 15, 512])
    dout("wkv_s", [2, NSEQ, 8, 64, 64]); dout("shift_s", [2, NSEQ, APJ]); dout("pool_s", [2, NSEQ, 15, 512])
    if dbg:
        dout("dbgX", [128, KC, T]); dout("dbgA", [128, 8, T])
    YAB = nc.dram_tensor("yab_scratch", [128, 8, T], BF16, kind="Internal").ap()

    ARN = 61696
    with ExitStack() as st:
        def sb(name, shape, dt):
            return st.enter_context(nc.sbuf_tensor(name, list(shape), dt))

        X = sb("X", [128, KC, T], F32)
        CF = sb("CF", [128, NCC], F32)
        CB = sb("CB", [128, NCC], BF16)
        ARt = sb("AR", [128, ARN], BF16)
        MOD = sb("MOD", [128, 72, 17], F32)
        VEC = sb("VEC", [128, NVR], F32)
        BM = sb("BM", [128, 72], F32)
        SCT = sb("SCT", [128, 8, 17], F32)
        GSp = sb("GSp", [128, 3, 8], F32); SHp = sb("SHp", [128, 3, 8], F32); COp = sb("COp", [128, 3, 8], F32)
        GSs = sb("GSs", [128, 3, 8, 16], F32); SHs = sb("SHs", [128, 3, 8, 16], F32); COs = sb("COs", [128, 3, 8, 16], F32)
        OMKA = sb("OMKA", [128, 4], F32)
        TMS = sb("TMS", [128, 64], F32)
        banks = [st.enter_context(nc.psum_tensor("ps%d" % i, [128, 512], F32)) for i in range(8)]
        S = Sched(nc)
        kb = KB(nc, S, banks)
        ar = Arena(ARt[:], ARN)

        def cf(name):
            c0, n = CCOLS[name]
            return CF[:, c0:c0 + n]

        def cb(name):
            c0, n = CCOLS[name]
            return CB[:, c0:c0 + n]

        def vec(name):
            r0, k = VR[name]
            return VEC[:, r0:r0 + k]

        kb.dma("sp", CF[:], I["consts"])
        kb.cp(CB[:], CF[:], eng="dve")

        ar.reset()
        XT = [ar.alloc([D], F32) for _ in range(2)]
        CROW = ar.alloc([D], F32)
        for i in range(17):
            n = 128 if i < 16 else 64
            xt = XT[i % 2]
            kb.dma("sp", xt[0:n, :], I["xin"][i * 128:i * 128 + n, :])
            for half in range(2):
                bk = kb.bank()
                for cc in range(4):
                    c = half * 4 + cc
                    kb.tr(bk[:, cc * 128:cc * 128 + n], xt[0:n, c * 128:(c + 1) * 128], cf("ident")[0:n, 0:n])
                kb.cp(X[:, half * 4:half * 4 + 4, i * 128:i * 128 + n], _view(bk[:, 0:512], [4, 128])[:, :, 0:n])
        kb.dma("sp", CROW[0:17, :], I["cin"])
        kb.act(CROW[0:17, :], CROW[0:17, :], AF.Silu)
        bk = kb.bank()
        for c in range(8):
            kb.mm(bk[:, c * 17:(c + 1) * 17], lhsT=CROW[0:17, c * 128:(c + 1) * 128], rhs=cf("ident")[0:17, 0:17])
        kb.cp(SCT[:], _view(bk[:, 0:136], [8, 17]), eng="dve")

        def load_layer_vectors(l):
            ar.reset()
            ROWS = ar.alloc([128], F32)
            BMR = ar.alloc([128], F32)
            WM = [ar.alloc([8, 512], F32) for _ in range(2)]
            kb.memset(ROWS[:], 0.0)

            def rows(name, src):
                r0, k = VR[name]
                kb.dma("sp", ROWS[r0:r0 + k, :], src)

            rows("norm_g", I["norm_g"][l].rearrange("j (c p) -> (j c) p", p=128))
            r0, _ = VR["mu"]
            kb.dma("sp", ROWS[r0:r0 + 14, :], I["mu_shift"][l, 0:1792].rearrange("(c p) -> c p", p=128))
            kb.dma("sp", ROWS[r0 + 14:r0 + 15, 0:32], I["mu_shift"][l:l + 1, 1792:1824])
            for nm, src in (("w0", "w0"), ("a0", "a0"), ("k_k", "k_k"), ("k_a", "k_a"), ("ln_w", "ln_x_w"),
                            ("ln_b", "ln_x_b"), ("pool_scale", "pool_scale")):
                rows(nm, I[src][l].rearrange("(c p) -> c p", p=128))
            rows("r_k", I["r_k"][l].rearrange("(c h) k -> c (h k)", h=2))
            rows("b_gate", I["b_gate"][l].rearrange("(c p) -> c p", p=128))
            rows("final_g", I["final_g"].rearrange("(c p) -> c p", p=128))
            bk = kb.bank()
            kb.mm(bk[:, 0:NVR], lhsT=ROWS[0:NVR, :], rhs=cf("ident")[0:NVR, 0:NVR])
            kb.cp(VEC[:], bk[:, 0:NVR], eng="dve")
            kb.dma("sp", BMR[0:72, :], I["b_mod"][l].rearrange("(c p) -> c p", p=128))
            bk = kb.bank()
            kb.mm(bk[:, 0:72], lhsT=BMR[0:72, :], rhs=cf("ident")[0:72, 0:72])
            kb.cp(BM[:], bk[:, 0:72], eng="dve")
            kb.ts(OMKA[:], vec("k_a"), -1.0, 1.0, ALU.mult, ALU.add)
            wm = I["w_mod"][l].rearrange("(k p) n -> p k n", p=128)
            kb.dma("sp", WM[0][:], wm[:, :, 0:512])
            bk = None
            for blk in range(18):
                if blk + 1 < 18:
                    kb.dma("sp", WM[(blk + 1) % 2][:], wm[:, :, (blk + 1) * 512:(blk + 2) * 512])
                w = WM[blk % 2]
                for oc in range(4):
                    mc = blk * 4 + oc
                    if mc % 24 == 0:
                        bk = kb.bank()
                    o = bk[:, (mc % 24) * 17:(mc % 24) * 17 + 17]
                    for k in range(8):
                        kb.mm(o, lhsT=w[:, k, oc * 128:(oc + 1) * 128], rhs=SCT[:, k, :], start=(k == 0), stop=(k == 7))
                    if mc % 24 == 23:
                        g = mc // 24
                        kb.tt(MOD[:, g * 24:(g + 1) * 24, :], _view(bk[:, 0:408], [24, 17]),
                              BM[:, g * 24:(g + 1) * 24, None].to_broadcast([128, 24, 17]), ALU.add)
            MODv = MOD[:].rearrange("p (j k c) s -> p j k c s", j=3, k=3)
            NG = _view(vec("norm_g"), [3, 8])
            kb.ts(GSp[:], MODv[:, :, 1, :, 0], 1.0, None, ALU.add)
            kb.tt(GSp[:], GSp[:], NG, ALU.mult)
            kb.cp(SHp[:], MODv[:, :, 0, :, 0], eng="dve")
            kb.cp(COp[:], MODv[:, :, 2, :, 0], eng="dve")
            kb.ts(COp[:, 0, :], COp[:, 0, :], 0.5, None, ALU.mult)
            kb.ts(COp[:, 2, :], COp[:, 2, :], 0.5, None, ALU.mult)
            for j in range(3):
                kb.ts(GSs[:, j], MODv[:, j, 1, :, 1:17], 1.0, None, ALU.add)
                kb.tt(GSs[:, j], GSs[:, j], NG[:, j, :, None].to_broadcast([128, 8, 16]), ALU.mult)
                kb.cp(SHs[:, j], MODv[:, j, 0, :, 1:17], eng="dve")
                kb.ts(COs[:, j], MODv[:, j, 2, :, 1:17], (1.0 if j == 1 else 0.5), None, ALU.mult)

        def modnorm(tok0, n, j, U, SQ, RS, TT, bank=None):
            is_s = tok0 >= TP
            kb.act(SQ[:, :, 0:n], X[:, :, tok0:tok0 + n], AF.Square)
            bk = kb.bank() if bank is None else bank()
            for c in range(8):
                kb.mm(bk[:, 0:n], lhsT=cb("ones"), rhs=SQ[:, c, 0:n], start=(c == 0), stop=(c == 7))
            kb.act(RS[:, 0:n], bk[:, 0:n], AF.Sqrt, bias=NORM_EPS, scale=1.0 / D)
            kb.recip(RS[:, 0:n], RS[:, 0:n])
            for c in range(8):
                t = TT[c % 2]
                if not is_s:
                    kb.stt(t[:, 0:n], X[:, c, tok0:tok0 + n], GSp[:, j, c:c + 1], RS[:, 0:n], ALU.mult, ALU.mult)
                    kb.act(U[:, c, 0:n], t[:, 0:n], AF.Identity, bias=SHp[:, j, c:c + 1], scale=1.0)
                else:
                    tv = _view(t[:, 0:n], [16, 4])
                    kb.tt(tv, _view(X[:, c, tok0:tok0 + n], [16, 4]), GSs[:, j, c, :, None].to_broadcast([128, 16, 4]), ALU.mult)
                    kb.tt(t[:, 0:n], t[:, 0:n], RS[:, 0:n], ALU.mult)
                    kb.tt(_view(U[:, c, 0:n], [16, 4]), tv, SHs[:, j, c, :, None].to_broadcast([128, 16, 4]), ALU.add)

        def resid(m, tok0, n, bo, j):
            if tok0 < TP:
                kb.stt(X[:, m, tok0:tok0 + n], bo[:, 0:n], COp[:, j, m:m + 1], X[:, m, tok0:tok0 + n], ALU.mult, ALU.add)
            else:
                kb.tt(_view(TMS[:, 0:n], [16, 4]), _view(bo[:, 0:n], [16, 4]),
                      COs[:, j, m, :, None].to_broadcast([128, 16, 4]), ALU.mult)
                kb.tt(X[:, m, tok0:tok0 + n], X[:, m, tok0:tok0 + n], TMS[:, 0:n], ALU.add)

        TILES = [(0, 512), (512, 512), (1024, 512), (1536, 512), (2048, 64)]

        def ffn(l, f):
            j = 0 if f == 0 else 2
            ar.reset()
            U = ar.alloc([8, T], BF16)
            WG = [ar.alloc([8, 512], BF16) for _ in range(2)]
            WU = [ar.alloc([8, 512], BF16) for _ in range(2)]
            WO = [ar.alloc([4, 1024], BF16) for _ in range(2)]
            H = [ar.alloc([4, 512], BF16) for _ in range(2)]
            SG = [ar.alloc([512], BF16) for _ in range(2)]
            SQ = ar.alloc([8, 512], BF16)
            RS = ar.alloc([512], F32)
            TT = [ar.alloc([512], F32) for _ in range(2)]
            win = I["w_ffn_in"][l, f].rearrange("(k p) n -> p k n", p=128)
            wout = I["w_ffn_out"][l, f].rearrange("(j p) n -> p j n", p=128)
            groups = [(0, 4), (4, 4), (8, 4), (12, 4), (16, 4), (20, 2)]

            def load(gi):
                c0, ng = groups[gi]
                b = gi % 2
                if gi == 0:
                    for h2 in range(0, ng, 2):
                        kb.dma("pool", WG[b][:, :, h2 * 128:(h2 + 2) * 128], win[:, :, (c0 + h2) * 128:(c0 + h2 + 2) * 128])
                        kb.dma("pool", WU[b][:, :, h2 * 128:(h2 + 2) * 128], win[:, :, DFF + (c0 + h2) * 128:DFF + (c0 + h2 + 2) * 128])
                else:
                    kb.dma("pool", WG[b][:, :, 0:ng * 128], win[:, :, c0 * 128:(c0 + ng) * 128])
                    kb.dma("pool", WU[b][:, :, 0:ng * 128], win[:, :, DFF + c0 * 128:DFF + (c0 + ng) * 128])
                kb.dma("pool", WO[b][:, 0:ng, :], wout[:, c0:c0 + ng, :])

            load(0)
            hb = 0
            for gi, (c0, ng) in enumerate(groups):
                if gi + 1 < len(groups):
                    load(gi + 1)
                b = gi % 2
                for ti, (t0, n) in enumerate(TILES):
                    if gi == 0:
                        if ti == 0:
                            modnorm(t0, n, j, U[:, :, t0:t0 + n], SQ, RS, TT)
                        if ti + 1 < len(TILES):
                            t1, n1 = TILES[ti + 1]
                            modnorm(t1, n1, j, U[:, :, t1:t1 + n1], SQ, RS, TT)
                    h = H[hb % 2]
                    hb += 1
                    for jj in range(ng):
                        bg = kb.bank()
                        bu = kb.bank()
                        for k in range(8):
                            kb.mm(bg[:, 0:n], lhsT=WG[b][:, k, jj * 128:(jj + 1) * 128], rhs=U[:, k, t0:t0 + n],
                                  start=(k == 0), stop=(k == 7))
                        for k in range(8):
                            kb.mm(bu[:, 0:n], lhsT=WU[b][:, k, jj * 128:(jj + 1) * 128], rhs=U[:, k, t0:t0 + n],
                                  start=(k == 0), stop=(k == 7))
                        sg = SG[jj % 2]
                        kb.act(sg[:, 0:n], bg[:, 0:n], AF.Silu)
                        kb.tt(h[:, jj, 0:n], bu[:, 0:n], sg[:, 0:n], ALU.mult)
                    for m in range(8):
                        bo = kb.bank()
                        for jj in range(ng):
                            kb.mm(bo[:, 0:n], lhsT=WO[b][:, jj, m * 128:(m + 1) * 128], rhs=h[:, jj, 0:n],
                                  start=(jj == 0), stop=(jj == ng - 1))
                        resid(m, t0, n, bo, j)

        def final_out():
            ar.reset()
            YT = [ar.alloc([D], F32) for _ in range(2)]
            SQ = ar.alloc([8, 128], BF16)
            RS = ar.alloc([128], F32)
            YN = [ar.alloc([8, 128], F32) for _ in range(2)]
            FG = vec("final_g")
            for i in range(17):
                n = 128 if i < 16 else 64
                t0 = i * 128
                yn = YN[i % 2]
                kb.act(SQ[:, :, 0:n], X[:, :, t0:t0 + n], AF.Square)
                bk = kb.bank()
                for c in range(8):
                    kb.mm(bk[:, 0:n], lhsT=cb("ones"), rhs=SQ[:, c, 0:n], start=(c == 0), stop=(c == 7))
                kb.act(RS[:, 0:n], bk[:, 0:n], AF.Sqrt, bias=NORM_EPS, scale=1.0 / D)
                kb.recip(RS[:, 0:n], RS[:, 0:n])
                for c in range(8):
                    kb.stt(yn[:, c, 0:n], X[:, c, t0:t0 + n], FG[:, c:c + 1], RS[:, 0:n], ALU.mult, ALU.mult)
                yt = YT[i % 2]
                for half in range(2):
                    bk = kb.bank()
                    for cc in range(4):
                        c = half * 4 + cc
                        kb.tr(bk[0:n, cc * 128:(cc + 1) * 128], yn[:, c, 0:n], cf("ident"))
                    kb.cp(yt[0:n, half * 512:(half + 1) * 512], bk[0:n, 0:512])
                kb.dma("sp", O["y"][t0:t0 + n, :], yt[0:n, :])

        def dbg_dump_x():
            if dbg:
                kb.dma("sp", O["dbgX"], X[:])

        mixer = _make_mixer(nc, kb, ar, I, O, X, YAB, cf, cb, vec, modnorm, resid, OMKA, TILES, dbg)

        def mark(name):
            DBG.setdefault("marks", []).append((name, sum(1 for o in S.ops if o.eng == "pe"), len(S.ops)))

        DBG["marks"] = []
        for l in range(nlayers):
            mark("vec%d" % l)
            load_layer_vectors(l)
            mark("ffn%d0" % l)
            ffn(l, 0)
            if stop == "ffn0":
                break
            mark("mixer%d" % l)
            mixer(l, only_a=(stop == "passA"))
            if stop in ("mix0", "passA"):
                break
            mark("ffn%d1" % l)
            ffn(l, 1)
        mark("final")
        dbg_dump_x()
        final_out()
        S.lower(st)
        print("ops", len(S.ops), "arena peak KiB", ar.peak * 2 / 1024.0)
    return nc


def _make_mixer(nc, kb, ar, I, O, X, YAB, cf, cb, vec, modnorm, resid, OMKA, TILES, dbg):
    NB = 64

    def bc(ap, shape):
        return ap.to_broadcast(list(shape))

    def pass_a(l):
        ar.reset()
        WIN = ar.alloc([8, APJ], BF16)
        W2T = ar.alloc([512], BF16)
        A2T = ar.alloc([512], BF16)
        G2T = ar.alloc([2, 512], BF16)
        kb.memset(W2T[:], 0.0)
        kb.memset(A2T[:], 0.0)
        kb.dma("pool", WIN, I["w_in"][l].rearrange("(k p) n -> p k n", p=128)[:, :, 0:APJ])
        kb.dma("pool", W2T[0:64, :], I["w2"][l])
        kb.dma("pool", A2T[64:128, :], I["a2"][l])
        kb.dma("pool", G2T[:, 0, :], I["g2"][l, 0:128, :])
        kb.dma("pool", G2T[0:32, 1, :], I["g2"][l, 128:160, :])
        UB = ar.alloc([8, NB], BF16)
        SQn = ar.alloc([8, NB], BF16)
        RSn = ar.alloc([NB], F32)
        TTn = [ar.alloc([NB], F32) for _ in range(2)]
        PA = ar.alloc([15, 80], F32)
        XS2 = [ar.alloc([15, NB], F32) for _ in range(2)]
        LAST = ar.alloc([15], F32)
        SHS = ar.alloc([15, 16], F32)
        SHO = ar.alloc([15, 16], F32)
        LIN = ar.alloc([3, NB], BF16)
        f4 = lambda: ar.alloc([4, NB], F32)
        t_wd, t_p, t_pex, t_ip, t_aa, t_kk, t_t1, t_t2, t_k2 = [f4() for _ in range(9)]
        SQK = ar.alloc([4, NB], BF16)
        RKR = ar.alloc([4, NB], BF16)
        blk = lambda: ar.alloc([512], BF16)
        AT3 = [blk() for _ in range(3)]
        RT3 = [blk() for _ in range(3)]
        PC3 = [ar.alloc([64], F32) for _ in range(3)]
        GG3 = [f4() for _ in range(3)]
        BON3 = [f4() for _ in range(3)]
        BT2 = [blk() for _ in range(2)]; KT2 = [blk() for _ in range(2)]; BH2 = [blk() for _ in range(3)]
        KH2 = [blk() for _ in range(3)]; VB2 = [blk() for _ in range(3)]
        NM, NTM, QB, QTB = [blk() for _ in range(4)]
        AAK2 = [blk() for _ in range(2)]; ARB2 = [blk() for _ in range(2)]; ARK2 = [blk() for _ in range(2)]
        MM2 = [blk() for _ in range(2)]
        BHT = ar.alloc([4, 128], BF16)
        KHT = ar.alloc([4, 128], BF16)
        VT = ar.alloc([4, 64], BF16)
        ZT = ar.alloc([4, 64], BF16)
        UT = ar.alloc([4, 64], BF16)
        SQY = ar.alloc([4, 64], F32)
        YC = ar.alloc([4, 64], F32)
        YNB = ar.alloc([4, 128], BF16)
        ST1 = ar.alloc([4], F32); ST2 = ar.alloc([4], F32); STM = ar.alloc([4], F32); STV = ar.alloc([4], F32)
        YF = ar.alloc([4, NB], F32)
        p2a = f4()
        YAb = ar.alloc([4, NB], BF16)
        S32 = ar.alloc([4, 64], F32)
        SBF = [ar.alloc([4, 64], BF16) for _ in range(2)]
        SI = ar.alloc([8, 64], F32)
        SO = ar.alloc([4, 128], F32)
        SHT = ar.alloc([15, 128], F32)
        SROW = SHT[:].rearrange("p m c -> p (m c)")[:, 0:APJ]
        DBG["passA_kib"] = ar.off * 2 / 1024.0

        for t in AT3 + RT3 + BT2 + KT2 + BH2 + KH2 + VB2:
            kb.memset(t[:], 0.0)
        kb.memset(PA[:], 0.0)
        kb.memset(LAST[:], 0.0)
        kb.memset(XS2[0][:], 0.0)
        kb.memset(XS2[1][:], 0.0)
        kb.memset(S32[:], 0.0)
        kb.memset(SBF[0][:], 0.0)
        kb.memset(LIN[:], 0.0)
        kb.dma("sp", SROW[0:16, :], I["sshift"][l])
        kb.memset(SHS[:], 0.0)
        for half in range(2):
            bk = kb.bank()
            ms = range(0, 8) if half == 0 else range(8, 15)
            for m in ms:
                Mm = 128 if m < 14 else 32
                kb.mm(bk[0:Mm, (m % 8) * 16:(m % 8) * 16 + 16], lhsT=SROW[0:16, m * 128:m * 128 + Mm], rhs=cf("ident")[0:16, 0:16])
            if half == 0:
                kb.cp(SHS[:, 0:8, :], _view(bk[:, 0:128], [8, 16]), eng="dve")
            else:
                kb.cp(SHS[:, 8:14, :], _view(bk[:, 0:96], [6, 16]), eng="dve")
                kb.cp(SHS[0:32, 14, :], bk[0:32, 96:112], eng="dve")

        MU = vec("mu")
        sbi = [0]
        NBLK = TP // NB
        bankA0 = lambda: kb.bank_of((0, 1))
        bankA = lambda: kb.bank_of((2, 3))
        bankB = lambda: kb.bank_of((4, 5))
        bankC = lambda: kb.bank_of((6, 7))

        def geom(b):
            is_s = (b == NBLK)
            C = 4 if is_s else 64
            R = 2 * C
            NQ = 512 // R
            return is_s, C, R, NQ, NQ // 4, ("4" if is_s else "64"), b * NB

        def stage1a(b):
            is_s, C, R, NQ, nch, sfx, tok0 = geom(b)
            n = NB
            XS = XS2[b % 2]
            modnorm(tok0, n, 1, UB, SQn, RSn, TTn, bank=bankA0)
            yield
            for half in range(2):
                bk = bankA0()
                ms = range(0, 8) if half == 0 else range(8, 15)
                for m in ms:
                    Mm = 128 if m < 14 else 32
                    for k in range(8):
                        kb.mm(bk[0:Mm, (m % 8) * 64:(m % 8) * 64 + 64], lhsT=WIN[:, k, m * 128:m * 128 + Mm], rhs=UB[:, k, :],
                              start=(k == 0), stop=(k == 7))
                    if m % 2 == 1:
                        yield
                if not is_s:
                    if half == 0:
                        kb.cp(PA[:, 0:8, 1:65], _view(bk[:, 0:512], [8, 64]), eng="act")
                    else:
                        kb.cp(PA[:, 8:14, 1:65], _view(bk[:, 0:384], [6, 64]), eng="act")
                        kb.cp(PA[0:32, 14, 1:65], bk[0:32, 384:448], eng="act")
                else:
                    PAs = PA[:].rearrange("p m (s t) -> p m s t", t=5)
                    if half == 0:
                        kb.cp(PAs[:, 0:8, :, 1:5], bk[:, 0:512].rearrange("p (m s t) -> p m s t", m=8, t=4), eng="act")
                    else:
                        kb.cp(PAs[:, 8:14, :, 1:5], bk[:, 0:384].rearrange("p (m s t) -> p m s t", m=6, t=4), eng="act")
                        kb.cp(PAs[0:32, 14, :, 1:5], bk[0:32, 384:448].rearrange("p (s t) -> p s t", t=4), eng="act")
                yield
            if not is_s:
                kb.cp(PA[:, :, 0], LAST[:], eng="act")
                kb.tt(XS[:], PA[:, :, 0:64], PA[:, :, 1:65], ALU.subtract)
                yield
                kb.tt(XS[:], XS[:], bc(MU[:, :, None], [128, 15, 64]), ALU.mult)
                yield
                kb.tt(XS[:], XS[:], PA[:, :, 1:65], ALU.add)
                kb.cp(LAST[:], PA[:, :, 64], eng="act")
                yield
            else:
                PAs = PA[:].rearrange("p m (s t) -> p m s t", t=5)
                XSs = XS[:].rearrange("p m (s t) -> p m s t", t=4)
                kb.cp(PAs[:, :, :, 0], SHS[:], eng="dve")
                for m0, m1 in ((0, 8), (8, 15)):
                    kb.tt(XSs[:, m0:m1], PAs[:, m0:m1, :, 0:4], PAs[:, m0:m1, :, 1:5], ALU.subtract)
                yield
                kb.tt(XS[:], XS[:], bc(MU[:, :, None], [128, 15, 64]), ALU.mult)
                yield
                for m0, m1 in ((0, 8), (8, 15)):
                    kb.tt(XSs[:, m0:m1], XSs[:, m0:m1], PAs[:, m0:m1, :, 1:5], ALU.add)
                kb.cp(SHO[:], PAs[:, :, :, 4], eng="dve")
                yield
            if b == NBLK - 1:
                bk = bankA0()
                kb.mm(bk[0:15, 0:128], lhsT=LAST[:, 0:15], rhs=cf("ident"))
                kb.cp(SHT[0:15, 0, :], bk[0:15, 0:128], eng="dve")
                kb.dma("sp", O["shift_p"][l, 0:1792].rearrange("(c p) -> c p", p=128), SHT[0:14, 0, :])
                kb.dma("sp", O["shift_p"][l:l + 1, 1792:1824], SHT[14:15, 0, 0:32])
                yield
            if is_s:
                for g4 in range(4):
                    bk = bankA0()
                    ms = range(g4 * 4, min(g4 * 4 + 4, 15))
                    for m in ms:
                        Mm = 128 if m < 14 else 32
                        kb.mm(bk[0:16, (m % 4) * 128:(m % 4) * 128 + Mm], lhsT=SHO[0:Mm, m, :], rhs=cf("ident")[0:Mm, 0:Mm])
                    if g4 < 3:
                        kb.cp(SHT[0:16, g4 * 4:g4 * 4 + 4, :], _view(bk[0:16, 0:512], [4, 128]), eng="dve")
                    else:
                        kb.cp(SHT[0:16, 12:14, :], _view(bk[0:16, 0:256], [2, 128]), eng="dve")
                        kb.cp(SHT[0:16, 14, 0:32], bk[0:16, 256:288], eng="dve")
                    yield
                kb.dma("sp", O["shift_s"][l, :, 0:1792], SHT[0:16, 0:14, :].rearrange("p m c -> p (m c)"))
                kb.dma("sp", O["shift_s"][l, :, 1792:1824], SHT[0:16, 14, 0:32])
                yield

        def stage1a2(b):
            is_s, C, R, NQ, nch, sfx, tok0 = geom(b)
            n = NB
            XS = XS2[b % 2]
            AT, RT, PC, t_gg, t_bon = AT3[b % 3], RT3[b % 3], PC3[b % 3], GG3[b % 3], BON3[b % 3]
            BT, KT, BH, KH, VB = BT2[b % 2], KT2[b % 2], BH2[b % 3], KH2[b % 3], VB2[b % 3]
            if is_s:
                for t in (AT, RT, BT, KT, BH, KH, VB):
                    kb.memset(t[:], 0.0)
                yield
            xr, xk, xv = XS[:, 0:4, :], XS[:, 4:8, :], XS[:, 8:12, :]
            kb.act(LIN[0:64, 0, :], XS[0:64, 12, :], AF.Tanh)
            kb.cp(LIN[64:128, 0, :], XS[64:128, 12, :], eng="act")
            kb.act(LIN[:, 1, :], XS[:, 13, :], AF.Sigmoid)
            kb.act(LIN[0:32, 2, :], XS[0:32, 14, :], AF.Sigmoid)
            yield
            bw = bankA()
            for j in range(4):
                kb.mm(bw[:, j * 64:(j + 1) * 64], lhsT=W2T[:, j * 128:(j + 1) * 128], rhs=LIN[:, 0, :])
                kb.mm(bw[:, 256 + j * 64:256 + (j + 1) * 64], lhsT=A2T[:, j * 128:(j + 1) * 128], rhs=LIN[:, 0, :])
            bg = bankA()
            for j in range(4):
                kb.mm(bg[:, j * 64:(j + 1) * 64], lhsT=G2T[:, 0, j * 128:(j + 1) * 128], rhs=LIN[:, 1, :], start=True, stop=False)
                kb.mm(bg[:, j * 64:(j + 1) * 64], lhsT=G2T[0:32, 1, j * 128:(j + 1) * 128], rhs=LIN[0:32, 2, :], start=False, stop=True)
            yield
            kb.tt(t_t1[:], _view(bw[:, 0:256], [4, 64]), bc(vec("w0")[:, :, None], [128, 4, 64]), ALU.add)
            kb.tt(t_aa[:], _view(bw[:, 256:512], [4, 64]), bc(vec("a0")[:, :, None], [128, 4, 64]), ALU.add)
            kb.cp(t_gg[:], _view(bg[:, 0:256], [4, 64]), eng="act")
            yield
            kb.act(t_t1[:], t_t1[:], AF.Sigmoid)
            kb.act(t_aa[:], t_aa[:], AF.Sigmoid)
            yield
            kb.act(t_wd[:], t_t1[:], AF.Exp, scale=-DECAY_K)
            kb.tt(t_kk[:], xk, bc(vec("k_k")[:, :, None], [128, 4, 64]), ALU.mult)
            yield
            kb.act(SQK[:], t_kk[:], AF.Square)
            kb.tt(t_t1[:], t_wd[:], bc(cf("nstart" + sfx)[:, None, 0:64], [128, 4, 64]), ALU.mult)
            kb.tt(t_t2[:], t_wd[:], bc(cf("start" + sfx)[:, None, 0:64], [128, 4, 64]), ALU.mult)
            yield
            bs = bankA()
            for j in range(4):
                kb.mm(bs[:, j * 64:(j + 1) * 64], lhsT=cb("bones"), rhs=SQK[:, j, :])
            for j in range(4):
                kb.scan(t_p[:, j, :], t_t1[:, j, :], t_t2[:, j, :], 1.0)
            yield
            kb.act(t_t1[:], _view(bs[:, 0:256], [4, 64]), AF.Sqrt)
            kb.recip(t_ip[:], t_p[:])
            kb.recip(t_t2[:], t_wd[:])
            yield
            kb.tt(t_pex[:], t_p[:], t_t2[:], ALU.mult)
            kb.cp(PC[:, 0:NQ].rearrange("p (s j) -> p j s", j=4),
                  t_p[:].rearrange("p j (s c) -> p j s c", c=C)[:, :, :, C - 1], eng="dve")
            kb.ts(t_t1[:], t_t1[:], 1e-12, None, ALU.max)
            yield
            kb.recip(t_t1[:], t_t1[:])
            yield
            kb.tt(t_kk[:], t_kk[:], t_t1[:], ALU.mult)
            yield
            kb.tt(t_t2[:], t_kk[:], t_aa[:], ALU.mult)
            kb.tt(t_t1[:], t_aa[:], bc(vec("k_a")[:, :, None], [128, 4, 64]), ALU.mult)
            kb.stt(t_wd[:], t_kk[:], -1.0, t_pex[:], ALU.mult, ALU.mult)
            yield
            kb.tt(t_t2[:], t_t2[:], t_ip[:], ALU.mult)
            kb.tt(t_t1[:], t_t1[:], bc(OMKA[:, :, None], [128, 4, 64]), ALU.add)
            yield
            kb.tt(t_k2[:], xk, t_t1[:], ALU.mult)
            yield
            kb.tt(t_t1[:], xr, t_k2[:], ALU.mult)
            kb.tt(t_aa[:], t_k2[:], t_ip[:], ALU.mult)
            yield
            kb.tt(RKR[:], t_t1[:], bc(vec("r_k")[:, :, None], [128, 4, 64]), ALU.mult)
            yield
            brk = bankA()
            for j in range(4):
                kb.mm(brk[:, j * 64:(j + 1) * 64], lhsT=cb("bones"), rhs=RKR[:, j, :])
            PCq = PC[:, 0:NQ].rearrange("p (s j) -> p j s", j=4)
            for h in range(2):
                ps_ = slice(64 * h, 64 * h + 64)

                def dst(tile):
                    return tile[ps_, :].rearrange("p (s j r) -> p j s r", j=4, r=R)[:, :, :, h * C:(h + 1) * C]

                def src(t3):
                    return t3[ps_].rearrange("p j (s c) -> p j s c", c=C)

                kb.cp(dst(AT), src(t_wd), eng="act")
                kb.tt(dst(RT), src(xr), src(t_p), ALU.mult)
                yield
                kb.cp(dst(BT), src(t_t2), eng="act")
                kb.cp(dst(KT), src(t_aa), eng="act")
                kb.tt(dst(BH), src(t_t2), bc(PCq[ps_, :, :, None], [64, 4, nch, C]), ALU.mult)
                yield
                kb.tt(dst(KH), src(t_aa), bc(PCq[ps_, :, :, None], [64, 4, nch, C]), ALU.mult)
                kb.cp(dst(VB), src(xv), eng="act")
                yield
            kb.tt(t_bon[:], _view(brk[:, 0:256], [4, 64]), xv, ALU.mult)
            yield

        def stage1b(b):
            is_s, C, R, NQ, nch, sfx, tok0 = geom(b)
            AT, RT = AT3[b % 3], RT3[b % 3]
            BT, KT = BT2[b % 2], KT2[b % 2]
            AAK, ARB, ARK, MM = AAK2[b % 2], ARB2[b % 2], ARK2[b % 2], MM2[b % 2]
            msu, msuT, mu_ = cf("msu" + sfx), cf("msuT" + sfx), cf("mu" + sfx)
            if is_s:
                for t in (NM, NTM, QB, QTB, AAK, ARB, ARK, MM):
                    kb.memset(t[:], 0.0)
                yield
            Mq = lambda tile: tile[0:R, :].rearrange("p (q r) -> p q r", r=R)
            MqK = lambda tile: tile[:, :].rearrange("p (q r) -> p q r", r=R)
            Fq = MqK

            def prod(lt, rt, mask, out_t):
                bk_ = bankB()
                for q in range(NQ):
                    kb.mm(bk_[0:R, q * R:(q + 1) * R], lhsT=Fq(lt)[:, q, :], rhs=Fq(rt)[:, q, :])
                    if q % 8 == 7:
                        yield
                kb.tt(Mq(out_t), bk_[0:R, :].rearrange("p (q r) -> p q r", r=R), bc(mask[0:R, None, 0:R], [R, NQ, R]), ALU.mult)
                yield

            yield from prod(BT, AT, msu, NM)
            yield from prod(AT, BT, msuT, NTM)
            kb.tt(Mq(MM), Mq(NM), bc(cf("ident")[0:R, None, 0:R], [R, NQ, R]), ALU.add)
            yield
            nlev = 5 if not is_s else 1
            Q, QT = NM, NTM
            Qn, QTn = QB, QTB
            extra = [(KT, AT, msu, AAK), (BT, RT, mu_, ARB), (KT, RT, mu_, ARK)]
            for lev in range(nlev):
                last = (lev == nlev - 1)
                b2 = bankB()
                for q in range(NQ):
                    kb.mm(b2[0:R, q * R:(q + 1) * R], lhsT=MqK(Q)[:, q, :], rhs=MqK(QT)[:, q, :])
                    if q % 8 == 7:
                        yield
                kb.cp(QTn[0:R, :], b2[0:R, :], eng="act")
                yield
                if not last:
                    b1 = bankB()
                    for q in range(NQ):
                        kb.mm(b1[0:R, q * R:(q + 1) * R], lhsT=MqK(QT)[:, q, :], rhs=MqK(Q)[:, q, :])
                        if q % 8 == 7:
                            yield
                    kb.cp(Qn[0:R, :], b1[0:R, :], eng="act")
                    yield
                b3 = bankB()
                for q in range(NQ):
                    o3 = b3[0:R, q * R:(q + 1) * R]
                    kb.mm(o3, lhsT=MqK(QTn)[:, q, :], rhs=MqK(MM)[:, q, :], start=True, stop=False)
                    kb.mm(o3, lhsT=cb("ident")[:, 0:R], rhs=MqK(MM)[:, q, :], start=False, stop=True)
                    if q % 8 == 7:
                        yield
                kb.cp(MM[0:R, :], b3[0:R, :], eng="act")
                yield
                Q, QT, Qn, QTn = Qn, QTn, Q, QT
                if extra:
                    yield from prod(*extra.pop(0))
            while extra:
                yield from prod(*extra.pop(0))

        def stage2(b):
            is_s, C, R, NQ, nch, sfx, tok0 = geom(b)
            n = NB
            AT, RT, PC, t_gg, t_bon = AT3[b % 3], RT3[b % 3], PC3[b % 3], GG3[b % 3], BON3[b % 3]
            BH, KH, VB = BH2[b % 3], KH2[b % 3], VB2[b % 3]
            AAK, ARB, ARK, MM = AAK2[b % 2], ARB2[b % 2], ARK2[b % 2], MM2[b % 2]
            istk, tokm = cb("istack" + sfx), cf("tokmask" + sfx)
            if is_s:
                for t in (BHT, KHT, VT, ZT, UT, YNB):
                    kb.memset(t[:], 0.0)
                yield
            KR = 128
            MqK = lambda tile: tile[:, :].rearrange("p (q r) -> p q r", r=R)
            Fq = MqK
            for gi in range(nch):
                q0 = gi * 4
                if is_s:
                    seq = gi
                    kb.dma("sp", SI[0:64, :, :], I["swkv"][l, seq].rearrange("h v k -> v h k"))
                    sb_in = SBF[sbi[0] % 2]
                    bk_ = bankC()
                    for j in range(4):
                        kb.mm(bk_[:, j * 64:(j + 1) * 64], lhsT=SI[0:64, 2 * j:2 * j + 2, :].rearrange("p h k -> p (h k)"),
                              rhs=cf("ident")[0:64, 0:64])
                    kb.cp(S32[:], _view(bk_[:, 0:256], [4, 64]), eng="dve")
                    kb.cp(sb_in[:], S32[:], eng="act")
                    yield
                else:
                    sb_in = SBF[sbi[0] % 2]
                sb_out = SBF[(sbi[0] + 1) % 2]
                sbi[0] += 1
                b_ = bankC()
                for j in range(4):
                    kb.mm(b_[0:R, j * 128:(j + 1) * 128], lhsT=Fq(BH)[:, q0 + j, :], rhs=cb("ident"))
                kb.cp(BHT[0:R], _view(b_[0:R, 0:512], [4, 128]), eng="act")
                yield
                b_ = bankC()
                for j in range(4):
                    kb.mm(b_[0:R, j * 128:(j + 1) * 128], lhsT=Fq(KH)[:, q0 + j, :], rhs=cb("ident"))
                kb.cp(KHT[0:R], _view(b_[0:R, 0:512], [4, 128]), eng="act")
                yield
                b_ = bankC()
                for j in range(4):
                    kb.mm(b_[0:R, j * 64:(j + 1) * 64], lhsT=Fq(VB)[:, q0 + j, :], rhs=cb("istack64"))
                kb.cp(VT[0:R], _view(b_[0:R, 0:256], [4, 64]), eng="act")
                yield
                bz = bankC()
                for j in range(4):
                    kb.mm(bz[0:R, j * 64:(j + 1) * 64], lhsT=MqK(AAK)[:, q0 + j, :], rhs=VT[0:KR, j, :], start=True, stop=False)
                    kb.mm(bz[0:R, j * 64:(j + 1) * 64], lhsT=Fq(AT)[:, q0 + j, :], rhs=sb_in[:, j, :], start=False, stop=True)
                kb.cp(ZT[0:R], _view(bz[0:R, 0:256], [4, 64]), eng="act")
                yield
                bu = bankC()
                for j in range(4):
                    kb.mm(bu[0:R, j * 64:(j + 1) * 64], lhsT=MqK(MM)[:, q0 + j, :], rhs=ZT[0:KR, j, :])
                kb.cp(UT[0:R], _view(bu[0:R, 0:256], [4, 64]), eng="dve")
                yield
                bs_ = bankC()
                for j in range(4):
                    o = bs_[:, j * 64:(j + 1) * 64]
                    kb.mm(o, lhsT=BHT[0:KR, j, :], rhs=UT[0:KR, j, :], start=True, stop=False)
                    kb.mm(o, lhsT=KHT[0:KR, j, :], rhs=VT[0:KR, j, :], start=False, stop=True)
                by = bankC()
                for j in range(4):
                    o = by[0:R, j * 64:(j + 1) * 64]
                    kb.mm(o, lhsT=Fq(RT)[:, q0 + j, :], rhs=sb_in[:, j, :], start=True, stop=False)
                    kb.mm(o, lhsT=MqK(ARB)[:, q0 + j, :], rhs=UT[0:KR, j, :], start=False, stop=False)
                    kb.mm(o, lhsT=MqK(ARK)[:, q0 + j, :], rhs=VT[0:KR, j, :], start=False, stop=True)
                kb.tt(S32[:], S32[:], bc(PC[:, q0:q0 + 4, None], [128, 4, 64]), ALU.mult)
                yield
                kb.tt(S32[:], S32[:], _view(bs_[:, 0:256], [4, 64]), ALU.add)
                kb.cp(YC[0:R], _view(by[0:R, 0:256], [4, 64]), eng="act")
                yield
                kb.cp(sb_out[:], S32[:], eng="act")
                Yv = YC[0:R]
                kb.reduce_sum(ST1[0:R], Yv)
                yield
                kb.act(SQY[0:R], Yv, AF.Square)
                kb.ts(STM[0:R], ST1[0:R], 1.0 / 64, None, ALU.mult)
                yield
                kb.reduce_sum(ST2[0:R], SQY[0:R])
                kb.tt(STV[0:R], STM[0:R], STM[0:R], ALU.mult)
                yield
                kb.stt(STV[0:R], ST2[0:R], 1.0 / 64, STV[0:R], ALU.mult, ALU.subtract)
                kb.tt(YC[0:R], Yv, bc(STM[0:R, :, None], [R, 4, 64]), ALU.subtract)
                yield
                kb.act(STV[0:R], STV[0:R], AF.Sqrt, bias=LN_EPS)
                yield
                kb.recip(STV[0:R], STV[0:R])
                yield
                kb.tt(YC[0:R], YC[0:R], bc(STV[0:R, :, None], [R, 4, 64]), ALU.mult)
                yield
                for h in range(2):
                    kb.ts(YNB[0:R, :, h * 64:(h + 1) * 64], YC[0:R], tokm[0:R, h:h + 1], None, ALU.mult)
                yield
                bf_ = bankC()
                CW = max(C, 8)
                for j in range(4):
                    kb.mm(bf_[:, j * CW:(j + 1) * CW], lhsT=YNB[0:KR, j, :], rhs=istk[0:KR, 0:CW])
                kb.cp(YF[:, :, gi * C:(gi + 1) * C], _view(bf_[:, 0:4 * CW], [4, CW])[:, :, 0:C], eng="act")
                yield
                if is_s or b == NBLK - 1:
                    bo_ = bankC()
                    for j in range(4):
                        kb.mm(bo_[0:64, j * 128:(j + 1) * 128], lhsT=S32[:, j, :], rhs=cf("ident"))
                    kb.cp(SO[0:64], _view(bo_[0:64, 0:512], [4, 128]), eng="dve")
                    dst_ = O["wkv_s"][l, gi] if is_s else O["wkv_p"][l]
                    kb.dma("sp", dst_.rearrange("h v k -> v h k"), SO[0:64].rearrange("p j (h k) -> p (j h) k", h=2))
                    yield
            kb.tt(p2a[:], YF[:], bc(vec("ln_w")[:, :, None], [128, 4, 64]), ALU.mult)
            yield
            kb.tt(p2a[:], p2a[:], bc(vec("ln_b")[:, :, None], [128, 4, 64]), ALU.add)
            yield
            kb.tt(p2a[:], p2a[:], t_bon[:], ALU.add)
            yield
            kb.tt(YAb[:], p2a[:], t_gg[:], ALU.mult)
            kb.dma("sp", YAB[:, 0:4, tok0:tok0 + n], YAb[:])
            yield

        nblocks = NBLK + 1
        if DBG.get("nblk") is not None:
            nblocks = DBG["nblk"]
        for step in range(nblocks + 3):
            gens = []
            if step < nblocks:
                gens.append(stage1a(step))
            if 0 <= step - 1 < nblocks:
                gens.append(stage1a2(step - 1))
            if 0 <= step - 2 < nblocks:
                gens.append(stage1b(step - 2))
            if 0 <= step - 3 < nblocks:
                gens.append(stage2(step - 3))
            if DBG.get("no_interleave", False):
                for g in gens[::-1]:
                    for _ in g:
                        pass
                continue
            alive = list(gens)
            while alive:
                for g in list(alive):
                    try:
                        next(g)
                    except StopIteration:
                        alive.remove(g)

    def pass_pool(l):
        ar.reset()
        WPB = ar.alloc([8, 512], BF16)
        WPL = ar.alloc([4, 128], BF16)
        kb.dma("pool", WPB, I["w_in"][l].rearrange("(k p) n -> p k n", p=128)[:, :, APJ:PT])
        kb.dma("pool", WPL, I["w_pool"][l].rearrange("g c d -> c g d"))
        UB = ar.alloc([8, 512], BF16)
        SQ = ar.alloc([8, 512], BF16)
        RS = ar.alloc([512], F32)
        TT = [ar.alloc([512], F32) for _ in range(2)]
        PBH = ar.alloc([4, 527], F32)
        SA = ar.alloc([4, 527], F32)
        SB = ar.alloc([4, 527], F32)
        DP = ar.alloc([4, 512], BF16)
        YBb = ar.alloc([4, 512], BF16)
        PROW = ar.alloc([512], F32)
        POUT = ar.alloc([512], F32)
        TMPH = ar.alloc([4, 120], F32)
        PS = vec("pool_scale")
        WIN_ = (2, 4, 8, 16)
        kb.memset(PBH[:], 0.0)

        def wsum(x, sa, sb_, L, nd):
            def sl(v, g0, g1, a, b_):
                return v[:, g0:g1, a:b_] if nd == 3 else v[:, g0:g1, :, a:b_]
            kb.tt(sl(sa, 0, 4, 1, L), sl(x, 0, 4, 1, L), sl(x, 0, 4, 0, L - 1), ALU.add)
            kb.tt(sl(sb_, 1, 4, 3, L), sl(sa, 1, 4, 3, L), sl(sa, 1, 4, 1, L - 2), ALU.add)
            kb.tt(sl(sa, 2, 4, 7, L), sl(sb_, 2, 4, 7, L), sl(sb_, 2, 4, 3, L - 4), ALU.add)
            kb.tt(sl(sb_, 3, 4, 15, L), sl(sa, 3, 4, 15, L), sl(sa, 3, 4, 7, L - 8), ALU.add)
            return [sa, sb_, sa, sb_]

        for (t0, n) in TILES[:4]:
            modnorm(t0, n, 1, UB, SQ, RS, TT)
            for g in range(4):
                bk = kb.bank()
                for k in range(8):
                    kb.mm(bk[:, 0:n], lhsT=WPB[:, k, g * 128:(g + 1) * 128], rhs=UB[:, k, 0:n], start=(k == 0), stop=(k == 7))
                kb.cp(PBH[:, g, 15:15 + n], bk[:, 0:n], eng="act")
            L = 15 + n
            fin = wsum(PBH, SA, SB, L, 3)
            for g in range(4):
                if t0 == 0:
                    kb.tt(fin[g][:, g, 15:30], fin[g][:, g, 15:30], cf("ratio")[:, g * 15:(g + 1) * 15], ALU.mult)
                kb.stt(DP[:, g, 0:n], fin[g][:, g, 15:L], 1.0 / WIN_[g], PBH[:, g, 15:L], ALU.mult, ALU.subtract)
            for g in range(4):
                bk = kb.bank()
                kb.mm(bk[:, 0:n], lhsT=WPL[:, g, :], rhs=DP[:, g, 0:n])
                kb.ts(YBb[:, g, 0:n], bk[:, 0:n], PS[:, g:g + 1], None, ALU.mult)
            kb.dma("sp", YAB[:, 4:8, t0:t0 + n], YBb[:, :, 0:n])
            kb.cp(TMPH[:, :, 0:15], PBH[:, :, n:n + 15], eng="dve")
            kb.cp(PBH[:, :, 0:15], TMPH[:, :, 0:15], eng="dve")
        bk = kb.bank()
        for g in range(4):
            kb.mm(bk[0:15, g * 128:(g + 1) * 128], lhsT=TMPH[:, g, 0:15], rhs=cf("ident"))
        kb.cp(POUT[0:15, :], bk[0:15, 0:512], eng="dve")
        kb.dma("sp", O["pool_p"][l], POUT[0:15, :])
        PBs = PBH[:, :, 0:304].rearrange("p g (s t) -> p g s t", t=19)
        SAs = SA[:, :, 0:304].rearrange("p g (s t) -> p g s t", t=19)
        SBs = SB[:, :, 0:304].rearrange("p g (s t) -> p g s t", t=19)
        sp_rows = I["spool"][l].rearrange("s i c -> (s i) c")
        for hh in range(2):
            kb.dma("sp", PROW[0:120, :], sp_rows[hh * 120:(hh + 1) * 120, :])
            bk = kb.bank()
            for g in range(4):
                kb.mm(bk[:, g * 120:(g + 1) * 120], lhsT=PROW[0:120, g * 128:(g + 1) * 128], rhs=cf("ident")[0:120, 0:120])
            kb.cp(PBs[:, :, hh * 8:(hh + 1) * 8, 0:15], bk[:, 0:480].rearrange("p (g s t) -> p g s t", g=4, t=15), eng="dve")
        modnorm(TP, 64, 1, UB, SQ, RS, TT)
        bk = kb.bank()
        for g in range(4):
            for k in range(8):
                kb.mm(bk[:, g * 64:(g + 1) * 64], lhsT=WPB[:, k, g * 128:(g + 1) * 128], rhs=UB[:, k, 0:64], start=(k == 0), stop=(k == 7))
        kb.cp(PBs[:, :, :, 15:19], bk[:, 0:256].rearrange("p (g s t) -> p g s t", g=4, t=4), eng="act")
        fin = wsum(PBs, SAs, SBs, 19, 4)
        fv = [SAs, SBs, SAs, SBs]
        for g in range(4):
            kb.stt(_view(DP[:, g, 0:64], [16, 4]), fv[g][:, g, :, 15:19], 1.0 / WIN_[g], PBs[:, g, :, 15:19], ALU.mult, ALU.subtract)
        bk = kb.bank()
        for g in range(4):
            kb.mm(bk[:, g * 64:(g + 1) * 64], lhsT=WPL[:, g, :], rhs=DP[:, g, 0:64])
        for g in range(4):
            kb.ts(YBb[:, g, 0:64], bk[:, g * 64:(g + 1) * 64], PS[:, g:g + 1], None, ALU.mult)
        kb.dma("sp", YAB[:, 4:8, TP:T], YBb[:, :, 0:64])
        po_rows = O["pool_s"][l].rearrange("s i c -> (s i) c")
        for hh in range(2):
            kb.cp(TMPH[:].rearrange("p g (s t) -> p g s t", t=15), PBs[:, :, hh * 8:(hh + 1) * 8, 4:19], eng="dve")
            bk = kb.bank()
            for g in range(4):
                kb.mm(bk[0:120, g * 128:(g + 1) * 128], lhsT=TMPH[:, g, :], rhs=cf("ident"))
            kb.cp(POUT[0:120, :], bk[0:120, 0:512], eng="dve")
            kb.dma("sp", po_rows[hh * 120:(hh + 1) * 120, :], POUT[0:120, :])

    def pass_b(l):
        ar.reset()
        WGT = ar.alloc([8, 2048], BF16)
        WBA = ar.alloc([4, 1024], BF16)
        WBB = ar.alloc([4, 1024], BF16)
        WOT = ar.alloc([8, 1024], BF16)
        wg_ = I["w_gate"][l].rearrange("(k p) n -> p k n", p=128)
        wa_ = I["w_br_a"][l].rearrange("(j p) n -> p j n", p=128)
        wb_ = I["w_br_b"][l].rearrange("(j p) n -> p j n", p=128)
        for m in range(0, 8, 2):
            c0, c1 = m * 128, (m + 2) * 128
            kb.dma("pool", WGT[:, :, c0:c1], wg_[:, :, c0:c1])
            kb.dma("pool", WGT[:, :, 1024 + c0:1024 + c1], wg_[:, :, 1024 + c0:1024 + c1])
            kb.dma("pool", WBA[:, :, c0:c1], wa_[:, :, c0:c1])
            kb.dma("pool", WBB[:, :, c0:c1], wb_[:, :, c0:c1])
        kb.dma("pool", WOT, I["w_out"][l].rearrange("(k p) n -> p k n", p=128))
        YT = ar.alloc([8, 512], BF16)
        UB = ar.alloc([8, 512], BF16)
        SQ = ar.alloc([8, 512], BF16)
        RS = ar.alloc([512], F32)
        TT = [ar.alloc([512], F32) for _ in range(2)]
        GA = ar.alloc([512], F32); GB = ar.alloc([512], F32); T1 = ar.alloc([512], F32); T2 = ar.alloc([512], F32)
        MG = ar.alloc([8, 512], BF16)
        BGv = vec("b_gate")
        for (t0, n) in TILES:
            kb.dma("sp", YT[:, :, 0:n], YAB[:, :, t0:t0 + n])
            modnorm(t0, n, 1, UB, SQ, RS, TT)
            for m in range(8):
                ba = kb.bank(); bb = kb.bank(); bc_ = kb.bank(); bd = kb.bank()
                for k in range(8):
                    kb.mm(ba[:, 0:n], lhsT=WGT[:, k, m * 128:(m + 1) * 128], rhs=UB[:, k, 0:n], start=(k == 0), stop=(k == 7))
                for k in range(8):
                    kb.mm(bb[:, 0:n], lhsT=WGT[:, k, 1024 + m * 128:1024 + (m + 1) * 128], rhs=UB[:, k, 0:n], start=(k == 0), stop=(k == 7))
                for j in range(4):
                    kb.mm(bc_[:, 0:n], lhsT=WBA[:, j, m * 128:(m + 1) * 128], rhs=YT[:, j, 0:n], start=(j == 0), stop=(j == 3))
                for j in range(4):
                    kb.mm(bd[:, 0:n], lhsT=WBB[:, j, m * 128:(m + 1) * 128], rhs=YT[:, 4 + j, 0:n], start=(j == 0), stop=(j == 3))
                kb.act(GA[:, 0:n], ba[:, 0:n], AF.Sigmoid, bias=BGv[:, m:m + 1], scale=1.0)
                kb.act(GB[:, 0:n], bb[:, 0:n], AF.Sigmoid, bias=BGv[:, 8 + m:9 + m], scale=1.0)
                kb.tt(T1[:, 0:n], bc_[:, 0:n], GA[:, 0:n], ALU.mult)
                kb.tt(T2[:, 0:n], bd[:, 0:n], GB[:, 0:n], ALU.mult)
                kb.tt(MG[:, m, 0:n], T1[:, 0:n], T2[:, 0:n], ALU.add)
            for m2 in range(8):
                bo = kb.bank()
                for m in range(8):
                    kb.mm(bo[:, 0:n], lhsT=WOT[:, m, m2 * 128:(m2 + 1) * 128], rhs=MG[:, m, 0:n], start=(m == 0), stop=(m == 7))
                resid(m2, t0, n, bo, 1)

    def mixer(l, only_a=False):
        ar.log = []
        pass_a(l)
        if only_a:
            DBG["passA_log"] = list(ar.log)
            return
        DBG["marks"].append(("pool%d" % l, sum(1 for o in kb.S.ops if o.eng == "pe"), len(kb.S.ops)))
        pass_pool(l)
        DBG["marks"].append(("passB%d" % l, sum(1 for o in kb.S.ops if o.eng == "pe"), len(kb.S.ops)))
        pass_b(l)

    return mixer


def _shard_inputs(inputs):
    maps = []
    shared = {n: np.ascontiguousarray(np.asarray(inputs[n], dtype=np.float32)) for n in WNAMES}
    xp = np.asarray(inputs["x_prompt"], dtype=np.float32)
    xs = np.asarray(inputs["x_sample"], dtype=np.float32)
    cp_ = np.asarray(inputs["c_prompt"], dtype=np.float32)
    cs = np.asarray(inputs["c_sample"], dtype=np.float32)
    swkv = np.asarray(inputs["state_wkv"], dtype=np.float32)
    ssh = np.asarray(inputs["state_shift"], dtype=np.float32)
    spl = np.asarray(inputs["state_pool"], dtype=np.float32)
    for i in range(NCORES):
        sl = slice(NSEQ * i, NSEQ * (i + 1))
        m = dict(shared)
        m["xin"] = np.ascontiguousarray(np.concatenate([xp[i], xs[sl].reshape(TS, D)], axis=0))
        m["cin"] = np.ascontiguousarray(np.concatenate([cp_[i:i + 1], cs[sl]], axis=0))
        m["swkv"] = np.ascontiguousarray(swkv[:, sl])
        m["sshift"] = np.ascontiguousarray(ssh[:, sl, 0, :])
        m["spool"] = np.ascontiguousarray(spl[:, sl])
        m["consts"] = CONSTS_NP
        maps.append(m)
    return maps


_NC_CACHE = {}
DBG = {}


def kernel(**inputs):
    if "nc" not in _NC_CACHE:
        _NC_CACHE["nc"] = build_program()
    nc = _NC_CACHE["nc"]
    maps = _shard_inputs(inputs)
    res = run_bass_kernel_spmd(nc, maps, core_ids=list(range(NCORES)))
    R = res.results
    y_p = np.stack([R[i]["y"][:TP] for i in range(NCORES)], axis=0)
    y_s = np.concatenate([R[i]["y"][TP:].reshape(NSEQ, DEC, D) for i in range(NCORES)], axis=0)
    wkv_p = np.stack([R[i]["wkv_p"] for i in range(NCORES)], axis=1)
    shift_p = np.stack([R[i]["shift_p"] for i in range(NCORES)], axis=1)[:, :, None, :]
    pool_p = np.stack([R[i]["pool_p"] for i in range(NCORES)], axis=1)
    wkv_s = np.concatenate([R[i]["wkv_s"] for i in range(NCORES)], axis=1)
    shift_s = np.concatenate([R[i]["shift_s"] for i in range(NCORES)], axis=1)[:, :, None, :]
    pool_s = np.concatenate([R[i]["pool_s"] for i in range(NCORES)], axis=1)
    f = lambda a: np.ascontiguousarray(a, dtype=np.float32)
    return (f(y_p), f(y_s), f(wkv_p), f(shift_p), f(pool_p), f(wkv_s), f(shift_s), f(pool_s))
```

```python
import numpy as np
from contextlib import ExitStack
import concourse.bass as bass
import concourse.mybir as mybir
from concourse.bass_utils import run_bass_kernel_spmd

F32 = mybir.dt.float32
BF16 = mybir.dt.bfloat16
ALU = mybir.AluOpType
AF = mybir.ActivationFunctionType
AX = mybir.AxisListType


class _Op:
    __slots__ = ("idx", "eng", "fn", "deps", "dma", "inc", "incval", "sem", "waits", "ring_prev", "gidx")

    def __init__(self, idx, eng, fn, deps, dma):
        self.idx, self.eng, self.fn, self.deps, self.dma = idx, eng, fn, deps, dma
        self.inc = False
        self.incval = 0
        self.sem = None
        self.waits = []
        self.ring_prev = None


def _region(ap):
    t = ap.tensor
    name = ap.name
    space = str(ap.space)
    pat = ap.ap
    off = int(ap.offset)
    es = mybir.dt.size(ap.dtype)
    if space == "DRAM":
        lo = off
        hi = off + 1
        for st, cnt in pat:
            hi += abs(int(st)) * (int(cnt) - 1)
        return (name, 0, 1, lo * es, hi * es)
    if "PSUM" in space.upper():
        return (name, 0, 128, 0, 1 << 30)
    shp = list(t.shape)
    pstep = 1
    for s in shp[1:]:
        pstep *= int(s)
    p0 = off // pstep
    f0 = off % pstep
    st0, cnt0 = pat[0]
    if int(st0) == pstep or int(cnt0) == 1:
        npart = int(cnt0)
        rest = pat[1:]
    else:
        npart = 1
        rest = pat
    hi = f0 + 1
    for st, cnt in rest:
        hi += abs(int(st)) * (int(cnt) - 1)
    return (name, p0, p0 + npart, f0 * es, hi * es)


class Sched:
    COMPUTE = ("pe", "act", "dve", "pool")
    RING = 8

    def __init__(self, nc):
        self.nc = nc
        self.ops = []
        self.rec = {}
        self.nd = {"sp": 0, "act": 0, "pool": 0}

    def op(self, eng, fn, reads=(), writes=(), dma=False):
        idx = len(self.ops)
        deps = set()
        rr = [_region(a) for a in reads]
        ww = [_region(a) for a in writes]
        for (name, p0, p1, f0, f1) in rr:
            for r in self.rec.get(name, ()):
                if r[5] and r[0] < p1 and p0 < r[1] and r[2] < f1 and f0 < r[3]:
                    deps.add((r[4], "raw"))
        for (name, p0, p1, f0, f1) in ww:
            for r in self.rec.get(name, ()):
                if r[0] < p1 and p0 < r[1] and r[2] < f1 and f0 < r[3]:
                    deps.add((r[4], "waw" if r[5] else "war"))
        o = _Op(idx, eng, fn, deps, dma)
        self.ops.append(o)
        for (name, p0, p1, f0, f1) in ww:
            lst = self.rec.setdefault(name, [])
            lst[:] = [r for r in lst if not (p0 <= r[0] and r[1] <= p1 and f0 <= r[2] and r[3] <= f1)]
            lst.append([p0, p1, f0, f1, idx, True])
        for (name, p0, p1, f0, f1) in rr:
            lst = self.rec.setdefault(name, [])
            lst[:] = [r for r in lst if not ((not r[5]) and self.ops[r[4]].eng == eng
                                             and (not self.ops[r[4]].dma) and (not dma)
                                             and p0 <= r[0] and r[1] <= p1 and f0 <= r[2] and r[3] <= f1)]
            lst.append([p0, p1, f0, f1, idx, False])
        return o

    NSEM = 12
    CH = 512

    def lower(self, stack):
        nc = self.nc
        ops = self.ops
        for o in ops:
            need = []
            best = {}
            for (d, kind) in o.deps:
                p = ops[d]
                if p.dma:
                    need.append(d)
                elif o.dma or p.eng != o.eng or o.eng != "pe":
                    if p.eng not in best or best[p.eng] < d:
                        best[p.eng] = d
            need.extend(best.values())
            o.deps = need
            for d in need:
                ops[d].inc = True
        self.csem = {e: [stack.enter_context(nc.semaphore("s_%s%d" % (e, i))) for i in range(self.NSEM)]
                     for e in self.COMPUTE}
        self.rings = {q: [stack.enter_context(nc.semaphore("r_%s%d" % (q, i))) for i in range(self.RING)]
                      for q in ("sp", "act", "pool")}
        cnt = {e: 0 for e in self.COMPUTE}
        dk = {"sp": 0, "act": 0, "pool": 0}
        dma_final = {}
        for o in ops:
            if o.dma:
                k = dk[o.eng]
                dk[o.eng] += 1
                o.sem = self.rings[o.eng][k % self.RING]
                o.incval = 16 * (k // self.RING + 1)
                o.ring_prev = (o.sem, 16 * (k // self.RING)) if k >= self.RING else None
                dma_final[(o.eng, k % self.RING)] = (o.sem, o.incval)
                o.gidx = None
            elif o.inc:
                g = cnt[o.eng]
                cnt[o.eng] += 1
                epoch = g // self.CH
                o.sem = self.csem[o.eng][epoch % self.NSEM]
                o.incval = (epoch // self.NSEM) * self.CH + (g % self.CH) + 1
                o.gidx = g
        waited_c = {e: {} for e in ("pe", "act", "dve", "pool", "sp")}
        waited_d = {e: {} for e in ("pe", "act", "dve", "pool", "sp")}
        for o in ops:
            wl = []
            wd = waited_d[o.eng]
            wc = waited_c[o.eng]
            if o.dma and o.ring_prev is not None:
                sem, val = o.ring_prev
                if wd.get(id(sem), 0) < val:
                    wd[id(sem)] = val
                    wl.append((sem, val))
            for d in o.deps:
                p = ops[d]
                if p.dma:
                    if wd.get(id(p.sem), 0) < p.incval:
                        wd[id(p.sem)] = p.incval
                        wl.append((p.sem, p.incval))
                else:
                    if wc.get(p.eng, -1) < p.gidx:
                        wc[p.eng] = p.gidx
                        wl.append((p.sem, p.incval))
            o.waits = wl
        self.final_waits = list(dma_final.values())
        per = {e: [] for e in ("pe", "act", "dve", "pool", "sp")}
        for o in ops:
            per[o.eng].append(o)

        def run(engobj, lst, final=False):
            for o in lst:
                for (sem, val) in o.waits:
                    engobj.wait_ge(sem, val)
                ins = o.fn(engobj)
                if o.dma:
                    ins.then_inc(o.sem, 16)
                elif o.inc:
                    ins.then_inc(o.sem, 1)
            if final:
                for (sem, val) in self.final_waits:
                    engobj.wait_ge(sem, val)

        with nc.Block() as block:
            @block.tensor
            def _(e):
                run(e, per["pe"])

            @block.scalar
            def _(e):
                run(e, per["act"])

            @block.vector
            def _(e):
                run(e, per["dve"])

            @block.gpsimd
            def _(e):
                run(e, per["pool"])

            @block.sync
            def _(e):
                run(e, per["sp"], final=True)


NCORES = 8
D = 1024
KC = 8
TP = 2048
NSEQ = 16
DEC = 4
TS = NSEQ * DEC
T = TP + TS
APJ = 1824
PT = 2336
DFF = 2816
NFC = DFF // 128
LN_EPS = 64e-5
NORM_EPS = 1e-6
DECAY_K = float(np.exp(-0.5))

WNAMES = ["norm_g", "w_mod", "b_mod", "w_ffn_in", "w_ffn_out", "w_in", "mu_shift", "w0", "w2", "a0", "a2", "g2",
          "k_k", "k_a", "r_k", "ln_x_w", "ln_x_b", "w_pool", "pool_scale", "w_br_a", "w_br_b", "w_gate", "b_gate",
          "w_out", "final_g"]
WSHAPES = {
    "norm_g": [2, 3, D], "w_mod": [2, D, 9 * D], "b_mod": [2, 9 * D], "w_ffn_in": [2, 2, D, 2 * DFF],
    "w_ffn_out": [2, 2, DFF, D], "w_in": [2, D, PT], "mu_shift": [2, APJ], "w0": [2, 512], "w2": [2, 64, 512],
    "a0": [2, 512], "a2": [2, 64, 512], "g2": [2, 160, 512], "k_k": [2, 512], "k_a": [2, 512], "r_k": [2, 8, 64],
    "ln_x_w": [2, 512], "ln_x_b": [2, 512], "w_pool": [2, 4, 128, 128], "pool_scale": [2, 512],
    "w_br_a": [2, 512, D], "w_br_b": [2, 512, D], "w_gate": [2, D, 2 * D], "b_gate": [2, 2 * D], "w_out": [2, D, D],
    "final_g": [D],
}


def _make_consts():
    cols = {}
    parts = []
    pos = [0]

    def add(name, arr):
        a = np.zeros((128, arr.shape[1]), np.float32)
        a[:arr.shape[0]] = arr
        cols[name] = (pos[0], arr.shape[1])
        pos[0] += arr.shape[1]
        parts.append(a)

    p = np.arange(128)
    add("ident", np.eye(128, dtype=np.float32))
    add("ones", np.ones((128, 128), np.float32))
    same = (p[:, None] // 64) == (p[None, :] // 64)
    add("bones", same.astype(np.float32))
    s = p[:, None] % 64
    t = p[None, :] % 64
    add("msu64", (same & (s < t)).astype(np.float32))
    add("msuT64", (same & (s > t)).astype(np.float32))
    add("mu64", (same & (s <= t)).astype(np.float32))
    add("istack64", (p[:, None] % 64 == np.arange(64)[None, :]).astype(np.float32))
    add("tokmask64", (p[:, None] // 64 == np.arange(2)[None, :]).astype(np.float32))
    q = np.arange(8)
    same4 = (q[:, None] // 4) == (q[None, :] // 4)
    s4 = q[:, None] % 4
    t4 = q[None, :] % 4
    add("msu4", (same4 & (s4 < t4)).astype(np.float32))
    add("msuT4", (same4 & (s4 > t4)).astype(np.float32))
    add("mu4", (same4 & (s4 <= t4)).astype(np.float32))
    add("istack4", (q[:, None] % 4 == np.arange(8)[None, :]).astype(np.float32))
    add("tokmask4", (q[:, None] // 4 == np.arange(2)[None, :]).astype(np.float32))
    tt = np.arange(128)
    add("start64", np.broadcast_to((tt % 64 == 0).astype(np.float32)[None, :], (128, 128)).copy())
    add("nstart64", np.broadcast_to((tt % 64 != 0).astype(np.float32)[None, :], (128, 128)).copy())
    add("start4", np.broadcast_to((tt[:64] % 4 == 0).astype(np.float32)[None, :], (128, 64)).copy())
    add("nstart4", np.broadcast_to((tt[:64] % 4 != 0).astype(np.float32)[None, :], (128, 64)).copy())
    ratio = np.zeros((4, 15), np.float32)
    for g, w in enumerate((2, 4, 8, 16)):
        for i in range(15):
            ratio[g, i] = w / min(w, i + 1)
    add("ratio", np.broadcast_to(ratio.reshape(1, 60), (128, 60)).copy())
    return np.concatenate(parts, axis=1), cols


DBG = {}
CONSTS_NP, CCOLS = _make_consts()
NCC = CONSTS_NP.shape[1]

VR = {}
_r = 0
for _n, _k in (("norm_g", 24), ("mu", 15), ("w0", 4), ("a0", 4), ("k_k", 4), ("k_a", 4), ("r_k", 4), ("ln_w", 4),
               ("ln_b", 4), ("pool_scale", 4), ("b_gate", 16), ("final_g", 8)):
    VR[_n] = (_r, _k)
    _r += _k
NVR = _r


def _prod(s):
    r = 1
    for v in s:
        r *= int(v)
    return r


def _view(ap2, shape):
    if len(shape) == 1:
        return ap2
    names = "abcdef"[:len(shape)]
    kw = {names[i]: int(shape[i]) for i in range(len(shape))}
    return ap2.rearrange("p (%s) -> p %s" % (" ".join(names), " ".join(names)), **kw)


class Arena:
    def __init__(self, base_bf16, nelem):
        self.base = base_bf16
        self.n = nelem
        self.off = 0
        self.peak = 0
        self.log = []

    def reset(self, off=0):
        self.off = off

    def alloc(self, shape, dtype):
        n = _prod(shape)
        nb = n * 2 if dtype == F32 else n
        off = (self.off + 15) // 16 * 16
        assert off + nb <= self.n, "arena overflow: need %d have %d" % (off + nb, self.n)
        v = self.base[:, off:off + nb]
        if dtype == F32:
            v = v.bitcast(F32)
        self.off = off + nb
        self.peak = max(self.peak, self.off)
        self.log.append((off, tuple(shape), "f32" if dtype == F32 else "bf16"))
        return _view(v, shape)


class KB:
    def __init__(self, nc, S, banks):
        self.nc, self.S, self.banks = nc, S, banks
        self.bi = 0
        self.flip = 0
        self.sub = {}

    def bank(self):
        b = self.banks[self.bi % len(self.banks)]
        self.bi += 1
        return b

    def bank_of(self, ids):
        c = self.sub.get(ids, 0)
        self.sub[ids] = c + 1
        return self.banks[ids[c % len(ids)]]

    def mm(self, out, lhsT, rhs, start=True, stop=True):
        self.nsl = getattr(self, "nsl", 0) + (4 if lhsT.dtype == F32 else 1)
        self.S.op("pe", lambda e: e.matmul(out, lhsT=lhsT, rhs=rhs, start=start, stop=stop),
                  reads=[lhsT, rhs], writes=[out])

    def tr(self, out, in_, ident):
        self.nsl = getattr(self, "nsl", 0) + 1
        self.S.op("pe", lambda e: e.transpose(out, in_, ident), reads=[in_, ident], writes=[out])

    def dma(self, q, out, in_):
        self.S.op(q, lambda e: e.dma_start(out=out, in_=in_), reads=[in_], writes=[out], dma=True)

    def tt(self, out, in0, in1, op, eng="dve"):
        self.S.op(eng, lambda e: e.tensor_tensor(out=out, in0=in0, in1=in1, op=op), reads=[in0, in1], writes=[out])

    def ts(self, out, in0, s1, s2, op0, op1=None, eng="dve"):
        rd = [in0] + [s for s in (s1, s2) if not isinstance(s, (int, float)) and s is not None]
        if op1 is None:
            self.S.op(eng, lambda e: e.tensor_scalar(out=out, in0=in0, scalar1=s1, scalar2=None, op0=op0),
                      reads=rd, writes=[out])
        else:
            self.S.op(eng, lambda e: e.tensor_scalar(out=out, in0=in0, scalar1=s1, scalar2=s2, op0=op0, op1=op1),
                      reads=rd, writes=[out])

    def stt(self, out, in0, scalar, in1, op0, op1, eng="dve"):
        rd = [in0, in1] + ([] if isinstance(scalar, (int, float)) else [scalar])
        self.S.op(eng, lambda e: e.scalar_tensor_tensor(out=out, in0=in0, scalar=scalar, in1=in1, op0=op0, op1=op1),
                  reads=rd, writes=[out])

    def act(self, out, in_, func, bias=None, scale=None):
        rd = [in_] + [s for s in (bias, scale) if s is not None and not isinstance(s, (int, float))]
        kw = {}
        if bias is not None:
            kw["bias"] = bias
        if scale is not None:
            kw["scale"] = scale
        self.S.op("act", lambda e: e.activation(out=out, in_=in_, func=func, **kw), reads=rd, writes=[out])

    def cp(self, out, in_, eng=None):
        if eng is None:
            self.flip ^= 1
            eng = "act" if self.flip else "dve"
        if eng == "act":
            self.S.op("act", lambda e: e.activation(out=out, in_=in_, func=AF.Copy), reads=[in_], writes=[out])
        else:
            self.S.op(eng, lambda e: e.tensor_copy(out=out, in_=in_), reads=[in_], writes=[out])

    def recip(self, out, in_):
        self.S.op("dve", lambda e: e.reciprocal(out=out, in_=in_), reads=[in_], writes=[out])

    def memset(self, out, val, eng="dve"):
        self.S.op(eng, lambda e: e.memset(out, val), writes=[out])

    def reduce_sum(self, out, in_):
        self.S.op("dve", lambda e: e.tensor_reduce(out=out, in_=in_, axis=AX.X, op=ALU.add), reads=[in_], writes=[out])

    def scan(self, out, d0, d1, init):
        self.S.op("dve", lambda e: e.tensor_tensor_scan(out=out, data0=d0, data1=d1, initial=init, op0=ALU.mult,
                                                        op1=ALU.add), reads=[d0, d1], writes=[out])


def build_program(stop=None, dbg=False, nlayers=2):
    nc = bass.Bass("TRN2", target_bir_lowering=False)
    I = {}

    def din(name, shape):
        I[name] = nc.dram_tensor(name, list(shape), F32, kind="ExternalInput").ap()

    din("xin", [T, D]); din("cin", [17, D]); din("swkv", [2, NSEQ, 8, 64, 64]); din("sshift", [2, NSEQ, APJ])
    din("spool", [2, NSEQ, 15, 512]); din("consts", [128, NCC])
    for n in WNAMES:
        din(n, WSHAPES[n])
    O = {}

    def dout(name, shape):
        O[name] = nc.dram_tensor(name, list(shape), F32, kind="ExternalOutput").ap()

    dout("y", [T, D]); dout("wkv_p", [2, 8, 64, 64]); dout("shift_p", [2, APJ]); dout("pool_p", [2, 15, 512])
    dout("wkv_s", [2, NSEQ, 8, 64, 64]); dout("shift_s", [2, NSEQ, APJ]); dout("pool_s", [2, NSEQ, 15, 512])
    if dbg:
        dout("dbgX", [128, KC, T]); dout("dbgA", [128, 8, T])
    YAB = nc.dram_tensor("yab_scratch", [128, 8, T], BF16, kind="Internal").ap()

    ARN = 61696
    with ExitStack() as st:
        def sb(name, shape, dt):
            return st.enter_context(nc.sbuf_tensor(name, list(shape), dt))

        X = sb("X", [128, KC, T], F32)
        CF = sb("CF", [128, NCC], F32)
        CB = sb("CB", [128, NCC], BF16)
        ARt = sb("AR", [128, ARN], BF16)
        MOD = sb("MOD", [128, 72, 17], F32)
        VEC = sb("VEC", [128, NVR], F32)
        BM = sb("BM", [128, 72], F32)
        SCT = sb("SCT", [128, 8, 17], F32)
        SCTb = sb("SCTb", [128, 8, 17], BF16)
        GSp = sb("GSp", [128, 3, 8], F32); SHp = sb("SHp", [128, 3, 8], F32); COp = sb("COp", [128, 3, 8], F32)
        GSs = sb("GSs", [128, 3, 8, 16], F32); SHs = sb("SHs", [128, 3, 8, 16], F32); COs = sb("COs", [128, 3, 8, 16], F32)
        OMKA = sb("OMKA", [128, 4], F32)
        TMS = sb("TMS", [128, 64], F32)
        banks = [st.enter_context(nc.psum_tensor("ps%d" % i, [128, 512], F32)) for i in range(8)]
        S = Sched(nc)
        kb = KB(nc, S, banks)
        ar = Arena(ARt[:], ARN)

        def cf(name):
            c0, n = CCOLS[name]
            return CF[:, c0:c0 + n]

        def cb(name):
            c0, n = CCOLS[name]
            return CB[:, c0:c0 + n]

        def vec(name):
            r0, k = VR[name]
            return VEC[:, r0:r0 + k]

        kb.dma("sp", CF[:], I["consts"])
        kb.cp(CB[:], CF[:], eng="dve")

        ar.reset()
        XT = [ar.alloc([D], F32) for _ in range(2)]
        CROW = ar.alloc([D], F32)
        for i in range(17):
            n = 128 if i < 16 else 64
            xt = XT[i % 2]
            kb.dma("sp", xt[0:n, :], I["xin"][i * 128:i * 128 + n, :])
            for half in range(2):
                bk = kb.bank()
                for cc in range(4):
                    c = half * 4 + cc
                    kb.tr(bk[:, cc * 128:cc * 128 + n], xt[0:n, c * 128:(c + 1) * 128], cf("ident")[0:n, 0:n])
                kb.cp(X[:, half * 4:half * 4 + 4, i * 128:i * 128 + n], _view(bk[:, 0:512], [4, 128])[:, :, 0:n])
        kb.dma("sp", CROW[0:17, :], I["cin"])
        kb.act(CROW[0:17, :], CROW[0:17, :], AF.Silu)
        bk = kb.bank()
        for c in range(8):
            kb.mm(bk[:, c * 17:(c + 1) * 17], lhsT=CROW[0:17, c * 128:(c + 1) * 128], rhs=cf("ident")[0:17, 0:17])
        kb.cp(SCT[:], _view(bk[:, 0:136], [8, 17]), eng="dve")
        kb.cp(SCTb[:], SCT[:], eng="dve")

        def load_layer_vectors(l):
            ar.reset()
            ROWS = ar.alloc([128], F32)
            BMR = ar.alloc([128], F32)
            WM = [ar.alloc([8, 1024], BF16) for _ in range(2)]
            kb.memset(ROWS[:], 0.0)

            def rows(name, src):
                r0, k = VR[name]
                kb.dma("sp", ROWS[r0:r0 + k, :], src)

            rows("norm_g", I["norm_g"][l].rearrange("j (c p) -> (j c) p", p=128))
            r0, _ = VR["mu"]
            kb.dma("sp", ROWS[r0:r0 + 14, :], I["mu_shift"][l, 0:1792].rearrange("(c p) -> c p", p=128))
            kb.dma("sp", ROWS[r0 + 14:r0 + 15, 0:32], I["mu_shift"][l:l + 1, 1792:1824])
            for nm, src in (("w0", "w0"), ("a0", "a0"), ("k_k", "k_k"), ("k_a", "k_a"), ("ln_w", "ln_x_w"),
                            ("ln_b", "ln_x_b"), ("pool_scale", "pool_scale")):
                rows(nm, I[src][l].rearrange("(c p) -> c p", p=128))
            rows("r_k", I["r_k"][l].rearrange("(c h) k -> c (h k)", h=2))
            rows("b_gate", I["b_gate"][l].rearrange("(c p) -> c p", p=128))
            rows("final_g", I["final_g"].rearrange("(c p) -> c p", p=128))
            bk = kb.bank()
            kb.mm(bk[:, 0:NVR], lhsT=ROWS[0:NVR, :], rhs=cf("ident")[0:NVR, 0:NVR])
            kb.cp(VEC[:], bk[:, 0:NVR], eng="dve")
            kb.dma("sp", BMR[0:72, :], I["b_mod"][l].rearrange("(c p) -> c p", p=128))
            bk = kb.bank()
            kb.mm(bk[:, 0:72], lhsT=BMR[0:72, :], rhs=cf("ident")[0:72, 0:72])
            kb.cp(BM[:], bk[:, 0:72], eng="dve")
            kb.ts(OMKA[:], vec("k_a"), -1.0, 1.0, ALU.mult, ALU.add)
            wm = I["w_mod"][l].rearrange("(k p) n -> p k n", p=128)
            kb.dma("pool", WM[0][:], wm[:, :, 0:1024])
            bk = None
            for blk in range(9):
                if blk + 1 < 9:
                    kb.dma("pool", WM[(blk + 1) % 2][:], wm[:, :, (blk + 1) * 1024:(blk + 2) * 1024])
                w = WM[blk % 2]
                for oc in range(8):
                    mc = blk * 8 + oc
                    if mc % 24 == 0:
                        bk = kb.bank()
                    o = bk[:, (mc % 24) * 17:(mc % 24) * 17 + 17]
                    for k in range(8):
                        kb.mm(o, lhsT=w[:, k, oc * 128:(oc + 1) * 128], rhs=SCTb[:, k, :], start=(k == 0), stop=(k == 7))
                    if mc % 24 == 23:
                        g = mc // 24
                        kb.tt(MOD[:, g * 24:(g + 1) * 24, :], _view(bk[:, 0:408], [24, 17]),
                              BM[:, g * 24:(g + 1) * 24, None].to_broadcast([128, 24, 17]), ALU.add)
            MODv = MOD[:].rearrange("p (j k c) s -> p j k c s", j=3, k=3)
            NG = _view(vec("norm_g"), [3, 8])
            kb.ts(GSp[:], MODv[:, :, 1, :, 0], 1.0, None, ALU.add)
            kb.tt(GSp[:], GSp[:], NG, ALU.mult)
            kb.cp(SHp[:], MODv[:, :, 0, :, 0], eng="dve")
            kb.cp(COp[:], MODv[:, :, 2, :, 0], eng="dve")
            kb.ts(COp[:, 0, :], COp[:, 0, :], 0.5, None, ALU.mult)
            kb.ts(COp[:, 2, :], COp[:, 2, :], 0.5, None, ALU.mult)
            for j in range(3):
                kb.ts(GSs[:, j], MODv[:, j, 1, :, 1:17], 1.0, None, ALU.add)
                kb.tt(GSs[:, j], GSs[:, j], NG[:, j, :, None].to_broadcast([128, 8, 16]), ALU.mult)
                kb.cp(SHs[:, j], MODv[:, j, 0, :, 1:17], eng="dve")
                kb.ts(COs[:, j], MODv[:, j, 2, :, 1:17], (1.0 if j == 1 else 0.5), None, ALU.mult)

        def modnorm(tok0, n, j, U, SQ, RS, TT, bank=None):
            is_s = tok0 >= TP
            kb.act(SQ[:, :, 0:n], X[:, :, tok0:tok0 + n], AF.Square)
            bk = kb.bank() if bank is None else bank()
            for c in range(8):
                kb.mm(bk[:, 0:n], lhsT=cb("ones"), rhs=SQ[:, c, 0:n], start=(c == 0), stop=(c == 7))
            kb.act(RS[:, 0:n], bk[:, 0:n], AF.Sqrt, bias=NORM_EPS, scale=1.0 / D)
            kb.recip(RS[:, 0:n], RS[:, 0:n])
            for c in range(8):
                t = TT[c % 2]
                if not is_s:
                    kb.stt(t[:, 0:n], X[:, c, tok0:tok0 + n], GSp[:, j, c:c + 1], RS[:, 0:n], ALU.mult, ALU.mult)
                    kb.act(U[:, c, 0:n], t[:, 0:n], AF.Identity, bias=SHp[:, j, c:c + 1], scale=1.0)
                else:
                    tv = _view(t[:, 0:n], [16, 4])
                    kb.tt(tv, _view(X[:, c, tok0:tok0 + n], [16, 4]), GSs[:, j, c, :, None].to_broadcast([128, 16, 4]), ALU.mult)
                    kb.tt(t[:, 0:n], t[:, 0:n], RS[:, 0:n], ALU.mult)
                    kb.tt(_view(U[:, c, 0:n], [16, 4]), tv, SHs[:, j, c, :, None].to_broadcast([128, 16, 4]), ALU.add)

        def resid(m, tok0, n, bo, j):
            if tok0 < TP:
                kb.stt(X[:, m, tok0:tok0 + n], bo[:, 0:n], COp[:, j, m:m + 1], X[:, m, tok0:tok0 + n], ALU.mult, ALU.add)
            else:
                kb.tt(_view(TMS[:, 0:n], [16, 4]), _view(bo[:, 0:n], [16, 4]),
                      COs[:, j, m, :, None].to_broadcast([128, 16, 4]), ALU.mult)
                kb.tt(X[:, m, tok0:tok0 + n], X[:, m, tok0:tok0 + n], TMS[:, 0:n], ALU.add)

        TILES = [(0, 512), (512, 512), (1024, 512), (1536, 512), (2048, 64)]

        def ffn(l, f):
            j = 0 if f == 0 else 2
            ar.reset()
            U = ar.alloc([8, T], BF16)
            WG = [ar.alloc([8, 512], BF16) for _ in range(2)]
            WU = [ar.alloc([8, 512], BF16) for _ in range(2)]
            WO = [ar.alloc([4, 1024], BF16) for _ in range(2)]
            H = [ar.alloc([4, 512], BF16) for _ in range(2)]
            SG = [ar.alloc([512], BF16) for _ in range(2)]
            SQ = ar.alloc([8, 512], BF16)
            RS = ar.alloc([512], F32)
            TT = [ar.alloc([512], F32) for _ in range(2)]
            win = I["w_ffn_in"][l, f].rearrange("(k p) n -> p k n", p=128)
            wout = I["w_ffn_out"][l, f].rearrange("(j p) n -> p j n", p=128)
            groups = [(0, 4), (4, 4), (8, 4), (12, 4), (16, 4), (20, 2)]

            def load(gi):
                c0, ng = groups[gi]
                b = gi % 2
                if gi == 0:
                    for h2 in range(0, ng, 2):
                        kb.dma("pool", WG[b][:, :, h2 * 128:(h2 + 2) * 128], win[:, :, (c0 + h2) * 128:(c0 + h2 + 2) * 128])
                        kb.dma("pool", WU[b][:, :, h2 * 128:(h2 + 2) * 128], win[:, :, DFF + (c0 + h2) * 128:DFF + (c0 + h2 + 2) * 128])
                else:
                    kb.dma("pool", WG[b][:, :, 0:ng * 128], win[:, :, c0 * 128:(c0 + ng) * 128])
                    kb.dma("pool", WU[b][:, :, 0:ng * 128], win[:, :, DFF + c0 * 128:DFF + (c0 + ng) * 128])
                kb.dma("pool", WO[b][:, 0:ng, :], wout[:, c0:c0 + ng, :])

            load(0)
            hb = 0
            for gi, (c0, ng) in enumerate(groups):
                if gi + 1 < len(groups):
                    load(gi + 1)
                b = gi % 2
                for ti, (t0, n) in enumerate(TILES):
                    if gi == 0:
                        if ti == 0:
                            modnorm(t0, n, j, U[:, :, t0:t0 + n], SQ, RS, TT)
                        if ti + 1 < len(TILES):
                            t1, n1 = TILES[ti + 1]
                            modnorm(t1, n1, j, U[:, :, t1:t1 + n1], SQ, RS, TT)
                    h = H[hb % 2]
                    hb += 1
                    for jj in range(ng):
                        bg = kb.bank()
                        bu = kb.bank()
                        for k in range(8):
                            kb.mm(bg[:, 0:n], lhsT=WG[b][:, k, jj * 128:(jj + 1) * 128], rhs=U[:, k, t0:t0 + n],
                                  start=(k == 0), stop=(k == 7))
                        for k in range(8):
                            kb.mm(bu[:, 0:n], lhsT=WU[b][:, k, jj * 128:(jj + 1) * 128], rhs=U[:, k, t0:t0 + n],
                                  start=(k == 0), stop=(k == 7))
                        sg = SG[jj % 2]
                        kb.act(sg[:, 0:n], bg[:, 0:n], AF.Silu)
                        kb.tt(h[:, jj, 0:n], bu[:, 0:n], sg[:, 0:n], ALU.mult)
                    for m in range(8):
                        bo = kb.bank()
                        for jj in range(ng):
                            kb.mm(bo[:, 0:n], lhsT=WO[b][:, jj, m * 128:(m + 1) * 128], rhs=h[:, jj, 0:n],
                                  start=(jj == 0), stop=(jj == ng - 1))
                        resid(m, t0, n, bo, j)

        def final_out():
            ar.reset()
            YT = [ar.alloc([D], F32) for _ in range(2)]
            SQ = ar.alloc([8, 128], BF16)
            RS = ar.alloc([128], F32)
            YN = [ar.alloc([8, 128], F32) for _ in range(2)]
            FG = vec("final_g")
            for i in range(17):
                n = 128 if i < 16 else 64
                t0 = i * 128
                yn = YN[i % 2]
                kb.act(SQ[:, :, 0:n], X[:, :, t0:t0 + n], AF.Square)
                bk = kb.bank()
                for c in range(8):
                    kb.mm(bk[:, 0:n], lhsT=cb("ones"), rhs=SQ[:, c, 0:n], start=(c == 0), stop=(c == 7))
                kb.act(RS[:, 0:n], bk[:, 0:n], AF.Sqrt, bias=NORM_EPS, scale=1.0 / D)
                kb.recip(RS[:, 0:n], RS[:, 0:n])
                for c in range(8):
                    kb.stt(yn[:, c, 0:n], X[:, c, t0:t0 + n], FG[:, c:c + 1], RS[:, 0:n], ALU.mult, ALU.mult)
                yt = YT[i % 2]
                for half in range(2):
                    bk = kb.bank()
                    for cc in range(4):
                        c = half * 4 + cc
                        kb.tr(bk[0:n, cc * 128:(cc + 1) * 128], yn[:, c, 0:n], cf("ident"))
                    kb.cp(yt[0:n, half * 512:(half + 1) * 512], bk[0:n, 0:512])
                kb.dma("sp", O["y"][t0:t0 + n, :], yt[0:n, :])

        def dbg_dump_x():
            if dbg:
                kb.dma("sp", O["dbgX"], X[:])

        mixer = _make_mixer(nc, kb, ar, I, O, X, YAB, cf, cb, vec, modnorm, resid, OMKA, TILES, dbg)

        def mark(name):
            DBG.setdefault("marks", []).append((name, getattr(kb, "nsl", 0), len(S.ops)))

        DBG["marks"] = []
        for l in range(nlayers):
            mark("vec%d" % l)
            load_layer_vectors(l)
            mark("ffn%d0" % l)
            ffn(l, 0)
            if stop == "ffn0":
                break
            mark("mixer%d" % l)
            mixer(l, only_a=(stop == "passA"))
            if stop in ("mix0", "passA"):
                break
            mark("ffn%d1" % l)
            ffn(l, 1)
        mark("final")
        dbg_dump_x()
        final_out()
        S.lower(st)
        print("ops", len(S.ops), "arena peak KiB", ar.peak * 2 / 1024.0)
    return nc


def _make_mixer(nc, kb, ar, I, O, X, YAB, cf, cb, vec, modnorm, resid, OMKA, TILES, dbg):
    NB = 64

    def bc(ap, shape):
        return ap.to_broadcast(list(shape))

    def pass_a(l):
        ar.reset()
        WIN = ar.alloc([8, APJ], BF16)
        W2T = ar.alloc([512], BF16)
        A2T = ar.alloc([512], BF16)
        G2T = ar.alloc([2, 512], BF16)
        kb.memset(W2T[:], 0.0)
        kb.memset(A2T[:], 0.0)
        kb.dma("pool", WIN, I["w_in"][l].rearrange("(k p) n -> p k n", p=128)[:, :, 0:APJ])
        kb.dma("pool", W2T[0:64, :], I["w2"][l])
        kb.dma("pool", A2T[64:128, :], I["a2"][l])
        kb.dma("pool", G2T[:, 0, :], I["g2"][l, 0:128, :])
        kb.dma("pool", G2T[0:32, 1, :], I["g2"][l, 128:160, :])
        dead_lo = (ar.off + 15) // 16 * 16
        UB = ar.alloc([8, NB], BF16)
        SQn = ar.alloc([8, NB], BF16)
        RSn = ar.alloc([NB], F32)
        TTn = [ar.alloc([NB], F32) for _ in range(2)]
        PA = ar.alloc([15, 80], F32)
        XS2 = [ar.alloc([15, NB], F32) for _ in range(2)]
        LAST = ar.alloc([15], F32)
        SHS = ar.alloc([15, 16], F32)
        SHO = ar.alloc([15, 16], F32)
        LIN = ar.alloc([3, NB], BF16)
        f4 = lambda: ar.alloc([4, NB], F32)
        t_wd, t_p, t_pex, t_ip, t_aa, t_kk, t_t1, t_t2, t_k2 = [f4() for _ in range(9)]
        SQK = ar.alloc([4, NB], BF16)
        RKR = ar.alloc([4, NB], BF16)
        blk = lambda: ar.alloc([512], BF16)
        dead_hi = ar.off
        AT3 = [blk() for _ in range(3)]
        RT3 = [blk() for _ in range(3)]
        PC3 = [ar.alloc([64], F32) for _ in range(3)]
        GG3 = [f4() for _ in range(3)]
        BON3 = [f4() for _ in range(3)]
        BT2 = [blk() for _ in range(2)]; KT2 = [blk() for _ in range(2)]; BH2 = [blk() for _ in range(3)]
        KH2 = [blk() for _ in range(3)]; VB2 = [blk() for _ in range(3)]
        NM, NTM, QB, QTB = [blk() for _ in range(4)]
        AAK2 = [blk() for _ in range(2)]; ARB2 = [blk() for _ in range(2)]; ARK2 = [blk() for _ in range(2)]
        MM2 = [blk() for _ in range(2)]
        BHT = ar.alloc([4, 128], BF16)
        KHT = ar.alloc([4, 128], BF16)
        VT = ar.alloc([4, 64], BF16)
        ZT = ar.alloc([4, 64], BF16)
        UT = ar.alloc([4, 64], BF16)
        SQY = ar.alloc([4, 64], F32)
        YC = ar.alloc([4, 64], F32)
        YNB = ar.alloc([4, 128], BF16)
        ST1 = ar.alloc([4], F32); ST2 = ar.alloc([4], F32); STM = ar.alloc([4], F32); STV = ar.alloc([4], F32)
        YF = ar.alloc([4, NB], F32)
        p2a = f4()
        YAb = ar.alloc([4, NB], BF16)
        S32 = ar.alloc([4, 64], F32)
        SBF = [ar.alloc([4, 64], BF16) for _ in range(2)]
        SI = ar.alloc([8, 64], F32)
        SO = ar.alloc([4, 128], F32)
        SHT = ar.alloc([15, 128], F32)
        SROW = SHT[:].rearrange("p m c -> p (m c)")[:, 0:APJ]
        DBG["passA_kib"] = ar.off * 2 / 1024.0

        for t in AT3 + RT3 + BT2 + KT2 + BH2 + KH2 + VB2:
            kb.memset(t[:], 0.0)
        kb.memset(PA[:], 0.0)
        kb.memset(LAST[:], 0.0)
        kb.memset(XS2[0][:], 0.0)
        kb.memset(XS2[1][:], 0.0)
        kb.memset(S32[:], 0.0)
        kb.memset(SBF[0][:], 0.0)
        kb.memset(LIN[:], 0.0)
        kb.dma("sp", SROW[0:16, :], I["sshift"][l])
        kb.memset(SHS[:], 0.0)
        for half in range(2):
            bk = kb.bank()
            ms = range(0, 8) if half == 0 else range(8, 15)
            for m in ms:
                Mm = 128 if m < 14 else 32
                kb.mm(bk[0:Mm, (m % 8) * 16:(m % 8) * 16 + 16], lhsT=SROW[0:16, m * 128:m * 128 + Mm], rhs=cf("ident")[0:16, 0:16])
            if half == 0:
                kb.cp(SHS[:, 0:8, :], _view(bk[:, 0:128], [8, 16]), eng="dve")
            else:
                kb.cp(SHS[:, 8:14, :], _view(bk[:, 0:96], [6, 16]), eng="dve")
                kb.cp(SHS[0:32, 14, :], bk[0:32, 96:112], eng="dve")

        MU = vec("mu")
        sbi = [0]
        NBLK = TP // NB
        bankA0 = lambda: kb.bank_of((0, 1))
        bankA = lambda: kb.bank_of((2, 3))
        bankB = lambda: kb.bank_of((4, 5))
        bankC = lambda: kb.bank_of((6, 7))

        def geom(b):
            is_s = (b == NBLK)
            C = 4 if is_s else 64
            R = 2 * C
            NQ = 512 // R
            return is_s, C, R, NQ, NQ // 4, ("4" if is_s else "64"), b * NB

        def stage1a(b):
            is_s, C, R, NQ, nch, sfx, tok0 = geom(b)
            n = NB
            XS = XS2[b % 2]
            modnorm(tok0, n, 1, UB, SQn, RSn, TTn, bank=bankA0)
            yield
            for half in range(2):
                bk = bankA0()
                ms = range(0, 8) if half == 0 else range(8, 15)
                for m in ms:
                    Mm = 128 if m < 14 else 32
                    for k in range(8):
                        kb.mm(bk[0:Mm, (m % 8) * 64:(m % 8) * 64 + 64], lhsT=WIN[:, k, m * 128:m * 128 + Mm], rhs=UB[:, k, :],
                              start=(k == 0), stop=(k == 7))
                    if m % 2 == 1:
                        yield
                if not is_s:
                    if half == 0:
                        kb.cp(PA[:, 0:8, 1:65], _view(bk[:, 0:512], [8, 64]), eng="act")
                    else:
                        kb.cp(PA[:, 8:14, 1:65], _view(bk[:, 0:384], [6, 64]), eng="act")
                        kb.cp(PA[0:32, 14, 1:65], bk[0:32, 384:448], eng="act")
                else:
                    PAs = PA[:].rearrange("p m (s t) -> p m s t", t=5)
                    if half == 0:
                        kb.cp(PAs[:, 0:8, :, 1:5], bk[:, 0:512].rearrange("p (m s t) -> p m s t", m=8, t=4), eng="act")
                    else:
                        kb.cp(PAs[:, 8:14, :, 1:5], bk[:, 0:384].rearrange("p (m s t) -> p m s t", m=6, t=4), eng="act")
                        kb.cp(PAs[0:32, 14, :, 1:5], bk[0:32, 384:448].rearrange("p (s t) -> p s t", t=4), eng="act")
                yield
            if not is_s:
                kb.cp(PA[:, :, 0], LAST[:], eng="act")
                kb.tt(XS[:], PA[:, :, 0:64], PA[:, :, 1:65], ALU.subtract)
                yield
                kb.tt(XS[:], XS[:], bc(MU[:, :, None], [128, 15, 64]), ALU.mult)
                yield
                kb.tt(XS[:], XS[:], PA[:, :, 1:65], ALU.add)
                kb.cp(LAST[:], PA[:, :, 64], eng="act")
                yield
            else:
                PAs = PA[:].rearrange("p m (s t) -> p m s t", t=5)
                XSs = XS[:].rearrange("p m (s t) -> p m s t", t=4)
                kb.cp(PAs[:, :, :, 0], SHS[:], eng="dve")
                for m0, m1 in ((0, 8), (8, 15)):
                    kb.tt(XSs[:, m0:m1], PAs[:, m0:m1, :, 0:4], PAs[:, m0:m1, :, 1:5], ALU.subtract)
                yield
                kb.tt(XS[:], XS[:], bc(MU[:, :, None], [128, 15, 64]), ALU.mult)
                yield
                for m0, m1 in ((0, 8), (8, 15)):
                    kb.tt(XSs[:, m0:m1], XSs[:, m0:m1], PAs[:, m0:m1, :, 1:5], ALU.add)
                kb.cp(SHO[:], PAs[:, :, :, 4], eng="dve")
                yield
            if b == NBLK - 1:
                bk = bankA0()
                kb.mm(bk[0:15, 0:128], lhsT=LAST[:, 0:15], rhs=cf("ident"))
                kb.cp(SHT[0:15, 0, :], bk[0:15, 0:128], eng="dve")
                kb.dma("sp", O["shift_p"][l, 0:1792].rearrange("(c p) -> c p", p=128), SHT[0:14, 0, :])
                kb.dma("sp", O["shift_p"][l:l + 1, 1792:1824], SHT[14:15, 0, 0:32])
                yield
            if is_s:
                for g4 in range(4):
                    bk = bankA0()
                    ms = range(g4 * 4, min(g4 * 4 + 4, 15))
                    for m in ms:
                        Mm = 128 if m < 14 else 32
                        kb.mm(bk[0:16, (m % 4) * 128:(m % 4) * 128 + Mm], lhsT=SHO[0:Mm, m, :], rhs=cf("ident")[0:Mm, 0:Mm])
                    if g4 < 3:
                        kb.cp(SHT[0:16, g4 * 4:g4 * 4 + 4, :], _view(bk[0:16, 0:512], [4, 128]), eng="dve")
                    else:
                        kb.cp(SHT[0:16, 12:14, :], _view(bk[0:16, 0:256], [2, 128]), eng="dve")
                        kb.cp(SHT[0:16, 14, 0:32], bk[0:16, 256:288], eng="dve")
                    yield
                kb.dma("sp", O["shift_s"][l, :, 0:1792], SHT[0:16, 0:14, :].rearrange("p m c -> p (m c)"))
                kb.dma("sp", O["shift_s"][l, :, 1792:1824], SHT[0:16, 14, 0:32])
                yield

        def stage1a2(b):
            is_s, C, R, NQ, nch, sfx, tok0 = geom(b)
            n = NB
            XS = XS2[b % 2]
            AT, RT, PC, t_gg, t_bon = AT3[b % 3], RT3[b % 3], PC3[b % 3], GG3[b % 3], BON3[b % 3]
            BT, KT, BH, KH, VB = BT2[b % 2], KT2[b % 2], BH2[b % 3], KH2[b % 3], VB2[b % 3]
            if is_s:
                for t in (AT, RT, BT, KT, BH, KH, VB):
                    kb.memset(t[:], 0.0)
                yield
            xr, xk, xv = XS[:, 0:4, :], XS[:, 4:8, :], XS[:, 8:12, :]
            kb.act(LIN[0:64, 0, :], XS[0:64, 12, :], AF.Tanh)
            kb.cp(LIN[64:128, 0, :], XS[64:128, 12, :], eng="act")
            kb.act(LIN[:, 1, :], XS[:, 13, :], AF.Sigmoid)
            kb.act(LIN[0:32, 2, :], XS[0:32, 14, :], AF.Sigmoid)
            yield
            bw = bankA()
            for j in range(4):
                kb.mm(bw[:, j * 64:(j + 1) * 64], lhsT=W2T[:, j * 128:(j + 1) * 128], rhs=LIN[:, 0, :])
                kb.mm(bw[:, 256 + j * 64:256 + (j + 1) * 64], lhsT=A2T[:, j * 128:(j + 1) * 128], rhs=LIN[:, 0, :])
            bg = bankA()
            for j in range(4):
                kb.mm(bg[:, j * 64:(j + 1) * 64], lhsT=G2T[:, 0, j * 128:(j + 1) * 128], rhs=LIN[:, 1, :], start=True, stop=False)
                kb.mm(bg[:, j * 64:(j + 1) * 64], lhsT=G2T[0:32, 1, j * 128:(j + 1) * 128], rhs=LIN[0:32, 2, :], start=False, stop=True)
            yield
            kb.tt(t_t1[:], _view(bw[:, 0:256], [4, 64]), bc(vec("w0")[:, :, None], [128, 4, 64]), ALU.add)
            kb.tt(t_aa[:], _view(bw[:, 256:512], [4, 64]), bc(vec("a0")[:, :, None], [128, 4, 64]), ALU.add)
            kb.cp(t_gg[:], _view(bg[:, 0:256], [4, 64]), eng="act")
            yield
            kb.act(t_t1[:], t_t1[:], AF.Sigmoid)
            kb.act(t_aa[:], t_aa[:], AF.Sigmoid)
            yield
            kb.act(t_wd[:], t_t1[:], AF.Exp, scale=-DECAY_K)
            kb.tt(t_kk[:], xk, bc(vec("k_k")[:, :, None], [128, 4, 64]), ALU.mult)
            yield
            kb.act(SQK[:], t_kk[:], AF.Square)
            kb.tt(t_t1[:], t_wd[:], bc(cf("nstart" + sfx)[:, None, 0:64], [128, 4, 64]), ALU.mult)
            kb.tt(t_t2[:], t_wd[:], bc(cf("start" + sfx)[:, None, 0:64], [128, 4, 64]), ALU.mult)
            yield
            bs = bankA()
            for j in range(4):
                kb.mm(bs[:, j * 64:(j + 1) * 64], lhsT=cb("bones"), rhs=SQK[:, j, :])
            for j in range(4):
                kb.scan(t_p[:, j, :], t_t1[:, j, :], t_t2[:, j, :], 1.0)
            yield
            kb.act(t_t1[:], _view(bs[:, 0:256], [4, 64]), AF.Sqrt)
            kb.recip(t_ip[:], t_p[:])
            kb.recip(t_t2[:], t_wd[:])
            yield
            kb.tt(t_pex[:], t_p[:], t_t2[:], ALU.mult)
            kb.cp(PC[:, 0:NQ].rearrange("p (s j) -> p j s", j=4),
                  t_p[:].rearrange("p j (s c) -> p j s c", c=C)[:, :, :, C - 1], eng="dve")
            kb.ts(t_t1[:], t_t1[:], 1e-12, None, ALU.max)
            yield
            kb.recip(t_t1[:], t_t1[:])
            yield
            kb.tt(t_kk[:], t_kk[:], t_t1[:], ALU.mult)
            yield
            kb.tt(t_t2[:], t_kk[:], t_aa[:], ALU.mult)
            kb.tt(t_t1[:], t_aa[:], bc(vec("k_a")[:, :, None], [128, 4, 64]), ALU.mult)
            kb.stt(t_wd[:], t_kk[:], -1.0, t_pex[:], ALU.mult, ALU.mult)
            yield
            kb.tt(t_t2[:], t_t2[:], t_ip[:], ALU.mult)
            kb.tt(t_t1[:], t_t1[:], bc(OMKA[:, :, None], [128, 4, 64]), ALU.add)
            yield
            kb.tt(t_k2[:], xk, t_t1[:], ALU.mult)
            yield
            kb.tt(t_t1[:], xr, t_k2[:], ALU.mult)
            kb.tt(t_aa[:], t_k2[:], t_ip[:], ALU.mult)
            yield
            kb.tt(RKR[:], t_t1[:], bc(vec("r_k")[:, :, None], [128, 4, 64]), ALU.mult)
            yield
            brk = bankA()
            for j in range(4):
                kb.mm(brk[:, j * 64:(j + 1) * 64], lhsT=cb("bones"), rhs=RKR[:, j, :])
            PCq = PC[:, 0:NQ].rearrange("p (s j) -> p j s", j=4)
            for h in range(2):
                ps_ = slice(64 * h, 64 * h + 64)

                def dst(tile):
                    return tile[ps_, :].rearrange("p (s j r) -> p j s r", j=4, r=R)[:, :, :, h * C:(h + 1) * C]

                def src(t3):
                    return t3[ps_].rearrange("p j (s c) -> p j s c", c=C)

                kb.cp(dst(AT), src(t_wd), eng="act")
                kb.tt(dst(RT), src(xr), src(t_p), ALU.mult)
                yield
                kb.cp(dst(BT), src(t_t2), eng="act")
                kb.cp(dst(KT), src(t_aa), eng="act")
                kb.tt(dst(BH), src(t_t2), bc(PCq[ps_, :, :, None], [64, 4, nch, C]), ALU.mult)
                yield
                kb.tt(dst(KH), src(t_aa), bc(PCq[ps_, :, :, None], [64, 4, nch, C]), ALU.mult)
                kb.cp(dst(VB), src(xv), eng="act")
                yield
            kb.tt(t_bon[:], _view(brk[:, 0:256], [4, 64]), xv, ALU.mult)
            yield

        def stage1b(b):
            is_s, C, R, NQ, nch, sfx, tok0 = geom(b)
            AT, RT = AT3[b % 3], RT3[b % 3]
            BT, KT = BT2[b % 2], KT2[b % 2]
            AAK, ARB, ARK, MM = AAK2[b % 2], ARB2[b % 2], ARK2[b % 2], MM2[b % 2]
            msu, msuT, mu_ = cf("msu" + sfx), cf("msuT" + sfx), cf("mu" + sfx)
            if is_s:
                for t in (NM, NTM, QB, QTB, AAK, ARB, ARK, MM):
                    kb.memset(t[:], 0.0)
                yield
            Mq = lambda tile: tile[0:R, :].rearrange("p (q r) -> p q r", r=R)
            MqK = lambda tile: tile[:, :].rearrange("p (q r) -> p q r", r=R)
            Fq = MqK

            def prod(lt, rt, mask, out_t):
                bk_ = bankB()
                for q in range(NQ):
                    kb.mm(bk_[0:R, q * R:(q + 1) * R], lhsT=Fq(lt)[:, q, :], rhs=Fq(rt)[:, q, :])
                    if q % 8 == 7:
                        yield
                kb.tt(Mq(out_t), bk_[0:R, :].rearrange("p (q r) -> p q r", r=R), bc(mask[0:R, None, 0:R], [R, NQ, R]), ALU.mult)
                yield

            yield from prod(BT, AT, msu, NM)
            yield from prod(AT, BT, msuT, NTM)
            kb.tt(Mq(MM), Mq(NM), bc(cf("ident")[0:R, None, 0:R], [R, NQ, R]), ALU.add)
            yield
            nlev = 5 if not is_s else 1
            Q, QT = NM, NTM
            Qn, QTn = QB, QTB
            extra = [(KT, AT, msu, AAK), (BT, RT, mu_, ARB), (KT, RT, mu_, ARK)]
            for lev in range(nlev):
                last = (lev == nlev - 1)
                b2 = bankB()
                for q in range(NQ):
                    kb.mm(b2[0:R, q * R:(q + 1) * R], lhsT=MqK(Q)[:, q, :], rhs=MqK(QT)[:, q, :])
                    if q % 8 == 7:
                        yield
                kb.cp(QTn[0:R, :], b2[0:R, :], eng="act")
                yield
                if not last:
                    b1 = bankB()
                    for q in range(NQ):
                        kb.mm(b1[0:R, q * R:(q + 1) * R], lhsT=MqK(QT)[:, q, :], rhs=MqK(Q)[:, q, :])
                        if q % 8 == 7:
                            yield
                    kb.cp(Qn[0:R, :], b1[0:R, :], eng="act")
                    yield
                b3 = bankB()
                for q in range(NQ):
                    o3 = b3[0:R, q * R:(q + 1) * R]
                    kb.mm(o3, lhsT=MqK(QTn)[:, q, :], rhs=MqK(MM)[:, q, :], start=True, stop=False)
                    kb.mm(o3, lhsT=cb("ident")[:, 0:R], rhs=MqK(MM)[:, q, :], start=False, stop=True)
                    if q % 8 == 7:
                        yield
                kb.cp(MM[0:R, :], b3[0:R, :], eng="act")
                yield
                Q, QT, Qn, QTn = Qn, QTn, Q, QT
                if extra:
                    yield from prod(*extra.pop(0))
            while extra:
                yield from prod(*extra.pop(0))

        base_ctx = dict(BHT=BHT, KHT=KHT, VT=VT, ZT=ZT, UT=UT, SQY=SQY, YC=YC, YNB=YNB, ST1=ST1, ST2=ST2, STM=STM,
                        STV=STV, S32=S32, SI=SI, SO=SO, SBS=SBF[0], bank=bankC)

        def stage2(b, ctx=None, seqs=None, finish=True):
            is_s, C, R, NQ, nch, sfx, tok0 = geom(b)
            n = NB
            if ctx is None:
                ctx = base_ctx
            BHT, KHT, VT, ZT, UT, SQY, YC, YNB = (ctx[k] for k in ("BHT", "KHT", "VT", "ZT", "UT", "SQY", "YC", "YNB"))
            ST1, ST2, STM, STV, S32, SI, SO = (ctx[k] for k in ("ST1", "ST2", "STM", "STV", "S32", "SI", "SO"))
            bankC = ctx["bank"]
            if seqs is None:
                seqs = range(nch)
            AT, RT, PC, t_gg, t_bon = AT3[b % 3], RT3[b % 3], PC3[b % 3], GG3[b % 3], BON3[b % 3]
            BH, KH, VB = BH2[b % 3], KH2[b % 3], VB2[b % 3]
            AAK, ARB, ARK, MM = AAK2[b % 2], ARB2[b % 2], ARK2[b % 2], MM2[b % 2]
            istk, tokm = cb("istack" + sfx), cf("tokmask" + sfx)
            if is_s:
                for t in (BHT, KHT, VT, ZT, UT, YNB):
                    kb.memset(t[:], 0.0)
                yield
            KR = 128
            MqK = lambda tile: tile[:, :].rearrange("p (q r) -> p q r", r=R)
            Fq = MqK
            for gi in seqs:
                q0 = gi * 4
                if is_s:
                    seq = gi
                    kb.dma("sp", SI[0:64, :, :], I["swkv"][l, seq].rearrange("h v k -> v h k"))
                    sb_in = ctx["SBS"]
                    bk_ = bankC()
                    for j in range(4):
                        kb.mm(bk_[:, j * 64:(j + 1) * 64], lhsT=SI[0:64, 2 * j:2 * j + 2, :].rearrange("p h k -> p (h k)"),
                              rhs=cf("ident")[0:64, 0:64])
                    kb.cp(S32[:], _view(bk_[:, 0:256], [4, 64]), eng="dve")
                    kb.cp(sb_in[:], S32[:], eng="act")
                    yield
                else:
                    sb_in = SBF[sbi[0] % 2]
                    sb_out = SBF[(sbi[0] + 1) % 2]
                    sbi[0] += 1
                b_ = bankC()
                for j in range(4):
                    kb.mm(b_[0:R, j * 128:(j + 1) * 128], lhsT=Fq(BH)[:, q0 + j, :], rhs=cb("ident"))
                kb.cp(BHT[0:R], _view(b_[0:R, 0:512], [4, 128]), eng="act")
                yield
                b_ = bankC()
                for j in range(4):
                    kb.mm(b_[0:R, j * 128:(j + 1) * 128], lhsT=Fq(KH)[:, q0 + j, :], rhs=cb("ident"))
                kb.cp(KHT[0:R], _view(b_[0:R, 0:512], [4, 128]), eng="act")
                yield
                b_ = bankC()
                for j in range(4):
                    kb.mm(b_[0:R, j * 64:(j + 1) * 64], lhsT=Fq(VB)[:, q0 + j, :], rhs=cb("istack64"))
                kb.cp(VT[0:R], _view(b_[0:R, 0:256], [4, 64]), eng="act")
                yield
                bz = bankC()
                for j in range(4):
                    kb.mm(bz[0:R, j * 64:(j + 1) * 64], lhsT=MqK(AAK)[:, q0 + j, :], rhs=VT[0:KR, j, :], start=True, stop=False)
                    kb.mm(bz[0:R, j * 64:(j + 1) * 64], lhsT=Fq(AT)[:, q0 + j, :], rhs=sb_in[:, j, :], start=False, stop=True)
                kb.cp(ZT[0:R], _view(bz[0:R, 0:256], [4, 64]), eng="act")
                yield
                bu = bankC()
                for j in range(4):
                    kb.mm(bu[0:R, j * 64:(j + 1) * 64], lhsT=MqK(MM)[:, q0 + j, :], rhs=ZT[0:KR, j, :])
                kb.cp(UT[0:R], _view(bu[0:R, 0:256], [4, 64]), eng="dve")
                yield
                bs_ = bankC()
                for j in range(4):
                    o = bs_[:, j * 64:(j + 1) * 64]
                    kb.mm(o, lhsT=BHT[0:KR, j, :], rhs=UT[0:KR, j, :], start=True, stop=False)
                    kb.mm(o, lhsT=KHT[0:KR, j, :], rhs=VT[0:KR, j, :], start=False, stop=True)
                by = bankC()
                for j in range(4):
                    o = by[0:R, j * 64:(j + 1) * 64]
                    kb.mm(o, lhsT=Fq(RT)[:, q0 + j, :], rhs=sb_in[:, j, :], start=True, stop=False)
                    kb.mm(o, lhsT=MqK(ARB)[:, q0 + j, :], rhs=UT[0:KR, j, :], start=False, stop=False)
                    kb.mm(o, lhsT=MqK(ARK)[:, q0 + j, :], rhs=VT[0:KR, j, :], start=False, stop=True)
                kb.tt(S32[:], S32[:], bc(PC[:, q0:q0 + 4, None], [128, 4, 64]), ALU.mult)
                yield
                kb.tt(S32[:], S32[:], _view(bs_[:, 0:256], [4, 64]), ALU.add)
                kb.cp(YC[0:R], _view(by[0:R, 0:256], [4, 64]), eng="act")
                yield
                if not is_s:
                    kb.cp(sb_out[:], S32[:], eng="act")
                Yv = YC[0:R]
                kb.reduce_sum(ST1[0:R], Yv)
                yield
                kb.act(SQY[0:R], Yv, AF.Square)
                kb.ts(STM[0:R], ST1[0:R], 1.0 / 64, None, ALU.mult)
                yield
                kb.reduce_sum(ST2[0:R], SQY[0:R])
                kb.tt(STV[0:R], STM[0:R], STM[0:R], ALU.mult)
                yield
                kb.stt(STV[0:R], ST2[0:R], 1.0 / 64, STV[0:R], ALU.mult, ALU.subtract)
                kb.tt(YC[0:R], Yv, bc(STM[0:R, :, None], [R, 4, 64]), ALU.subtract)
                yield
                kb.act(STV[0:R], STV[0:R], AF.Sqrt, bias=LN_EPS)
                yield
                kb.recip(STV[0:R], STV[0:R])
                yield
                kb.tt(YC[0:R], YC[0:R], bc(STV[0:R, :, None], [R, 4, 64]), ALU.mult)
                yield
                for h in range(2):
                    kb.ts(YNB[0:R, :, h * 64:(h + 1) * 64], YC[0:R], tokm[0:R, h:h + 1], None, ALU.mult)
                yield
                bf_ = bankC()
                CW = max(C, 8)
                for j in range(4):
                    kb.mm(bf_[:, j * CW:(j + 1) * CW], lhsT=YNB[0:KR, j, :], rhs=istk[0:KR, 0:CW])
                kb.cp(YF[:, :, gi * C:(gi + 1) * C], _view(bf_[:, 0:4 * CW], [4, CW])[:, :, 0:C], eng="act")
                yield
                if is_s or b == NBLK - 1:
                    bo_ = bankC()
                    for j in range(4):
                        kb.mm(bo_[0:64, j * 128:(j + 1) * 128], lhsT=S32[:, j, :], rhs=cf("ident"))
                    kb.cp(SO[0:64], _view(bo_[0:64, 0:512], [4, 128]), eng="dve")
                    dst_ = O["wkv_s"][l, gi] if is_s else O["wkv_p"][l]
                    kb.dma("sp", dst_.rearrange("h v k -> v h k"), SO[0:64].rearrange("p j (h k) -> p (j h) k", h=2))
                    yield
            if not finish:
                return
            kb.tt(p2a[:], YF[:], bc(vec("ln_w")[:, :, None], [128, 4, 64]), ALU.mult)
            yield
            kb.tt(p2a[:], p2a[:], bc(vec("ln_b")[:, :, None], [128, 4, 64]), ALU.add)
            yield
            kb.tt(p2a[:], p2a[:], t_bon[:], ALU.add)
            yield
            kb.tt(YAb[:], p2a[:], t_gg[:], ALU.mult)
            kb.dma("sp", YAB[:, 0:4, tok0:tok0 + n], YAb[:])
            yield

        nblocks = NBLK + 1
        if DBG.get("nblk") is not None:
            nblocks = DBG["nblk"]
        for step in range(nblocks + 3):
            gens = []
            if step < nblocks:
                gens.append(stage1a(step))
            if 0 <= step - 1 < nblocks:
                gens.append(stage1a2(step - 1))
            if 0 <= step - 2 < nblocks:
                gens.append(stage1b(step - 2))
            if 0 <= step - 3 < nblocks:
                if step - 3 == NBLK:
                    ctxs = [base_ctx]
                    ar2 = Arena(ar.base[:, dead_lo:dead_hi], dead_hi - dead_lo)
                    for si in range(2):
                        c2 = dict(BHT=ar2.alloc([4, 128], BF16), KHT=ar2.alloc([4, 128], BF16), VT=ar2.alloc([4, 64], BF16),
                                  ZT=ar2.alloc([4, 64], BF16), UT=ar2.alloc([4, 64], BF16), SQY=ar2.alloc([4, 64], F32),
                                  YC=ar2.alloc([4, 64], F32), YNB=ar2.alloc([4, 128], BF16), ST1=ar2.alloc([4], F32),
                                  ST2=ar2.alloc([4], F32), STM=ar2.alloc([4], F32), STV=ar2.alloc([4], F32),
                                  S32=ar2.alloc([4, 64], F32), SI=ar2.alloc([8, 64], F32), SO=ar2.alloc([4, 128], F32),
                                  SBS=ar2.alloc([4, 64], BF16))
                        ctxs.append(c2)
                    ctxs[0]["bank"] = lambda: kb.bank_of((6, 7))
                    ctxs[1]["bank"] = lambda: kb.bank_of((0, 1, 2))
                    ctxs[2]["bank"] = lambda: kb.bank_of((3, 4, 5))
                    sq = [range(0, 6), range(6, 11), range(11, 16)]
                    sg = [stage2(NBLK, ctx=ctxs[i], seqs=sq[i], finish=False) for i in range(3)]
                    alive = list(sg)
                    while alive:
                        for g in list(alive):
                            try:
                                next(g)
                            except StopIteration:
                                alive.remove(g)
                    for _ in stage2(NBLK, ctx=base_ctx, seqs=[], finish=True):
                        pass
                    continue
                gens.append(stage2(step - 3))
            if DBG.get("no_interleave", False):
                for g in gens[::-1]:
                    for _ in g:
                        pass
                continue
            alive = list(gens)
            while alive:
                for g in list(alive):
                    try:
                        next(g)
                    except StopIteration:
                        alive.remove(g)

    def pass_pool(l):
        ar.reset()
        WPB = ar.alloc([8, 512], BF16)
        WPL = ar.alloc([4, 128], BF16)
        kb.dma("pool", WPB, I["w_in"][l].rearrange("(k p) n -> p k n", p=128)[:, :, APJ:PT])
        kb.dma("pool", WPL, I["w_pool"][l].rearrange("g c d -> c g d"))
        UB = ar.alloc([8, 512], BF16)
        SQ = ar.alloc([8, 512], BF16)
        RS = ar.alloc([512], F32)
        TT = [ar.alloc([512], F32) for _ in range(2)]
        PBH = ar.alloc([4, 527], F32)
        SA = ar.alloc([4, 527], F32)
        SB = ar.alloc([4, 527], F32)
        DP = ar.alloc([4, 512], BF16)
        YBb = ar.alloc([4, 512], BF16)
        PROW = ar.alloc([512], F32)
        POUT = ar.alloc([512], F32)
        TMPH = ar.alloc([4, 120], F32)
        PS = vec("pool_scale")
        WIN_ = (2, 4, 8, 16)
        kb.memset(PBH[:], 0.0)

        def wsum(x, sa, sb_, L, nd):
            def sl(v, g0, g1, a, b_):
                return v[:, g0:g1, a:b_] if nd == 3 else v[:, g0:g1, :, a:b_]
            kb.tt(sl(sa, 0, 4, 1, L), sl(x, 0, 4, 1, L), sl(x, 0, 4, 0, L - 1), ALU.add)
            kb.tt(sl(sb_, 1, 4, 3, L), sl(sa, 1, 4, 3, L), sl(sa, 1, 4, 1, L - 2), ALU.add)
            kb.tt(sl(sa, 2, 4, 7, L), sl(sb_, 2, 4, 7, L), sl(sb_, 2, 4, 3, L - 4), ALU.add)
            kb.tt(sl(sb_, 3, 4, 15, L), sl(sa, 3, 4, 15, L), sl(sa, 3, 4, 7, L - 8), ALU.add)
            return [sa, sb_, sa, sb_]

        for (t0, n) in TILES[:4]:
            modnorm(t0, n, 1, UB, SQ, RS, TT)
            for g in range(4):
                bk = kb.bank()
                for k in range(8):
                    kb.mm(bk[:, 0:n], lhsT=WPB[:, k, g * 128:(g + 1) * 128], rhs=UB[:, k, 0:n], start=(k == 0), stop=(k == 7))
                kb.cp(PBH[:, g, 15:15 + n], bk[:, 0:n], eng="act")
            L = 15 + n
            fin = wsum(PBH, SA, SB, L, 3)
            for g in range(4):
                if t0 == 0:
                    kb.tt(fin[g][:, g, 15:30], fin[g][:, g, 15:30], cf("ratio")[:, g * 15:(g + 1) * 15], ALU.mult)
                kb.stt(DP[:, g, 0:n], fin[g][:, g, 15:L], 1.0 / WIN_[g], PBH[:, g, 15:L], ALU.mult, ALU.subtract)
            for g in range(4):
                bk = kb.bank()
                kb.mm(bk[:, 0:n], lhsT=WPL[:, g, :], rhs=DP[:, g, 0:n])
                kb.ts(YBb[:, g, 0:n], bk[:, 0:n], PS[:, g:g + 1], None, ALU.mult)
            kb.dma("sp", YAB[:, 4:8, t0:t0 + n], YBb[:, :, 0:n])
            kb.cp(TMPH[:, :, 0:15], PBH[:, :, n:n + 15], eng="dve")
            kb.cp(PBH[:, :, 0:15], TMPH[:, :, 0:15], eng="dve")
        bk = kb.bank()
        for g in range(4):
            kb.mm(bk[0:15, g * 128:(g + 1) * 128], lhsT=TMPH[:, g, 0:15], rhs=cf("ident"))
        kb.cp(POUT[0:15, :], bk[0:15, 0:512], eng="dve")
        kb.dma("sp", O["pool_p"][l], POUT[0:15, :])
        PBs = PBH[:, :, 0:304].rearrange("p g (s t) -> p g s t", t=19)
        SAs = SA[:, :, 0:304].rearrange("p g (s t) -> p g s t", t=19)
        SBs = SB[:, :, 0:304].rearrange("p g (s t) -> p g s t", t=19)
        sp_rows = I["spool"][l].rearrange("s i c -> (s i) c")
        for hh in range(2):
            kb.dma("sp", PROW[0:120, :], sp_rows[hh * 120:(hh + 1) * 120, :])
            bk = kb.bank()
            for g in range(4):
                kb.mm(bk[:, g * 120:(g + 1) * 120], lhsT=PROW[0:120, g * 128:(g + 1) * 128], rhs=cf("ident")[0:120, 0:120])
            kb.cp(PBs[:, :, hh * 8:(hh + 1) * 8, 0:15], bk[:, 0:480].rearrange("p (g s t) -> p g s t", g=4, t=15), eng="dve")
        modnorm(TP, 64, 1, UB, SQ, RS, TT)
        bk = kb.bank()
        for g in range(4):
            for k in range(8):
                kb.mm(bk[:, g * 64:(g + 1) * 64], lhsT=WPB[:, k, g * 128:(g + 1) * 128], rhs=UB[:, k, 0:64], start=(k == 0), stop=(k == 7))
        kb.cp(PBs[:, :, :, 15:19], bk[:, 0:256].rearrange("p (g s t) -> p g s t", g=4, t=4), eng="act")
        fin = wsum(PBs, SAs, SBs, 19, 4)
        fv = [SAs, SBs, SAs, SBs]
        for g in range(4):
            kb.stt(_view(DP[:, g, 0:64], [16, 4]), fv[g][:, g, :, 15:19], 1.0 / WIN_[g], PBs[:, g, :, 15:19], ALU.mult, ALU.subtract)
        bk = kb.bank()
        for g in range(4):
            kb.mm(bk[:, g * 64:(g + 1) * 64], lhsT=WPL[:, g, :], rhs=DP[:, g, 0:64])
        for g in range(4):
            kb.ts(YBb[:, g, 0:64], bk[:, g * 64:(g + 1) * 64], PS[:, g:g + 1], None, ALU.mult)
        kb.dma("sp", YAB[:, 4:8, TP:T], YBb[:, :, 0:64])
        po_rows = O["pool_s"][l].rearrange("s i c -> (s i) c")
        for hh in range(2):
            kb.cp(TMPH[:].rearrange("p g (s t) -> p g s t", t=15), PBs[:, :, hh * 8:(hh + 1) * 8, 4:19], eng="dve")
            bk = kb.bank()
            for g in range(4):
                kb.mm(bk[0:120, g * 128:(g + 1) * 128], lhsT=TMPH[:, g, :], rhs=cf("ident"))
            kb.cp(POUT[0:120, :], bk[0:120, 0:512], eng="dve")
            kb.dma("sp", po_rows[hh * 120:(hh + 1) * 120, :], POUT[0:120, :])

    def pass_b(l):
        ar.reset()
        WGT = ar.alloc([8, 2048], BF16)
        WBA = ar.alloc([4, 1024], BF16)
        WBB = ar.alloc([4, 1024], BF16)
        WOT = ar.alloc([8, 1024], BF16)
        wg_ = I["w_gate"][l].rearrange("(k p) n -> p k n", p=128)
        wa_ = I["w_br_a"][l].rearrange("(j p) n -> p j n", p=128)
        wb_ = I["w_br_b"][l].rearrange("(j p) n -> p j n", p=128)
        for m in range(0, 8, 2):
            c0, c1 = m * 128, (m + 2) * 128
            kb.dma("pool", WGT[:, :, c0:c1], wg_[:, :, c0:c1])
            kb.dma("pool", WGT[:, :, 1024 + c0:1024 + c1], wg_[:, :, 1024 + c0:1024 + c1])
            kb.dma("pool", WBA[:, :, c0:c1], wa_[:, :, c0:c1])
            kb.dma("pool", WBB[:, :, c0:c1], wb_[:, :, c0:c1])
        kb.dma("pool", WOT, I["w_out"][l].rearrange("(k p) n -> p k n", p=128))
        YT = ar.alloc([8, 512], BF16)
        UB = ar.alloc([8, 512], BF16)
        SQ = ar.alloc([8, 512], BF16)
        RS = ar.alloc([512], F32)
        TT = [ar.alloc([512], F32) for _ in range(2)]
        GA = ar.alloc([512], F32); GB = ar.alloc([512], F32); T1 = ar.alloc([512], F32); T2 = ar.alloc([512], F32)
        MG = ar.alloc([8, 512], BF16)
        BGv = vec("b_gate")
        for (t0, n) in TILES:
            kb.dma("sp", YT[:, :, 0:n], YAB[:, :, t0:t0 + n])
            modnorm(t0, n, 1, UB, SQ, RS, TT)
            for m in range(8):
                ba = kb.bank(); bb = kb.bank(); bc_ = kb.bank(); bd = kb.bank()
                for k in range(8):
                    kb.mm(ba[:, 0:n], lhsT=WGT[:, k, m * 128:(m + 1) * 128], rhs=UB[:, k, 0:n], start=(k == 0), stop=(k == 7))
                for k in range(8):
                    kb.mm(bb[:, 0:n], lhsT=WGT[:, k, 1024 + m * 128:1024 + (m + 1) * 128], rhs=UB[:, k, 0:n], start=(k == 0), stop=(k == 7))
                for j in range(4):
                    kb.mm(bc_[:, 0:n], lhsT=WBA[:, j, m * 128:(m + 1) * 128], rhs=YT[:, j, 0:n], start=(j == 0), stop=(j == 3))
                for j in range(4):
                    kb.mm(bd[:, 0:n], lhsT=WBB[:, j, m * 128:(m + 1) * 128], rhs=YT[:, 4 + j, 0:n], start=(j == 0), stop=(j == 3))
                kb.act(GA[:, 0:n], ba[:, 0:n], AF.Sigmoid, bias=BGv[:, m:m + 1], scale=1.0)
                kb.act(GB[:, 0:n], bb[:, 0:n], AF.Sigmoid, bias=BGv[:, 8 + m:9 + m], scale=1.0)
                kb.tt(T1[:, 0:n], bc_[:, 0:n], GA[:, 0:n], ALU.mult)
                kb.tt(T2[:, 0:n], bd[:, 0:n], GB[:, 0:n], ALU.mult)
                kb.tt(MG[:, m, 0:n], T1[:, 0:n], T2[:, 0:n], ALU.add)
            for m2 in range(8):
                bo = kb.bank()
                for m in range(8):
                    kb.mm(bo[:, 0:n], lhsT=WOT[:, m, m2 * 128:(m2 + 1) * 128], rhs=MG[:, m, 0:n], start=(m == 0), stop=(m == 7))
                resid(m2, t0, n, bo, 1)

    def mixer(l, only_a=False):
        ar.log = []
        pass_a(l)
        if only_a:
            DBG["passA_log"] = list(ar.log)
            return
        DBG["marks"].append(("pool%d" % l, getattr(kb, "nsl", 0), len(kb.S.ops)))
        pass_pool(l)
        DBG["marks"].append(("passB%d" % l, getattr(kb, "nsl", 0), len(kb.S.ops)))
        pass_b(l)

    return mixer


def _shard_inputs(inputs):
    maps = []
    shared = {n: np.ascontiguousarray(np.asarray(inputs[n], dtype=np.float32)) for n in WNAMES}
    xp = np.asarray(inputs["x_prompt"], dtype=np.float32)
    xs = np.asarray(inputs["x_sample"], dtype=np.float32)
    cp_ = np.asarray(inputs["c_prompt"], dtype=np.float32)
    cs = np.asarray(inputs["c_sample"], dtype=np.float32)
    swkv = np.asarray(inputs["state_wkv"], dtype=np.float32)
    ssh = np.asarray(inputs["state_shift"], dtype=np.float32)
    spl = np.asarray(inputs["state_pool"], dtype=np.float32)
    for i in range(NCORES):
        sl = slice(NSEQ * i, NSEQ * (i + 1))
        m = dict(shared)
        m["xin"] = np.ascontiguousarray(np.concatenate([xp[i], xs[sl].reshape(TS, D)], axis=0))
        m["cin"] = np.ascontiguousarray(np.concatenate([cp_[i:i + 1], cs[sl]], axis=0))
        m["swkv"] = np.ascontiguousarray(swkv[:, sl])
        m["sshift"] = np.ascontiguousarray(ssh[:, sl, 0, :])
        m["spool"] = np.ascontiguousarray(spl[:, sl])
        m["consts"] = CONSTS_NP
        maps.append(m)
    return maps


_NC_CACHE = {}
DBG = {}


def kernel(**inputs):
    if "nc" not in _NC_CACHE:
        _NC_CACHE["nc"] = build_program()
    nc = _NC_CACHE["nc"]
    maps = _shard_inputs(inputs)
    res = run_bass_kernel_spmd(nc, maps, core_ids=list(range(NCORES)))
    R = res.results
    y_p = np.stack([R[i]["y"][:TP] for i in range(NCORES)], axis=0)
    y_s = np.concatenate([R[i]["y"][TP:].reshape(NSEQ, DEC, D) for i in range(NCORES)], axis=0)
    wkv_p = np.stack([R[i]["wkv_p"] for i in range(NCORES)], axis=1)
    shift_p = np.stack([R[i]["shift_p"] for i in range(NCORES)], axis=1)[:, :, None, :]
    pool_p = np.stack([R[i]["pool_p"] for i in range(NCORES)], axis=1)
    wkv_s = np.concatenate([R[i]["wkv_s"] for i in range(NCORES)], axis=1)
    shift_s = np.concatenate([R[i]["shift_s"] for i in range(NCORES)], axis=1)[:, :, None, :]
    pool_s = np.concatenate([R[i]["pool_s"] for i in range(NCORES)], axis=1)
    f = lambda a: np.ascontiguousarray(a, dtype=np.float32)
    return (f(y_p), f(y_s), f(wkv_p), f(shift_p), f(pool_p), f(wkv_s), f(shift_s), f(pool_s))
```

```python
import numpy as np
from contextlib import ExitStack
import concourse.bass as bass
import concourse.mybir as mybir
from concourse.bass_utils import run_bass_kernel_spmd

F32 = mybir.dt.float32
BF16 = mybir.dt.bfloat16
ALU = mybir.AluOpType
AF = mybir.ActivationFunctionType
AX = mybir.AxisListType


class _Op:
    __slots__ = ("idx", "eng", "fn", "deps", "dma", "inc", "incval", "sem", "waits", "ring_prev", "gidx")

    def __init__(self, idx, eng, fn, deps, dma):
        self.idx, self.eng, self.fn, self.deps, self.dma = idx, eng, fn, deps, dma
        self.inc = False
        self.incval = 0
        self.sem = None
        self.waits = []
        self.ring_prev = None


def _region(ap):
    t = ap.tensor
    name = ap.name
    space = str(ap.space)
    pat = ap.ap
    off = int(ap.offset)
    es = mybir.dt.size(ap.dtype)
    if space == "DRAM":
        lo = off
        hi = off + 1
        for st, cnt in pat:
            hi += abs(int(st)) * (int(cnt) - 1)
        return (name, 0, 1, lo * es, hi * es)
    if "PSUM" in space.upper():
        return (name, 0, 128, 0, 1 << 30)
    shp = list(t.shape)
    pstep = 1
    for s in shp[1:]:
        pstep *= int(s)
    p0 = off // pstep
    f0 = off % pstep
    st0, cnt0 = pat[0]
    if int(st0) == pstep or int(cnt0) == 1:
        npart = int(cnt0)
        rest = pat[1:]
    else:
        npart = 1
        rest = pat
    hi = f0 + 1
    for st, cnt in rest:
        hi += abs(int(st)) * (int(cnt) - 1)
    return (name, p0, p0 + npart, f0 * es, hi * es)


class Sched:
    COMPUTE = ("pe", "act", "dve", "pool")
    RING = 8

    def __init__(self, nc):
        self.nc = nc
        self.ops = []
        self.rec = {}
        self.nd = {"sp": 0, "act": 0, "pool": 0}

    def op(self, eng, fn, reads=(), writes=(), dma=False):
        idx = len(self.ops)
        deps = set()
        rr = [_region(a) for a in reads]
        ww = [_region(a) for a in writes]
        for (name, p0, p1, f0, f1) in rr:
            for r in self.rec.get(name, ()):
                if r[5] and r[0] < p1 and p0 < r[1] and r[2] < f1 and f0 < r[3]:
                    deps.add((r[4], "raw"))
        for (name, p0, p1, f0, f1) in ww:
            for r in self.rec.get(name, ()):
                if r[0] < p1 and p0 < r[1] and r[2] < f1 and f0 < r[3]:
                    deps.add((r[4], "waw" if r[5] else "war"))
        o = _Op(idx, eng, fn, deps, dma)
        self.ops.append(o)
        for (name, p0, p1, f0, f1) in ww:
            lst = self.rec.setdefault(name, [])
            lst[:] = [r for r in lst if not (p0 <= r[0] and r[1] <= p1 and f0 <= r[2] and r[3] <= f1)]
            lst.append([p0, p1, f0, f1, idx, True])
        for (name, p0, p1, f0, f1) in rr:
            lst = self.rec.setdefault(name, [])
            lst[:] = [r for r in lst if not ((not r[5]) and self.ops[r[4]].eng == eng
                                             and (not self.ops[r[4]].dma) and (not dma)
                                             and p0 <= r[0] and r[1] <= p1 and f0 <= r[2] and r[3] <= f1)]
            lst.append([p0, p1, f0, f1, idx, False])
        return o

    NSEM = 12
    CH = 512

    def lower(self, stack):
        nc = self.nc
        ops = self.ops
        for o in ops:
            need = []
            best = {}
            for (d, kind) in o.deps:
                p = ops[d]
                if p.dma:
                    need.append(d)
                elif o.dma or p.eng != o.eng or o.eng != "pe":
                    if p.eng not in best or best[p.eng] < d:
                        best[p.eng] = d
            need.extend(best.values())
            o.deps = need
            for d in need:
                ops[d].inc = True
        self.csem = {e: [stack.enter_context(nc.semaphore("s_%s%d" % (e, i))) for i in range(self.NSEM)]
                     for e in self.COMPUTE}
        self.rings = {q: [stack.enter_context(nc.semaphore("r_%s%d" % (q, i))) for i in range(self.RING)]
                      for q in ("sp", "act", "pool")}
        cnt = {e: 0 for e in self.COMPUTE}
        dk = {"sp": 0, "act": 0, "pool": 0}
        dma_final = {}
        for o in ops:
            if o.dma:
                k = dk[o.eng]
                dk[o.eng] += 1
                o.sem = self.rings[o.eng][k % self.RING]
                o.incval = 16 * (k // self.RING + 1)
                o.ring_prev = (o.sem, 16 * (k // self.RING)) if k >= self.RING else None
                dma_final[(o.eng, k % self.RING)] = (o.sem, o.incval)
                o.gidx = None
            elif o.inc:
                g = cnt[o.eng]
                cnt[o.eng] += 1
                epoch = g // self.CH
                o.sem = self.csem[o.eng][epoch % self.NSEM]
                o.incval = (epoch // self.NSEM) * self.CH + (g % self.CH) + 1
                o.gidx = g
        waited_c = {e: {} for e in ("pe", "act", "dve", "pool", "sp")}
        waited_d = {e: {} for e in ("pe", "act", "dve", "pool", "sp")}
        for o in ops:
            wl = []
            wd = waited_d[o.eng]
            wc = waited_c[o.eng]
            if o.dma and o.ring_prev is not None:
                sem, val = o.ring_prev
                if wd.get(id(sem), 0) < val:
                    wd[id(sem)] = val
                    wl.append((sem, val))
            for d in o.deps:
                p = ops[d]
                if p.dma:
                    if wd.get(id(p.sem), 0) < p.incval:
                        wd[id(p.sem)] = p.incval
                        wl.append((p.sem, p.incval))
                else:
                    if wc.get(p.eng, -1) < p.gidx:
                        wc[p.eng] = p.gidx
                        wl.append((p.sem, p.incval))
            o.waits = wl
        self.final_waits = list(dma_final.values())
        per = {e: [] for e in ("pe", "act", "dve", "pool", "sp")}
        for o in ops:
            per[o.eng].append(o)

        def run(engobj, lst, final=False):
            for o in lst:
                for (sem, val) in o.waits:
                    engobj.wait_ge(sem, val)
                ins = o.fn(engobj)
                if o.dma:
                    ins.then_inc(o.sem, 16)
                elif o.inc:
                    ins.then_inc(o.sem, 1)
            if final:
                for (sem, val) in self.final_waits:
                    engobj.wait_ge(sem, val)

        with nc.Block() as block:
            @block.tensor
            def _(e):
                run(e, per["pe"])

            @block.scalar
            def _(e):
                run(e, per["act"])

            @block.vector
            def _(e):
                run(e, per["dve"])

            @block.gpsimd
            def _(e):
                run(e, per["pool"])

            @block.sync
            def _(e):
                run(e, per["sp"], final=True)


NCORES = 8
D = 1024
KC = 8
TP = 2048
NSEQ = 16
DEC = 4
TS = NSEQ * DEC
T = TP + TS
APJ = 1824
PT = 2336
DFF = 2816
NFC = DFF // 128
LN_EPS = 64e-5
NORM_EPS = 1e-6
DECAY_K = float(np.exp(-0.5))

WNAMES = ["norm_g", "w_mod", "b_mod", "w_ffn_in", "w_ffn_out", "w_in", "mu_shift", "w0", "w2", "a0", "a2", "g2",
          "k_k", "k_a", "r_k", "ln_x_w", "ln_x_b", "w_pool", "pool_scale", "w_br_a", "w_br_b", "w_gate", "b_gate",
          "w_out", "final_g"]
WSHAPES = {
    "norm_g": [2, 3, D], "w_mod": [2, D, 9 * D], "b_mod": [2, 9 * D], "w_ffn_in": [2, 2, D, 2 * DFF],
    "w_ffn_out": [2, 2, DFF, D], "w_in": [2, D, PT], "mu_shift": [2, APJ], "w0": [2, 512], "w2": [2, 64, 512],
    "a0": [2, 512], "a2": [2, 64, 512], "g2": [2, 160, 512], "k_k": [2, 512], "k_a": [2, 512], "r_k": [2, 8, 64],
    "ln_x_w": [2, 512], "ln_x_b": [2, 512], "w_pool": [2, 4, 128, 128], "pool_scale": [2, 512],
    "w_br_a": [2, 512, D], "w_br_b": [2, 512, D], "w_gate": [2, D, 2 * D], "b_gate": [2, 2 * D], "w_out": [2, D, D],
    "final_g": [D],
}


def _make_consts():
    cols = {}
    parts = []
    pos = [0]

    def add(name, arr):
        a = np.zeros((128, arr.shape[1]), np.float32)
        a[:arr.shape[0]] = arr
        cols[name] = (pos[0], arr.shape[1])
        pos[0] += arr.shape[1]
        parts.append(a)

    p = np.arange(128)
    add("ident", np.eye(128, dtype=np.float32))
    add("ones", np.ones((128, 128), np.float32))
    same = (p[:, None] // 64) == (p[None, :] // 64)
    add("bones", same.astype(np.float32))
    s = p[:, None] % 64
    t = p[None, :] % 64
    add("msu64", (same & (s < t)).astype(np.float32))
    add("msuT64", (same & (s > t)).astype(np.float32))
    add("mu64", (same & (s <= t)).astype(np.float32))
    add("istack64", (p[:, None] % 64 == np.arange(64)[None, :]).astype(np.float32))
    add("tokmask64", (p[:, None] // 64 == np.arange(2)[None, :]).astype(np.float32))
    q = np.arange(8)
    same4 = (q[:, None] // 4) == (q[None, :] // 4)
    s4 = q[:, None] % 4
    t4 = q[None, :] % 4
    add("msu4", (same4 & (s4 < t4)).astype(np.float32))
    add("msuT4", (same4 & (s4 > t4)).astype(np.float32))
    add("mu4", (same4 & (s4 <= t4)).astype(np.float32))
    add("istack4", (q[:, None] % 4 == np.arange(8)[None, :]).astype(np.float32))
    add("tokmask4", (q[:, None] // 4 == np.arange(2)[None, :]).astype(np.float32))
    tt = np.arange(128)
    add("start64", np.broadcast_to((tt % 64 == 0).astype(np.float32)[None, :], (128, 128)).copy())
    add("nstart64", np.broadcast_to((tt % 64 != 0).astype(np.float32)[None, :], (128, 128)).copy())
    add("start4", np.broadcast_to((tt[:64] % 4 == 0).astype(np.float32)[None, :], (128, 64)).copy())
    add("nstart4", np.broadcast_to((tt[:64] % 4 != 0).astype(np.float32)[None, :], (128, 64)).copy())
    ratio = np.zeros((4, 15), np.float32)
    for g, w in enumerate((2, 4, 8, 16)):
        for i in range(15):
            ratio[g, i] = w / min(w, i + 1)
    add("ratio", np.broadcast_to(ratio.reshape(1, 60), (128, 60)).copy())
    return np.concatenate(parts, axis=1), cols


DBG = {}
CONSTS_NP, CCOLS = _make_consts()
NCC = CONSTS_NP.shape[1]

VR = {}
_r = 0
for _n, _k in (("norm_g", 24), ("mu", 15), ("w0", 4), ("a0", 4), ("k_k", 4), ("k_a", 4), ("r_k", 4), ("ln_w", 4),
               ("ln_b", 4), ("pool_scale", 4), ("b_gate", 16), ("final_g", 8)):
    VR[_n] = (_r, _k)
    _r += _k
NVR = _r


def _prod(s):
    r = 1
    for v in s:
        r *= int(v)
    return r


def _view(ap2, shape):
    if len(shape) == 1:
        return ap2
    names = "abcdef"[:len(shape)]
    kw = {names[i]: int(shape[i]) for i in range(len(shape))}
    return ap2.rearrange("p (%s) -> p %s" % (" ".join(names), " ".join(names)), **kw)


class Arena:
    def __init__(self, base_bf16, nelem):
        self.base = base_bf16
        self.n = nelem
        self.off = 0
        self.peak = 0
        self.log = []

    def reset(self, off=0):
        self.off = off

    def alloc(self, shape, dtype):
        n = _prod(shape)
        nb = n * 2 if dtype == F32 else n
        off = (self.off + 15) // 16 * 16
        assert off + nb <= self.n, "arena overflow: need %d have %d" % (off + nb, self.n)
        v = self.base[:, off:off + nb]
        if dtype == F32:
            v = v.bitcast(F32)
        self.off = off + nb
        self.peak = max(self.peak, self.off)
        self.log.append((off, tuple(shape), "f32" if dtype == F32 else "bf16"))
        return _view(v, shape)


class KB:
    def __init__(self, nc, S, banks):
        self.nc, self.S, self.banks = nc, S, banks
        self.bi = 0
        self.flip = 0
        self.sub = {}

    def bank(self):
        b = self.banks[self.bi % len(self.banks)]
        self.bi += 1
        return b

    def bank_of(self, ids):
        c = self.sub.get(ids, 0)
        self.sub[ids] = c + 1
        return self.banks[ids[c % len(ids)]]

    def mm(self, out, lhsT, rhs, start=True, stop=True):
        self.nsl = getattr(self, "nsl", 0) + (4 if lhsT.dtype == F32 else 1)
        self.S.op("pe", lambda e: e.matmul(out, lhsT=lhsT, rhs=rhs, start=start, stop=stop),
                  reads=[lhsT, rhs], writes=[out])

    def tr(self, out, in_, ident):
        self.nsl = getattr(self, "nsl", 0) + 1
        self.S.op("pe", lambda e: e.transpose(out, in_, ident), reads=[in_, ident], writes=[out])

    def dma(self, q, out, in_):
        self.S.op(q, lambda e: e.dma_start(out=out, in_=in_), reads=[in_], writes=[out], dma=True)

    def tt(self, out, in0, in1, op, eng="dve"):
        self.S.op(eng, lambda e: e.tensor_tensor(out=out, in0=in0, in1=in1, op=op), reads=[in0, in1], writes=[out])

    def ts(self, out, in0, s1, s2, op0, op1=None, eng="dve"):
        rd = [in0] + [s for s in (s1, s2) if not isinstance(s, (int, float)) and s is not None]
        if op1 is None:
            self.S.op(eng, lambda e: e.tensor_scalar(out=out, in0=in0, scalar1=s1, scalar2=None, op0=op0),
                      reads=rd, writes=[out])
        else:
            self.S.op(eng, lambda e: e.tensor_scalar(out=out, in0=in0, scalar1=s1, scalar2=s2, op0=op0, op1=op1),
                      reads=rd, writes=[out])

    def stt(self, out, in0, scalar, in1, op0, op1, eng="dve"):
        rd = [in0, in1] + ([] if isinstance(scalar, (int, float)) else [scalar])
        self.S.op(eng, lambda e: e.scalar_tensor_tensor(out=out, in0=in0, scalar=scalar, in1=in1, op0=op0, op1=op1),
                  reads=rd, writes=[out])

    def act(self, out, in_, func, bias=None, scale=None):
        rd = [in_] + [s for s in (bias, scale) if s is not None and not isinstance(s, (int, float))]
        kw = {}
        if bias is not None:
            kw["bias"] = bias
        if scale is not None:
            kw["scale"] = scale
        self.S.op("act", lambda e: e.activation(out=out, in_=in_, func=func, **kw), reads=rd, writes=[out])

    def cp(self, out, in_, eng=None):
        if eng is None:
            self.flip ^= 1
            eng = "act" if self.flip else "dve"
        if eng == "act":
            self.S.op("act", lambda e: e.activation(out=out, in_=in_, func=AF.Copy), reads=[in_], writes=[out])
        else:
            self.S.op(eng, lambda e: e.tensor_copy(out=out, in_=in_), reads=[in_], writes=[out])

    def recip(self, out, in_):
        self.S.op("dve", lambda e: e.reciprocal(out=out, in_=in_), reads=[in_], writes=[out])

    def memset(self, out, val, eng="dve"):
        self.S.op(eng, lambda e: e.memset(out, val), writes=[out])

    def reduce_sum(self, out, in_):
        self.S.op("dve", lambda e: e.tensor_reduce(out=out, in_=in_, axis=AX.X, op=ALU.add), reads=[in_], writes=[out])

    def scan(self, out, d0, d1, init):
        self.S.op("dve", lambda e: e.tensor_tensor_scan(out=out, data0=d0, data1=d1, initial=init, op0=ALU.mult,
                                                        op1=ALU.add), reads=[d0, d1], writes=[out])


def build_program(stop=None, dbg=False, nlayers=2):
    nc = bass.Bass("TRN2", target_bir_lowering=False)
    I = {}

    def din(name, shape):
        I[name] = nc.dram_tensor(name, list(shape), F32, kind="ExternalInput").ap()

    din("xin", [T, D]); din("cin", [17, D]); din("swkv", [2, NSEQ, 8, 64, 64]); din("sshift", [2, NSEQ, APJ])
    din("spool", [2, NSEQ, 15, 512]); din("consts", [128, NCC])
    for n in WNAMES:
        din(n, WSHAPES[n])
    O = {}

    def dout(name, shape):
        O[name] = nc.dram_tensor(name, list(shape), F32, kind="ExternalOutput").ap()

    dout("y", [T, D]); dout("wkv_p", [2, 8, 64, 64]); dout("shift_p", [2, APJ]); dout("pool_p", [2, 15, 512])
    dout("wkv_s", [2, NSEQ, 8, 64, 64]); dout("shift_s", [2, NSEQ, APJ]); dout("pool_s", [2, NSEQ, 15, 512])
    if dbg:
        dout("dbgX", [128, KC, T]); dout("dbgA", [128, 8, T])
    YAB = nc.dram_tensor("yab_scratch", [128, 8, T], BF16, kind="Internal").ap()

    ARN = 61696
    with ExitStack() as st:
        def sb(name, shape, dt):
            return st.enter_context(nc.sbuf_tensor(name, list(shape), dt))

        X = sb("X", [128, KC, T], F32)
        CF = sb("CF", [128, NCC], F32)
        CB = sb("CB", [128, NCC], BF16)
        ARt = sb("AR", [128, ARN], BF16)
        MOD = sb("MOD", [128, 72, 17], F32)
        VEC = sb("VEC", [128, NVR], F32)
        BM = sb("BM", [128, 72], F32)
        SCT = sb("SCT", [128, 8, 17], F32)
        SCTb = sb("SCTb", [128, 8, 17], BF16)
        GSp = sb("GSp", [128, 3, 8], F32); SHp = sb("SHp", [128, 3, 8], F32); COp = sb("COp", [128, 3, 8], F32)
        GSs = sb("GSs", [128, 3, 8, 16], F32); SHs = sb("SHs", [128, 3, 8, 16], F32); COs = sb("COs", [128, 3, 8, 16], F32)
        OMKA = sb("OMKA", [128, 4], F32)
        TMS = sb("TMS", [128, 64], F32)
        banks = [st.enter_context(nc.psum_tensor("ps%d" % i, [128, 512], F32)) for i in range(8)]
        S = Sched(nc)
        kb = KB(nc, S, banks)
        ar = Arena(ARt[:], ARN)

        def cf(name):
            c0, n = CCOLS[name]
            return CF[:, c0:c0 + n]

        def cb(name):
            c0, n = CCOLS[name]
            return CB[:, c0:c0 + n]

        def vec(name):
            r0, k = VR[name]
            return VEC[:, r0:r0 + k]

        kb.dma("sp", CF[:], I["consts"])
        kb.cp(CB[:], CF[:], eng="dve")

        ar.reset()
        XT = [ar.alloc([D], F32) for _ in range(2)]
        CROW = ar.alloc([D], F32)
        for i in range(17):
            n = 128 if i < 16 else 64
            xt = XT[i % 2]
            kb.dma("sp", xt[0:n, :], I["xin"][i * 128:i * 128 + n, :])
            for half in range(2):
                bk = kb.bank()
                for cc in range(4):
                    c = half * 4 + cc
                    kb.tr(bk[:, cc * 128:cc * 128 + n], xt[0:n, c * 128:(c + 1) * 128], cf("ident")[0:n, 0:n])
                kb.cp(X[:, half * 4:half * 4 + 4, i * 128:i * 128 + n], _view(bk[:, 0:512], [4, 128])[:, :, 0:n])
        kb.dma("sp", CROW[0:17, :], I["cin"])
        kb.act(CROW[0:17, :], CROW[0:17, :], AF.Silu)
        bk = kb.bank()
        for c in range(8):
            kb.mm(bk[:, c * 17:(c + 1) * 17], lhsT=CROW[0:17, c * 128:(c + 1) * 128], rhs=cf("ident")[0:17, 0:17])
        kb.cp(SCT[:], _view(bk[:, 0:136], [8, 17]), eng="dve")
        kb.cp(SCTb[:], SCT[:], eng="dve")

        def load_layer_vectors(l):
            ar.reset()
            ROWS = ar.alloc([128], F32)
            BMR = ar.alloc([128], F32)
            WM = [ar.alloc([8, 1024], BF16) for _ in range(2)]
            kb.memset(ROWS[:], 0.0)

            def rows(name, src):
                r0, k = VR[name]
                kb.dma("sp", ROWS[r0:r0 + k, :], src)

            rows("norm_g", I["norm_g"][l].rearrange("j (c p) -> (j c) p", p=128))
            r0, _ = VR["mu"]
            kb.dma("sp", ROWS[r0:r0 + 14, :], I["mu_shift"][l, 0:1792].rearrange("(c p) -> c p", p=128))
            kb.dma("sp", ROWS[r0 + 14:r0 + 15, 0:32], I["mu_shift"][l:l + 1, 1792:1824])
            for nm, src in (("w0", "w0"), ("a0", "a0"), ("k_k", "k_k"), ("k_a", "k_a"), ("ln_w", "ln_x_w"),
                            ("ln_b", "ln_x_b"), ("pool_scale", "pool_scale")):
                rows(nm, I[src][l].rearrange("(c p) -> c p", p=128))
            rows("r_k", I["r_k"][l].rearrange("(c h) k -> c (h k)", h=2))
            rows("b_gate", I["b_gate"][l].rearrange("(c p) -> c p", p=128))
            rows("final_g", I["final_g"].rearrange("(c p) -> c p", p=128))
            bk = kb.bank()
            kb.mm(bk[:, 0:NVR], lhsT=ROWS[0:NVR, :], rhs=cf("ident")[0:NVR, 0:NVR])
            kb.cp(VEC[:], bk[:, 0:NVR], eng="dve")
            kb.dma("sp", BMR[0:72, :], I["b_mod"][l].rearrange("(c p) -> c p", p=128))
            bk = kb.bank()
            kb.mm(bk[:, 0:72], lhsT=BMR[0:72, :], rhs=cf("ident")[0:72, 0:72])
            kb.cp(BM[:], bk[:, 0:72], eng="dve")
            kb.ts(OMKA[:], vec("k_a"), -1.0, 1.0, ALU.mult, ALU.add)
            wm = I["w_mod"][l].rearrange("(k p) n -> p k n", p=128)
            kb.dma("pool", WM[0][:], wm[:, :, 0:1024])
            bk = None
            for blk in range(9):
                if blk + 1 < 9:
                    kb.dma("pool", WM[(blk + 1) % 2][:], wm[:, :, (blk + 1) * 1024:(blk + 2) * 1024])
                w = WM[blk % 2]
                for oc in range(8):
                    mc = blk * 8 + oc
                    if mc % 24 == 0:
                        bk = kb.bank()
                    o = bk[:, (mc % 24) * 17:(mc % 24) * 17 + 17]
                    for k in range(8):
                        kb.mm(o, lhsT=w[:, k, oc * 128:(oc + 1) * 128], rhs=SCTb[:, k, :], start=(k == 0), stop=(k == 7))
                    if mc % 24 == 23:
                        g = mc // 24
                        kb.tt(MOD[:, g * 24:(g + 1) * 24, :], _view(bk[:, 0:408], [24, 17]),
                              BM[:, g * 24:(g + 1) * 24, None].to_broadcast([128, 24, 17]), ALU.add)
            MODv = MOD[:].rearrange("p (j k c) s -> p j k c s", j=3, k=3)
            NG = _view(vec("norm_g"), [3, 8])
            kb.ts(GSp[:], MODv[:, :, 1, :, 0], 1.0, None, ALU.add)
            kb.tt(GSp[:], GSp[:], NG, ALU.mult)
            kb.cp(SHp[:], MODv[:, :, 0, :, 0], eng="dve")
            kb.cp(COp[:], MODv[:, :, 2, :, 0], eng="dve")
            kb.ts(COp[:, 0, :], COp[:, 0, :], 0.5, None, ALU.mult)
            kb.ts(COp[:, 2, :], COp[:, 2, :], 0.5, None, ALU.mult)
            for j in range(3):
                kb.ts(GSs[:, j], MODv[:, j, 1, :, 1:17], 1.0, None, ALU.add)
                kb.tt(GSs[:, j], GSs[:, j], NG[:, j, :, None].to_broadcast([128, 8, 16]), ALU.mult)
                kb.cp(SHs[:, j], MODv[:, j, 0, :, 1:17], eng="dve")
                kb.ts(COs[:, j], MODv[:, j, 2, :, 1:17], (1.0 if j == 1 else 0.5), None, ALU.mult)

        def modnorm(tok0, n, j, U, SQ, RS, TT, bank=None):
            is_s = tok0 >= TP
            kb.act(SQ[:, :, 0:n], X[:, :, tok0:tok0 + n], AF.Square)
            bk = kb.bank() if bank is None else bank()
            for c in range(8):
                kb.mm(bk[:, 0:n], lhsT=cb("ones"), rhs=SQ[:, c, 0:n], start=(c == 0), stop=(c == 7))
            kb.act(RS[:, 0:n], bk[:, 0:n], AF.Sqrt, bias=NORM_EPS, scale=1.0 / D)
            kb.recip(RS[:, 0:n], RS[:, 0:n])
            for c in range(8):
                t = TT[c % 2]
                if not is_s:
                    kb.stt(t[:, 0:n], X[:, c, tok0:tok0 + n], GSp[:, j, c:c + 1], RS[:, 0:n], ALU.mult, ALU.mult)
                    kb.act(U[:, c, 0:n], t[:, 0:n], AF.Identity, bias=SHp[:, j, c:c + 1], scale=1.0)
                else:
                    tv = _view(t[:, 0:n], [16, 4])
                    kb.tt(tv, _view(X[:, c, tok0:tok0 + n], [16, 4]), GSs[:, j, c, :, None].to_broadcast([128, 16, 4]), ALU.mult)
                    kb.tt(t[:, 0:n], t[:, 0:n], RS[:, 0:n], ALU.mult)
                    kb.tt(_view(U[:, c, 0:n], [16, 4]), tv, SHs[:, j, c, :, None].to_broadcast([128, 16, 4]), ALU.add)

        def modnorm_gen(tok0, n, j, U, SQ, RS, TT, bank):
            is_s = tok0 >= TP
            kb.act(SQ[:, :, 0:n], X[:, :, tok0:tok0 + n], AF.Square)
            yield
            bk = bank()
            for c in range(8):
                kb.mm(bk[:, 0:n], lhsT=cb("ones"), rhs=SQ[:, c, 0:n], start=(c == 0), stop=(c == 7))
            yield
            yield
            kb.act(RS[:, 0:n], bk[:, 0:n], AF.Sqrt, bias=NORM_EPS, scale=1.0 / D)
            yield
            kb.recip(RS[:, 0:n], RS[:, 0:n])
            yield
            for c in range(8):
                t = TT[c % 2]
                if not is_s:
                    kb.stt(t[:, 0:n], X[:, c, tok0:tok0 + n], GSp[:, j, c:c + 1], RS[:, 0:n], ALU.mult, ALU.mult)
                    if c > 0:
                        kb.act(U[:, c - 1, 0:n], TT[(c - 1) % 2][:, 0:n], AF.Identity, bias=SHp[:, j, c - 1:c], scale=1.0)
                    yield
                else:
                    tv = _view(t[:, 0:n], [16, 4])
                    kb.tt(tv, _view(X[:, c, tok0:tok0 + n], [16, 4]), GSs[:, j, c, :, None].to_broadcast([128, 16, 4]), ALU.mult)
                    kb.tt(t[:, 0:n], t[:, 0:n], RS[:, 0:n], ALU.mult)
                    kb.tt(_view(U[:, c, 0:n], [16, 4]), tv, SHs[:, j, c, :, None].to_broadcast([128, 16, 4]), ALU.add)
                    yield
            if not is_s:
                kb.act(U[:, 7, 0:n], TT[1][:, 0:n], AF.Identity, bias=SHp[:, j, 7:8], scale=1.0)
                yield

        def resid(m, tok0, n, bo, j):
            if tok0 < TP:
                kb.stt(X[:, m, tok0:tok0 + n], bo[:, 0:n], COp[:, j, m:m + 1], X[:, m, tok0:tok0 + n], ALU.mult, ALU.add)
            else:
                kb.tt(_view(TMS[:, 0:n], [16, 4]), _view(bo[:, 0:n], [16, 4]),
                      COs[:, j, m, :, None].to_broadcast([128, 16, 4]), ALU.mult)
                kb.tt(X[:, m, tok0:tok0 + n], X[:, m, tok0:tok0 + n], TMS[:, 0:n], ALU.add)

        TILES = [(0, 512), (512, 512), (1024, 512), (1536, 512), (2048, 64)]

        def ffn(l, f):
            j = 0 if f == 0 else 2
            ar.reset()
            U = ar.alloc([8, T], BF16)
            WG = [ar.alloc([8, 512], BF16) for _ in range(2)]
            WU = [ar.alloc([8, 512], BF16) for _ in range(2)]
            WO = [ar.alloc([4, 1024], BF16) for _ in range(2)]
            H = [ar.alloc([4, 512], BF16) for _ in range(2)]
            SG = [ar.alloc([512], BF16) for _ in range(2)]
            SQ = ar.alloc([8, 512], BF16)
            RS = ar.alloc([512], F32)
            TT = [ar.alloc([512], F32) for _ in range(2)]
            win = I["w_ffn_in"][l, f].rearrange("(k p) n -> p k n", p=128)
            wout = I["w_ffn_out"][l, f].rearrange("(j p) n -> p j n", p=128)
            groups = [(0, 4), (4, 4), (8, 4), (12, 4), (16, 4), (20, 2)]

            def load(gi):
                c0, ng = groups[gi]
                b = gi % 2
                if gi == 0:
                    for h2 in range(0, ng, 2):
                        kb.dma("pool", WG[b][:, :, h2 * 128:(h2 + 2) * 128], win[:, :, (c0 + h2) * 128:(c0 + h2 + 2) * 128])
                        kb.dma("pool", WU[b][:, :, h2 * 128:(h2 + 2) * 128], win[:, :, DFF + (c0 + h2) * 128:DFF + (c0 + h2 + 2) * 128])
                else:
                    kb.dma("pool", WG[b][:, :, 0:ng * 128], win[:, :, c0 * 128:(c0 + ng) * 128])
                    kb.dma("pool", WU[b][:, :, 0:ng * 128], win[:, :, DFF + c0 * 128:DFF + (c0 + ng) * 128])
                kb.dma("pool", WO[b][:, 0:ng, :], wout[:, c0:c0 + ng, :])

            load(0)
            hb = 0
            for gi, (c0, ng) in enumerate(groups):
                if gi + 1 < len(groups):
                    load(gi + 1)
                b = gi % 2
                for ti, (t0, n) in enumerate(TILES):
                    if gi == 0:
                        if ti == 0:
                            modnorm(t0, n, j, U[:, :, t0:t0 + n], SQ, RS, TT)
                        if ti + 1 < len(TILES):
                            t1, n1 = TILES[ti + 1]
                            modnorm(t1, n1, j, U[:, :, t1:t1 + n1], SQ, RS, TT)
                    h = H[hb % 2]
                    hb += 1
                    for jj in range(ng):
                        bg = kb.bank()
                        bu = kb.bank()
                        for k in range(8):
                            kb.mm(bg[:, 0:n], lhsT=WG[b][:, k, jj * 128:(jj + 1) * 128], rhs=U[:, k, t0:t0 + n],
                                  start=(k == 0), stop=(k == 7))
                        for k in range(8):
                            kb.mm(bu[:, 0:n], lhsT=WU[b][:, k, jj * 128:(jj + 1) * 128], rhs=U[:, k, t0:t0 + n],
                                  start=(k == 0), stop=(k == 7))
                        sg = SG[jj % 2]
                        kb.act(sg[:, 0:n], bg[:, 0:n], AF.Silu)
                        kb.tt(h[:, jj, 0:n], bu[:, 0:n], sg[:, 0:n], ALU.mult)
                    for m in range(8):
                        bo = kb.bank()
                        for jj in range(ng):
                            kb.mm(bo[:, 0:n], lhsT=WO[b][:, jj, m * 128:(m + 1) * 128], rhs=h[:, jj, 0:n],
                                  start=(jj == 0), stop=(jj == ng - 1))
                        resid(m, t0, n, bo, j)

        def final_out():
            ar.reset()
            YT = [ar.alloc([D], F32) for _ in range(2)]
            SQ = ar.alloc([8, 128], BF16)
            RS = ar.alloc([128], F32)
            YN = [ar.alloc([8, 128], F32) for _ in range(2)]
            FG = vec("final_g")
            for i in range(17):
                n = 128 if i < 16 else 64
                t0 = i * 128
                yn = YN[i % 2]
                kb.act(SQ[:, :, 0:n], X[:, :, t0:t0 + n], AF.Square)
                bk = kb.bank()
                for c in range(8):
                    kb.mm(bk[:, 0:n], lhsT=cb("ones"), rhs=SQ[:, c, 0:n], start=(c == 0), stop=(c == 7))
                kb.act(RS[:, 0:n], bk[:, 0:n], AF.Sqrt, bias=NORM_EPS, scale=1.0 / D)
                kb.recip(RS[:, 0:n], RS[:, 0:n])
                for c in range(8):
                    kb.stt(yn[:, c, 0:n], X[:, c, t0:t0 + n], FG[:, c:c + 1], RS[:, 0:n], ALU.mult, ALU.mult)
                yt = YT[i % 2]
                for half in range(2):
                    bk = kb.bank()
                    for cc in range(4):
                        c = half * 4 + cc
                        kb.tr(bk[0:n, cc * 128:(cc + 1) * 128], yn[:, c, 0:n], cf("ident"))
                    kb.cp(yt[0:n, half * 512:(half + 1) * 512], bk[0:n, 0:512])
                kb.dma("sp", O["y"][t0:t0 + n, :], yt[0:n, :])

        def dbg_dump_x():
            if dbg:
                kb.dma("sp", O["dbgX"], X[:])

        mixer = _make_mixer(nc, kb, ar, I, O, X, YAB, cf, cb, vec, modnorm, resid, OMKA, TILES, dbg, modnorm_gen)

        def mark(name):
            DBG.setdefault("marks", []).append((name, getattr(kb, "nsl", 0), len(S.ops)))

        DBG["marks"] = []
        for l in range(nlayers):
            mark("vec%d" % l)
            load_layer_vectors(l)
            mark("ffn%d0" % l)
            ffn(l, 0)
            if stop == "ffn0":
                break
            mark("mixer%d" % l)
            mixer(l, only_a=(stop == "passA"))
            if stop in ("mix0", "passA"):
                break
            mark("ffn%d1" % l)
            ffn(l, 1)
        mark("final")
        dbg_dump_x()
        final_out()
        S.lower(st)
        print("ops", len(S.ops), "arena peak KiB", ar.peak * 2 / 1024.0)
    return nc


def _make_mixer(nc, kb, ar, I, O, X, YAB, cf, cb, vec, modnorm, resid, OMKA, TILES, dbg, modnorm_gen=None):
    NB = 64

    def bc(ap, shape):
        return ap.to_broadcast(list(shape))

    def pass_a(l):
        ar.reset()
        WIN = ar.alloc([8, APJ], BF16)
        W2T = ar.alloc([512], BF16)
        A2T = ar.alloc([512], BF16)
        G2T = ar.alloc([2, 512], BF16)
        kb.memset(W2T[:], 0.0)
        kb.memset(A2T[:], 0.0)
        kb.dma("pool", WIN, I["w_in"][l].rearrange("(k p) n -> p k n", p=128)[:, :, 0:APJ])
        kb.dma("pool", W2T[0:64, :], I["w2"][l])
        kb.dma("pool", A2T[64:128, :], I["a2"][l])
        kb.dma("pool", G2T[:, 0, :], I["g2"][l, 0:128, :])
        kb.dma("pool", G2T[0:32, 1, :], I["g2"][l, 128:160, :])
        dead_lo = (ar.off + 15) // 16 * 16
        UB = ar.alloc([8, NB], BF16)
        SQn = ar.alloc([8, NB], BF16)
        RSn = ar.alloc([NB], F32)
        TTn = [ar.alloc([NB], F32) for _ in range(2)]
        PA = ar.alloc([15, 80], F32)
        XS2 = [ar.alloc([15, NB], F32) for _ in range(2)]
        LAST = ar.alloc([15], F32)
        SHS = ar.alloc([15, 16], F32)
        SHO = ar.alloc([15, 16], F32)
        LIN = ar.alloc([3, NB], BF16)
        f4 = lambda: ar.alloc([4, NB], F32)
        t_wd, t_p, t_pex, t_ip, t_aa, t_kk, t_t1, t_t2, t_k2 = [f4() for _ in range(9)]
        SQK = ar.alloc([4, NB], BF16)
        RKR = ar.alloc([4, NB], BF16)
        blk = lambda: ar.alloc([512], BF16)
        dead_hi = ar.off
        AT3 = [blk() for _ in range(3)]
        RT3 = [blk() for _ in range(3)]
        PC3 = [ar.alloc([64], F32) for _ in range(3)]
        GG3 = [f4() for _ in range(3)]
        BON3 = [f4() for _ in range(3)]
        BT2 = [blk() for _ in range(2)]; KT2 = [blk() for _ in range(2)]; BH2 = [blk() for _ in range(3)]
        KH2 = [blk() for _ in range(3)]; VB2 = [blk() for _ in range(3)]
        NM, NTM, QB, QTB = [blk() for _ in range(4)]
        AAK2 = [blk() for _ in range(2)]; ARB2 = [blk() for _ in range(2)]; ARK2 = [blk() for _ in range(2)]
        MM2 = [blk() for _ in range(2)]
        BHT = ar.alloc([4, 128], BF16)
        KHT = ar.alloc([4, 128], BF16)
        VT = ar.alloc([4, 64], BF16)
        ZT = ar.alloc([4, 64], BF16)
        UT = ar.alloc([4, 64], BF16)
        SQY = ar.alloc([4, 64], F32)
        YC = ar.alloc([4, 64], F32)
        YNB = ar.alloc([4, 128], BF16)
        ST1 = ar.alloc([4], F32); ST2 = ar.alloc([4], F32); STM = ar.alloc([4], F32); STV = ar.alloc([4], F32)
        YF = ar.alloc([4, NB], F32)
        p2a = f4()
        YAb = ar.alloc([4, NB], BF16)
        S32 = ar.alloc([4, 64], F32)
        SBF = [ar.alloc([4, 64], BF16) for _ in range(2)]
        SI = ar.alloc([8, 64], F32)
        SO = ar.alloc([4, 128], F32)
        SHT = ar.alloc([15, 128], F32)
        SROW = SHT[:].rearrange("p m c -> p (m c)")[:, 0:APJ]
        DBG["passA_kib"] = ar.off * 2 / 1024.0

        for t in AT3 + RT3 + BT2 + KT2 + BH2 + KH2 + VB2:
            kb.memset(t[:], 0.0)
        kb.memset(PA[:], 0.0)
        kb.memset(LAST[:], 0.0)
        kb.memset(XS2[0][:], 0.0)
        kb.memset(XS2[1][:], 0.0)
        kb.memset(S32[:], 0.0)
        kb.memset(SBF[0][:], 0.0)
        kb.memset(LIN[:], 0.0)
        kb.dma("sp", SROW[0:16, :], I["sshift"][l])
        kb.memset(SHS[:], 0.0)
        for half in range(2):
            bk = kb.bank()
            ms = range(0, 8) if half == 0 else range(8, 15)
            for m in ms:
                Mm = 128 if m < 14 else 32
                kb.mm(bk[0:Mm, (m % 8) * 16:(m % 8) * 16 + 16], lhsT=SROW[0:16, m * 128:m * 128 + Mm], rhs=cf("ident")[0:16, 0:16])
            if half == 0:
                kb.cp(SHS[:, 0:8, :], _view(bk[:, 0:128], [8, 16]), eng="dve")
            else:
                kb.cp(SHS[:, 8:14, :], _view(bk[:, 0:96], [6, 16]), eng="dve")
                kb.cp(SHS[0:32, 14, :], bk[0:32, 96:112], eng="dve")

        MU = vec("mu")
        sbi = [0]
        NBLK = TP // NB
        bankA0 = lambda: kb.bank_of((0, 1))
        bankA = lambda: kb.bank_of((2, 3))
        bankB = lambda: kb.bank_of((4, 5))
        bankC = lambda: kb.bank_of((6, 7))

        def geom(b):
            is_s = (b == NBLK)
            C = 4 if is_s else 64
            R = 2 * C
            NQ = 512 // R
            return is_s, C, R, NQ, NQ // 4, ("4" if is_s else "64"), b * NB

        def stage1a(b):
            is_s, C, R, NQ, nch, sfx, tok0 = geom(b)
            n = NB
            XS = XS2[b % 2]
            yield from modnorm_gen(tok0, n, 1, UB, SQn, RSn, TTn, bankA0)
            for half in range(2):
                bk = bankA0()
                ms = range(0, 8) if half == 0 else range(8, 15)
                for m in ms:
                    Mm = 128 if m < 14 else 32
                    for k in range(8):
                        kb.mm(bk[0:Mm, (m % 8) * 64:(m % 8) * 64 + 64], lhsT=WIN[:, k, m * 128:m * 128 + Mm], rhs=UB[:, k, :],
                              start=(k == 0), stop=(k == 7))
                    if m % 2 == 1:
                        yield
                yield
                if not is_s:
                    if half == 0:
                        kb.cp(PA[:, 0:8, 1:65], _view(bk[:, 0:512], [8, 64]), eng="act")
                    else:
                        kb.cp(PA[:, 8:14, 1:65], _view(bk[:, 0:384], [6, 64]), eng="act")
                        kb.cp(PA[0:32, 14, 1:65], bk[0:32, 384:448], eng="act")
                else:
                    PAs = PA[:].rearrange("p m (s t) -> p m s t", t=5)
                    if half == 0:
                        kb.cp(PAs[:, 0:8, :, 1:5], bk[:, 0:512].rearrange("p (m s t) -> p m s t", m=8, t=4), eng="act")
                    else:
                        kb.cp(PAs[:, 8:14, :, 1:5], bk[:, 0:384].rearrange("p (m s t) -> p m s t", m=6, t=4), eng="act")
                        kb.cp(PAs[0:32, 14, :, 1:5], bk[0:32, 384:448].rearrange("p (s t) -> p s t", t=4), eng="act")
                yield
            if not is_s:
                kb.cp(PA[:, :, 0], LAST[:], eng="act")
                kb.tt(XS[:], PA[:, :, 0:64], PA[:, :, 1:65], ALU.subtract)
                yield
                kb.tt(XS[:], XS[:], bc(MU[:, :, None], [128, 15, 64]), ALU.mult)
                yield
                kb.tt(XS[:], XS[:], PA[:, :, 1:65], ALU.add)
                kb.cp(LAST[:], PA[:, :, 64], eng="act")
                yield
            else:
                PAs = PA[:].rearrange("p m (s t) -> p m s t", t=5)
                XSs = XS[:].rearrange("p m (s t) -> p m s t", t=4)
                kb.cp(PAs[:, :, :, 0], SHS[:], eng="dve")
                for m0, m1 in ((0, 8), (8, 15)):
                    kb.tt(XSs[:, m0:m1], PAs[:, m0:m1, :, 0:4], PAs[:, m0:m1, :, 1:5], ALU.subtract)
                yield
                kb.tt(XS[:], XS[:], bc(MU[:, :, None], [128, 15, 64]), ALU.mult)
                yield
                for m0, m1 in ((0, 8), (8, 15)):
                    kb.tt(XSs[:, m0:m1], XSs[:, m0:m1], PAs[:, m0:m1, :, 1:5], ALU.add)
                kb.cp(SHO[:], PAs[:, :, :, 4], eng="dve")
                yield
            if b == NBLK - 1:
                bk = bankA0()
                kb.mm(bk[0:15, 0:128], lhsT=LAST[:, 0:15], rhs=cf("ident"))
                kb.cp(SHT[0:15, 0, :], bk[0:15, 0:128], eng="dve")
                kb.dma("sp", O["shift_p"][l, 0:1792].rearrange("(c p) -> c p", p=128), SHT[0:14, 0, :])
                kb.dma("sp", O["shift_p"][l:l + 1, 1792:1824], SHT[14:15, 0, 0:32])
                yield
            if is_s:
                for g4 in range(4):
                    bk = bankA0()
                    ms = range(g4 * 4, min(g4 * 4 + 4, 15))
                    for m in ms:
                        Mm = 128 if m < 14 else 32
                        kb.mm(bk[0:16, (m % 4) * 128:(m % 4) * 128 + Mm], lhsT=SHO[0:Mm, m, :], rhs=cf("ident")[0:Mm, 0:Mm])
                    if g4 < 3:
                        kb.cp(SHT[0:16, g4 * 4:g4 * 4 + 4, :], _view(bk[0:16, 0:512], [4, 128]), eng="dve")
                    else:
                        kb.cp(SHT[0:16, 12:14, :], _view(bk[0:16, 0:256], [2, 128]), eng="dve")
                        kb.cp(SHT[0:16, 14, 0:32], bk[0:16, 256:288], eng="dve")
                    yield
                kb.dma("sp", O["shift_s"][l, :, 0:1792], SHT[0:16, 0:14, :].rearrange("p m c -> p (m c)"))
                kb.dma("sp", O["shift_s"][l, :, 1792:1824], SHT[0:16, 14, 0:32])
                yield

        def stage1a2(b):
            is_s, C, R, NQ, nch, sfx, tok0 = geom(b)
            n = NB
            XS = XS2[b % 2]
            AT, RT, PC, t_gg, t_bon = AT3[b % 3], RT3[b % 3], PC3[b % 3], GG3[b % 3], BON3[b % 3]
            BT, KT, BH, KH, VB = BT2[b % 2], KT2[b % 2], BH2[b % 3], KH2[b % 3], VB2[b % 3]
            if is_s:
                for t in (AT, RT, BT, KT, BH, KH, VB):
                    kb.memset(t[:], 0.0)
                yield
            xr, xk, xv = XS[:, 0:4, :], XS[:, 4:8, :], XS[:, 8:12, :]
            kb.act(LIN[0:64, 0, :], XS[0:64, 12, :], AF.Tanh)
            kb.cp(LIN[64:128, 0, :], XS[64:128, 12, :], eng="act")
            kb.act(LIN[:, 1, :], XS[:, 13, :], AF.Sigmoid)
            kb.act(LIN[0:32, 2, :], XS[0:32, 14, :], AF.Sigmoid)
            yield
            bw = bankA()
            for j in range(4):
                kb.mm(bw[:, j * 64:(j + 1) * 64], lhsT=W2T[:, j * 128:(j + 1) * 128], rhs=LIN[:, 0, :])
                kb.mm(bw[:, 256 + j * 64:256 + (j + 1) * 64], lhsT=A2T[:, j * 128:(j + 1) * 128], rhs=LIN[:, 0, :])
            bg = bankA()
            for j in range(4):
                kb.mm(bg[:, j * 64:(j + 1) * 64], lhsT=G2T[:, 0, j * 128:(j + 1) * 128], rhs=LIN[:, 1, :], start=True, stop=False)
                kb.mm(bg[:, j * 64:(j + 1) * 64], lhsT=G2T[0:32, 1, j * 128:(j + 1) * 128], rhs=LIN[0:32, 2, :], start=False, stop=True)
            yield
            kb.tt(t_t1[:], _view(bw[:, 0:256], [4, 64]), bc(vec("w0")[:, :, None], [128, 4, 64]), ALU.add)
            kb.tt(t_aa[:], _view(bw[:, 256:512], [4, 64]), bc(vec("a0")[:, :, None], [128, 4, 64]), ALU.add)
            kb.cp(t_gg[:], _view(bg[:, 0:256], [4, 64]), eng="act")
            yield
            kb.act(t_t1[:], t_t1[:], AF.Sigmoid)
            kb.act(t_aa[:], t_aa[:], AF.Sigmoid)
            yield
            kb.act(t_wd[:], t_t1[:], AF.Exp, scale=-DECAY_K)
            kb.tt(t_kk[:], xk, bc(vec("k_k")[:, :, None], [128, 4, 64]), ALU.mult)
            yield
            kb.act(SQK[:], t_kk[:], AF.Square)
            kb.tt(t_t1[:], t_wd[:], bc(cf("nstart" + sfx)[:, None, 0:64], [128, 4, 64]), ALU.mult)
            kb.tt(t_t2[:], t_wd[:], bc(cf("start" + sfx)[:, None, 0:64], [128, 4, 64]), ALU.mult)
            yield
            bs = bankA()
            for j in range(4):
                kb.mm(bs[:, j * 64:(j + 1) * 64], lhsT=cb("bones"), rhs=SQK[:, j, :])
            for j in range(4):
                kb.scan(t_p[:, j, :], t_t1[:, j, :], t_t2[:, j, :], 1.0)
            yield
            kb.act(t_t1[:], _view(bs[:, 0:256], [4, 64]), AF.Sqrt)
            kb.recip(t_ip[:], t_p[:])
            kb.recip(t_t2[:], t_wd[:])
            yield
            kb.tt(t_pex[:], t_p[:], t_t2[:], ALU.mult)
            kb.cp(PC[:, 0:NQ].rearrange("p (s j) -> p j s", j=4),
                  t_p[:].rearrange("p j (s c) -> p j s c", c=C)[:, :, :, C - 1], eng="dve")
            kb.ts(t_t1[:], t_t1[:], 1e-12, None, ALU.max)
            yield
            kb.recip(t_t1[:], t_t1[:])
            yield
            kb.tt(t_kk[:], t_kk[:], t_t1[:], ALU.mult)
            yield
            kb.tt(t_t2[:], t_kk[:], t_aa[:], ALU.mult)
            kb.tt(t_t1[:], t_aa[:], bc(vec("k_a")[:, :, None], [128, 4, 64]), ALU.mult)
            kb.stt(t_wd[:], t_kk[:], -1.0, t_pex[:], ALU.mult, ALU.mult)
            yield
            kb.tt(t_t2[:], t_t2[:], t_ip[:], ALU.mult)
            kb.tt(t_t1[:], t_t1[:], bc(OMKA[:, :, None], [128, 4, 64]), ALU.add)
            yield
            kb.tt(t_k2[:], xk, t_t1[:], ALU.mult)
            yield
            kb.tt(t_t1[:], xr, t_k2[:], ALU.mult)
            kb.tt(t_aa[:], t_k2[:], t_ip[:], ALU.mult)
            yield
            kb.tt(RKR[:], t_t1[:], bc(vec("r_k")[:, :, None], [128, 4, 64]), ALU.mult)
            yield
            brk = bankA()
            for j in range(4):
                kb.mm(brk[:, j * 64:(j + 1) * 64], lhsT=cb("bones"), rhs=RKR[:, j, :])
            PCq = PC[:, 0:NQ].rearrange("p (s j) -> p j s", j=4)
            for h in range(2):
                ps_ = slice(64 * h, 64 * h + 64)

                def dst(tile):
                    return tile[ps_, :].rearrange("p (s j r) -> p j s r", j=4, r=R)[:, :, :, h * C:(h + 1) * C]

                def src(t3):
                    return t3[ps_].rearrange("p j (s c) -> p j s c", c=C)

                kb.cp(dst(AT), src(t_wd), eng="act")
                kb.tt(dst(RT), src(xr), src(t_p), ALU.mult)
                yield
                kb.cp(dst(BT), src(t_t2), eng="act")
                kb.cp(dst(KT), src(t_aa), eng="act")
                kb.tt(dst(BH), src(t_t2), bc(PCq[ps_, :, :, None], [64, 4, nch, C]), ALU.mult)
                yield
                kb.tt(dst(KH), src(t_aa), bc(PCq[ps_, :, :, None], [64, 4, nch, C]), ALU.mult)
                kb.cp(dst(VB), src(xv), eng="act")
                yield
            kb.tt(t_bon[:], _view(brk[:, 0:256], [4, 64]), xv, ALU.mult)
            yield

        def stage1b(b):
            is_s, C, R, NQ, nch, sfx, tok0 = geom(b)
            AT, RT = AT3[b % 3], RT3[b % 3]
            BT, KT = BT2[b % 2], KT2[b % 2]
            AAK, ARB, ARK, MM = AAK2[b % 2], ARB2[b % 2], ARK2[b % 2], MM2[b % 2]
            msu, msuT, mu_ = cf("msu" + sfx), cf("msuT" + sfx), cf("mu" + sfx)
            if is_s:
                for t in (NM, NTM, QB, QTB, AAK, ARB, ARK, MM):
                    kb.memset(t[:], 0.0)
                yield
            Mq = lambda tile: tile[0:R, :].rearrange("p (q r) -> p q r", r=R)
            MqK = lambda tile: tile[:, :].rearrange("p (q r) -> p q r", r=R)
            Fq = MqK

            def prod(lt, rt, mask, out_t):
                bk_ = bankB()
                for q in range(NQ):
                    kb.mm(bk_[0:R, q * R:(q + 1) * R], lhsT=Fq(lt)[:, q, :], rhs=Fq(rt)[:, q, :])
                    if q % 8 == 7:
                        yield
                yield
                kb.tt(Mq(out_t), bk_[0:R, :].rearrange("p (q r) -> p q r", r=R), bc(mask[0:R, None, 0:R], [R, NQ, R]), ALU.mult)
                yield

            yield from prod(BT, AT, msu, NM)
            yield from prod(AT, BT, msuT, NTM)
            kb.tt(Mq(MM), Mq(NM), bc(cf("ident")[0:R, None, 0:R], [R, NQ, R]), ALU.add)
            yield
            nlev = 5 if not is_s else 1
            Q, QT = NM, NTM
            Qn, QTn = QB, QTB
            extra = [(KT, AT, msu, AAK), (BT, RT, mu_, ARB), (KT, RT, mu_, ARK)]
            for lev in range(nlev):
                last = (lev == nlev - 1)
                b2 = bankB()
                for q in range(NQ):
                    kb.mm(b2[0:R, q * R:(q + 1) * R], lhsT=MqK(Q)[:, q, :], rhs=MqK(QT)[:, q, :])
                    if q % 8 == 7:
                        yield
                yield
                kb.cp(QTn[0:R, :], b2[0:R, :], eng="act")
                yield
                if not last:
                    b1 = bankB()
                    for q in range(NQ):
                        kb.mm(b1[0:R, q * R:(q + 1) * R], lhsT=MqK(QT)[:, q, :], rhs=MqK(Q)[:, q, :])
                        if q % 8 == 7:
                            yield
                    yield
                    kb.cp(Qn[0:R, :], b1[0:R, :], eng="act")
                    yield
                b3 = bankB()
                for q in range(NQ):
                    o3 = b3[0:R, q * R:(q + 1) * R]
                    kb.mm(o3, lhsT=MqK(QTn)[:, q, :], rhs=MqK(MM)[:, q, :], start=True, stop=False)
                    kb.mm(o3, lhsT=cb("ident")[:, 0:R], rhs=MqK(MM)[:, q, :], start=False, stop=True)
                    if q % 8 == 7:
                        yield
                yield
                kb.cp(MM[0:R, :], b3[0:R, :], eng="act")
                yield
                Q, QT, Qn, QTn = Qn, QTn, Q, QT
                if extra:
                    yield from prod(*extra.pop(0))
            while extra:
                yield from prod(*extra.pop(0))

        base_ctx = dict(BHT=BHT, KHT=KHT, VT=VT, ZT=ZT, UT=UT, SQY=SQY, YC=YC, YNB=YNB, ST1=ST1, ST2=ST2, STM=STM,
                        STV=STV, S32=S32, SI=SI, SO=SO, SBS=SBF[0], bank=bankC)

        def stage2(b, ctx=None, seqs=None, finish=True):
            is_s, C, R, NQ, nch, sfx, tok0 = geom(b)
            n = NB
            if ctx is None:
                ctx = base_ctx
            BHT, KHT, VT, ZT, UT, SQY, YC, YNB = (ctx[k] for k in ("BHT", "KHT", "VT", "ZT", "UT", "SQY", "YC", "YNB"))
            ST1, ST2, STM, STV, S32, SI, SO = (ctx[k] for k in ("ST1", "ST2", "STM", "STV", "S32", "SI", "SO"))
            bankC = ctx["bank"]
            if seqs is None:
                seqs = range(nch)
            AT, RT, PC, t_gg, t_bon = AT3[b % 3], RT3[b % 3], PC3[b % 3], GG3[b % 3], BON3[b % 3]
            BH, KH, VB = BH2[b % 3], KH2[b % 3], VB2[b % 3]
            AAK, ARB, ARK, MM = AAK2[b % 2], ARB2[b % 2], ARK2[b % 2], MM2[b % 2]
            istk, tokm = cb("istack" + sfx), cf("tokmask" + sfx)
            if is_s:
                for t in (BHT, KHT, VT, ZT, UT, YNB):
                    kb.memset(t[:], 0.0)
                yield
            KR = 128
            MqK = lambda tile: tile[:, :].rearrange("p (q r) -> p q r", r=R)
            Fq = MqK
            for gi in seqs:
                q0 = gi * 4
                if is_s:
                    seq = gi
                    kb.dma("sp", SI[0:64, :, :], I["swkv"][l, seq].rearrange("h v k -> v h k"))
                    sb_in = ctx["SBS"]
                    bk_ = bankC()
                    for j in range(4):
                        kb.mm(bk_[:, j * 64:(j + 1) * 64], lhsT=SI[0:64, 2 * j:2 * j + 2, :].rearrange("p h k -> p (h k)"),
                              rhs=cf("ident")[0:64, 0:64])
                    kb.cp(S32[:], _view(bk_[:, 0:256], [4, 64]), eng="dve")
                    kb.cp(sb_in[:], S32[:], eng="act")
                    yield
                else:
                    sb_in = SBF[sbi[0] % 2]
                    sb_out = SBF[(sbi[0] + 1) % 2]
                    sbi[0] += 1
                b_ = bankC()
                for j in range(4):
                    kb.mm(b_[0:R, j * 128:(j + 1) * 128], lhsT=Fq(BH)[:, q0 + j, :], rhs=cb("ident"))
                yield
                kb.cp(BHT[0:R], _view(b_[0:R, 0:512], [4, 128]), eng="act")
                yield
                b_ = bankC()
                for j in range(4):
                    kb.mm(b_[0:R, j * 128:(j + 1) * 128], lhsT=Fq(KH)[:, q0 + j, :], rhs=cb("ident"))
                yield
                kb.cp(KHT[0:R], _view(b_[0:R, 0:512], [4, 128]), eng="act")
                yield
                b_ = bankC()
                for j in range(4):
                    kb.mm(b_[0:R, j * 64:(j + 1) * 64], lhsT=Fq(VB)[:, q0 + j, :], rhs=cb("istack64"))
                yield
                kb.cp(VT[0:R], _view(b_[0:R, 0:256], [4, 64]), eng="act")
                yield
                bz = bankC()
                for j in range(4):
                    kb.mm(bz[0:R, j * 64:(j + 1) * 64], lhsT=MqK(AAK)[:, q0 + j, :], rhs=VT[0:KR, j, :], start=True, stop=False)
                    kb.mm(bz[0:R, j * 64:(j + 1) * 64], lhsT=Fq(AT)[:, q0 + j, :], rhs=sb_in[:, j, :], start=False, stop=True)
                yield
                kb.cp(ZT[0:R], _view(bz[0:R, 0:256], [4, 64]), eng="act")
                yield
                bu = bankC()
                for j in range(4):
                    kb.mm(bu[0:R, j * 64:(j + 1) * 64], lhsT=MqK(MM)[:, q0 + j, :], rhs=ZT[0:KR, j, :])
                yield
                kb.cp(UT[0:R], _view(bu[0:R, 0:256], [4, 64]), eng="dve")
                yield
                bs_ = bankC()
                for j in range(4):
                    o = bs_[:, j * 64:(j + 1) * 64]
                    kb.mm(o, lhsT=BHT[0:KR, j, :], rhs=UT[0:KR, j, :], start=True, stop=False)
                    kb.mm(o, lhsT=KHT[0:KR, j, :], rhs=VT[0:KR, j, :], start=False, stop=True)
                by = bankC()
                for j in range(4):
                    o = by[0:R, j * 64:(j + 1) * 64]
                    kb.mm(o, lhsT=Fq(RT)[:, q0 + j, :], rhs=sb_in[:, j, :], start=True, stop=False)
                    kb.mm(o, lhsT=MqK(ARB)[:, q0 + j, :], rhs=UT[0:KR, j, :], start=False, stop=False)
                    kb.mm(o, lhsT=MqK(ARK)[:, q0 + j, :], rhs=VT[0:KR, j, :], start=False, stop=True)
                kb.tt(S32[:], S32[:], bc(PC[:, q0:q0 + 4, None], [128, 4, 64]), ALU.mult)
                yield
                yield
                kb.tt(S32[:], S32[:], _view(bs_[:, 0:256], [4, 64]), ALU.add)
                kb.cp(YC[0:R], _view(by[0:R, 0:256], [4, 64]), eng="act")
                yield
                if not is_s:
                    kb.cp(sb_out[:], S32[:], eng="act")
                Yv = YC[0:R]
                kb.reduce_sum(ST1[0:R], Yv)
                yield
                kb.act(SQY[0:R], Yv, AF.Square)
                kb.ts(STM[0:R], ST1[0:R], 1.0 / 64, None, ALU.mult)
                yield
                kb.reduce_sum(ST2[0:R], SQY[0:R])
                kb.tt(STV[0:R], STM[0:R], STM[0:R], ALU.mult)
                yield
                kb.stt(STV[0:R], ST2[0:R], 1.0 / 64, STV[0:R], ALU.mult, ALU.subtract)
                kb.tt(YC[0:R], Yv, bc(STM[0:R, :, None], [R, 4, 64]), ALU.subtract)
                yield
                kb.act(STV[0:R], STV[0:R], AF.Sqrt, bias=LN_EPS)
                yield
                kb.recip(STV[0:R], STV[0:R])
                yield
                kb.tt(YC[0:R], YC[0:R], bc(STV[0:R, :, None], [R, 4, 64]), ALU.mult)
                yield
                for h in range(2):
                    kb.ts(YNB[0:R, :, h * 64:(h + 1) * 64], YC[0:R], tokm[0:R, h:h + 1], None, ALU.mult)
                yield
                bf_ = bankC()
                CW = max(C, 8)
                for j in range(4):
                    kb.mm(bf_[:, j * CW:(j + 1) * CW], lhsT=YNB[0:KR, j, :], rhs=istk[0:KR, 0:CW])
                yield
                kb.cp(YF[:, :, gi * C:(gi + 1) * C], _view(bf_[:, 0:4 * CW], [4, CW])[:, :, 0:C], eng="act")
                yield
                if is_s or b == NBLK - 1:
                    bo_ = bankC()
                    for j in range(4):
                        kb.mm(bo_[0:64, j * 128:(j + 1) * 128], lhsT=S32[:, j, :], rhs=cf("ident"))
                    kb.cp(SO[0:64], _view(bo_[0:64, 0:512], [4, 128]), eng="dve")
                    dst_ = O["wkv_s"][l, gi] if is_s else O["wkv_p"][l]
                    kb.dma("sp", dst_.rearrange("h v k -> v h k"), SO[0:64].rearrange("p j (h k) -> p (j h) k", h=2))
                    yield
            if not finish:
                return
            kb.tt(p2a[:], YF[:], bc(vec("ln_w")[:, :, None], [128, 4, 64]), ALU.mult)
            yield
            kb.tt(p2a[:], p2a[:], bc(vec("ln_b")[:, :, None], [128, 4, 64]), ALU.add)
            yield
            kb.tt(p2a[:], p2a[:], t_bon[:], ALU.add)
            yield
            kb.tt(YAb[:], p2a[:], t_gg[:], ALU.mult)
            kb.dma("sp", YAB[:, 0:4, tok0:tok0 + n], YAb[:])
            yield

        nblocks = NBLK + 1
        if DBG.get("nblk") is not None:
            nblocks = DBG["nblk"]
        for step in range(nblocks + 3):
            gens = []
            if step < nblocks:
                gens.append(stage1a(step))
            if 0 <= step - 1 < nblocks:
                gens.append(stage1a2(step - 1))
            if 0 <= step - 2 < nblocks:
                gens.append(stage1b(step - 2))
            if 0 <= step - 3 < nblocks:
                if step - 3 == NBLK:
                    ctxs = [base_ctx]
                    ar2 = Arena(ar.base[:, dead_lo:dead_hi], dead_hi - dead_lo)
                    for si in range(2):
                        c2 = dict(BHT=ar2.alloc([4, 128], BF16), KHT=ar2.alloc([4, 128], BF16), VT=ar2.alloc([4, 64], BF16),
                                  ZT=ar2.alloc([4, 64], BF16), UT=ar2.alloc([4, 64], BF16), SQY=ar2.alloc([4, 64], F32),
                                  YC=ar2.alloc([4, 64], F32), YNB=ar2.alloc([4, 128], BF16), ST1=ar2.alloc([4], F32),
                                  ST2=ar2.alloc([4], F32), STM=ar2.alloc([4], F32), STV=ar2.alloc([4], F32),
                                  S32=ar2.alloc([4, 64], F32), SI=ar2.alloc([8, 64], F32), SO=ar2.alloc([4, 128], F32),
                                  SBS=ar2.alloc([4, 64], BF16))
                        ctxs.append(c2)
                    ctxs[0]["bank"] = lambda: kb.bank_of((6, 7))
                    ctxs[1]["bank"] = lambda: kb.bank_of((0, 1, 2))
                    ctxs[2]["bank"] = lambda: kb.bank_of((3, 4, 5))
                    sq = [range(0, 6), range(6, 11), range(11, 16)]
                    sg = [stage2(NBLK, ctx=ctxs[i], seqs=sq[i], finish=False) for i in range(3)]
                    alive = list(sg)
                    while alive:
                        for g in list(alive):
                            try:
                                next(g)
                            except StopIteration:
                                alive.remove(g)
                    for _ in stage2(NBLK, ctx=base_ctx, seqs=[], finish=True):
                        pass
                    continue
                gens.append(stage2(step - 3))
            if DBG.get("no_interleave", False):
                for g in gens[::-1]:
                    for _ in g:
                        pass
                continue
            alive = list(gens)
            while alive:
                for g in list(alive):
                    try:
                        next(g)
                    except StopIteration:
                        alive.remove(g)

    def pass_pool(l):
        ar.reset()
        WPB = ar.alloc([8, 512], BF16)
        WPL = ar.alloc([4, 128], BF16)
        kb.dma("pool", WPB, I["w_in"][l].rearrange("(k p) n -> p k n", p=128)[:, :, APJ:PT])
        kb.dma("pool", WPL, I["w_pool"][l].rearrange("g c d -> c g d"))
        UB = ar.alloc([8, 512], BF16)
        SQ = ar.alloc([8, 512], BF16)
        RS = ar.alloc([512], F32)
        TT = [ar.alloc([512], F32) for _ in range(2)]
        PBH = ar.alloc([4, 527], F32)
        SA = ar.alloc([4, 527], F32)
        SB = ar.alloc([4, 527], F32)
        DP = ar.alloc([4, 512], BF16)
        YBb = ar.alloc([4, 512], BF16)
        PROW = ar.alloc([512], F32)
        POUT = ar.alloc([512], F32)
        TMPH = ar.alloc([4, 120], F32)
        PS = vec("pool_scale")
        WIN_ = (2, 4, 8, 16)
        kb.memset(PBH[:], 0.0)

        def wsum(x, sa, sb_, L, nd):
            def sl(v, g0, g1, a, b_):
                return v[:, g0:g1, a:b_] if nd == 3 else v[:, g0:g1, :, a:b_]
            kb.tt(sl(sa, 0, 4, 1, L), sl(x, 0, 4, 1, L), sl(x, 0, 4, 0, L - 1), ALU.add)
            kb.tt(sl(sb_, 1, 4, 3, L), sl(sa, 1, 4, 3, L), sl(sa, 1, 4, 1, L - 2), ALU.add)
            kb.tt(sl(sa, 2, 4, 7, L), sl(sb_, 2, 4, 7, L), sl(sb_, 2, 4, 3, L - 4), ALU.add)
            kb.tt(sl(sb_, 3, 4, 15, L), sl(sa, 3, 4, 15, L), sl(sa, 3, 4, 7, L - 8), ALU.add)
            return [sa, sb_, sa, sb_]

        for (t0, n) in TILES[:4]:
            modnorm(t0, n, 1, UB, SQ, RS, TT)
            for g in range(4):
                bk = kb.bank()
                for k in range(8):
                    kb.mm(bk[:, 0:n], lhsT=WPB[:, k, g * 128:(g + 1) * 128], rhs=UB[:, k, 0:n], start=(k == 0), stop=(k == 7))
                kb.cp(PBH[:, g, 15:15 + n], bk[:, 0:n], eng="act")
            L = 15 + n
            fin = wsum(PBH, SA, SB, L, 3)
            for g in range(4):
                if t0 == 0:
                    kb.tt(fin[g][:, g, 15:30], fin[g][:, g, 15:30], cf("ratio")[:, g * 15:(g + 1) * 15], ALU.mult)
                kb.stt(DP[:, g, 0:n], fin[g][:, g, 15:L], 1.0 / WIN_[g], PBH[:, g, 15:L], ALU.mult, ALU.subtract)
            for g in range(4):
                bk = kb.bank()
                kb.mm(bk[:, 0:n], lhsT=WPL[:, g, :], rhs=DP[:, g, 0:n])
                kb.ts(YBb[:, g, 0:n], bk[:, 0:n], PS[:, g:g + 1], None, ALU.mult)
            kb.dma("sp", YAB[:, 4:8, t0:t0 + n], YBb[:, :, 0:n])
            kb.cp(TMPH[:, :, 0:15], PBH[:, :, n:n + 15], eng="dve")
            kb.cp(PBH[:, :, 0:15], TMPH[:, :, 0:15], eng="dve")
        bk = kb.bank()
        for g in range(4):
            kb.mm(bk[0:15, g * 128:(g + 1) * 128], lhsT=TMPH[:, g, 0:15], rhs=cf("ident"))
        kb.cp(POUT[0:15, :], bk[0:15, 0:512], eng="dve")
        kb.dma("sp", O["pool_p"][l], POUT[0:15, :])
        PBs = PBH[:, :, 0:304].rearrange("p g (s t) -> p g s t", t=19)
        SAs = SA[:, :, 0:304].rearrange("p g (s t) -> p g s t", t=19)
        SBs = SB[:, :, 0:304].rearrange("p g (s t) -> p g s t", t=19)
        sp_rows = I["spool"][l].rearrange("s i c -> (s i) c")
        for hh in range(2):
            kb.dma("sp", PROW[0:120, :], sp_rows[hh * 120:(hh + 1) * 120, :])
            bk = kb.bank()
            for g in range(4):
                kb.mm(bk[:, g * 120:(g + 1) * 120], lhsT=PROW[0:120, g * 128:(g + 1) * 128], rhs=cf("ident")[0:120, 0:120])
            kb.cp(PBs[:, :, hh * 8:(hh + 1) * 8, 0:15], bk[:, 0:480].rearrange("p (g s t) -> p g s t", g=4, t=15), eng="dve")
        modnorm(TP, 64, 1, UB, SQ, RS, TT)
        bk = kb.bank()
        for g in range(4):
            for k in range(8):
                kb.mm(bk[:, g * 64:(g + 1) * 64], lhsT=WPB[:, k, g * 128:(g + 1) * 128], rhs=UB[:, k, 0:64], start=(k == 0), stop=(k == 7))
        kb.cp(PBs[:, :, :, 15:19], bk[:, 0:256].rearrange("p (g s t) -> p g s t", g=4, t=4), eng="act")
        fin = wsum(PBs, SAs, SBs, 19, 4)
        fv = [SAs, SBs, SAs, SBs]
        for g in range(4):
            kb.stt(_view(DP[:, g, 0:64], [16, 4]), fv[g][:, g, :, 15:19], 1.0 / WIN_[g], PBs[:, g, :, 15:19], ALU.mult, ALU.subtract)
        bk = kb.bank()
        for g in range(4):
            kb.mm(bk[:, g * 64:(g + 1) * 64], lhsT=WPL[:, g, :], rhs=DP[:, g, 0:64])
        for g in range(4):
            kb.ts(YBb[:, g, 0:64], bk[:, g * 64:(g + 1) * 64], PS[:, g:g + 1], None, ALU.mult)
        kb.dma("sp", YAB[:, 4:8, TP:T], YBb[:, :, 0:64])
        po_rows = O["pool_s"][l].rearrange("s i c -> (s i) c")
        for hh in range(2):
            kb.cp(TMPH[:].rearrange("p g (s t) -> p g s t", t=15), PBs[:, :, hh * 8:(hh + 1) * 8, 4:19], eng="dve")
            bk = kb.bank()
            for g in range(4):
                kb.mm(bk[0:120, g * 128:(g + 1) * 128], lhsT=TMPH[:, g, :], rhs=cf("ident"))
            kb.cp(POUT[0:120, :], bk[0:120, 0:512], eng="dve")
            kb.dma("sp", po_rows[hh * 120:(hh + 1) * 120, :], POUT[0:120, :])

    def pass_b(l):
        ar.reset()
        WGT = ar.alloc([8, 2048], BF16)
        WBA = ar.alloc([4, 1024], BF16)
        WBB = ar.alloc([4, 1024], BF16)
        WOT = ar.alloc([8, 1024], BF16)
        wg_ = I["w_gate"][l].rearrange("(k p) n -> p k n", p=128)
        wa_ = I["w_br_a"][l].rearrange("(j p) n -> p j n", p=128)
        wb_ = I["w_br_b"][l].rearrange("(j p) n -> p j n", p=128)
        for m in range(0, 8, 2):
            c0, c1 = m * 128, (m + 2) * 128
            kb.dma("pool", WGT[:, :, c0:c1], wg_[:, :, c0:c1])
            kb.dma("pool", WGT[:, :, 1024 + c0:1024 + c1], wg_[:, :, 1024 + c0:1024 + c1])
            kb.dma("pool", WBA[:, :, c0:c1], wa_[:, :, c0:c1])
            kb.dma("pool", WBB[:, :, c0:c1], wb_[:, :, c0:c1])
        kb.dma("pool", WOT, I["w_out"][l].rearrange("(k p) n -> p k n", p=128))
        YT = ar.alloc([8, 512], BF16)
        UB = ar.alloc([8, 512], BF16)
        SQ = ar.alloc([8, 512], BF16)
        RS = ar.alloc([512], F32)
        TT = [ar.alloc([512], F32) for _ in range(2)]
        GA = ar.alloc([512], F32); GB = ar.alloc([512], F32); T1 = ar.alloc([512], F32); T2 = ar.alloc([512], F32)
        MG = ar.alloc([8, 512], BF16)
        BGv = vec("b_gate")
        for (t0, n) in TILES:
            kb.dma("sp", YT[:, :, 0:n], YAB[:, :, t0:t0 + n])
            modnorm(t0, n, 1, UB, SQ, RS, TT)
            for m in range(8):
                ba = kb.bank(); bb = kb.bank(); bc_ = kb.bank(); bd = kb.bank()
                for k in range(8):
                    kb.mm(ba[:, 0:n], lhsT=WGT[:, k, m * 128:(m + 1) * 128], rhs=UB[:, k, 0:n], start=(k == 0), stop=(k == 7))
                for k in range(8):
                    kb.mm(bb[:, 0:n], lhsT=WGT[:, k, 1024 + m * 128:1024 + (m + 1) * 128], rhs=UB[:, k, 0:n], start=(k == 0), stop=(k == 7))
                for j in range(4):
                    kb.mm(bc_[:, 0:n], lhsT=WBA[:, j, m * 128:(m + 1) * 128], rhs=YT[:, j, 0:n], start=(j == 0), stop=(j == 3))
                for j in range(4):
                    kb.mm(bd[:, 0:n], lhsT=WBB[:, j, m * 128:(m + 1) * 128], rhs=YT[:, 4 + j, 0:n], start=(j == 0), stop=(j == 3))
                kb.act(GA[:, 0:n], ba[:, 0:n], AF.Sigmoid, bias=BGv[:, m:m + 1], scale=1.0)
                kb.act(GB[:, 0:n], bb[:, 0:n], AF.Sigmoid, bias=BGv[:, 8 + m:9 + m], scale=1.0)
                kb.tt(T1[:, 0:n], bc_[:, 0:n], GA[:, 0:n], ALU.mult)
                kb.tt(T2[:, 0:n], bd[:, 0:n], GB[:, 0:n], ALU.mult)
                kb.tt(MG[:, m, 0:n], T1[:, 0:n], T2[:, 0:n], ALU.add)
            for m2 in range(8):
                bo = kb.bank()
                for m in range(8):
                    kb.mm(bo[:, 0:n], lhsT=WOT[:, m, m2 * 128:(m2 + 1) * 128], rhs=MG[:, m, 0:n], start=(m == 0), stop=(m == 7))
                resid(m2, t0, n, bo, 1)

    def mixer(l, only_a=False):
        ar.log = []
        pass_a(l)
        if only_a:
            DBG["passA_log"] = list(ar.log)
            return
        DBG["marks"].append(("pool%d" % l, getattr(kb, "nsl", 0), len(kb.S.ops)))
        pass_pool(l)
        DBG["marks"].append(("passB%d" % l, getattr(kb, "nsl", 0), len(kb.S.ops)))
        pass_b(l)

    return mixer


def _shard_inputs(inputs):
    maps = []
    shared = {n: np.ascontiguousarray(np.asarray(inputs[n], dtype=np.float32)) for n in WNAMES}
    xp = np.asarray(inputs["x_prompt"], dtype=np.float32)
    xs = np.asarray(inputs["x_sample"], dtype=np.float32)
    cp_ = np.asarray(inputs["c_prompt"], dtype=np.float32)
    cs = np.asarray(inputs["c_sample"], dtype=np.float32)
    swkv = np.asarray(inputs["state_wkv"], dtype=np.float32)
    ssh = np.asarray(inputs["state_shift"], dtype=np.float32)
    spl = np.asarray(inputs["state_pool"], dtype=np.float32)
    for i in range(NCORES):
        sl = slice(NSEQ * i, NSEQ * (i + 1))
        m = dict(shared)
        m["xin"] = np.ascontiguousarray(np.concatenate([xp[i], xs[sl].reshape(TS, D)], axis=0))
        m["cin"] = np.ascontiguousarray(np.concatenate([cp_[i:i + 1], cs[sl]], axis=0))
        m["swkv"] = np.ascontiguousarray(swkv[:, sl])
        m["sshift"] = np.ascontiguousarray(ssh[:, sl, 0, :])
        m["spool"] = np.ascontiguousarray(spl[:, sl])
        m["consts"] = CONSTS_NP
        maps.append(m)
    return maps


_NC_CACHE = {}
DBG = {}


def kernel(**inputs):
    if "nc" not in _NC_CACHE:
        _NC_CACHE["nc"] = build_program()
    nc = _NC_CACHE["nc"]
    maps = _shard_inputs(inputs)
    res = run_bass_kernel_spmd(nc, maps, core_ids=list(range(NCORES)))
    R = res.results
    y_p = np.stack([R[i]["y"][:TP] for i in range(NCORES)], axis=0)
    y_s = np.concatenate([R[i]["y"][TP:].reshape(NSEQ, DEC, D) for i in range(NCORES)], axis=0)
    wkv_p = np.stack([R[i]["wkv_p"] for i in range(NCORES)], axis=1)
    shift_p = np.stack([R[i]["shift_p"] for i in range(NCORES)], axis=1)[:, :, None, :]
    pool_p = np.stack([R[i]["pool_p"] for i in range(NCORES)], axis=1)
    wkv_s = np.concatenate([R[i]["wkv_s"] for i in range(NCORES)], axis=1)
    shift_s = np.concatenate([R[i]["shift_s"] for i in range(NCORES)], axis=1)[:, :, None, :]
    pool_s = np.concatenate([R[i]["pool_s"] for i in range(NCORES)], axis=1)
    f = lambda a: np.ascontiguousarray(a, dtype=np.float32)
    return (f(y_p), f(y_s), f(wkv_p), f(shift_p), f(pool_p), f(wkv_s), f(shift_s), f(pool_s))
```
